# Optimizing a Trainium2 kernel written in Bass

```python
import math
import jax, jax.numpy as jnp
from jax import lax
import numpy as np

D_MODEL = 2048
BATCH = 4
SEQ = 4096
DEPTH = 2

PL_DIM = 256
D_FF = 4 * D_MODEL
NORM_EPS = 1e-6
N_EVEN = (DEPTH + 1) // 2
N_ODD = DEPTH // 2
S5_WIDTH = D_MODEL // 4
S5_GROUP = 16
S5_GROUPS = S5_WIDTH // S5_GROUP
S5_STATE = 64
SSD_WIDTH = D_MODEL - S5_WIDTH
SSD_HEAD_DIM = 64
SSD_HEADS = SSD_WIDTH // SSD_HEAD_DIM
SSD_GROUPS = 4
SSD_STATE = 128
SSD_CONV = 4
SSD_CHUNK = 128
SSD_CONV_DIM = SSD_WIDTH + 2 * SSD_GROUPS * SSD_STATE
EVEN_IN = S5_WIDTH + SSD_WIDTH + SSD_CONV_DIM + SSD_HEADS
EVEN_MIX = S5_WIDTH + SSD_WIDTH
RWKV_WIDTH = D_MODEL // 2
RWKV_HEAD_DIM = 64
RWKV_HEADS = RWKV_WIDTH // RWKV_HEAD_DIM
RWKV_DECAY_LORA = 96
RWKV_AAA_LORA = 96
RWKV_GATE_LORA = 256
RWKV_GN_EPS = 64e-5
RWKV_IN = 3 * RWKV_WIDTH + RWKV_DECAY_LORA + RWKV_AAA_LORA + RWKV_GATE_LORA
LRU_WIDTH = D_MODEL - RWKV_WIDTH
LRU_BLOCKS = 16
LRU_BLOCK = LRU_WIDTH // LRU_BLOCKS
LRU_CONV = 4
LRU_C = 8.0
ODD_IN = RWKV_IN + 2 * LRU_WIDTH
ODD_MIX = RWKV_WIDTH + LRU_WIDTH

kernel_name = 'hybrid_s5_ssd_rwkv7_rglru_trunk'


def rmsnorm(x, g):
    xf = x.astype(jnp.float32)
    y = xf * lax.rsqrt(jnp.mean(xf * xf, axis=-1, keepdims=True) + NORM_EPS)
    return (y * g.astype(jnp.float32)).astype(x.dtype)


def causal_dwconv(x, w, b):
    k = w.shape[0]
    y = lax.conv_general_dilated(x, w.astype(x.dtype)[:, None, :], window_strides=(1,),
                                 padding=[(k - 1, 0)], dimension_numbers=('NWC', 'WIO', 'NWC'),
                                 feature_group_count=x.shape[-1])
    return y + b.astype(x.dtype)


def token_shift(x):
    return jnp.pad(x, ((0, 0), (1, 0), (0, 0)))[:, :-1]


def s5_mixer(u, lam_re, lam_im, log_step, b_re, b_im, c_re, c_im, d_skip, glu_w, glu_b):
    f32 = jnp.float32
    bsz, seq, _ = u.shape
    uf = u.astype(f32)
    ug = uf.reshape(bsz, seq, S5_GROUPS, S5_GROUP)
    step = jnp.exp(log_step.astype(f32))[:, None]
    lr, li = lam_re.astype(f32), lam_im.astype(f32)
    mag = jnp.exp(lr * step)
    abar_re, abar_im = mag * jnp.cos(li * step), mag * jnp.sin(li * step)
    den = lr * lr + li * li
    nr = abar_re - 1.0
    coef_re = ((nr * lr + abar_im * li) / den)[..., None]
    coef_im = ((abar_im * lr - nr * li) / den)[..., None]
    br, bi = b_re.astype(f32), b_im.astype(f32)
    bbar_re = coef_re * br - coef_im * bi
    bbar_im = coef_re * bi + coef_im * br
    bu_re = jnp.einsum('bsgh,gph->bsgp', ug, bbar_re)
    bu_im = jnp.einsum('bsgh,gph->bsgp', ug, bbar_im)
    a_re = jnp.broadcast_to(abar_re, bu_re.shape)
    a_im = jnp.broadcast_to(abar_im, bu_re.shape)

    def combine(e1, e2):
        a1r, a1i, b1r, b1i = e1
        a2r, a2i, b2r, b2i = e2
        return (a2r * a1r - a2i * a1i, a2r * a1i + a2i * a1r,
                a2r * b1r - a2i * b1i + b2r, a2r * b1i + a2i * b1r + b2i)

    _, _, xr, xi = lax.associative_scan(combine, (a_re, a_im, bu_re, bu_im), axis=1)
    y = (jnp.einsum('ghp,bsgp->bsgh', c_re.astype(f32), xr)
         - jnp.einsum('ghp,bsgp->bsgh', c_im.astype(f32), xi))
    y = y.reshape(bsz, seq, S5_WIDTH) + d_skip.astype(f32) * uf
    act = jax.nn.gelu(y)
    return act * jax.nn.sigmoid(act @ glu_w.astype(f32) + glu_b.astype(f32))


def ssd_chunked(x, da, bm, cm):
    bsz, seq, nh, hd = x.shape
    nc, L, g = seq // SSD_CHUNK, SSD_CHUNK, SSD_GROUPS
    j = nh // g
    x = x.reshape(bsz, nc, L, g, j, hd)
    bm = bm.reshape(bsz, nc, L, g, SSD_STATE)
    cm = cm.reshape(bsz, nc, L, g, SSD_STATE)
    a_cum = jnp.cumsum(da.reshape(bsz, nc, L, g, j).transpose(0, 3, 4, 1, 2), axis=-1)
    mask = jnp.tril(jnp.ones((L, L), dtype=bool))
    seg = a_cum[..., :, None] - a_cum[..., None, :]
    decay = jnp.exp(jnp.where(mask, seg, -jnp.inf))
    scores = jnp.einsum('bclgn,bcsgn->bgcls', cm, bm)
    y_diag = jnp.einsum('bgjcls,bcsgjp->bclgjp', scores[:, :, None] * decay, x)
    decay_states = jnp.exp(a_cum[..., -1:] - a_cum).transpose(0, 3, 4, 1, 2)[..., None]
    states = jnp.einsum('bclgn,bclgjp->bcgjpn', bm, x * decay_states)
    chunk_decay = jnp.exp(a_cum[..., -1])

    def step(carry, inp):
        st, dec = inp
        return carry * dec[..., None, None] + st, carry

    init = jnp.zeros((bsz, g, j, hd, SSD_STATE), x.dtype)
    _, prev = lax.scan(step, init, (jnp.moveaxis(states, 1, 0), jnp.moveaxis(chunk_decay, -1, 0)))
    prev = jnp.moveaxis(prev, 0, 1)
    y_off = (jnp.einsum('bclgn,bcgjpn->bclgjp', cm, prev)
             * jnp.exp(a_cum).transpose(0, 3, 4, 1, 2)[..., None])
    return (y_diag + y_off).reshape(bsz, seq, nh, hd)


def ssd_mixer(z, xbc, dt_raw, conv_w, conv_b, dt_bias, a_log, d_skip, norm_g):
    f32 = jnp.float32
    bsz, seq, _ = z.shape
    xbc = jax.nn.silu(causal_dwconv(xbc, conv_w, conv_b).astype(f32))
    xs, bm, cm = jnp.split(xbc, [SSD_WIDTH, SSD_WIDTH + SSD_GROUPS * SSD_STATE], axis=-1)
    xs = xs.reshape(bsz, seq, SSD_HEADS, SSD_HEAD_DIM)
    bm = bm.reshape(bsz, seq, SSD_GROUPS, SSD_STATE)
    cm = cm.reshape(bsz, seq, SSD_GROUPS, SSD_STATE)
    dt = jax.nn.softplus(dt_raw.astype(f32) + dt_bias.astype(f32))
    da = dt * (-jnp.exp(a_log.astype(f32)))
    y = ssd_chunked(xs * dt[..., None], da, bm, cm) + xs * d_skip.astype(f32)[:, None]
    y = y.reshape(bsz, seq, SSD_WIDTH) * jax.nn.silu(z.astype(f32))
    y = y.reshape(bsz, seq, SSD_GROUPS, SSD_WIDTH // SSD_GROUPS)
    y = y * lax.rsqrt(jnp.mean(y * y, axis=-1, keepdims=True) + NORM_EPS)
    return y.reshape(bsz, seq, SSD_WIDTH) * norm_g.astype(f32)


def rwkv7_mixer(f, mu, w0, w_up, a0, a_up, g_up, k_k, k_a, r_k, ln_g, ln_b):
    f32 = jnp.float32
    f = f.astype(f32)
    f = f + (token_shift(f) - f) * mu.astype(f32)
    W = RWKV_WIDTH
    r, k, v, wl, al, gl = jnp.split(
        f, [W, 2 * W, 3 * W, 3 * W + RWKV_DECAY_LORA, 3 * W + RWKV_DECAY_LORA + RWKV_AAA_LORA], axis=-1)
    w = -jax.nn.softplus(-(w0.astype(f32) + jnp.tanh(wl) @ w_up.astype(f32))) - 0.5
    decay = jnp.exp(-jnp.exp(w))
    a = jax.nn.sigmoid(a0.astype(f32) + al @ a_up.astype(f32))
    g = jax.nn.sigmoid(gl) @ g_up.astype(f32)
    kk = k * k_k.astype(f32)
    k = k * (1.0 + (a - 1.0) * k_a.astype(f32))
    bsz, seq, _ = f.shape
    hs = lambda t: t.reshape(bsz, seq, RWKV_HEADS, RWKV_HEAD_DIM)
    r, k, v, kk, a, decay = hs(r), hs(k), hs(v), hs(kk), hs(a), hs(decay)
    kk = kk * lax.rsqrt(jnp.maximum(jnp.sum(kk * kk, axis=-1, keepdims=True), 1e-24))

    def step(state, inp):
        r_t, w_t, k_t, v_t, kk_t, a_t = inp
        sa = jnp.einsum('bhvk,bhk->bhv', state, -kk_t)
        state = (state * w_t[:, :, None, :] + sa[..., None] * (kk_t * a_t)[:, :, None, :]
                 + v_t[..., None] * k_t[:, :, None, :])
        return state, jnp.einsum('bhvk,bhk->bhv', state, r_t)

    tm = lambda t: jnp.moveaxis(t, 1, 0)
    init = jnp.zeros((bsz, RWKV_HEADS, RWKV_HEAD_DIM, RWKV_HEAD_DIM), f32)
    _, y = lax.scan(step, init, (tm(r), tm(decay), tm(k), tm(v), tm(kk), tm(a)))
    y = jnp.moveaxis(y, 0, 1)
    mean = jnp.mean(y, axis=-1, keepdims=True)
    var = jnp.mean(jnp.square(y - mean), axis=-1, keepdims=True)
    y = ((y - mean) * lax.rsqrt(var + RWKV_GN_EPS)).reshape(bsz, seq, W)
    y = y * ln_g.astype(f32) + ln_b.astype(f32)
    bonus = jnp.sum(r * k * r_k.astype(f32), axis=-1, keepdims=True) * v
    return (y + bonus.reshape(bsz, seq, W)) * g


def rglru_mixer(xl, gl, conv_w, conv_b, w_a, b_a, w_x, b_x, lam):
    f32 = jnp.float32
    bsz, seq, _ = xl.shape
    xc = causal_dwconv(xl, conv_w, conv_b).astype(f32)
    xb = xc.reshape(bsz, seq, LRU_BLOCKS, LRU_BLOCK)
    gate_r = jax.nn.sigmoid(jnp.einsum('bshi,hij->bshj', xb, w_a.astype(f32)) + b_a.astype(f32))
    gate_i = jax.nn.sigmoid(jnp.einsum('bshi,hij->bshj', xb, w_x.astype(f32)) + b_x.astype(f32))
    log_a = -LRU_C * gate_r * jax.nn.softplus(-lam.astype(f32))
    a = jnp.exp(log_a)
    mult = jnp.sqrt(jnp.maximum(-jnp.expm1(2.0 * log_a), 0.0))
    mult = mult.at[:, 0].set(1.0)
    bx = xb * gate_i * mult

    def combine(e1, e2):
        a1, b1 = e1
        a2, b2 = e2
        return a1 * a2, a2 * b1 + b2

    _, hseq = lax.associative_scan(combine, (a, bx), axis=1)
    return hseq.reshape(bsz, seq, LRU_WIDTH) * jax.nn.gelu(gl.astype(f32))


def even_mixer(hn, in_proj, out_proj, lam_re, lam_im, log_step, b_re, b_im, c_re, c_im, s5_d,
               glu_w, glu_b, conv_w, conv_b, dt_bias, a_log, ssd_d, ssd_norm):
    proj = hn @ in_proj
    u, z, xbc, dt_raw = jnp.split(
        proj, [S5_WIDTH, S5_WIDTH + SSD_WIDTH, S5_WIDTH + SSD_WIDTH + SSD_CONV_DIM], axis=-1)
    y_a = s5_mixer(u, lam_re, lam_im, log_step, b_re, b_im, c_re, c_im, s5_d, glu_w, glu_b)
    y_b = ssd_mixer(z, xbc, dt_raw, conv_w, conv_b, dt_bias, a_log, ssd_d, ssd_norm)
    y = jnp.concatenate([y_a, y_b], axis=-1).astype(hn.dtype)
    return y @ out_proj


def odd_mixer(hn, in_proj, out_proj, mu, w0, w_up, a0, a_up, g_up, k_k, k_a, r_k, ln_g, ln_b,
              conv_w, conv_b, w_a, b_a, w_x, b_x, lam):
    proj = hn @ in_proj
    rw, xl, gl = jnp.split(proj, [RWKV_IN, RWKV_IN + LRU_WIDTH], axis=-1)
    y_c = rwkv7_mixer(rw, mu, w0, w_up, a0, a_up, g_up, k_k, k_a, r_k, ln_g, ln_b)
    y_d = rglru_mixer(xl, gl, conv_w, conv_b, w_a, b_a, w_x, b_x, lam)
    y = jnp.concatenate([y_c, y_d], axis=-1).astype(hn.dtype)
    return y @ out_proj


def squared_relu_mlp(h, w1, w2):
    return jnp.square(jax.nn.relu(h @ w1)) @ w2


def setup_inputs(seed: int = 0) -> dict:
    key = jax.random.key(seed)
    ks = iter(jax.random.split(key, 64))
    f32 = jnp.float32

    def nrm(shape, scale):
        return scale * jax.random.normal(next(ks), shape, f32)

    def unif(shape, lo, hi):
        return jax.random.uniform(next(ks), shape, f32, lo, hi)

    D = D_MODEL
    ne, no = N_EVEN, N_ODD
    x = nrm((BATCH, SEQ, D), 1.0)
    p = nrm((DEPTH, BATCH, SEQ, PL_DIM), 1.0)
    norm_mix = 1.0 + nrm((DEPTH, D), 0.02)
    norm_ffn = 1.0 + nrm((DEPTH, D), 0.02)
    norm_pl = 1.0 + nrm((DEPTH, D), 0.02)
    mlp_w1 = nrm((DEPTH, D, D_FF), D ** -0.5)
    mlp_w2 = nrm((DEPTH, D_FF, D), D_FF ** -0.5)
    pl_proj = nrm((DEPTH, PL_DIM, D), PL_DIM ** -0.5)
    pl_gate = nrm((DEPTH, D, D), D ** -0.5)
    e_in_proj = nrm((ne, D, EVEN_IN), D ** -0.5)
    e_out_proj = nrm((ne, EVEN_MIX, D), EVEN_MIX ** -0.5)
    s5_lam_re = -0.5 + nrm((ne, S5_GROUPS, S5_STATE), 0.01)
    s5_lam_im = math.pi * jnp.arange(S5_STATE, dtype=f32) + nrm((ne, S5_GROUPS, S5_STATE), 0.01)
    s5_log_step = unif((ne, S5_GROUPS), math.log(1e-3), math.log(1e-1))
    s5_b_re = nrm((ne, S5_GROUPS, S5_STATE, S5_GROUP), (2 * S5_GROUP) ** -0.5)
    s5_b_im = nrm((ne, S5_GROUPS, S5_STATE, S5_GROUP), (2 * S5_GROUP) ** -0.5)
    s5_c_re = nrm((ne, S5_GROUPS, S5_GROUP, S5_STATE), (2 * S5_STATE) ** -0.5)
    s5_c_im = nrm((ne, S5_GROUPS, S5_GROUP, S5_STATE), (2 * S5_STATE) ** -0.5)
    s5_d = nrm((ne, S5_WIDTH), 0.5)
    s5_glu_w = nrm((ne, S5_WIDTH, S5_WIDTH), S5_WIDTH ** -0.5)
    s5_glu_b = nrm((ne, S5_WIDTH), 0.02)
    ssd_conv_w = nrm((ne, SSD_CONV, SSD_CONV_DIM), SSD_CONV ** -0.5)
    ssd_conv_b = nrm((ne, SSD_CONV_DIM), 0.02)
    dt0 = jnp.exp(unif((ne, SSD_HEADS), math.log(1e-3), math.log(1e-1)))
    ssd_dt_bias = dt0 + jnp.log(-jnp.expm1(-dt0))
    ssd_a_log = jnp.log(unif((ne, SSD_HEADS), 1.0, 16.0))
    ssd_d = 1.0 + nrm((ne, SSD_HEADS), 0.02)
    ssd_norm = 1.0 + nrm((ne, SSD_WIDTH), 0.02)
    o_in_proj = nrm((no, D, ODD_IN), D ** -0.5)
    o_out_proj = nrm((no, ODD_MIX, D), ODD_MIX ** -0.5)
    rwkv_mu = unif((no, RWKV_IN), 0.0, 1.0)
    rwkv_w0 = jnp.linspace(-6.0, -1.0, RWKV_WIDTH, dtype=f32) + nrm((no, RWKV_WIDTH), 0.1)
    rwkv_w_up = nrm((no, RWKV_DECAY_LORA, RWKV_WIDTH), 0.5 * RWKV_DECAY_LORA ** -0.5)
    rwkv_a0 = nrm((no, RWKV_WIDTH), 0.1)
    rwkv_a_up = nrm((no, RWKV_AAA_LORA, RWKV_WIDTH), 0.5 * RWKV_AAA_LORA ** -0.5)
    rwkv_g_up = nrm((no, RWKV_GATE_LORA, RWKV_WIDTH), RWKV_GATE_LORA ** -0.5)
    rwkv_k_k = 0.85 + nrm((no, RWKV_WIDTH), 0.02)
    rwkv_k_a = 1.0 + nrm((no, RWKV_WIDTH), 0.02)
    rwkv_r_k = nrm((no, RWKV_HEADS, RWKV_HEAD_DIM), 0.1)
    rwkv_ln_g = 1.0 + nrm((no, RWKV_WIDTH), 0.02)
    rwkv_ln_b = nrm((no, RWKV_WIDTH), 0.02)
    lru_conv_w = nrm((no, LRU_CONV, LRU_WIDTH), LRU_CONV ** -0.5)
    lru_conv_b = nrm((no, LRU_WIDTH), 0.02)
    lru_w_a = nrm((no, LRU_BLOCKS, LRU_BLOCK, LRU_BLOCK), LRU_BLOCK ** -0.5)
    lru_b_a = nrm((no, LRU_BLOCKS, LRU_BLOCK), 0.02)
    lru_w_x = nrm((no, LRU_BLOCKS, LRU_BLOCK, LRU_BLOCK), LRU_BLOCK ** -0.5)
    lru_b_x = nrm((no, LRU_BLOCKS, LRU_BLOCK), 0.02)
    a_pow = unif((no, LRU_BLOCKS, LRU_BLOCK), 0.9, 0.999)
    a_base = a_pow ** (1.0 / LRU_C)
    lru_lam = jnp.log(a_base) - jnp.log1p(-a_base)
    norm_final = 1.0 + nrm((D,), 0.02)
    return {'x': x, 'p': p, 'norm_mix': norm_mix, 'norm_ffn': norm_ffn, 'norm_pl': norm_pl,
            'mlp_w1': mlp_w1, 'mlp_w2': mlp_w2, 'pl_proj': pl_proj, 'pl_gate': pl_gate,
            'e_in_proj': e_in_proj, 'e_out_proj': e_out_proj,
            's5_lam_re': s5_lam_re, 's5_lam_im': s5_lam_im, 's5_log_step': s5_log_step,
            's5_b_re': s5_b_re, 's5_b_im': s5_b_im, 's5_c_re': s5_c_re, 's5_c_im': s5_c_im,
            's5_d': s5_d, 's5_glu_w': s5_glu_w, 's5_glu_b': s5_glu_b,
            'ssd_conv_w': ssd_conv_w, 'ssd_conv_b': ssd_conv_b, 'ssd_dt_bias': ssd_dt_bias,
            'ssd_a_log': ssd_a_log, 'ssd_d': ssd_d, 'ssd_norm': ssd_norm,
            'o_in_proj': o_in_proj, 'o_out_proj': o_out_proj,
            'rwkv_mu': rwkv_mu, 'rwkv_w0': rwkv_w0, 'rwkv_w_up': rwkv_w_up, 'rwkv_a0': rwkv_a0,
            'rwkv_a_up': rwkv_a_up, 'rwkv_g_up': rwkv_g_up, 'rwkv_k_k': rwkv_k_k, 'rwkv_k_a': rwkv_k_a,
            'rwkv_r_k': rwkv_r_k, 'rwkv_ln_g': rwkv_ln_g, 'rwkv_ln_b': rwkv_ln_b,
            'lru_conv_w': lru_conv_w, 'lru_conv_b': lru_conv_b, 'lru_w_a': lru_w_a, 'lru_b_a': lru_b_a,
            'lru_w_x': lru_w_x, 'lru_b_x': lru_b_x, 'lru_lam': lru_lam, 'norm_final': norm_final}


def reference(x, p, norm_mix, norm_ffn, norm_pl, mlp_w1, mlp_w2, pl_proj, pl_gate,
              e_in_proj, e_out_proj, s5_lam_re, s5_lam_im, s5_log_step, s5_b_re, s5_b_im,
              s5_c_re, s5_c_im, s5_d, s5_glu_w, s5_glu_b, ssd_conv_w, ssd_conv_b, ssd_dt_bias,
              ssd_a_log, ssd_d, ssd_norm, o_in_proj, o_out_proj, rwkv_mu, rwkv_w0, rwkv_w_up,
              rwkv_a0, rwkv_a_up, rwkv_g_up, rwkv_k_k, rwkv_k_a, rwkv_r_k, rwkv_ln_g, rwkv_ln_b,
              lru_conv_w, lru_conv_b, lru_w_a, lru_b_a, lru_w_x, lru_b_x, lru_lam, norm_final):
    h = x
    for i in range(DEPTH):
        hn = rmsnorm(h, norm_mix[i])
        j = i // 2
        if i % 2 == 0:
            mix = even_mixer(hn, e_in_proj[j], e_out_proj[j], s5_lam_re[j], s5_lam_im[j],
                             s5_log_step[j], s5_b_re[j], s5_b_im[j], s5_c_re[j], s5_c_im[j],
                             s5_d[j], s5_glu_w[j], s5_glu_b[j], ssd_conv_w[j], ssd_conv_b[j],
                             ssd_dt_bias[j], ssd_a_log[j], ssd_d[j], ssd_norm[j])
        else:
            mix = odd_mixer(hn, o_in_proj[j], o_out_proj[j], rwkv_mu[j], rwkv_w0[j], rwkv_w_up[j],
                            rwkv_a0[j], rwkv_a_up[j], rwkv_g_up[j], rwkv_k_k[j], rwkv_k_a[j],
                            rwkv_r_k[j], rwkv_ln_g[j], rwkv_ln_b[j], lru_conv_w[j], lru_conv_b[j],
                            lru_w_a[j], lru_b_a[j], lru_w_x[j], lru_b_x[j], lru_lam[j])
        h = h + mix
        h = h + squared_relu_mlp(rmsnorm(h, norm_ffn[i]), mlp_w1[i], mlp_w2[i])
        gate = jax.nn.sigmoid(rmsnorm(h, norm_pl[i]) @ pl_gate[i])
        h = h + gate * (p[i] @ pl_proj[i])
    return rmsnorm(h, norm_final)
```

```python
import math, contextlib
import numpy as np
import concourse.bass as bass
import concourse.mybir as mybir
from concourse.bass_utils import run_bass_kernel_spmd

F32 = mybir.dt.float32
BF16 = mybir.dt.bfloat16
AF = mybir.ActivationFunctionType
ALU = mybir.AluOpType
AX = mybir.AxisListType

SAME_ENG_SYNC = True
CC_INC = 1


class Buf:
    __slots__ = ("name", "wconds", "rconds", "wsem", "wcount", "rsem", "rcount")

    ALL = []

    def __init__(self, name):
        Buf.ALL.append(self)
        self.name = name
        self.wconds = {}
        self.rconds = {}
        self.wsem = None
        self.wcount = 0
        self.rsem = None
        self.rcount = 0


class V:
    __slots__ = ("ap", "buf")

    def __init__(self, ap, buf):
        self.ap = ap
        self.buf = buf

    def __getitem__(self, key):
        return V(self.ap[key], self.buf)

    def re(self, s, **kw):
        return V(self.ap.rearrange(s, **kw), self.buf)

    def bc(self, shape):
        return V(self.ap.to_broadcast(shape), self.buf)

    def bitcast(self, dt):
        return V(self.ap.bitcast(dt), self.buf)

    @property
    def shape(self):
        return self.ap.shape


class Prog:
    ENG = ("pe", "dve", "act", "pool", "sp")

    def __init__(self, nc, stack):
        self.nc = nc
        Buf.ALL = []
        self.stack = stack
        self.engobj = {"pe": nc.tensor, "dve": nc.vector, "act": nc.scalar,
                       "pool": nc.gpsimd, "sp": nc.sync}
        self.q = {e: [] for e in self.ENG}
        self.cnt = {e: 0 for e in self.ENG}
        self.sems = {}
        self.nsem = 0
        for e in self.ENG:
            self.sems[("eng", e)] = self._newsem("c_" + e)
        self.known = {e: {} for e in self.ENG}
        self.uid = 0
        self.stacks = [stack]
        self.free_sems = []
        self.scope_sems = [[]]
        self.semval = {}
        self.ring_store = {}

    def _newsem(self, name):
        self.nsem += 1
        return self.stack.enter_context(self.nc.semaphore(name + "_%d" % self.nsem))

    def _dma_sem(self, key, name):
        if key not in self.sems:
            if self.free_sems:
                h, v = self.free_sems.pop()
            else:
                h, v = self._newsem("d"), 0
            self.sems[key] = h
            self.semval[key] = v
            self.scope_sems[-1].append(key)
        return self.sems[key]

    @contextlib.contextmanager
    def scope(self):
        es = contextlib.ExitStack()
        self.stacks.append(es)
        self.scope_sems.append([])
        mark = set(self.ring_store.keys())
        try:
            yield
        finally:
            self.barrier()
            for k in list(self.ring_store.keys()):
                if k not in mark:
                    del self.ring_store[k]
            for key in self.scope_sems.pop():
                self.free_sems.append((self.sems.pop(key), self.semval.pop(key)))
                for e in self.ENG:
                    self.known[e].pop(key, None)
            self.stacks.pop()
            es.close()

    def barrier(self):
        conds = {("eng", e): self.cnt[e] for e in self.ENG if self.cnt[e] > 0}
        for key, v in self.semval.items():
            if v > 0:
                conds[key] = v
        for e in self.ENG:
            waits = {}
            kn = self.known[e]
            for k, val in conds.items():
                if k == ("eng", e):
                    if e == "sp":
                        continue
                if kn.get(k, 0) < val:
                    waits[k] = val
            wl = self._emit_waits(e, waits)

            def thunk(en, wl=wl):
                for s_, v_ in wl:
                    en.wait_ge(s_, v_)
            self.q[e].append(thunk)
        for b in Buf.ALL:
            b.wconds = {}
            b.rconds = {}

    def ring(self, name, n, shape, dt=F32):
        if name not in self.ring_store:
            self.ring_store[name] = [[self.sbuf("%s_%d" % (name, i), shape, dt) for i in range(n)], 0]
        r = self.ring_store[name]
        b = r[0][r[1] % n]
        r[1] += 1
        return b

    def sbuf(self, name, shape, dt=F32):
        self.uid += 1
        t = self.stacks[-1].enter_context(self.nc.sbuf_tensor("%s_u%d" % (name, self.uid), list(shape), dt))
        return V(t.ap() if hasattr(t, "ap") and callable(getattr(t, "ap")) else t[:], Buf(name))

    def psum(self, name, shape, dt=F32):
        t = self.stack.enter_context(self.nc.psum_tensor(name, list(shape), dt))
        return V(t.ap() if hasattr(t, "ap") and callable(getattr(t, "ap")) else t[:], Buf(name))

    def dram(self, name, shape, dt=F32, kind="Internal"):
        t = self.nc.dram_tensor(name, list(shape), dt, kind=kind)
        return V(t.ap(), Buf(name))

    def alias(self, v, name):
        return V(v.ap, Buf(name))

    def _need(self, eng, conds, waits):
        kn = self.known[eng]
        for k, val in conds.items():
            if k not in self.sems:
                continue
            if k == ("eng", eng):
                if eng == "pe" or not SAME_ENG_SYNC:
                    continue
            if kn.get(k, 0) >= val:
                continue
            if waits.get(k, 0) < val:
                waits[k] = val

    def _emit_waits(self, eng, waits):
        kn = self.known[eng]
        out = []
        for k, val in waits.items():
            kn[k] = max(kn.get(k, 0), val)
            out.append((self.sems[k], val))
        return out

    def op(self, eng, fn, reads=(), writes=()):
        waits = {}
        for v in reads:
            self._need(eng, v.buf.wconds, waits)
        for v in writes:
            self._need(eng, v.buf.wconds, waits)
            self._need(eng, v.buf.rconds, waits)
        wl = self._emit_waits(eng, waits)
        self.cnt[eng] += 1
        n = self.cnt[eng]
        k = ("eng", eng)
        sem = self.sems[k]

        def thunk(e, wl=wl, fn=fn, sem=sem):
            for s, val in wl:
                e.wait_ge(s, val)
            fn(e).then_inc(sem, 1)
        self.q[eng].append(thunk)
        for v in reads:
            v.buf.rconds[k] = n
        for v in writes:
            v.buf.wconds = {k: n}
            v.buf.rconds = {}
        return n

    def dma(self, queue, out, in_, **kw):
        eng = queue
        waits = {}
        self._need(eng, in_.buf.wconds, waits)
        own_w = ("w", id(out.buf))
        for kk, val in out.buf.wconds.items():
            if kk == own_w:
                continue
            self._need(eng, {kk: val}, waits)
        self._need(eng, out.buf.rconds, waits)
        wl = self._emit_waits(eng, waits)
        b = out.buf
        key = ("w", id(b))
        sem = self._dma_sem(key, b.name)
        self.semval[key] += 16
        val = self.semval[key]
        b.wcount = val
        self._keep = getattr(self, "_keep", [])
        self._keep.append(b)
        rb = in_.buf
        rkey = None

        def thunk(e, wl=wl, sem=sem, o=out.ap, i=in_.ap, kw=kw):
            for s, v_ in wl:
                e.wait_ge(s, v_)
            e.dma_start(out=o, in_=i, **kw).then_inc(sem, 16)
        self.q[eng].append(thunk)
        if own_w in out.buf.wconds or not out.buf.wconds or True:
            newc = {key: val}
            out.buf.wconds = newc
            out.buf.rconds = {}
        in_.buf.rconds[key] = max(in_.buf.rconds.get(key, 0), val)

    def collective(self, kind, out, in_, groups, op=None):
        eng = "pool"
        waits = {}
        self._need(eng, in_.buf.wconds, waits)
        self._need(eng, out.buf.wconds, waits)
        self._need(eng, out.buf.rconds, waits)
        wl = self._emit_waits(eng, waits)
        key = ("cc", id(out.buf))
        if key not in self.sems:
            self.sems[key] = self._newsem("cc")
            self.semval[key] = 0
        self.semval[key] += CC_INC
        val = self.semval[key]
        sem = self.sems[key]
        self._keep = getattr(self, "_keep", [])
        self._keep.append(out.buf)
        op = ALU.bypass if op is None else op

        def thunk(e, wl=wl, sem=sem, o=out.ap, i=in_.ap):
            for s_, v_ in wl:
                e.wait_ge(s_, v_)
            e.collective_compute(kind, op, replica_groups=groups, ins=[i], outs=[o]).then_inc(sem, CC_INC)
        self.q[eng].append(thunk)
        out.buf.wconds = {key: val}
        out.buf.rconds = {}
        in_.buf.rconds[key] = val

    def wait_all(self, eng, views):
        waits = {}
        for v in views:
            self._need(eng, v.buf.wconds, waits)
        wl = self._emit_waits(eng, waits)

        def thunk(e, wl=wl):
            for s, v_ in wl:
                e.wait_ge(s, v_)
        self.q[eng].append(thunk)

    def finish(self):
        nc = self.nc
        with nc.Block() as block:
            @block.tensor
            def _(e):
                for t in self.q["pe"]:
                    t(e)

            @block.vector
            def _(e):
                for t in self.q["dve"]:
                    t(e)

            @block.scalar
            def _(e):
                for t in self.q["act"]:
                    t(e)

            @block.gpsimd
            def _(e):
                for t in self.q["pool"]:
                    t(e)

            @block.sync
            def _(e):
                for t in self.q["sp"]:
                    t(e)

D = 2048
KT_D = 16
EPS = 1e-6
CH = 512
MMDT = BF16


class Shared:
    def __init__(self, P):
        self.P = P
        self.ps = [P.psum("psr%d" % i, [128, 512]) for i in range(8)]
        self.pi = 0
        self.ones = P.sbuf("ones", [128, 128])
        P.op("dve", lambda e: e.memset(self.ones.ap, 1.0), writes=[self.ones])
        self.epsc = P.sbuf("epsc", [128, 1])
        P.op("dve", lambda e: e.memset(self.epsc.ap, EPS), writes=[self.epsc])
        self.rr = {}
        self.castn = 0

    def next_psum(self):
        p = self.ps[self.pi % 8]
        self.pi += 1
        return p

    def ring(self, name, n, shape, dt=F32):
        return self.P.ring(name, n, shape, dt)


def load_cols(P, dst, vec_dram, KT):
    with P.nc.allow_non_contiguous_dma(reason="tiny param vector"):
        pass
    P.dma("sp", dst, vec_dram.re("(k p) -> p k", p=128), allow_slow_non_contiguous=True)


def rmsnorm_T(P, S, src, gcol, out, KT, NT, out_scale_extra=None):
    pss = S.next_psum()
    for k in range(KT):
        sq = S.ring("sq", 3, [128, CH])
        P.op("act", lambda e, sq=sq, k=k: e.activation(out=sq.ap[:, 0:NT], in_=src.ap[:, k, :], func=AF.Square),
             reads=[src], writes=[sq])
        P.op("pe", lambda e, sq=sq, k=k: e.matmul(pss.ap[:, 0:NT], lhsT=S.ones.ap, rhs=sq.ap[:, 0:NT],
                                                 start=(k == 0), stop=(k == KT - 1)),
             reads=[S.ones, sq], writes=[pss])
    rs = S.ring("rstd", 2, [128, CH])
    P.op("act", lambda e: e.activation(out=rs.ap[:, 0:NT], in_=pss.ap[:, 0:NT], func=AF.Sqrt,
                                      bias=S.epsc.ap, scale=1.0 / (KT * 128)),
         reads=[pss, S.epsc], writes=[rs])
    P.op("dve", lambda e: e.reciprocal(out=rs.ap[:, 0:NT], in_=rs.ap[:, 0:NT]), reads=[rs], writes=[rs])
    for k in range(KT):
        P.op("dve", lambda e, k=k: e.scalar_tensor_tensor(out=out.ap[:, k, :], in0=src.ap[:, k, :],
                                                         scalar=gcol.ap[:, k:k + 1], in1=rs.ap[:, 0:NT],
                                                         op0=ALU.mult, op1=ALU.mult),
             reads=[src, gcol, rs], writes=[out])


def stream_mm(P, S, Wt, KT, NCB, rhs_fn, nchunk, NTc, evac, wname="wb", nwb=3, Ms=None, pre=None):
    wbs = {}

    def prep(c):
        wb = S.ring(wname + str(KT), nwb, [128, KT, 128], MMDT)
        for k0 in range(0, KT, 16):
            k1 = min(KT, k0 + 16)
            stg = S.ring("wstg", 3, [128, 16, 128])
            P.dma("sp", stg[:, 0:k1 - k0, :], Wt[c][:, k0:k1, :])
            P.op("dve", lambda e, stg=stg, wb=wb, k0=k0, k1=k1: e.tensor_copy(out=wb.ap[:, k0:k1, :], in_=stg.ap[:, 0:k1 - k0, :]),
                 reads=[stg], writes=[wb])
        wbs[c] = wb

    prep(0)
    for c in range(NCB):
        if c + 1 < NCB:
            prep(c + 1)
        wb = wbs.pop(c)
        M = 128 if Ms is None else Ms[c]
        if pre is not None:
            pre(c)
        for j in range(nchunk):
            ps = S.next_psum()
            for k in range(KT):
                r = rhs_fn(k, j)
                P.op("pe", lambda e, wb=wb, ps=ps, r=r, k=k, M=M: e.matmul(
                    ps.ap[0:M, 0:NTc], lhsT=wb.ap[:, k, 0:M], rhs=r.ap, start=(k == 0), stop=(k == KT - 1)),
                    reads=[wb, r], writes=[ps])
            evac(c, j, ps, M)


def emit_dense_in(P, S, hT, gvec, Wt, NCB, Ms, projT, NT):
    nch = NT // CH
    gcol = P.sbuf("gcolA", [128, KT_D])
    load_cols(P, gcol, gvec, KT_D)
    hn = P.sbuf("hnA", [128, KT_D, NT], MMDT)
    hnj = [P.alias(hn, "hnA_%d" % j) for j in range(nch)]
    for j in range(nch):
        ht = S.ring("htA", 2, [128, KT_D, CH])
        for q in range(8):
            P.dma("sp", ht[:, 2 * q:2 * q + 2, :], hT[q].re("(k p) n -> p k n", p=128)[:, :, j * CH:(j + 1) * CH])
        rmsnorm_T(P, S, ht, gcol, hnj[j][:, :, j * CH:(j + 1) * CH], KT_D, CH)

    def rhs_fn(k, j):
        return hnj[j][:, k, j * CH:(j + 1) * CH]

    def evac(c, j, ps, M):
        ob = S.ring("evA", 4, [128, CH])
        P.op("act", lambda e: e.copy(out=ob.ap[0:M, :], in_=ps.ap[0:M, :]), reads=[ps], writes=[ob])
        P.dma("act", projT[c * 128:c * 128 + M, j * CH:(j + 1) * CH], ob[0:M, :])

    stream_mm(P, S, Wt, KT_D, NCB, rhs_fn, nch, CH, evac, Ms=Ms)


def emit_dense_out(P, S, L, hT, yT, pT, hT_out, NT, W, glu=None, final=None, outT=None, h1T=None):
    nch = NT // CH
    gF = P.sbuf("gF%d" % L, [128, KT_D]); load_cols(P, gF, W["nffn"], KT_D)
    gP = P.sbuf("gP%d" % L, [128, KT_D]); load_cols(P, gP, W["npl"], KT_D)
    if final is not None:
        gN = P.sbuf("gN%d" % L, [128, KT_D]); load_cols(P, gN, final, KT_D)
    yTv = yT.re("(k p) n -> p k n", p=128)
    hTv = hT.re("(k p) n -> p k n", p=128)
    h1Tv = h1T.re("(k p) n -> p k n", p=128)
    hoTv = hT_out.re("(k p) n -> p k n", p=128) if hT_out is not None else None
    sc1 = P.scope(); sc1.__enter__()
    yb = P.sbuf("ybC", [128, KT_D, NT], MMDT)
    k_start = 0
    if glu is not None:
        gluw_t, glub = glu
        gb = P.sbuf("glub_sb", [128, 4]); load_cols(P, gb, glub, 4)
        actf = P.sbuf("actf", [128, 4, NT])
        P.dma("sp", actf, yTv[:, 0:4, :])
        actb = P.sbuf("actb", [128, 4, NT], MMDT)
        P.op("dve", lambda e: e.tensor_copy(out=actb.ap, in_=actf.ap), reads=[actf], writes=[actb])

        def evac_glu(c, j, ps, M):
            sg = S.ring("sgl", 2, [128, CH])
            P.op("act", lambda e: e.activation(out=sg.ap, in_=ps.ap, func=AF.Sigmoid, bias=gb.ap[:, c:c + 1]),
                 reads=[ps, gb], writes=[sg])
            P.op("dve", lambda e: e.tensor_tensor(out=yb.ap[:, c, j * CH:(j + 1) * CH],
                                                  in0=actf.ap[:, c, j * CH:(j + 1) * CH], in1=sg.ap, op=ALU.mult),
                 reads=[actf, sg], writes=[yb])
        stream_mm(P, S, gluw_t, 4, 4, lambda k, j: actb[:, k, j * CH:(j + 1) * CH], nch, CH, evac_glu)
        k_start = 4
    for k in range(k_start, KT_D, 4):
        P.dma("pool", yb[:, k:k + 4, :], yTv[:, k:k + 4, :])

    def pre_o(c):
        pass

    def evac_o(c, j, ps, M):
        hb = S.ring("hbC", 3, [128, CH])
        P.dma("sp", hb, hT[c * 128:(c + 1) * 128, j * CH:(j + 1) * CH])
        P.op("dve", lambda e: e.tensor_tensor(out=hb.ap, in0=ps.ap, in1=hb.ap, op=ALU.add),
             reads=[ps, hb], writes=[hb])
        P.dma("sp", h1T[c * 128:(c + 1) * 128, j * CH:(j + 1) * CH], hb)
    stream_mm(P, S, W["out_t"], KT_D, 16, lambda k, j: yb[:, k, j * CH:(j + 1) * CH], nch, CH, evac_o)
    sc1.__exit__(None, None, None)
    sc2 = P.scope(); sc2.__enter__()
    plp = P.sbuf("plpC", [128, 16, 2, 128], MMDT)
    for c in range(16):
        P.dma("pool", plp[:, c, :, :], W["plp_t"][c])
    pTv = pT.re("(k p) n -> p k n", p=128)
    for j in range(nch):
        sl = slice(j * CH, (j + 1) * CH)
        h1 = S.ring("h1C", 1, [128, KT_D, CH])
        P.dma("sp", h1, h1Tv[:, :, sl])
        hn = S.ring("hnC", 1, [128, KT_D, CH], MMDT)
        rmsnorm_T(P, S, h1, gF, hn, KT_D, CH)
        hid = S.ring("hidC", 1, [128, 64, CH], MMDT)

        def evac1(c, jj, ps, M):
            rl = S.ring("rlC", 3, [128, CH])
            P.op("act", lambda e: e.activation(out=rl.ap, in_=ps.ap, func=AF.Relu), reads=[ps], writes=[rl])
            P.op("pool", lambda e: e.tensor_tensor(out=hid.ap[:, c, :], in0=rl.ap, in1=rl.ap, op=ALU.mult),
                 reads=[rl], writes=[hid])
        stream_mm(P, S, W["w1_t"], KT_D, 64, lambda k, jj: hn[:, k, :], 1, CH, evac1)

        def evac2(c, jj, ps, M):
            P.op("dve", lambda e: e.tensor_tensor(out=h1.ap[:, c, :], in0=ps.ap, in1=h1.ap[:, c, :], op=ALU.add),
                 reads=[ps, h1], writes=[h1])
        stream_mm(P, S, W["w2_t"], 64, 16, lambda k, jj: hid[:, k, :], 1, CH, evac2)
        rmsnorm_T(P, S, h1, gP, hn, KT_D, CH)
        pb = S.ring("pbC", 1, [128, 2, CH], MMDT)
        P.dma("pool", pb, pTv[:, :, sl])

        def evac3(c, jj, ps, M):
            psp = S.next_psum()
            for k in range(2):
                P.op("pe", lambda e, k=k: e.matmul(psp.ap, lhsT=plp.ap[:, c, k, :], rhs=pb.ap[:, k, :],
                                                   start=(k == 0), stop=(k == 1)),
                     reads=[plp, pb], writes=[psp])
            sg = S.ring("sgC", 2, [128, CH])
            P.op("act", lambda e: e.activation(out=sg.ap, in_=ps.ap, func=AF.Sigmoid), reads=[ps], writes=[sg])
            P.op("dve", lambda e: e.tensor_tensor(out=sg.ap, in0=psp.ap, in1=sg.ap, op=ALU.mult),
                 reads=[psp, sg], writes=[sg])
            P.op("pool", lambda e: e.tensor_tensor(out=h1.ap[:, c, :], in0=h1.ap[:, c, :], in1=sg.ap, op=ALU.add),
                 reads=[h1, sg], writes=[h1])
        stream_mm(P, S, W["gate_t"], KT_D, 16, lambda k, jj: hn[:, k, :], 1, CH, evac3)
        if hT_out is not None:
            P.dma("sp", hoTv[:, :, sl], h1)
        if final is not None:
            rmsnorm_T(P, S, h1, gN, h1, KT_D, CH)
            P.dma("sp", outT.re("(k p) n -> p k n", p=128)[:, :, sl], h1)
    sc2.__exit__(None, None, None)


def emit_glu_partial(P, S, actT, gluw_t, zp, T):
    nch = T // CH
    with P.scope():
        ab = P.sbuf("glu_ab", [128, 2, T], MMDT)
        av = actT.re("(k p) n -> p k n", p=128)
        for k in range(2):
            for j0 in range(0, T, 2048):
                P.dma("pool", ab[:, k, j0:j0 + 2048], av[:, k, j0:j0 + 2048])

        def evac(c, j, ps, M):
            ob = S.ring("glu_ev", 4, [128, CH])
            P.op("act", lambda e: e.copy(out=ob.ap, in_=ps.ap), reads=[ps], writes=[ob])
            P.dma("act", zp[c // 2, (c % 2) * 128:(c % 2 + 1) * 128, j * CH:(j + 1) * CH], ob)
        stream_mm(P, S, gluw_t, 2, 4, lambda k, j: ab[:, k, j * CH:(j + 1) * CH], nch, CH, evac)


def emit_outproj_partial(P, S, yT, wout_t, mp, T, glu=None):
    nch = T // CH
    half = T // 2
    with P.scope():
        yb = P.sbuf("op_yb", [128, 8, T], MMDT)
        yv = yT.re("(k p) n -> p k n", p=128)
        k0 = 0
        if glu is not None:
            zT, glub = glu
            gb = P.sbuf("op_gb", [128, 2]); load_cols(P, gb, glub, 2)
            zv = zT.re("(k p) n -> p k n", p=128)
            for k in range(2):
                for j in range(nch):
                    sl = slice(j * CH, (j + 1) * CH)
                    a = S.ring("op_a", 3, [128, CH]); z = S.ring("op_z", 3, [128, CH])
                    P.dma("sp", a, yv[:, k, sl]); P.dma("sp", z, zv[:, k, sl])
                    P.op("act", lambda e, z=z, k=k: e.activation(out=z.ap, in_=z.ap, func=AF.Sigmoid, bias=gb.ap[:, k:k + 1]), reads=[z, gb], writes=[z])
                    P.op("dve", lambda e, a=a, z=z, k=k, sl=sl: e.tensor_tensor(out=yb.ap[:, k, sl], in0=a.ap, in1=z.ap, op=ALU.mult), reads=[a, z], writes=[yb])
            k0 = 2
        for k in range(k0, 8):
            for j0 in range(0, T, 2048):
                P.dma("pool", yb[:, k, j0:j0 + 2048], yv[:, k, j0:j0 + 2048])

        def evac(c, j, ps, M):
            ob = S.ring("op_ev", 4, [128, CH])
            P.op("act", lambda e: e.copy(out=ob.ap, in_=ps.ap), reads=[ps], writes=[ob])
            t0 = j * CH
            P.dma("act", mp[c // 4, t0 // half, (c % 4) * 128:(c % 4 + 1) * 128, t0 % half:t0 % half + CH], ob)
        stream_mm(P, S, wout_t, 8, 16, lambda k, j: yb[:, k, j * CH:(j + 1) * CH], nch, CH, evac)


def emit_mlp_gate(P, S, L, hT, mixT, pT, hT_out, NT, W, final=None, outT=None):
    nch = NT // CH
    with P.scope():
        gF = P.sbuf("gF%d" % L, [128, KT_D]); load_cols(P, gF, W["nffn"], KT_D)
        gP = P.sbuf("gP%d" % L, [128, KT_D]); load_cols(P, gP, W["npl"], KT_D)
        if final is not None:
            gN = P.sbuf("gN%d" % L, [128, KT_D]); load_cols(P, gN, final, KT_D)
        hTv = hT.re("(k p) n -> p k n", p=128)
        mTv = mixT.re("(k p) n -> p k n", p=128)
        hoTv = hT_out.re("(k p) n -> p k n", p=128) if hT_out is not None else None
        plp = P.sbuf("plpC", [128, 16, 2, 128], MMDT)
        for c in range(16):
            P.dma("pool", plp[:, c, :, :], W["plp_t"][c])
        pTv = pT.re("(k p) n -> p k n", p=128)

        def tile_body(j):
            sl = slice(j * CH, (j + 1) * CH)
            h1 = S.ring("h1C", 1, [128, KT_D, CH])
            hn = S.ring("hnC", 1, [128, KT_D, CH], MMDT)
            hid = S.ring("hidC", 1, [128, 64, CH], MMDT)
            P.dma("sp", h1, hTv[:, :, sl])
            for q in range(8):
                mt = S.ring("mixC", 2, [128, 2, CH])
                P.dma("sp", mt, mTv[:, q * 2:(q + 1) * 2, sl])
                P.op("dve", lambda e, mt=mt, q=q: e.tensor_tensor(out=h1.ap[:, q * 2:(q + 1) * 2, :], in0=h1.ap[:, q * 2:(q + 1) * 2, :], in1=mt.ap, op=ALU.add),
                     reads=[h1, mt], writes=[h1])
            rmsnorm_T(P, S, h1, gF, hn, KT_D, CH)

            def evac1(c, jj, ps, M):
                rl = S.ring("rlC", 3, [128, CH])
                P.op("act", lambda e: e.activation(out=rl.ap, in_=ps.ap, func=AF.Relu), reads=[ps], writes=[rl])
                P.op("pool", lambda e: e.tensor_tensor(out=hid.ap[:, c, :], in0=rl.ap, in1=rl.ap, op=ALU.mult), reads=[rl], writes=[hid])
            stream_mm(P, S, W["w1_t"], KT_D, 64, lambda k, jj: hn[:, k, :], 1, CH, evac1)

            def evac2(c, jj, ps, M):
                P.op("dve", lambda e: e.tensor_tensor(out=h1.ap[:, c, :], in0=ps.ap, in1=h1.ap[:, c, :], op=ALU.add), reads=[ps, h1], writes=[h1])
            stream_mm(P, S, W["w2_t"], 64, 16, lambda k, jj: hid[:, k, :], 1, CH, evac2)
            rmsnorm_T(P, S, h1, gP, hn, KT_D, CH)
            pb = S.ring("pbC", 1, [128, 2, CH], MMDT)
            P.dma("pool", pb, pTv[:, :, sl])

            def evac3(c, jj, ps, M):
                psp = S.next_psum()
                for k in range(2):
                    P.op("pe", lambda e, k=k: e.matmul(psp.ap, lhsT=plp.ap[:, c, k, :], rhs=pb.ap[:, k, :], start=(k == 0), stop=(k == 1)),
                         reads=[plp, pb], writes=[psp])
                sg = S.ring("sgC", 2, [128, CH])
                P.op("act", lambda e: e.activation(out=sg.ap, in_=ps.ap, func=AF.Sigmoid), reads=[ps], writes=[sg])
                P.op("dve", lambda e: e.tensor_tensor(out=sg.ap, in0=psp.ap, in1=sg.ap, op=ALU.mult), reads=[psp, sg], writes=[sg])
                P.op("pool", lambda e: e.tensor_tensor(out=h1.ap[:, c, :], in0=h1.ap[:, c, :], in1=sg.ap, op=ALU.add), reads=[h1, sg], writes=[h1])
            stream_mm(P, S, W["gate_t"], KT_D, 16, lambda k, jj: hn[:, k, :], 1, CH, evac3)
            if hT_out is not None:
                P.dma("sp", hoTv[:, :, sl], h1)
            if final is not None:
                rmsnorm_T(P, S, h1, gN, h1, KT_D, CH)
                P.dma("sp", outT.re("(k p) n -> p k n", p=128)[:, :, sl], h1)

        for j in range(nch):
            tile_body(j)


def emit_mlp_gate_v2(P, S, L, hT, mixT, pT, hT_out, NT, W, h1d, final=None, outT=None):
    nch = NT // CH
    hTv = hT.re("(k p) n -> p k n", p=128)
    mTv = mixT.re("(k p) n -> p k n", p=128)
    h1v = h1d.re("(k p) n -> p k n", p=128)
    hreg = [[P.alias(h1d, "h1d_%d_%d" % (c, j)) for j in range(nch)] for c in range(16)]
    houts = None
    if hT_out is not None:
        houts = [[P.alias(hT_out, "hout_%d" % c)] * nch for c in range(16)]

    def norm_pass(src_v, gcol, dst_fn, add_v=None, store_v=None, tag=""):
        with P.scope():
            for j in range(nch):
                sl = slice(j * CH, (j + 1) * CH)
                ht = S.ring("npH", 2, [128, KT_D, CH])
                P.dma("sp", ht, src_v[:, :, sl])
                if add_v is not None:
                    for q in range(8):
                        mt = S.ring("npM", 2, [128, 2, CH])
                        P.dma("sp", mt, add_v[:, q * 2:(q + 1) * 2, sl])
                        P.op("dve", lambda e, mt=mt, q=q, ht=ht: e.tensor_tensor(out=ht.ap[:, q * 2:(q + 1) * 2, :], in0=ht.ap[:, q * 2:(q + 1) * 2, :],
                             in1=mt.ap, op=ALU.add), reads=[ht, mt], writes=[ht])
                if store_v is not None:
                    P.dma("sp", store_v[:, :, sl], ht)
                rmsnorm_T(P, S, ht, gcol, dst_fn(j), KT_D, CH)

    with P.scope():
        gF = P.sbuf("gF%d" % L, [128, KT_D]); load_cols(P, gF, W["nffn"], KT_D)
        gP = P.sbuf("gP%d" % L, [128, KT_D]); load_cols(P, gP, W["npl"], KT_D)
        hn = P.sbuf("hnM", [128, KT_D, NT], MMDT)
        norm_pass(hTv, gF, lambda j: hn[:, :, j * CH:(j + 1) * CH], add_v=mTv, store_v=h1v)
        with P.scope():
            hid = P.sbuf("hidM", [128, 16, NT], MMDT)
            for q in range(4):
                def evac1(c, j, ps, M):
                    rl = S.ring("rlM", 3, [128, CH])
                    P.op("act", lambda e: e.activation(out=rl.ap, in_=ps.ap, func=AF.Relu), reads=[ps], writes=[rl])
                    P.op("pool", lambda e: e.tensor_tensor(out=hid.ap[:, c, j * CH:(j + 1) * CH], in0=rl.ap, in1=rl.ap, op=ALU.mult), reads=[rl], writes=[hid])
                stream_mm(P, S, W["w1_t"][q * 16:(q + 1) * 16], KT_D, 16, lambda k, j: hn[:, k, j * CH:(j + 1) * CH], nch, CH, evac1)

                pend = {}

                def pre2(c):
                    for j in range(nch):
                        hb = S.ring("hbM", 8, [128, CH])
                        reg = V(h1d.ap[c * 128:(c + 1) * 128, j * CH:(j + 1) * CH], hreg[c][j].buf)
                        P.dma("pool", hb, reg)
                        pend[(c, j)] = (hb, reg)

                def evac2(c, j, ps, M):
                    hb, reg = pend.pop((c, j))
                    P.op("dve", lambda e: e.tensor_tensor(out=hb.ap, in0=ps.ap, in1=hb.ap, op=ALU.add), reads=[ps, hb], writes=[hb])
                    P.dma("act", reg, hb)
                stream_mm(P, S, W["w2_t"][:, :, q * 16:(q + 1) * 16, :], KT_D, 16, lambda k, j: hid[:, k, j * CH:(j + 1) * CH], nch, CH, evac2, pre=pre2)
        norm_pass(h1v, gP, lambda j: hn[:, :, j * CH:(j + 1) * CH])
        with P.scope():
            plp = P.sbuf("plpM", [128, 16, 2, 128], MMDT)
            for c in range(16):
                P.dma("pool", plp[:, c, :, :], W["plp_t"][c])
            pb = P.sbuf("pbM", [128, 2, NT], MMDT)
            pTv = pT.re("(k p) n -> p k n", p=128)
            for k in range(2):
                P.dma("pool", pb[:, k, :], pTv[:, k, :])
            dst = hT_out if hT_out is not None else h1d
            dregs = houts if hT_out is not None else hreg

            pend3 = {}

            def pre3(c):
                for j in range(nch):
                    sl = slice(j * CH, (j + 1) * CH)
                    hb = S.ring("hbG", 8, [128, CH])
                    P.dma("pool", hb, V(h1d.ap[c * 128:(c + 1) * 128, sl], hreg[c][j].buf))
                    pend3[(c, j)] = hb

            def evac3(c, j, ps, M):
                sl = slice(j * CH, (j + 1) * CH)
                psp = S.next_psum()
                for k in range(2):
                    P.op("pe", lambda e, k=k: e.matmul(psp.ap, lhsT=plp.ap[:, c, k, :], rhs=pb.ap[:, k, sl], start=(k == 0), stop=(k == 1)),
                         reads=[plp, pb], writes=[psp])
                sg = S.ring("sgM", 2, [128, CH])
                P.op("act", lambda e: e.activation(out=sg.ap, in_=ps.ap, func=AF.Sigmoid), reads=[ps], writes=[sg])
                P.op("dve", lambda e: e.tensor_tensor(out=sg.ap, in0=psp.ap, in1=sg.ap, op=ALU.mult), reads=[psp, sg], writes=[sg])
                hb = pend3.pop((c, j))
                P.op("pool", lambda e: e.tensor_tensor(out=hb.ap, in0=hb.ap, in1=sg.ap, op=ALU.add), reads=[hb, sg], writes=[hb])
                P.dma("act", V(dst.ap[c * 128:(c + 1) * 128, sl], dregs[c][j].buf), hb)
            stream_mm(P, S, W["gate_t"], KT_D, 16, lambda k, j: hn[:, k, j * CH:(j + 1) * CH], nch, CH, evac3, pre=pre3)
    if final is not None:
        with P.scope():
            gN = P.sbuf("gN%d" % L, [128, KT_D]); load_cols(P, gN, final, KT_D)
            oTv = outT.re("(k p) n -> p k n", p=128)
            for j in range(nch):
                sl = slice(j * CH, (j + 1) * CH)
                ht = S.ring("npF", 2, [128, KT_D, CH])
                P.dma("sp", ht, h1v[:, :, sl])
                rmsnorm_T(P, S, ht, gN, ht, KT_D, CH)
                P.dma("sp", oTv[:, :, sl], ht)
CH = 512
PI = math.pi
FAST32 = True


def r32(ap):
    return ap.bitcast(mybir.dt.float32r) if FAST32 else ap


def tt(P, eng, out, a, b, op):
    P.op(eng, lambda e: e.tensor_tensor(out=out.ap, in0=a.ap, in1=b.ap, op=op), reads=[a, b], writes=[out])


def ts(P, eng, out, a, s1, op0, s2=None, op1=None):
    rd = [a] + [x for x in (s1, s2) if isinstance(x, V)]
    g = lambda x: x.ap if isinstance(x, V) else x
    if op1 is None:
        P.op(eng, lambda e: e.tensor_scalar(out=out.ap, in0=a.ap, scalar1=g(s1), scalar2=None, op0=op0), reads=rd, writes=[out])
    else:
        P.op(eng, lambda e: e.tensor_scalar(out=out.ap, in0=a.ap, scalar1=g(s1), scalar2=g(s2), op0=op0, op1=op1), reads=rd, writes=[out])


def act(P, out, a, func, scale=None, bias=None):
    rd = [a] + [x for x in (scale, bias) if isinstance(x, V)]
    kw = {}
    if scale is not None:
        kw["scale"] = scale.ap if isinstance(scale, V) else scale
    if bias is not None:
        kw["bias"] = bias.ap if isinstance(bias, V) else bias
    P.op("act", lambda e: e.activation(out=out.ap, in_=a.ap, func=func, **kw), reads=rd, writes=[out])


def sin_reduced(P, tmp, out, ang, shift, zero_col):
    f1, f2, i1 = tmp
    ts(P, "dve", f1, ang, shift, ALU.add)
    ts(P, "dve", f2, f1, 1.0 / (2 * PI), ALU.mult)
    P.op("dve", lambda e: e.tensor_copy(out=i1.ap, in_=f2.ap), reads=[f2], writes=[i1])
    P.op("dve", lambda e: e.tensor_copy(out=f2.ap, in_=i1.ap), reads=[i1], writes=[f2])
    P.op("dve", lambda e: e.scalar_tensor_tensor(out=f1.ap, in0=f2.ap, scalar=-2 * PI, in1=f1.ap, op0=ALU.mult, op1=ALU.add),
         reads=[f1, f2], writes=[f1])
    ts(P, "dve", f2, f1, PI, ALU.is_gt, -2 * PI, ALU.mult)
    tt(P, "dve", f1, f1, f2, ALU.add)
    ts(P, "dve", f2, f1, -PI, ALU.is_lt, 2 * PI, ALU.mult)
    tt(P, "dve", f1, f1, f2, ALU.add)
    act(P, out, f1, AF.Sin)


def s5_abar(P, lr, li, lstep, shape, tag):
    mk = lambda n, dt=F32: P.sbuf("s5%s_%s" % (tag, n), shape, dt)
    step = mk("step"); mag = mk("mag"); ang = mk("ang"); ar = mk("ar"); ai = mk("ai")
    f1 = mk("f1"); f2 = mk("f2"); i1 = mk("i1", mybir.dt.int32)
    act(P, step, lstep, AF.Exp)
    tt(P, "dve", mag, lr, step, ALU.mult)
    act(P, mag, mag, AF.Exp)
    tt(P, "dve", ang, li, step, ALU.mult)
    sin_reduced(P, (f1, f2, i1), ai, ang, 0.0, None)
    sin_reduced(P, (f1, f2, i1), ar, ang, PI / 2, None)
    tt(P, "dve", ar, ar, mag, ALU.mult)
    tt(P, "dve", ai, ai, mag, ALU.mult)
    return ar, ai, (f1, f2)


def emit_s5(P, S, uT, prm, cst, yT, T, NG):
    nch = T // CH
    nlev = int(round(math.log2(T)))
    assert 2 ** nlev == T
    with P.scope():
        ld = lambda name, shape, src: (lambda t: (P.dma("sp", t, src), t)[1])(P.sbuf(name, shape))
        shp = [16, NG, 64]
        lr = ld("s5r_lr", shp, prm["lr_row"]); li = ld("s5r_li", shp, prm["li_row"]); ls = ld("s5r_ls", shp, prm["ls_row"])
        bre = ld("s5r_bre", shp, prm["bT_re"]); bim = ld("s5r_bim", shp, prm["bT_im"])
        ar, ai, (f1, f2) = s5_abar(P, lr, li, ls, shp, "r")
        den = P.sbuf("s5r_den", shp); cre = P.sbuf("s5r_cre", shp); cim = P.sbuf("s5r_cim", shp)
        tt(P, "dve", den, lr, lr, ALU.mult); tt(P, "dve", f1, li, li, ALU.mult); tt(P, "dve", den, den, f1, ALU.add)
        P.op("dve", lambda e: e.reciprocal(out=den.ap, in_=den.ap), reads=[den], writes=[den])
        ts(P, "dve", ar, ar, -1.0, ALU.add)
        tt(P, "dve", cre, ar, lr, ALU.mult); tt(P, "dve", f1, ai, li, ALU.mult); tt(P, "dve", cre, cre, f1, ALU.add)
        tt(P, "dve", cre, cre, den, ALU.mult)
        tt(P, "dve", cim, ai, lr, ALU.mult); tt(P, "dve", f1, ar, li, ALU.mult); tt(P, "dve", cim, cim, f1, ALU.subtract)
        tt(P, "dve", cim, cim, den, ALU.mult)
        BT = P.sbuf("s5_BT", [16, NG, 128])
        tt(P, "dve", f1, cre, bre, ALU.mult); tt(P, "dve", f2, cim, bim, ALU.mult)
        tt(P, "dve", BT[:, :, 0:64], f1, f2, ALU.subtract)
        tt(P, "dve", f1, cre, bim, ALU.mult); tt(P, "dve", f2, cim, bre, ALU.mult)
        tt(P, "dve", BT[:, :, 64:128], f1, f2, ALU.add)
        shc = [128, NG]
        lrc = ld("s5c_lr", shc, prm["lr_col"]); lic = ld("s5c_li", shc, prm["li_col"]); lsc = ld("s5c_ls", shc, prm["ls_col"])
        arc, aic, (g1, g2) = s5_abar(P, lrc, lic, lsc, shc, "c")
        sgn = ld("s5_sgn", [128, 1], cst["sgn"])
        ident = ld("s5_id", [128, 128], cst["ident"]); psw = ld("s5_psw", [128, 128], cst["psw"])
        pw = P.sbuf("s5_pw", [128, nlev, 2, NG])
        P.op("dve", lambda e: e.tensor_copy(out=pw.ap[:, 0, 0, :], in_=arc.ap), reads=[arc], writes=[pw])
        ts(P, "dve", pw[:, 0, 1, :], aic, sgn[:, 0:1], ALU.mult)
        for k in range(1, nlev):
            a0 = pw[:, k - 1, 0, :]; s0 = pw[:, k - 1, 1, :]
            tt(P, "dve", g1, a0, a0, ALU.mult); tt(P, "dve", g2, s0, s0, ALU.mult)
            tt(P, "dve", pw[:, k, 0, :], g1, g2, ALU.subtract)
            tt(P, "dve", g1, a0, s0, ALU.mult)
            ts(P, "dve", pw[:, k, 1, :], g1, 2.0, ALU.mult)
        CT = ld("s5_CT", [128, NG, 16], prm["cT"])
        ts(P, "dve", CT[64:128], CT[64:128], -1.0, ALU.mult)
        dcol = ld("s5_dcol", [16, NG], prm["d_col"])
        Xa = P.sbuf("s5_Xa", [128, T]); Xb = P.sbuf("s5_Xb", [128, T])
        Xc = [[P.alias(Xa, "s5Xa%d" % j) for j in range(nch)], [P.alias(Xb, "s5Xb%d" % j) for j in range(nch)]]
        for g in range(NG):
            ug = P.ring("s5_u", 2, [16, T])
            P.dma("sp", ug, uT[g * 16:(g + 1) * 16, :])
            Mk = P.ring("s5_M", 2, [128, nlev, 128])
            for k in range(nlev):
                P.op("pool", lambda e, Mk=Mk, k=k, g=g: e.tensor_scalar(out=r32(Mk.ap[:, k, :]), in0=ident.ap, scalar1=pw.ap[:, k, 0, g:g + 1],
                     scalar2=0.0, op0=ALU.mult, op1=ALU.add), reads=[ident, pw], writes=[Mk])
                P.op("dve", lambda e, Mk=Mk, k=k, g=g: e.scalar_tensor_tensor(out=r32(Mk.ap[:, k, :]), in0=psw.ap, scalar=pw.ap[:, k, 1, g:g + 1],
                     in1=Mk.ap[:, k, :], op0=ALU.mult, op1=ALU.add), reads=[psw, pw, Mk], writes=[Mk])
            for j in range(nch):
                ps = S.next_psum(); sl = slice(j * CH, (j + 1) * CH)
                P.op("pe", lambda e, ps=ps, ug=ug, sl=sl, g=g: e.matmul(ps.ap, lhsT=BT.ap[:, g, :], rhs=ug.ap[:, sl], start=True, stop=True),
                     reads=[BT, ug], writes=[ps])
                P.op("act", lambda e, ps=ps, sl=sl: e.copy(out=r32(Xa.ap[:, sl]), in_=ps.ap), reads=[ps], writes=[Xc[0][j]])
            for k in range(nlev):
                d = 2 ** k
                src, dst = Xc[k % 2], Xc[(k + 1) % 2]
                sa, da = (Xa, Xb) if k % 2 == 0 else (Xb, Xa)
                for j in range(nch):
                    t0 = j * CH; t1 = t0 + CH
                    lo = max(t0, d)
                    if lo > t0:
                        hi = min(lo, t1)
                        P.op("pool", lambda e, sa=sa, da=da, t0=t0, hi=hi: e.tensor_copy(out=r32(da.ap[:, t0:hi]), in_=sa.ap[:, t0:hi]),
                             reads=[src[j]], writes=[dst[j]])
                    if lo >= t1:
                        continue
                    n = t1 - lo
                    s0, s1 = lo - d, t1 - d
                    rds = [src[c] for c in range(s0 // CH, (s1 - 1) // CH + 1)]
                    ps = S.next_psum()
                    P.op("pe", lambda e, ps=ps, Mk=Mk, k=k, sa=sa, s0=s0, s1=s1, n=n: e.matmul(ps.ap[:, 0:n], lhsT=(r32(Mk.ap[:, k, :]) if (n % 2 == 0 and s0 % 2 == 0) else Mk.ap[:, k, :]), rhs=(r32(sa.ap[:, s0:s1]) if (n % 2 == 0 and s0 % 2 == 0) else sa.ap[:, s0:s1]),
                         start=True, stop=True), reads=[Mk] + rds, writes=[ps])
                    P.op("dve", lambda e, ps=ps, sa=sa, da=da, lo=lo, t1=t1, n=n: e.tensor_tensor(out=r32(da.ap[:, lo:t1]), in0=ps.ap[:, 0:n], in1=sa.ap[:, lo:t1],
                         op=ALU.add), reads=[ps, src[j]], writes=[dst[j]])
            fin = Xc[nlev % 2]; fa = Xa if nlev % 2 == 0 else Xb
            og = P.ring("s5_o", 2, [16, T])
            for j in range(nch):
                ps = S.next_psum(); sl = slice(j * CH, (j + 1) * CH)
                P.op("pe", lambda e, ps=ps, sl=sl, g=g, fa=fa: e.matmul(ps.ap[0:16, :], lhsT=CT.ap[:, g, :], rhs=fa.ap[:, sl], start=True, stop=True),
                     reads=[CT, fin[j]], writes=[ps])
                P.op("dve", lambda e, ps=ps, sl=sl, g=g, ug=ug, og=og: e.scalar_tensor_tensor(out=og.ap[:, sl], in0=ug.ap[:, sl], scalar=dcol.ap[:, g:g + 1],
                     in1=ps.ap[0:16, :], op0=ALU.mult, op1=ALU.add), reads=[ug, dcol, ps], writes=[og])
            P.op("act", lambda e, og=og: e.activation(out=og.ap, in_=og.ap, func=AF.Gelu_apprx_tanh), reads=[og], writes=[og])
            P.dma("sp", yT[g * 16:(g + 1) * 16, :], og)


def emit_ssd(P, S, zT, xsT, bT, cT, dtT, prm, cst, yT, T, NH, NG, dbg=None):
    J = NH // NG
    NP = NH // 2
    TPG = NP // NG
    L = 128
    SC = CH
    nsc = T // SC
    ncs = SC // L
    with P.scope():
        ld = lambda name, shape, src, **kw: (lambda t: (P.dma("sp", t, src, **kw), t)[1])(P.sbuf(name, shape))
        ident = ld("sd_id", [128, 128], cst["ident"]); tri = ld("sd_tri", [128, 128], cst["tri"])
        ustr = ld("sd_us", [128, 128], cst["ustr"]); sel = ld("sd_sel", [NH, NH * 64], cst["sel"])
        ones = S.ones
        cwx = P.sbuf("sd_cwx", [128, NP, 4]); cwb = P.sbuf("sd_cwb", [128, NG, 4]); cwc = P.sbuf("sd_cwc", [128, NG, 4])
        for k in range(4):
            P.dma("sp", cwx[:, :, k], prm["cw_x"][k].re("(t p) -> p t", p=128), allow_slow_non_contiguous=True)
            P.dma("sp", cwb[:, :, k], prm["cw_b"][k].re("(t p) -> p t", p=128), allow_slow_non_contiguous=True)
            P.dma("sp", cwc[:, :, k], prm["cw_c"][k].re("(t p) -> p t", p=128), allow_slow_non_contiguous=True)
        colv = lambda nm, n: ld("sd_" + nm, [128, n], prm[nm].re("(t p) -> p t", p=128), allow_slow_non_contiguous=True)
        cbx = colv("cb_x", NP); cbb = colv("cb_b", NG); cbc = colv("cb_c", NG); dch = colv("d_ch", NP); ngc = colv("ng", NP)
        dtb = ld("sd_dtb", [NH, 1], prm["dt_bias"].re("(h o) -> h o", o=1)); alog = ld("sd_alog", [NH, 1], prm["a_log"].re("(h o) -> h o", o=1))
        negA = P.sbuf("sd_negA", [NH, 1])
        act(P, negA, alog, AF.Exp)
        ts(P, "dve", negA, negA, -1.0, ALU.mult)
        one12 = P.sbuf("sd_one", [128, 1]); P.op("dve", lambda e: e.memset(one12.ap, 1.0), writes=[one12])
        prevT = P.sbuf("sd_prev", [128, NH, 64])
        P.op("dve", lambda e: e.memset(prevT.ap, 0.0), writes=[prevT])

        def conv_silu(srcT, r0, t0, cw, cb, i, dst):
            xp = P.ring("sd_xp", 2, [128, SC + 3])
            if t0 == 0:
                P.op("pool", lambda e: e.memset(xp.ap[:, 0:3], 0.0), writes=[xp])
                P.dma("sp", xp[:, 3:SC + 3], srcT[r0:r0 + 128, 0:SC])
            else:
                P.dma("sp", xp, srcT[r0:r0 + 128, t0 - 3:t0 + SC])
            P.op("dve", lambda e: e.tensor_scalar(out=dst.ap, in0=xp.ap[:, 0:SC], scalar1=cw.ap[:, i, 0:1], scalar2=cb.ap[:, i:i + 1],
                 op0=ALU.mult, op1=ALU.add), reads=[xp, cw, cb], writes=[dst])
            for k in range(1, 4):
                P.op("dve", lambda e, k=k: e.scalar_tensor_tensor(out=dst.ap, in0=xp.ap[:, k:k + SC], scalar=cw.ap[:, i, k:k + 1], in1=dst.ap,
                     op0=ALU.mult, op1=ALU.add), reads=[xp, cw, dst], writes=[dst])
            act(P, dst, dst, AF.Silu)

        def sc_body(sc):
            t0 = sc * SC
            xsc = [P.ring("sd_xsc%d" % i, 1, [128, SC]) for i in range(NP)]
            Bc = [P.ring("sd_Bc%d" % g, 1, [128, SC]) for g in range(NG)]
            Cc = [P.ring("sd_Cc%d" % g, 1, [128, SC]) for g in range(NG)]
            for i in range(NP):
                conv_silu(xsT, i * 128, t0, cwx, cbx, i, xsc[i])
            for g in range(NG):
                conv_silu(bT, g * 128, t0, cwb, cbb, g, Bc[g])
                conv_silu(cT, g * 128, t0, cwc, cbc, g, Cc[g])
            dtv = P.ring("sd_dtv", 1, [NH, SC]); da = P.ring("sd_da", 1, [NH, SC]); acT = P.ring("sd_acT", 1, [NH, SC])
            dsT = P.ring("sd_dsT", 1, [NH, SC])
            P.dma("sp", dtv, dtT[:, t0:t0 + SC])
            act(P, dtv, dtv, AF.Exp, bias=dtb[:, 0:1])
            act(P, dtv, dtv, AF.Ln, bias=one12[0:NH, 0:1])
            ts(P, "dve", da, dtv, negA[:, 0:1], ALU.mult)
            for c in range(ncs):
                cs = slice(c * L, (c + 1) * L)
                P.op("dve", lambda e, cs=cs: e.tensor_tensor_scan(out=acT.ap[:, cs], data0=one12.ap[0:NH, 0:1].to_broadcast([NH, L]), data1=da.ap[:, cs],
                     initial=0.0, op0=ALU.mult, op1=ALU.add), reads=[da, one12], writes=[acT])
            for c in range(ncs):
                cs = slice(c * L, (c + 1) * L)
                act(P, dsT[:, cs], acT[:, cs], AF.Exp, scale=-1.0, bias=acT[:, (c + 1) * L - 1:(c + 1) * L])
            tt(P, "dve", dsT, dsT, dtv, ALU.mult)
            xT = [P.ring("sd_xT%d" % i, 1, [128, SC]) for i in range(NP)]
            xdT = [P.ring("sd_xdT%d" % i, 1, [128, SC]) for i in range(NP)]
            for i in range(NP):
                for (srcrow, dst) in ((dtv, xT[i]), (dsT, xdT[i])):
                    ps = S.next_psum()
                    P.op("pe", lambda e, ps=ps, srcrow=srcrow, i=i: e.matmul(ps.ap, lhsT=sel.ap[:, i * 128:(i + 1) * 128], rhs=srcrow.ap, start=True, stop=True),
                         reads=[sel, srcrow], writes=[ps])
                    tt(P, "dve", dst, ps, xsc[i], ALU.mult)
            ysb = [P.ring("sd_ysb%d" % i, 1, [128, SC]) for i in range(NP)]
            if dbg and sc == 0:
                P.dma("sp", dbg["xsc0"], xsc[0]); P.dma("sp", dbg["dtv"], dtv); P.dma("sp", dbg["acT"], acT); P.dma("sp", dbg["dsT"], dsT)
                P.dma("sp", dbg["xT0"], xT[0]); P.dma("sp", dbg["xdT0"], xdT[0]); P.dma("sp", dbg["Bc0"], Bc[0])
            def chunk_body(c):
                cs = slice(c * L, (c + 1) * L)
                xtok = P.ring("sd_xtok", 2, [128, NP * 128]); xdtok = P.ring("sd_xdtok", 2, [128, NP * 128])
                for (srcs, dst) in ((xT, xtok), (xdT, xdtok)):
                    for i0 in range(0, NP, 4):
                        ps = S.next_psum(); n = min(4, NP - i0)
                        for i in range(i0, i0 + n):
                            P.op("pe", lambda e, ps=ps, srcs=srcs, i=i, i0=i0: e.transpose(ps.ap[:, (i - i0) * 128:(i - i0 + 1) * 128], srcs[i].ap[:, cs], ident.ap),
                                 reads=[srcs[i], ident], writes=[ps])
                        P.op("act", lambda e, ps=ps, dst=dst, i0=i0, n=n: e.copy(out=dst.ap[:, i0 * 128:(i0 + n) * 128], in_=ps.ap[:, 0:n * 128]), reads=[ps], writes=[dst])
                btok = P.ring("sd_btok", 2, [128, NG * 128])
                ps = S.next_psum()
                for g in range(NG):
                    P.op("pe", lambda e, ps=ps, g=g: e.transpose(ps.ap[:, g * 128:(g + 1) * 128], Bc[g].ap[:, cs], ident.ap), reads=[Bc[g], ident], writes=[ps])
                P.op("act", lambda e, ps=ps, btok=btok: e.copy(out=btok.ap, in_=ps.ap[:, 0:NG * 128]), reads=[ps], writes=[btok])
                datok = P.ring("sd_datok", 2, [128, NH])
                ps = S.next_psum()
                P.op("pe", lambda e, ps=ps: e.transpose(ps.ap[:, 0:NH], da.ap[:, cs], ident.ap[0:NH, 0:NH]), reads=[da, ident], writes=[ps])
                P.op("act", lambda e, ps=ps, datok=datok: e.copy(out=datok.ap, in_=ps.ap[:, 0:NH]), reads=[ps], writes=[datok])
                SM = P.ring("sd_SM", 2, [128, NG, L])
                for g in range(NG):
                    ps = S.next_psum()
                    P.op("pe", lambda e, ps=ps, g=g: e.matmul(ps.ap[:, 0:L], lhsT=Bc[g].ap[:, cs], rhs=Cc[g].ap[:, cs], start=True, stop=True),
                         reads=[Bc[g], Cc[g]], writes=[ps])
                    P.op("dve", lambda e, ps=ps, g=g, SM=SM: e.tensor_tensor(out=SM.ap[:, g, :], in0=ps.ap[:, 0:L], in1=tri.ap, op=ALU.mult),
                         reads=[ps, tri], writes=[SM])
                Vt = P.ring("sd_V", 2, [128, NH, L])
                P.op("dve", lambda e, Vt=Vt, datok=datok: e.tensor_tensor(out=Vt.ap, in0=tri.ap.unsqueeze(1).to_broadcast([128, NH, L]),
                     in1=datok.ap.unsqueeze(2).to_broadcast([128, NH, L]), op=ALU.mult), reads=[tri, datok], writes=[Vt])
                E = P.ring("sd_E", 2, [128, NH, L]); EA = P.ring("sd_EA", 2, [128, NH, L])
                for (lh, dst) in ((ustr, E), (ones, EA)):
                    for h0 in range(0, NH, 4):
                        n = min(4, NH - h0)
                        ps = S.next_psum()
                        P.op("pe", lambda e, ps=ps, lh=lh, Vt=Vt, h0=h0, n=n: e.matmul(ps.ap[:, 0:n * L], lhsT=lh.ap, rhs=Vt.ap[:, h0:h0 + n, :], start=True, stop=True),
                             reads=[lh, Vt], writes=[ps])
                        P.op("act", lambda e, ps=ps, dst=dst, h0=h0, n=n: e.activation(out=dst.ap[:, h0:h0 + n, :], in_=ps.ap[:, 0:n * L], func=AF.Exp),
                             reads=[ps], writes=[dst])
                Cs = P.ring("sd_Cs", 2, [128, NH, L])
                for g in range(NG):
                    hs = slice(g * J, (g + 1) * J)
                    P.op("dve", lambda e, E=E, SM=SM, g=g, hs=hs: e.tensor_tensor(out=E.ap[:, hs, :], in0=E.ap[:, hs, :],
                         in1=SM.ap[:, g:g + 1, :].to_broadcast([128, J, L]), op=ALU.mult), reads=[E, SM], writes=[E])
                    P.op("pool", lambda e, EA=EA, Cs=Cs, g=g, hs=hs: e.tensor_tensor(out=Cs.ap[:, hs, :], in0=EA.ap[:, hs, :],
                         in1=Cc[g].ap[:, cs].unsqueeze(1).to_broadcast([128, J, L]), op=ALU.mult), reads=[EA, Cc[g]], writes=[Cs])
                if dbg and sc == 0 and c == 1:
                    P.dma("sp", dbg["xtok"], xtok); P.dma("sp", dbg["btok"], btok); P.dma("sp", dbg["datok"], datok); P.dma("sp", dbg["SM"], SM.re("p g l -> p (g l)"))
                    P.dma("sp", dbg["E"], E.re("p g l -> p (g l)")); P.dma("sp", dbg["EA"], EA.re("p g l -> p (g l)")); P.dma("sp", dbg["Cs"], Cs.re("p g l -> p (g l)"))
                    P.dma("sp", dbg["prev1"], prevT.re("p g l -> p (g l)"))
                for i0 in range(0, NP, 4):
                    n = min(4, NP - i0)
                    ps = S.next_psum()
                    for i in range(i0, i0 + n):
                        for hh in range(2):
                            h = 2 * i + hh
                            o = ps.ap[hh * 64:(hh + 1) * 64, (i - i0) * L:(i - i0 + 1) * L]
                            P.op("pe", lambda e, o=o, h=h, xtok=xtok, E=E: e.matmul(o, lhsT=xtok.ap[:, h * 64:(h + 1) * 64], rhs=E.ap[:, h, :], start=True, stop=False),
                                 reads=[xtok, E], writes=[ps])
                            P.op("pe", lambda e, o=o, h=h, Cs=Cs: e.matmul(o, lhsT=prevT.ap[:, h, :], rhs=Cs.ap[:, h, :], start=False, stop=True),
                                 reads=[prevT, Cs], writes=[ps])
                    for i in range(i0, i0 + n):
                        P.op("dve", lambda e, ps=ps, i=i, i0=i0: e.scalar_tensor_tensor(out=ysb[i].ap[:, cs], in0=xsc[i].ap[:, cs], scalar=dch.ap[:, i:i + 1],
                             in1=ps.ap[:, (i - i0) * L:(i - i0 + 1) * L], op0=ALU.mult, op1=ALU.add), reads=[xsc[i], dch, ps], writes=[ysb[i]])
                P.op("dve", lambda e, EA=EA: e.tensor_tensor(out=prevT.ap, in0=prevT.ap, in1=EA.ap[:, :, L - 1:L].to_broadcast([128, NH, 64]), op=ALU.mult),
                     reads=[prevT, EA], writes=[prevT])
                for g in range(NG):
                    ps = S.next_psum()
                    P.op("pe", lambda e, ps=ps, g=g, btok=btok, xdtok=xdtok: e.matmul(ps.ap[:, 0:J * 64], lhsT=btok.ap[:, g * 128:(g + 1) * 128],
                         rhs=xdtok.ap[:, g * J * 64:(g + 1) * J * 64], start=True, stop=True), reads=[btok, xdtok], writes=[ps])
                    P.op("dve", lambda e, ps=ps, g=g: e.tensor_tensor(out=prevT.ap[:, g * J:(g + 1) * J, :], in0=prevT.ap[:, g * J:(g + 1) * J, :],
                         in1=ps.ap[:, 0:J * 64].rearrange("p (j d) -> p j d", d=64), op=ALU.add), reads=[ps, prevT], writes=[prevT])
            for c in range(ncs):
                chunk_body(c)
            if dbg and sc == 0:
                P.dma("sp", dbg["ysb0"], ysb[0])
            for g in range(NG):
                pss = S.next_psum()
                for ii in range(TPG):
                    i = g * TPG + ii
                    zt = P.ring("sd_z", 2, [128, SC])
                    P.dma("sp", zt, zT[i * 128:(i + 1) * 128, t0:t0 + SC])
                    act(P, zt, zt, AF.Silu)
                    tt(P, "dve", ysb[i], ysb[i], zt, ALU.mult)
                    act(P, zt, ysb[i], AF.Square)
                    P.op("pe", lambda e, pss=pss, zt=zt, ii=ii: e.matmul(pss.ap, lhsT=ones.ap, rhs=zt.ap, start=(ii == 0), stop=(ii == TPG - 1)),
                         reads=[ones, zt], writes=[pss])
                rs = P.ring("sd_rs", 2, [128, SC])
                act(P, rs, pss, AF.Sqrt, scale=1.0 / (TPG * 128), bias=S.epsc)
                P.op("dve", lambda e, rs=rs: e.reciprocal(out=rs.ap, in_=rs.ap), reads=[rs], writes=[rs])
                for ii in range(TPG):
                    i = g * TPG + ii
                    P.op("dve", lambda e, i=i, rs=rs: e.scalar_tensor_tensor(out=ysb[i].ap, in0=ysb[i].ap, scalar=ngc.ap[:, i:i + 1], in1=rs.ap,
                         op0=ALU.mult, op1=ALU.mult), reads=[ysb[i], ngc, rs], writes=[ysb[i]])
                    P.dma("sp", yT[i * 128:(i + 1) * 128, t0:t0 + SC], ysb[i])

        for sc in range(nsc):
            sc_body(sc)
CH = 512


def emit_lru(P, S, xlT, glT, prm, yT, T, ntile):
    C = ntile * 128
    nch = T // CH
    with P.scope():
        cw = P.sbuf("l_cw", [128, ntile, 4])
        for k in range(4):
            P.dma("sp", cw[:, :, k], prm["conv_w"][k].re("(t p) -> p t", p=128), allow_slow_non_contiguous=True)
        cols = {}
        for nm in ("conv_b", "b_a", "b_x", "lam"):
            cols[nm] = P.sbuf("l_" + nm, [128, ntile])
            P.dma("sp", cols[nm], prm[nm].re("(t p) -> p t", p=128), allow_slow_non_contiguous=True)
        one = P.sbuf("l_one", [128, 1])
        P.op("dve", lambda e: e.memset(one.ap, 1.0), writes=[one])
        c1 = P.sbuf("l_c1", [128, ntile])
        P.op("act", lambda e: e.activation(out=c1.ap, in_=cols["lam"].ap, func=AF.Exp, scale=-1.0), reads=[cols["lam"]], writes=[c1])
        P.op("act", lambda e: e.activation(out=c1.ap, in_=c1.ap, func=AF.Ln, bias=one.ap), reads=[c1, one], writes=[c1])
        P.op("dve", lambda e: e.tensor_scalar(out=c1.ap, in0=c1.ap, scalar1=-8.0, scalar2=None, op0=ALU.mult), reads=[c1], writes=[c1])
        wa = P.sbuf("l_wa", [128, ntile, 128]); wx = P.sbuf("l_wx", [128, ntile, 128])
        P.dma("sp", wa, prm["wa_bd"].re("t p m -> p t m"))
        P.dma("sp", wx, prm["wx_bd"].re("t p m -> p t m"))
        for i in range(ntile):
            rows = slice(i * 128, (i + 1) * 128)
            xl = P.ring("l_xl", 1, [128, T + 3])
            P.op("pool", lambda e, xl=xl: e.memset(xl.ap[:, 0:3], 0.0), writes=[xl])
            P.dma("sp", xl[:, 3:T + 3], xlT[rows, :])
            gl = P.ring("l_gl", 1, [128, T])
            P.dma("sp", gl, glT[rows, :])
            xc = P.ring("l_xc", 1, [128, T])
            P.op("dve", lambda e, xl=xl, xc=xc, i=i: e.tensor_scalar(out=xc.ap, in0=xl.ap[:, 0:T], scalar1=cw.ap[:, i, 0:1],
                 scalar2=cols["conv_b"].ap[:, i:i + 1], op0=ALU.mult, op1=ALU.add), reads=[xl, cw, cols["conv_b"]], writes=[xc])
            for k in range(1, 4):
                P.op("dve", lambda e, xl=xl, xc=xc, i=i, k=k: e.scalar_tensor_tensor(out=xc.ap, in0=xl.ap[:, k:k + T],
                     scalar=cw.ap[:, i, k:k + 1], in1=xc.ap, op0=ALU.mult, op1=ALU.add), reads=[xl, cw, xc], writes=[xc])
            ga = P.ring("l_ga", 1, [128, T]); gi = P.ring("l_gi", 1, [128, T])
            for j in range(nch):
                sl = slice(j * CH, (j + 1) * CH)
                for (wm, bn, dst) in ((wa, "b_a", ga), (wx, "b_x", gi)):
                    ps = S.next_psum()
                    P.op("pe", lambda e, ps=ps, wm=wm, xc=xc, sl=sl, i=i: e.matmul(ps.ap, lhsT=wm.ap[:, i, :], rhs=xc.ap[:, sl], start=True, stop=True),
                         reads=[wm, xc], writes=[ps])
                    P.op("act", lambda e, ps=ps, dst=dst, bn=bn, sl=sl, i=i: e.activation(out=dst.ap[:, sl], in_=ps.ap, func=AF.Sigmoid,
                         bias=cols[bn].ap[:, i:i + 1]), reads=[ps, cols[bn]], writes=[dst])
            P.op("act", lambda e, ga=ga, i=i: e.activation(out=ga.ap, in_=ga.ap, func=AF.Exp, scale=c1.ap[:, i:i + 1]), reads=[ga, c1], writes=[ga])
            mu = P.ring("l_mu", 1, [128, T])
            P.op("pool", lambda e, ga=ga, mu=mu: e.tensor_tensor(out=mu.ap, in0=ga.ap, in1=ga.ap, op=ALU.mult), reads=[ga], writes=[mu])
            P.op("act", lambda e, mu=mu: e.activation(out=mu.ap, in_=mu.ap, func=AF.Sqrt, scale=-1.0, bias=one.ap), reads=[mu, one], writes=[mu])
            P.op("pool", lambda e, mu=mu: e.memset(mu.ap[:, 0:1], 1.0), reads=[mu], writes=[mu])
            P.op("dve", lambda e, gi=gi, xc=xc: e.tensor_tensor(out=gi.ap, in0=gi.ap, in1=xc.ap, op=ALU.mult), reads=[gi, xc], writes=[gi])
            P.op("pool", lambda e, gi=gi, mu=mu: e.tensor_tensor(out=gi.ap, in0=gi.ap, in1=mu.ap, op=ALU.mult), reads=[gi, mu], writes=[gi])
            P.op("dve", lambda e, gi=gi, ga=ga, xc=xc: e.tensor_tensor_scan(out=xc.ap, data0=ga.ap, data1=gi.ap, initial=0.0, op0=ALU.mult, op1=ALU.add),
                 reads=[ga, gi], writes=[xc])
            P.op("act", lambda e, gl=gl: e.activation(out=gl.ap, in_=gl.ap, func=AF.Gelu_apprx_tanh), reads=[gl], writes=[gl])
            P.op("dve", lambda e, gl=gl, xc=xc: e.tensor_tensor(out=xc.ap, in0=xc.ap, in1=gl.ap, op=ALU.mult), reads=[gl, xc], writes=[xc])
            P.dma("sp", yT[rows, :], xc)


def _tt(P, eng, out, a, b, op, r=False):
    o = rr(out.ap) if r else out.ap
    P.op(eng, lambda e: e.tensor_tensor(out=o, in0=a.ap, in1=b.ap, op=op), reads=[a, b], writes=[out])


def _ts(P, eng, out, a, s1, op0, s2=None, op1=None, r=False):
    rd = [a] + [x for x in (s1, s2) if isinstance(x, V)]
    g = lambda x: x.ap if isinstance(x, V) else x
    o = rr(out.ap) if r else out.ap
    if op1 is None:
        P.op(eng, lambda e: e.tensor_scalar(out=o, in0=a.ap, scalar1=g(s1), scalar2=None, op0=op0), reads=rd, writes=[out])
    else:
        P.op(eng, lambda e: e.tensor_scalar(out=o, in0=a.ap, scalar1=g(s1), scalar2=g(s2), op0=op0, op1=op1), reads=rd, writes=[out])


def _act(P, out, a, func, scale=None, bias=None):
    rd = [a] + [x for x in (scale, bias) if isinstance(x, V)]
    kw = {}
    if scale is not None:
        kw["scale"] = scale.ap if isinstance(scale, V) else scale
    if bias is not None:
        kw["bias"] = bias.ap if isinstance(bias, V) else bias
    P.op("act", lambda e: e.activation(out=out.ap, in_=a.ap, func=func, **kw), reads=rd, writes=[out])


def _stt(P, out, in0, scalar, in1, op0, op1):
    rd = [in0, in1] + ([scalar] if isinstance(scalar, V) else [])
    sc = scalar.ap if isinstance(scalar, V) else scalar
    P.op("dve", lambda e: e.scalar_tensor_tensor(out=out.ap, in0=in0.ap, scalar=sc, in1=in1.ap, op0=op0, op1=op1), reads=rd, writes=[out])


R32 = True
F32R = mybir.dt.float32r


def rr(ap):
    return ap.bitcast(F32R) if R32 else ap


def _mm(P, out, lhsT, rhs, start=True, stop=True, fast=False):
    if fast and R32:
        P.op("pe", lambda e: e.matmul(out.ap, lhsT=rr(lhsT.ap), rhs=rr(rhs.ap), start=start, stop=stop), reads=[lhsT, rhs], writes=[out])
    else:
        P.op("pe", lambda e: e.matmul(out.ap, lhsT=lhsT.ap, rhs=rhs.ap, start=start, stop=stop), reads=[lhsT, rhs], writes=[out])


def _tr(P, out, in_, ident):
    P.op("pe", lambda e: e.transpose(out.ap, in_.ap, ident.ap), reads=[in_, ident], writes=[out])


def emit_rwkv(P, S, rT, kT, vT, wlT, alT, glT, prm, cst, yT, T, NP, stage=99, dbg=None):
    C = 64
    SC = 256
    NH = NP * 2
    nsc = T // SC
    ncs = SC // C
    GN_EPS = 64e-5
    with P.scope():
        ld = lambda name, shape, src, **kw: (lambda t: (P.dma("sp", t, src, **kw), t)[1])(P.sbuf(name, shape))
        ident = ld("rw_id", [128, 128], cst["ident"]); bones = ld("rw_bo", [128, 128], cst["bones"])
        mask2 = ld("rw_m2", [64, 128], cst["mask2"]); maskL = ld("rw_mL", [64, 64], cst["maskL"]); rmask = ld("rw_rm", [128, SC], cst["rmask"])
        colv = lambda nm, n, rows=128: ld("rw_" + nm, [rows, n], prm[nm].re("(t p) -> p t", p=rows), allow_slow_non_contiguous=True)
        mu = {x: colv("mu_" + x, NP) for x in "rkv"}
        mu_wl = colv("mu_wl", 1, 96); mu_al = colv("mu_al", 1, 96); mu_gl = colv("mu_gl", 2)
        w0 = colv("w0", NP); a0 = colv("a0", NP); kkc = colv("k_k", NP); kac = colv("k_a", NP); rkc = colv("r_k", NP)
        lng = colv("ln_g", NP); lnb = colv("ln_b", NP)
        wup = ld("rw_wup", [96, NP * 128], prm["w_up"]); aup = ld("rw_aup", [96, NP * 128], prm["a_up"])
        gup = ld("rw_gup", [128, 2, NP * 128], prm["g_up"].re("(k p) n -> p k n", p=128))
        def one_minus(src, name):
            t = P.sbuf(name, list(src.shape))
            _ts(P, "dve", t, src, -1.0, ALU.mult, 1.0, ALU.add)
            return t
        imu = {x: one_minus(mu[x], "rw_imu" + x) for x in "rkv"}
        imu_wl = one_minus(mu_wl, "rw_imuwl"); imu_al = one_minus(mu_al, "rw_imual"); imu_gl = one_minus(mu_gl, "rw_imugl")
        ika = one_minus(kac, "rw_ika")
        gne = P.sbuf("rw_gne", [128, 1]); P.op("dve", lambda e: e.memset(gne.ap, GN_EPS), writes=[gne])
        Hst = P.sbuf("rw_H", [64, NH, 64])
        P.op("dve", lambda e: e.memset(Hst.ap, 0.0), writes=[Hst])

        def shift_mix(srcT, r0, nr, t0, muc, imuc, dst):
            xp = P.ring("rw_xp", 3, [128, SC + 1])
            if t0 == 0:
                P.op("pool", lambda e: e.memset(xp.ap[0:nr, 0:1], 0.0), writes=[xp])
                P.dma("sp", xp[0:nr, 1:SC + 1], srcT[r0:r0 + nr, 0:SC])
            else:
                P.dma("sp", xp[0:nr], srcT[r0:r0 + nr, t0 - 1:t0 + SC])
            tmp = P.ring("rw_smt", 2, [128, SC])
            P.op("act", lambda e: e.activation(out=tmp.ap[0:nr], in_=xp.ap[0:nr, 0:SC], func=AF.Copy, scale=muc.ap),
                 reads=[xp, muc], writes=[tmp])
            P.op("dve", lambda e: e.scalar_tensor_tensor(out=dst.ap, in0=xp.ap[0:nr, 1:SC + 1], scalar=imuc.ap, in1=tmp.ap[0:nr], op0=ALU.mult, op1=ALU.add),
                 reads=[xp, imuc, tmp], writes=[dst])

        def sc_body(sc):
            t0 = sc * SC
            tw = P.ring("rw_tw", 1, [96, SC]); al = P.ring("rw_al", 1, [96, SC]); sg = P.ring("rw_sg", 1, [128, 2, SC])
            shift_mix(wlT, 0, 96, t0, mu_wl[:, 0:1], imu_wl[:, 0:1], tw)
            _act(P, tw, tw, AF.Tanh)
            shift_mix(alT, 0, 96, t0, mu_al[:, 0:1], imu_al[:, 0:1], al)
            for kx in range(2):
                shift_mix(glT, kx * 128, 128, t0, mu_gl[:, kx:kx + 1], imu_gl[:, kx:kx + 1], sg[:, kx, :])
            _act(P, sg, sg, AF.Sigmoid)
            Gc = P.ring("rw_Gc", 1, [64, NH, ncs])
            KRo = []; bho = []; kho = []
            rp = []; vp = []; k2 = []; KR = []; bh = []; kh = []; gt = []; gC = []; bon = []
            for i in range(NP):
                cols = slice(i * 128, (i + 1) * 128)
                r_ = P.ring("rw_r%d" % i, 1, [128, SC]); k_ = P.ring("rw_k%d" % i, 1, [128, SC]); v_ = P.ring("rw_v%d" % i, 1, [128, SC])
                shift_mix(rT, i * 128, 128, t0, mu["r"][:, i:i + 1], imu["r"][:, i:i + 1], r_)
                shift_mix(kT, i * 128, 128, t0, mu["k"][:, i:i + 1], imu["k"][:, i:i + 1], k_)
                shift_mix(vT, i * 128, 128, t0, mu["v"][:, i:i + 1], imu["v"][:, i:i + 1], v_)
                lw = P.ring("rw_lw", 1, [128, SC]); a_ = P.ring("rw_a", 1, [128, SC]); g_ = P.ring("rw_g%d" % i, 1, [128, SC])
                ps = S.next_psum()
                _mm(P, ps[:, 0:SC], wup[:, cols], tw)
                _act(P, lw, ps[:, 0:SC], AF.Sigmoid, bias=w0[:, i:i + 1])
                _ts(P, "dve", lw, lw, -math.exp(-0.5), ALU.mult)
                ps = S.next_psum()
                _mm(P, ps[:, 0:SC], aup[:, cols], al)
                _act(P, a_, ps[:, 0:SC], AF.Sigmoid, bias=a0[:, i:i + 1])
                ps = S.next_psum()
                _mm(P, ps[:, 0:SC], gup[:, 0, cols], sg[:, 0, :], True, False)
                _mm(P, ps[:, 0:SC], gup[:, 1, cols], sg[:, 1, :], False, True)
                P.op("act", lambda e, g_=g_, ps=ps: e.copy(out=g_.ap, in_=ps.ap[:, 0:SC]), reads=[ps], writes=[g_])
                kap = P.ring("rw_kap", 1, [128, SC]); t1 = P.ring("rw_t1", 1, [128, SC]); t2 = P.ring("rw_t2", 1, [128, SC])
                _ts(P, "dve", kap, k_, kkc[:, i:i + 1], ALU.mult)
                _tt(P, "pool", t1, kap, kap, ALU.mult)
                ps = S.next_psum()
                _mm(P, ps[:, 0:SC], bones, t1)
                _ts(P, "dve", t1, ps[:, 0:SC], 1e-24, ALU.max)
                _act(P, t1, t1, AF.Sqrt)
                P.op("dve", lambda e, t1=t1: e.reciprocal(out=t1.ap, in_=t1.ap), reads=[t1], writes=[t1])
                _tt(P, "dve", kap, kap, t1, ALU.mult)
                k2_ = P.ring("rw_k2%d" % i, 1, [128, SC])
                _ts(P, "dve", t1, a_, kac[:, i:i + 1], ALU.mult, ika[:, i:i + 1], ALU.add)
                _tt(P, "dve", k2_, k_, t1, ALU.mult)
                bet = P.ring("rw_bet", 1, [128, SC])
                _tt(P, "pool", bet, kap, a_, ALU.mult)
                cl = P.ring("rw_cl", 1, [128, SC])
                P.op("dve", lambda e, cl=cl, lw=lw: e.tensor_tensor_scan(out=cl.ap, data0=rmask.ap, data1=lw.ap, initial=0.0, op0=ALU.mult, op1=ALU.add),
                     reads=[rmask, lw], writes=[cl])
                eG = P.ring("rw_eG", 1, [128, SC]); eN = P.ring("rw_eN", 1, [128, SC])
                _act(P, eG, cl, AF.Exp)
                _act(P, eN, cl, AF.Exp, scale=-1.0)
                _tt(P, "dve", t2, cl, lw, ALU.subtract)
                _act(P, t2, t2, AF.Exp)
                KR_ = P.ring("rw_KR%d" % i, 1, [128, ncs, 2, C])
                _tt(P, "dve", KR_[:, :, 0, :], kap.re("p (c t) -> p c t", t=C), t2.re("p (c t) -> p c t", t=C), ALU.mult, r=True)
                _tt(P, "pool", KR_[:, :, 1, :], r_.re("p (c t) -> p c t", t=C), eG.re("p (c t) -> p c t", t=C), ALU.mult, r=True)
                bh_ = P.ring("rw_bh%d" % i, 1, [128, SC]); kh_ = P.ring("rw_kh%d" % i, 1, [128, SC])
                _tt(P, "dve", bh_, bet, eN, ALU.mult, r=True)
                _tt(P, "pool", kh_, k2_, eN, ALU.mult, r=True)
                gC_ = P.ring("rw_gC%d" % i, 1, [128, ncs])
                P.op("act", lambda e, gC_=gC_, eG=eG: e.copy(out=gC_.ap, in_=eG.ap.rearrange("p (c t) -> p c t", t=C)[:, :, C - 1]), reads=[eG], writes=[gC_])
                bon_ = P.ring("rw_bon%d" % i, 1, [128, SC])
                _stt(P, t1, r_, rkc[:, i:i + 1], k2_, ALU.mult, ALU.mult)
                ps = S.next_psum()
                _mm(P, ps[:, 0:SC], bones, t1)
                _tt(P, "dve", bon_, ps[:, 0:SC], v_, ALU.mult)
                KRo_ = P.ring("rw_KRo%d" % i, 1, [64, ncs, 2, C]); bho_ = P.ring("rw_bho%d" % i, 1, [64, SC]); kho_ = P.ring("rw_kho%d" % i, 1, [64, SC])
                P.dma("sp", KRo_.bitcast(F32R) if R32 else KRo_, KR_[64:128].bitcast(F32R) if R32 else KR_[64:128])
                P.dma("sp", bho_.bitcast(F32R) if R32 else bho_, bh_[64:128].bitcast(F32R) if R32 else bh_[64:128])
                P.dma("sp", kho_.bitcast(F32R) if R32 else kho_, kh_[64:128].bitcast(F32R) if R32 else kh_[64:128])
                P.op("pool", lambda e, gC_=gC_, i=i: e.tensor_copy(out=Gc.ap[:, 2 * i, :], in_=gC_.ap[0:64, :]), reads=[gC_], writes=[Gc])
                P.dma("sp", Gc[:, 2 * i + 1, :], gC_[64:128, :])
                KRo.append(KRo_); bho.append(bho_); kho.append(kho_)
                rp.append(r_); vp.append(v_); k2.append(k2_); KR.append(KR_); bh.append(bh_); kh.append(kh_); gt.append(g_); gC.append(gC_); bon.append(bon_)
            ysb = [P.ring("rw_y%d" % i, 1, [128, SC]) for i in range(NP)]
            KRh = lambda h: KR[h // 2][0:64] if h % 2 == 0 else KRo[h // 2]
            bhh = lambda h: bh[h // 2][0:64] if h % 2 == 0 else bho[h // 2]
            khh = lambda h: kh[h // 2][0:64] if h % 2 == 0 else kho[h // 2]

            def prep_chunk(c):
                cs = slice(c * C, (c + 1) * C)
                toks = {}
                for nm, srcs in (("v", vp), ("b", bh), ("k", kh)):
                    ps = S.next_psum()
                    for i in range(NP):
                        _tr(P, ps[0:C, i * 128:(i + 1) * 128], srcs[i][:, cs], ident)
                    tk = P.ring("rw_tok" + nm, 2, [64, NP * 128])
                    P.op("act", lambda e, tk=tk, ps=ps: e.copy(out=rr(tk.ap), in_=ps.ap[0:C, 0:NP * 128]), reads=[ps], writes=[tk])
                    toks[nm] = tk
                psN = S.next_psum(); psB = S.next_psum(); psK = S.next_psum(); psB2 = S.next_psum(); psK2 = S.next_psum()
                nb = 4
                for h in range(NH):
                    kapc = KRh(h)[:, c, 0, :]; krc = KRh(h)[:, c, :, :].re("p a t -> p (a t)")
                    _mm(P, psN[0:C, h * C:(h + 1) * C], kapc, bhh(h)[:, cs], fast=True)
                    pb_ = psB if h < nb else psB2
                    pk_ = psK if h < nb else psK2
                    _mm(P, pb_[0:C, (h % nb) * 128:(h % nb + 1) * 128], bhh(h)[:, cs], krc, fast=True)
                    _mm(P, pk_[0:C, (h % nb) * 128:(h % nb + 1) * 128], khh(h)[:, cs], krc, fast=True)
                Q = P.ring("rw_Q", 2, [64, NH, C]); R = P.ring("rw_R", 2, [64, NH, C])
                AB = P.ring("rw_AB", 2, [64, NH, 2, C]); AK = P.ring("rw_AK", 2, [64, NH, 2, C])
                P.op("dve", lambda e, Q=Q: e.scalar_tensor_tensor(out=rr(Q.ap), in0=psN.ap[0:C, 0:NH * C].rearrange("p (h t) -> p h t", t=C), scalar=-1.0,
                     in1=maskL.ap.unsqueeze(1).to_broadcast([64, NH, C]), op0=ALU.mult, op1=ALU.mult), reads=[psN, maskL], writes=[Q])
                for (pp, dst, h0) in ((psB, AB, 0), (psB2, AB, nb), (psK, AK, 0), (psK2, AK, nb)):
                    n = min(nb, NH - h0)
                    if n <= 0:
                        continue
                    P.op("dve", lambda e, pp=pp, dst=dst, h0=h0, n=n: e.tensor_tensor(out=rr(dst.ap[:, h0:h0 + n].rearrange("p h a t -> p h (a t)")),
                         in0=pp.ap[0:C, 0:n * 128].rearrange("p (h x) -> p h x", x=128), in1=mask2.ap.unsqueeze(1).to_broadcast([64, n, 128]), op=ALU.mult),
                         reads=[pp, mask2], writes=[dst])
                QTR = P.ring("rw_QTR", 2, [64, NH, 2, C])
                _ts(P, "dve", QTR[:, :, 0, :], AB[:, :, 0, :], -1.0, ALU.mult, r=True)
                nlev = 6
                for m in range(nlev):
                    last = (m == nlev - 1)
                    if m == 0:
                        psQ = S.next_psum(); psQT = S.next_psum()
                        for h in range(NH):
                            hsl = slice(h * C, (h + 1) * C)
                            _mm(P, psQ[0:C, hsl], QTR[:, h, 0, :], Q[:, h, :], fast=True)
                            _mm(P, psQT[0:C, hsl], Q[:, h, :], QTR[:, h, 0, :], fast=True)
                        Qn = P.ring("rw_Q", 2, [64, NH, C]); QTRn = P.ring("rw_QTR", 2, [64, NH, 2, C])
                        P.op("act", lambda e, Qn=Qn, psQ=psQ: e.copy(out=rr(Qn.ap), in_=psQ.ap[0:C, 0:NH * C].rearrange("p (h t) -> p h t", t=C)), reads=[psQ], writes=[Qn])
                        P.op("act", lambda e, QTRn=QTRn, psQT=psQT: e.copy(out=rr(QTRn.ap[:, :, 0, :]), in_=psQT.ap[0:C, 0:NH * C].rearrange("p (h t) -> p h t", t=C)),
                             reads=[psQT], writes=[QTRn])
                        P.op("dve", lambda e, QTRn=QTRn, QTR=QTR: e.tensor_tensor(out=rr(QTRn.ap[:, :, 1, :]), in0=QTR.ap[:, :, 0, :],
                             in1=ident.ap[0:64, 0:64].unsqueeze(1).to_broadcast([64, NH, C]), op=ALU.add), reads=[QTR, ident], writes=[QTRn])
                        Q, QTR = Qn, QTRn
                    elif m < nlev - 2:
                        psQ = S.next_psum(); psX = [S.next_psum() for _ in range((NH + 3) // 4)]
                        for h in range(NH):
                            _mm(P, psQ[0:C, h * C:(h + 1) * C], QTR[:, h, 0, :], Q[:, h, :], fast=True)
                            _mm(P, psX[h // 4][0:C, (h % 4) * 128:(h % 4 + 1) * 128], Q[:, h, :], QTR[:, h, :, :].re("p a t -> p (a t)"), fast=True)
                        Qn = P.ring("rw_Q", 2, [64, NH, C]); QTRn = P.ring("rw_QTR", 2, [64, NH, 2, C])
                        P.op("act", lambda e, Qn=Qn, psQ=psQ: e.copy(out=rr(Qn.ap), in_=psQ.ap[0:C, 0:NH * C].rearrange("p (h t) -> p h t", t=C)), reads=[psQ], writes=[Qn])
                        for bi, pX in enumerate(psX):
                            h0 = bi * 4; n = min(4, NH - h0)
                            pv = pX.ap[0:C, 0:n * 128].rearrange("p (h a t) -> p h a t", a=2, t=C)
                            P.op("act", lambda e, QTRn=QTRn, pv=pv, h0=h0, n=n: e.copy(out=rr(QTRn.ap[:, h0:h0 + n, 0, :]), in_=pv[:, :, 0, :]), reads=[pX], writes=[QTRn])
                            P.op("dve", lambda e, QTRn=QTRn, QTR=QTR, pv=pv, h0=h0, n=n: e.tensor_tensor(out=rr(QTRn.ap[:, h0:h0 + n, 1, :]), in0=pv[:, :, 1, :],
                                 in1=QTR.ap[:, h0:h0 + n, 1, :], op=ALU.add), reads=[pX, QTR], writes=[QTRn])
                        Q, QTR = Qn, QTRn
                    elif not last:
                        psQ = S.next_psum(); psR = S.next_psum()
                        for h in range(NH):
                            _mm(P, psQ[0:C, h * C:(h + 1) * C], QTR[:, h, 0, :], Q[:, h, :], fast=True)
                            _mm(P, psR[0:C, h * C:(h + 1) * C], Q[:, h, :], QTR[:, h, 1, :], fast=True)
                        Qn = P.ring("rw_Q", 2, [64, NH, C])
                        Rm = P.ring("rw_Rm", 2, [64, NH, C])
                        P.op("act", lambda e, Qn=Qn, psQ=psQ: e.copy(out=rr(Qn.ap), in_=psQ.ap[0:C, 0:NH * C].rearrange("p (h t) -> p h t", t=C)), reads=[psQ], writes=[Qn])
                        P.op("dve", lambda e, Rm=Rm, psR=psR, QTR=QTR: e.tensor_tensor(out=rr(Rm.ap), in0=psR.ap[0:C, 0:NH * C].rearrange("p (h t) -> p h t", t=C),
                             in1=QTR.ap[:, :, 1, :], op=ALU.add), reads=[psR, QTR], writes=[Rm])
                        Q = Qn
                    else:
                        psR = S.next_psum()
                        for h in range(NH):
                            _mm(P, psR[0:C, h * C:(h + 1) * C], Q[:, h, :], Rm[:, h, :], fast=True)
                        P.op("dve", lambda e, psR=psR, Rm=Rm: e.tensor_tensor(out=rr(R.ap), in0=psR.ap[0:C, 0:NH * C].rearrange("p (h t) -> p h t", t=C),
                             in1=Rm.ap, op=ALU.add), reads=[psR, Rm], writes=[R])
                if dbg and sc == 0 and c == 0:
                    P.dma("sp", dbg["R"], R.re("p h t -> p (h t)")); P.dma("sp", dbg["AB"], AB.re("p h a t -> p (h a t)")); P.dma("sp", dbg["AK"], AK.re("p h a t -> p (h a t)"))
                    P.dma("sp", dbg["vt"], toks["v"]); P.dma("sp", dbg["bt"], toks["b"])
                return dict(toks=toks, R=R, AB=AB, AK=AK)

            def seq_chunk(c, pc):
                cs = slice(c * C, (c + 1) * C)
                toks, R, AB, AK = pc["toks"], pc["R"], pc["AB"], pc["AK"]
                vt, bt, kt = toks["v"], toks["b"], toks["k"]
                psW = S.next_psum()
                Hr = P.ring("rw_Hr", 2, [64, NH, 64])
                P.op("act", lambda e: e.copy(out=rr(Hr.ap), in_=Hst.ap), reads=[Hst], writes=[Hr])
                for h in range(NH):
                    _mm(P, psW[0:C, h * 64:(h + 1) * 64], KRh(h)[:, c, 0, :], Hr[:, h, :], True, False, fast=True)
                    _mm(P, psW[0:C, h * 64:(h + 1) * 64], AK[:, h, 0, :], vt[:, h * 64:(h + 1) * 64], False, True, fast=True)
                Wsb = P.ring("rw_W", 2, [64, NH * 64])
                P.op("act", lambda e: e.copy(out=rr(Wsb.ap), in_=psW.ap[0:C, 0:NH * 64]), reads=[psW], writes=[Wsb])
                psU = S.next_psum()
                for h in range(NH):
                    _mm(P, psU[0:C, h * 64:(h + 1) * 64], R[:, h, :], Wsb[:, h * 64:(h + 1) * 64], fast=True)
                Usb = P.ring("rw_U", 2, [64, NH * 64])
                _ts(P, "dve", Usb, psU[0:C, 0:NH * 64], -1.0, ALU.mult, r=True)
                if dbg and sc == 0 and c == 0:
                    P.dma("sp", dbg["W"], Wsb); P.dma("sp", dbg["U"], Usb)
                psY = S.next_psum(); psH = S.next_psum()
                for h in range(NH):
                    i, hh = divmod(h, 2); pr = slice(hh * 64, (hh + 1) * 64)
                    oy = psY[pr, i * C:(i + 1) * C]
                    fy = (hh == 0)
                    _mm(P, oy, Hr[:, h, :], KRh(h)[:, c, 1, :], True, False, fast=fy)
                    _mm(P, oy, Usb[:, h * 64:(h + 1) * 64], AB[:, h, 1, :], False, False, fast=fy)
                    _mm(P, oy, vt[:, h * 64:(h + 1) * 64], AK[:, h, 1, :], False, True, fast=fy)
                for h in range(NH):
                    oh = psH[0:64, h * 64:(h + 1) * 64]
                    _mm(P, oh, bt[:, h * 64:(h + 1) * 64], Usb[:, h * 64:(h + 1) * 64], True, False, fast=True)
                    _mm(P, oh, kt[:, h * 64:(h + 1) * 64], vt[:, h * 64:(h + 1) * 64], False, True, fast=True)
                for i in range(NP):
                    P.op("act", lambda e, i=i: e.copy(out=ysb[i].ap[:, cs], in_=psY.ap[:, i * C:(i + 1) * C]), reads=[psY], writes=[ysb[i]])
                P.op("dve", lambda e: e.tensor_tensor(out=Hst.ap, in0=Hst.ap, in1=psH.ap[0:64, 0:NH * 64].rearrange("p (i v) -> p i v", v=64), op=ALU.add),
                     reads=[Hst, psH], writes=[Hst])
                P.op("dve", lambda e: e.tensor_tensor(out=Hst.ap, in0=Hst.ap, in1=Gc.ap[:, :, c:c + 1].to_broadcast([64, NH, 64]), op=ALU.mult),
                     reads=[Hst, Gc], writes=[Hst])

            if dbg and sc == 0:
                P.dma("sp", dbg["KR0"], KR[0].re("p c a t -> p (c a t)")); P.dma("sp", dbg["bh0"], bh[0]); P.dma("sp", dbg["kh0"], kh[0])
                P.dma("sp", dbg["KRo0"], KRo[0].re("p c a t -> p (c a t)")); P.dma("sp", dbg["Gc"], Gc.re("p h c -> p (h c)"))
            if stage == 1:
                for i in range(NP):
                    P.dma("sp", yT[i * 128:(i + 1) * 128, t0:t0 + SC], bon[i])
                return
            pcs = prep_chunk(0)
            if stage == 2:
                return
            if stage == 3:
                seq_chunk(0, pcs)
                return
            for c in range(ncs):
                nxt = prep_chunk(c + 1) if c + 1 < ncs else None
                seq_chunk(c, pcs)
                pcs = nxt
            if dbg and sc == 0:
                P.dma("sp", dbg["y0"], ysb[0]); P.dma("sp", dbg["H"], Hst.re("p h v -> p (h v)"))
            for i in range(NP):
                y = ysb[i]
                t1 = P.ring("rw_t1", 1, [128, SC]); t2 = P.ring("rw_t2", 1, [128, SC])
                ps = S.next_psum()
                _mm(P, ps[:, 0:SC], bones, y)
                _stt(P, y, ps[:, 0:SC], -1.0 / 64, y, ALU.mult, ALU.add)
                _tt(P, "pool", t1, y, y, ALU.mult)
                ps = S.next_psum()
                _mm(P, ps[:, 0:SC], bones, t1)
                _act(P, t2, ps[:, 0:SC], AF.Sqrt, scale=1.0 / 64, bias=gne)
                P.op("dve", lambda e, t2=t2: e.reciprocal(out=t2.ap, in_=t2.ap), reads=[t2], writes=[t2])
                _stt(P, y, y, lng[:, i:i + 1], t2, ALU.mult, ALU.mult)
                _stt(P, y, y, lnb[:, i:i + 1], bon[i], ALU.add, ALU.add)
                _tt(P, "dve", y, y, gt[i], ALU.mult)
                P.dma("sp", yT[i * 128:(i + 1) * 128, t0:t0 + SC], y)

        for sc in range(nsc):
            sc_body(sc)
import numpy as _np

NT_CORE = 2048
SEQ = 4096
NB = 4


def tile_w(W):
    K, N = W.shape
    NCB = (N + 127) // 128
    Wp = _np.zeros((K, NCB * 128), _np.float32)
    Wp[:, :N] = W
    return _np.ascontiguousarray(Wp.reshape(K // 128, 128, NCB, 128).transpose(2, 1, 0, 3))


def consts_all():
    ident = _np.eye(128, dtype=_np.float32)
    psw = _np.zeros((128, 128), _np.float32)
    for k in range(128):
        psw[k, (k + 64) % 128] = 1
    sgn = _np.ones((128, 1), _np.float32); sgn[64:] = -1
    s = _np.arange(128)
    tri = (s[:, None] <= s[None, :]).astype(_np.float32)
    ustr = (s[:, None] > s[None, :]).astype(_np.float32)
    sel = _np.zeros((12, 768), _np.float32)
    for h in range(12):
        sel[h, h * 64:(h + 1) * 64] = 1
    bones = _np.zeros((128, 128), _np.float32); bones[:64, :64] = 1; bones[64:, 64:] = 1
    s6 = _np.arange(64)
    mU = (s6[:, None] < s6[None, :]).astype(_np.float32); mUi = (s6[:, None] <= s6[None, :]).astype(_np.float32)
    mask2 = _np.concatenate([mU, mUi], 1)
    maskL = (s6[None, :] < s6[:, None]).astype(_np.float32)
    rmask = _np.ones((128, 256), _np.float32); rmask[:, ::64] = 0
    return dict(ident=ident, psw=psw, sgn=sgn, tri=tri, ustr=ustr, sel=sel, bones=bones, mask2=mask2, maskL=maskL, rmask=rmask)


def s5_host(lam_re, lam_im, log_step, b_re, b_im, c_re, c_im, d):
    G = lam_re.shape[0]
    rep = lambda a: _np.ascontiguousarray(_np.broadcast_to(a[None], (16,) + a.shape)).astype(_np.float32)
    col = lambda a: _np.ascontiguousarray(_np.concatenate([a.T, a.T], 0)).astype(_np.float32)
    ls = _np.broadcast_to(log_step[:, None], (G, 64))
    return dict(lr_row=rep(lam_re), li_row=rep(lam_im), ls_row=rep(ls),
                bT_re=_np.ascontiguousarray(b_re.transpose(2, 0, 1)), bT_im=_np.ascontiguousarray(b_im.transpose(2, 0, 1)),
                lr_col=col(lam_re), li_col=col(lam_im), ls_col=col(ls),
                cT=_np.ascontiguousarray(_np.concatenate([c_re.transpose(2, 0, 1), c_im.transpose(2, 0, 1)], 0)),
                d_col=_np.ascontiguousarray(d.reshape(G, 16).T))


def even_params(inp, hh):
    g0 = hh * 16
    p = {"s5_" + k: v for k, v in s5_host(inp["s5_lam_re"][0, g0:g0 + 16], inp["s5_lam_im"][0, g0:g0 + 16], inp["s5_log_step"][0, g0:g0 + 16],
                                          inp["s5_b_re"][0, g0:g0 + 16], inp["s5_b_im"][0, g0:g0 + 16], inp["s5_c_re"][0, g0:g0 + 16],
                                          inp["s5_c_im"][0, g0:g0 + 16], inp["s5_d"][0, hh * 256:(hh + 1) * 256]).items()}
    cw = inp["ssd_conv_w"][0]; cb = inp["ssd_conv_b"][0]
    xs = slice(hh * 768, (hh + 1) * 768); bs = slice(1536 + hh * 256, 1536 + (hh + 1) * 256); cs = slice(2048 + hh * 256, 2048 + (hh + 1) * 256)
    hs = slice(hh * 12, (hh + 1) * 12)
    c = lambda a: _np.ascontiguousarray(a, dtype=_np.float32)
    p.update(sd_cw_x=c(cw[:, xs]), sd_cb_x=c(cb[xs]), sd_cw_b=c(cw[:, bs]), sd_cb_b=c(cb[bs]), sd_cw_c=c(cw[:, cs]), sd_cb_c=c(cb[cs]),
             sd_dt_bias=c(inp["ssd_dt_bias"][0, hs]), sd_a_log=c(inp["ssd_a_log"][0, hs]),
             sd_d_ch=c(_np.repeat(inp["ssd_d"][0, hs], 64)), sd_ng=c(inp["ssd_norm"][0, xs]))
    return p


def lru_bd(w, hh):
    o = _np.zeros((4, 128, 128), _np.float32)
    for bl in range(8):
        t, h = divmod(bl, 2)
        o[t, h * 64:(h + 1) * 64, h * 64:(h + 1) * 64] = w[hh * 8 + bl]
    return o


def odd_params(inp, hh):
    c = lambda a: _np.ascontiguousarray(a, dtype=_np.float32)
    mu = inp["rwkv_mu"][0]
    ch = slice(hh * 512, (hh + 1) * 512)
    p = dict(rw_mu_r=c(mu[0:1024][ch]), rw_mu_k=c(mu[1024:2048][ch]), rw_mu_v=c(mu[2048:3072][ch]), rw_mu_wl=c(mu[3072:3168]), rw_mu_al=c(mu[3168:3264]),
             rw_mu_gl=c(mu[3264:3520]), rw_w0=c(inp["rwkv_w0"][0, ch]), rw_a0=c(inp["rwkv_a0"][0, ch]), rw_k_k=c(inp["rwkv_k_k"][0, ch]),
             rw_k_a=c(inp["rwkv_k_a"][0, ch]), rw_r_k=c(inp["rwkv_r_k"][0].reshape(-1)[ch]), rw_ln_g=c(inp["rwkv_ln_g"][0, ch]), rw_ln_b=c(inp["rwkv_ln_b"][0, ch]),
             rw_w_up=c(inp["rwkv_w_up"][0][:, ch]), rw_a_up=c(inp["rwkv_a_up"][0][:, ch]), rw_g_up=c(inp["rwkv_g_up"][0][:, ch]))
    p.update(lr_conv_w=c(inp["lru_conv_w"][0][:, ch]), lr_conv_b=c(inp["lru_conv_b"][0, ch]), lr_wa_bd=lru_bd(inp["lru_w_a"][0], hh), lr_wx_bd=lru_bd(inp["lru_w_x"][0], hh),
             lr_b_a=c(inp["lru_b_a"][0].reshape(-1)[ch]), lr_b_x=c(inp["lru_b_x"][0].reshape(-1)[ch]), lr_lam=c(inp["lru_lam"][0].reshape(-1)[ch]))
    return p


def dense_w(inp, L):
    c = lambda a: _np.ascontiguousarray(a, dtype=_np.float32)
    return dict(out_t=tile_w(inp["e_out_proj" if L == 0 else "o_out_proj"][0]), w1_t=tile_w(inp["mlp_w1"][L]), w2_t=tile_w(inp["mlp_w2"][L]),
                gate_t=tile_w(inp["pl_gate"][L]), plp_t=tile_w(inp["pl_proj"][L]), nffn=c(inp["norm_ffn"][L]), npl=c(inp["norm_pl"][L]))


class Launch:
    def __init__(self):
        self.nc = bass.Bass("TRN2", target_bir_lowering=False)
        self.st = contextlib.ExitStack()
        self.P = Prog(self.nc, self.st)
        self.S = Shared(self.P)
        self.outs = []

    def inp(self, name, arr):
        return self.P.dram(name, list(arr.shape), kind="ExternalInput")

    def inps(self, d, prefix=""):
        return {k: self.inp(prefix + k, v) for k, v in d.items()}

    def out(self, name, shape):
        v = self.P.dram(name, list(shape), kind="ExternalOutput")
        self.outs.append(v)
        return v

    def run(self, in_maps):
        self.P.wait_all("sp", self.outs)
        self.P.finish()
        self.st.close()
        res = run_bass_kernel_spmd(self.nc, in_maps, core_ids=list(range(len(in_maps))))
        return res.results


def strip(d, prefix):
    return {k[len(prefix):]: v for k, v in d.items() if k.startswith(prefix)}


PAIRS = [[0, 1], [2, 3], [4, 5], [6, 7]]
ALL8 = [list(range(8))]


def pad_cols(W, n):
    o = _np.zeros((W.shape[0], n), _np.float32)
    o[:, :W.shape[1]] = W
    return o


def host_inputs(inp):
    f32c = lambda a: _np.ascontiguousarray(a, dtype=_np.float32)
    x = inp["x"]; p = inp["p"]
    cst = consts_all()
    ein = inp["e_in_proj"][0]; oin = inp["o_in_proj"][0]
    eout = inp["e_out_proj"][0]; oout = inp["o_out_proj"][0]
    shared = {}
    for L in range(2):
        shared["w1_%d" % L] = tile_w(inp["mlp_w1"][L]); shared["w2_%d" % L] = tile_w(inp["mlp_w2"][L])
        shared["gate_%d" % L] = tile_w(inp["pl_gate"][L]); shared["plp_%d" % L] = tile_w(inp["pl_proj"][L])
    per_hh = []
    for hh in range(2):
        d = {}
        cols0 = _np.concatenate([ein[:, hh * 256:(hh + 1) * 256], ein[:, 512 + hh * 768:512 + (hh + 1) * 768], ein[:, 2048 + hh * 768:2048 + (hh + 1) * 768],
                                 ein[:, 3584 + hh * 256:3584 + (hh + 1) * 256], ein[:, 4096 + hh * 256:4096 + (hh + 1) * 256],
                                 pad_cols(ein[:, 4608 + hh * 12:4608 + (hh + 1) * 12], 128)], 1)
        d["win0_t"] = tile_w(cols0)
        ch = lambda o: oin[:, o + hh * 512:o + (hh + 1) * 512]
        cols1 = _np.concatenate([ch(0), ch(1024), ch(2048), pad_cols(oin[:, 3072:3168], 128), pad_cols(oin[:, 3168:3264], 128), oin[:, 3264:3520], ch(3520), ch(4544)], 1)
        d["win1_t"] = tile_w(cols1)
        d["wout0_t"] = tile_w(_np.concatenate([eout[hh * 256:(hh + 1) * 256], eout[512 + hh * 768:512 + (hh + 1) * 768]], 0))
        d["wout1_t"] = tile_w(_np.concatenate([oout[hh * 512:(hh + 1) * 512], oout[1024 + hh * 512:1024 + (hh + 1) * 512]], 0))
        d["gluw_t"] = tile_w(inp["s5_glu_w"][0][hh * 256:(hh + 1) * 256])
        d["glub"] = f32c(inp["s5_glu_b"][0][hh * 256:(hh + 1) * 256])
        d.update(even_params(inp, hh)); d.update(odd_params(inp, hh))
        per_hh.append(d)
    ins = []
    for c in range(8):
        b, hh = divmod(c, 2)
        ts_ = slice(hh * NT_CORE, (hh + 1) * NT_CORE)
        d = dict(xT=f32c(x[b, ts_, :].T), pT0=f32c(p[0, b, ts_, :].T), pT1=f32c(p[1, b, ts_, :].T))
        for k, v in shared.items():
            d[k + "_t"] = v
        xf = x[b].T.reshape(8, 256, 2, NT_CORE).transpose(0, 2, 1, 3)
        d["xfull"] = f32c(xf)
        for L in range(2):
            d["nmix%d" % L] = f32c(inp["norm_mix"][L]); d["nffn%d" % L] = f32c(inp["norm_ffn"][L]); d["npl%d" % L] = f32c(inp["norm_pl"][L])
        d["nfin"] = f32c(inp["norm_final"])
        d.update(per_hh[hh]); d.update({"c_" + k: v for k, v in cst.items()})
        ins.append(d)
    return ins


def build_fused(ex):
    la = Launch()
    P, S = la.P, la.S
    dd = la.inps(ex)
    cs_d = strip(dd, "c_")
    T = SEQ; NT = NT_CORE
    hfull0 = dd["xfull"]
    Wf = [{}, {}]
    for L in range(2):
        for nm in ("w1", "w2", "gate", "plp"):
            Wf[L][nm + "_t"] = dd["%s_%d_t" % (nm, L)]
        Wf[L]["nffn"] = dd["nffn%d" % L]; Wf[L]["npl"] = dd["npl%d" % L]

    def rs_mix(mp, mix):
        for q in range(4):
            P.collective("ReduceScatter", mix[q * 512:(q + 1) * 512, :], mp[q].re("s f t -> (s f) t"), PAIRS, op=ALU.add)
    pin0 = P.dram("pin0", [19 * 128, T])
    for s in range(2):
        with P.scope():
            emit_dense_in(P, S, hfull0[:, s], dd["nmix0"], dd["win0_t"], 19, None, pin0[:, s * NT:(s + 1) * NT], NT)
    yT0 = P.dram("yT0", [1024, T])
    emit_s5(P, S, pin0[0:256], strip(dd, "s5_"), cs_d, yT0[0:256], T, 16)
    emit_ssd(P, S, pin0[256:1024], pin0[1024:1792], pin0[1792:2048], pin0[2048:2304], pin0[2304:2316], strip(dd, "sd_"), cs_d, yT0[256:1024], T, 12, 2)
    zp = P.dram("zp", [2, 256, T]); zr = P.dram("zr", [256, T])
    emit_glu_partial(P, S, yT0[0:256], dd["gluw_t"], zp, T)
    P.collective("ReduceScatter", zr, zp.re("s r t -> (s r) t"), PAIRS, op=ALU.add)
    mp0 = P.dram("mp0", [4, 2, 512, NT]); mix0 = P.dram("mix0", [2048, NT])
    emit_outproj_partial(P, S, yT0, dd["wout0_t"], mp0, T, glu=(zr, dd["glub"]))
    rs_mix(mp0, mix0)
    hb1 = P.dram("hb1", [2048, NT]); h1d0 = P.dram("h1d0", [2048, NT])
    emit_mlp_gate_v2(P, S, 0, dd["xT"], mix0, dd["pT0"], hb1, NT, Wf[0], h1d0)
    hfull1 = P.dram("hfull1", [8, 2, 256, NT])
    for q in range(8):
        P.collective("AllGather", hfull1[q].re("s f t -> (s f) t"), hb1[q * 256:(q + 1) * 256, :], PAIRS)
    pin1 = P.dram("pin1", [24 * 128, T])
    for s in range(2):
        with P.scope():
            emit_dense_in(P, S, hfull1[:, s], dd["nmix1"], dd["win1_t"], 24, None, pin1[:, s * NT:(s + 1) * NT], NT)
    yT1 = P.dram("yT1", [1024, T])
    emit_rwkv(P, S, pin1[0:512], pin1[512:1024], pin1[1024:1536], pin1[1536:1632], pin1[1664:1760], pin1[1792:2048], strip(dd, "rw_"), cs_d, yT1[0:512], T, 4)
    emit_lru(P, S, pin1[2048:2560], pin1[2560:3072], strip(dd, "lr_"), yT1[512:1024], T, 4)
    mp1 = P.dram("mp1", [4, 2, 512, NT]); mix1 = P.dram("mix1", [2048, NT])
    emit_outproj_partial(P, S, yT1, dd["wout1_t"], mp1, T)
    rs_mix(mp1, mix1)
    oT = la.out("outT", [2048, NT])
    h1d1 = P.dram("h1d1", [2048, NT])
    emit_mlp_gate_v2(P, S, 1, hb1, mix1, dd["pT1"], None, NT, Wf[1], h1d1, final=dd["nfin"], outT=oT)
    return la


def kernel(**inp):
    inp = {k: _np.asarray(v) for k, v in inp.items()}
    ins = host_inputs(inp)
    la = build_fused(ins[0])
    res = la.run(ins)
    out = _np.zeros((NB, SEQ, 2048), _np.float32)
    for c in range(8):
        b, hh = divmod(c, 2)
        out[b, hh * NT_CORE:(hh + 1) * NT_CORE, :] = res[c]["outT"].T
    return out
```

```python
import math, contextlib
import numpy as np
import concourse.bass as bass
import concourse.mybir as mybir
from concourse.bass_utils import run_bass_kernel_spmd

F32 = mybir.dt.float32
BF16 = mybir.dt.bfloat16
AF = mybir.ActivationFunctionType
ALU = mybir.AluOpType
AX = mybir.AxisListType

SAME_ENG_SYNC = True
CC_INC = 1


class Buf:
    __slots__ = ("name", "wconds", "rconds", "wsem", "wcount", "rsem", "rcount")

    ALL = []

    def __init__(self, name):
        Buf.ALL.append(self)
        self.name = name
        self.wconds = {}
        self.rconds = {}
        self.wsem = None
        self.wcount = 0
        self.rsem = None
        self.rcount = 0


class V:
    __slots__ = ("ap", "buf")

    def __init__(self, ap, buf):
        self.ap = ap
        self.buf = buf

    def __getitem__(self, key):
        return V(self.ap[key], self.buf)

    def re(self, s, **kw):
        return V(self.ap.rearrange(s, **kw), self.buf)

    def bc(self, shape):
        return V(self.ap.to_broadcast(shape), self.buf)

    def bitcast(self, dt):
        return V(self.ap.bitcast(dt), self.buf)

    @property
    def shape(self):
        return self.ap.shape


class Prog:
    ENG = ("pe", "dve", "act", "pool", "sp")

    def __init__(self, nc, stack):
        self.nc = nc
        Buf.ALL = []
        self.stack = stack
        self.engobj = {"pe": nc.tensor, "dve": nc.vector, "act": nc.scalar,
                       "pool": nc.gpsimd, "sp": nc.sync}
        self.q = {e: [] for e in self.ENG}
        self.cnt = {e: 0 for e in self.ENG}
        self.sems = {}
        self.nsem = 0
        for e in self.ENG:
            self.sems[("eng", e)] = self._newsem("c_" + e)
        self.known = {e: {} for e in self.ENG}
        self.uid = 0
        self.stacks = [stack]
        self.free_sems = []
        self.scope_sems = [[]]
        self.semval = {}
        self.ring_store = {}

    def _newsem(self, name):
        self.nsem += 1
        return self.stack.enter_context(self.nc.semaphore(name + "_%d" % self.nsem))

    def _dma_sem(self, key, name):
        if key not in self.sems:
            if self.free_sems:
                h, v = self.free_sems.pop()
            else:
                h, v = self._newsem("d"), 0
            self.sems[key] = h
            self.semval[key] = v
            self.scope_sems[-1].append(key)
        return self.sems[key]

    @contextlib.contextmanager
    def scope(self):
        es = contextlib.ExitStack()
        self.stacks.append(es)
        self.scope_sems.append([])
        mark = set(self.ring_store.keys())
        try:
            yield
        finally:
            self.barrier()
            for k in list(self.ring_store.keys()):
                if k not in mark:
                    del self.ring_store[k]
            for key in self.scope_sems.pop():
                self.free_sems.append((self.sems.pop(key), self.semval.pop(key)))
                for e in self.ENG:
                    self.known[e].pop(key, None)
            self.stacks.pop()
            es.close()

    def barrier(self):
        conds = {("eng", e): self.cnt[e] for e in self.ENG if self.cnt[e] > 0}
        for key, v in self.semval.items():
            if v > 0:
                conds[key] = v
        for e in self.ENG:
            waits = {}
            kn = self.known[e]
            for k, val in conds.items():
                if k == ("eng", e):
                    if e == "sp":
                        continue
                if kn.get(k, 0) < val:
                    waits[k] = val
            wl = self._emit_waits(e, waits)

            def thunk(en, wl=wl):
                for s_, v_ in wl:
                    en.wait_ge(s_, v_)
            self.q[e].append(thunk)
        for b in Buf.ALL:
            b.wconds = {}
            b.rconds = {}

    def ring(self, name, n, shape, dt=F32):
        if name not in self.ring_store:
            self.ring_store[name] = [[self.sbuf("%s_%d" % (name, i), shape, dt) for i in range(n)], 0]
        r = self.ring_store[name]
        b = r[0][r[1] % n]
        r[1] += 1
        return b

    def sbuf(self, name, shape, dt=F32):
        self.uid += 1
        t = self.stacks[-1].enter_context(self.nc.sbuf_tensor("%s_u%d" % (name, self.uid), list(shape), dt))
        return V(t.ap() if hasattr(t, "ap") and callable(getattr(t, "ap")) else t[:], Buf(name))

    def psum(self, name, shape, dt=F32):
        t = self.stack.enter_context(self.nc.psum_tensor(name, list(shape), dt))
        return V(t.ap() if hasattr(t, "ap") and callable(getattr(t, "ap")) else t[:], Buf(name))

    def dram(self, name, shape, dt=F32, kind="Internal"):
        t = self.nc.dram_tensor(name, list(shape), dt, kind=kind)
        return V(t.ap(), Buf(name))

    def alias(self, v, name):
        return V(v.ap, Buf(name))

    def _need(self, eng, conds, waits):
        kn = self.known[eng]
        for k, val in conds.items():
            if k not in self.sems:
                continue
            if k == ("eng", eng):
                if eng == "pe" or not SAME_ENG_SYNC:
                    continue
            if kn.get(k, 0) >= val:
                continue
            if waits.get(k, 0) < val:
                waits[k] = val

    def _emit_waits(self, eng, waits):
        kn = self.known[eng]
        out = []
        for k, val in waits.items():
            kn[k] = max(kn.get(k, 0), val)
            out.append((self.sems[k], val))
        return out

    def op(self, eng, fn, reads=(), writes=()):
        waits = {}
        for v in reads:
            self._need(eng, v.buf.wconds, waits)
        for v in writes:
            self._need(eng, v.buf.wconds, waits)
            self._need(eng, v.buf.rconds, waits)
        wl = self._emit_waits(eng, waits)
        self.cnt[eng] += 1
        n = self.cnt[eng]
        k = ("eng", eng)
        sem = self.sems[k]

        def thunk(e, wl=wl, fn=fn, sem=sem):
            for s, val in wl:
                e.wait_ge(s, val)
            fn(e).then_inc(sem, 1)
        self.q[eng].append(thunk)
        for v in reads:
            v.buf.rconds[k] = n
        for v in writes:
            v.buf.wconds = {k: n}
            v.buf.rconds = {}
        return n

    def dma(self, queue, out, in_, **kw):
        eng = queue
        waits = {}
        self._need(eng, in_.buf.wconds, waits)
        own_w = ("w", id(out.buf))
        for kk, val in out.buf.wconds.items():
            if kk == own_w:
                continue
            self._need(eng, {kk: val}, waits)
        self._need(eng, out.buf.rconds, waits)
        wl = self._emit_waits(eng, waits)
        b = out.buf
        key = ("w", id(b))
        sem = self._dma_sem(key, b.name)
        self.semval[key] += 16
        val = self.semval[key]
        b.wcount = val
        self._keep = getattr(self, "_keep", [])
        self._keep.append(b)
        rb = in_.buf
        rkey = None

        def thunk(e, wl=wl, sem=sem, o=out.ap, i=in_.ap, kw=kw):
            for s, v_ in wl:
                e.wait_ge(s, v_)
            e.dma_start(out=o, in_=i, **kw).then_inc(sem, 16)
        self.q[eng].append(thunk)
        if own_w in out.buf.wconds or not out.buf.wconds or True:
            newc = {key: val}
            out.buf.wconds = newc
            out.buf.rconds = {}
        in_.buf.rconds[key] = max(in_.buf.rconds.get(key, 0), val)

    def collective(self, kind, out, in_, groups, op=None):
        eng = "pool"
        waits = {}
        self._need(eng, in_.buf.wconds, waits)
        self._need(eng, out.buf.wconds, waits)
        self._need(eng, out.buf.rconds, waits)
        wl = self._emit_waits(eng, waits)
        key = ("cc", id(out.buf))
        if key not in self.sems:
            self.sems[key] = self._newsem("cc")
            self.semval[key] = 0
        self.semval[key] += CC_INC
        val = self.semval[key]
        sem = self.sems[key]
        self._keep = getattr(self, "_keep", [])
        self._keep.append(out.buf)
        op = ALU.bypass if op is None else op

        def thunk(e, wl=wl, sem=sem, o=out.ap, i=in_.ap):
            for s_, v_ in wl:
                e.wait_ge(s_, v_)
            e.collective_compute(kind, op, replica_groups=groups, ins=[i], outs=[o]).then_inc(sem, CC_INC)
        self.q[eng].append(thunk)
        out.buf.wconds = {key: val}
        out.buf.rconds = {}
        in_.buf.rconds[key] = val

    def wait_all(self, eng, views):
        waits = {}
        for v in views:
            self._need(eng, v.buf.wconds, waits)
        wl = self._emit_waits(eng, waits)

        def thunk(e, wl=wl):
            for s, v_ in wl:
                e.wait_ge(s, v_)
        self.q[eng].append(thunk)

    def finish(self):
        nc = self.nc
        with nc.Block() as block:
            @block.tensor
            def _(e):
                for t in self.q["pe"]:
                    t(e)

            @block.vector
            def _(e):
                for t in self.q["dve"]:
                    t(e)

            @block.scalar
            def _(e):
                for t in self.q["act"]:
                    t(e)

            @block.gpsimd
            def _(e):
                for t in self.q["pool"]:
                    t(e)

            @block.sync
            def _(e):
                for t in self.q["sp"]:
                    t(e)

D = 2048
KT_D = 16
EPS = 1e-6
CH = 512
MMDT = BF16


class Shared:
    def __init__(self, P):
        self.P = P
        self.ps = [P.psum("psr%d" % i, [128, 512]) for i in range(8)]
        self.pi = 0
        self.ones = P.sbuf("ones", [128, 128])
        P.op("dve", lambda e: e.memset(self.ones.ap, 1.0), writes=[self.ones])
        self.epsc = P.sbuf("epsc", [128, 1])
        P.op("dve", lambda e: e.memset(self.epsc.ap, EPS), writes=[self.epsc])
        self.rr = {}
        self.castn = 0

    def next_psum(self):
        p = self.ps[self.pi % 8]
        self.pi += 1
        return p

    def ring(self, name, n, shape, dt=F32):
        return self.P.ring(name, n, shape, dt)


def load_cols(P, dst, vec_dram, KT):
    with P.nc.allow_non_contiguous_dma(reason="tiny param vector"):
        pass
    P.dma("sp", dst, vec_dram.re("(k p) -> p k", p=128), allow_slow_non_contiguous=True)


def rmsnorm_T(P, S, src, gcol, out, KT, NT, out_scale_extra=None):
    pss = S.next_psum()
    for k in range(KT):
        sq = S.ring("sq", 3, [128, CH])
        P.op("act", lambda e, sq=sq, k=k: e.activation(out=sq.ap[:, 0:NT], in_=src.ap[:, k, :], func=AF.Square),
             reads=[src], writes=[sq])
        P.op("pe", lambda e, sq=sq, k=k: e.matmul(pss.ap[:, 0:NT], lhsT=S.ones.ap, rhs=sq.ap[:, 0:NT],
                                                 start=(k == 0), stop=(k == KT - 1)),
             reads=[S.ones, sq], writes=[pss])
    rs = S.ring("rstd", 2, [128, CH])
    P.op("act", lambda e: e.activation(out=rs.ap[:, 0:NT], in_=pss.ap[:, 0:NT], func=AF.Sqrt,
                                      bias=S.epsc.ap, scale=1.0 / (KT * 128)),
         reads=[pss, S.epsc], writes=[rs])
    P.op("dve", lambda e: e.reciprocal(out=rs.ap[:, 0:NT], in_=rs.ap[:, 0:NT]), reads=[rs], writes=[rs])
    for k in range(KT):
        P.op("dve", lambda e, k=k: e.scalar_tensor_tensor(out=out.ap[:, k, :], in0=src.ap[:, k, :],
                                                         scalar=gcol.ap[:, k:k + 1], in1=rs.ap[:, 0:NT],
                                                         op0=ALU.mult, op1=ALU.mult),
             reads=[src, gcol, rs], writes=[out])


def stream_mm(P, S, Wt, KT, NCB, rhs_fn, nchunk, NTc, evac, wname="wb", nwb=3, Ms=None, pre=None):
    wbs = {}

    def prep(c):
        wb = S.ring(wname + str(KT), nwb, [128, KT, 128], MMDT)
        for k0 in range(0, KT, 16):
            k1 = min(KT, k0 + 16)
            stg = S.ring("wstg", 3, [128, 16, 128])
            P.dma("sp", stg[:, 0:k1 - k0, :], Wt[c][:, k0:k1, :])
            P.op("dve", lambda e, stg=stg, wb=wb, k0=k0, k1=k1: e.tensor_copy(out=wb.ap[:, k0:k1, :], in_=stg.ap[:, 0:k1 - k0, :]),
                 reads=[stg], writes=[wb])
        wbs[c] = wb

    prep(0)
    for c in range(NCB):
        if c + 1 < NCB:
            prep(c + 1)
        wb = wbs.pop(c)
        M = 128 if Ms is None else Ms[c]
        if pre is not None:
            pre(c)
        for j in range(nchunk):
            ps = S.next_psum()
            for k in range(KT):
                r = rhs_fn(k, j)
                P.op("pe", lambda e, wb=wb, ps=ps, r=r, k=k, M=M: e.matmul(
                    ps.ap[0:M, 0:NTc], lhsT=wb.ap[:, k, 0:M], rhs=r.ap, start=(k == 0), stop=(k == KT - 1)),
                    reads=[wb, r], writes=[ps])
            evac(c, j, ps, M)


def emit_dense_in(P, S, hT, gvec, Wt, NCB, Ms, projT, NT):
    nch = NT // CH
    gcol = P.sbuf("gcolA", [128, KT_D])
    load_cols(P, gcol, gvec, KT_D)
    hn = P.sbuf("hnA", [128, KT_D, NT], MMDT)
    hnj = [P.alias(hn, "hnA_%d" % j) for j in range(nch)]
    for j in range(nch):
        ht = S.ring("htA", 2, [128, KT_D, CH])
        for q in range(8):
            P.dma("sp", ht[:, 2 * q:2 * q + 2, :], hT[q].re("(k p) n -> p k n", p=128)[:, :, j * CH:(j + 1) * CH])
        rmsnorm_T(P, S, ht, gcol, hnj[j][:, :, j * CH:(j + 1) * CH], KT_D, CH)

    def rhs_fn(k, j):
        return hnj[j][:, k, j * CH:(j + 1) * CH]

    def evac(c, j, ps, M):
        ob = S.ring("evA", 4, [128, CH])
        P.op("act", lambda e: e.copy(out=ob.ap[0:M, :], in_=ps.ap[0:M, :]), reads=[ps], writes=[ob])
        P.dma("act", projT[c * 128:c * 128 + M, j * CH:(j + 1) * CH], ob[0:M, :])

    stream_mm(P, S, Wt, KT_D, NCB, rhs_fn, nch, CH, evac, Ms=Ms)


def emit_dense_out(P, S, L, hT, yT, pT, hT_out, NT, W, glu=None, final=None, outT=None, h1T=None):
    nch = NT // CH
    gF = P.sbuf("gF%d" % L, [128, KT_D]); load_cols(P, gF, W["nffn"], KT_D)
    gP = P.sbuf("gP%d" % L, [128, KT_D]); load_cols(P, gP, W["npl"], KT_D)
    if final is not None:
        gN = P.sbuf("gN%d" % L, [128, KT_D]); load_cols(P, gN, final, KT_D)
    yTv = yT.re("(k p) n -> p k n", p=128)
    hTv = hT.re("(k p) n -> p k n", p=128)
    h1Tv = h1T.re("(k p) n -> p k n", p=128)
    hoTv = hT_out.re("(k p) n -> p k n", p=128) if hT_out is not None else None
    sc1 = P.scope(); sc1.__enter__()
    yb = P.sbuf("ybC", [128, KT_D, NT], MMDT)
    k_start = 0
    if glu is not None:
        gluw_t, glub = glu
        gb = P.sbuf("glub_sb", [128, 4]); load_cols(P, gb, glub, 4)
        actf = P.sbuf("actf", [128, 4, NT])
        P.dma("sp", actf, yTv[:, 0:4, :])
        actb = P.sbuf("actb", [128, 4, NT], MMDT)
        P.op("dve", lambda e: e.tensor_copy(out=actb.ap, in_=actf.ap), reads=[actf], writes=[actb])

        def evac_glu(c, j, ps, M):
            sg = S.ring("sgl", 2, [128, CH])
            P.op("act", lambda e: e.activation(out=sg.ap, in_=ps.ap, func=AF.Sigmoid, bias=gb.ap[:, c:c + 1]),
                 reads=[ps, gb], writes=[sg])
            P.op("dve", lambda e: e.tensor_tensor(out=yb.ap[:, c, j * CH:(j + 1) * CH],
                                                  in0=actf.ap[:, c, j * CH:(j + 1) * CH], in1=sg.ap, op=ALU.mult),
                 reads=[actf, sg], writes=[yb])
        stream_mm(P, S, gluw_t, 4, 4, lambda k, j: actb[:, k, j * CH:(j + 1) * CH], nch, CH, evac_glu)
        k_start = 4
    for k in range(k_start, KT_D, 4):
        P.dma("pool", yb[:, k:k + 4, :], yTv[:, k:k + 4, :])

    def pre_o(c):
        pass

    def evac_o(c, j, ps, M):
        hb = S.ring("hbC", 3, [128, CH])
        P.dma("sp", hb, hT[c * 128:(c + 1) * 128, j * CH:(j + 1) * CH])
        P.op("dve", lambda e: e.tensor_tensor(out=hb.ap, in0=ps.ap, in1=hb.ap, op=ALU.add),
             reads=[ps, hb], writes=[hb])
        P.dma("sp", h1T[c * 128:(c + 1) * 128, j * CH:(j + 1) * CH], hb)
    stream_mm(P, S, W["out_t"], KT_D, 16, lambda k, j: yb[:, k, j * CH:(j + 1) * CH], nch, CH, evac_o)
    sc1.__exit__(None, None, None)
    sc2 = P.scope(); sc2.__enter__()
    plp = P.sbuf("plpC", [128, 16, 2, 128], MMDT)
    for c in range(16):
        P.dma("pool", plp[:, c, :, :], W["plp_t"][c])
    pTv = pT.re("(k p) n -> p k n", p=128)
    for j in range(nch):
        sl = slice(j * CH, (j + 1) * CH)
        h1 = S.ring("h1C", 1, [128, KT_D, CH])
        P.dma("sp", h1, h1Tv[:, :, sl])
        hn = S.ring("hnC", 1, [128, KT_D, CH], MMDT)
        rmsnorm_T(P, S, h1, gF, hn, KT_D, CH)
        hid = S.ring("hidC", 1, [128, 64, CH], MMDT)

        def evac1(c, jj, ps, M):
            rl = S.ring("rlC", 3, [128, CH])
            P.op("act", lambda e: e.activation(out=rl.ap, in_=ps.ap, func=AF.Relu), reads=[ps], writes=[rl])
            P.op("pool", lambda e: e.tensor_tensor(out=hid.ap[:, c, :], in0=rl.ap, in1=rl.ap, op=ALU.mult),
                 reads=[rl], writes=[hid])
        stream_mm(P, S, W["w1_t"], KT_D, 64, lambda k, jj: hn[:, k, :], 1, CH, evac1)

        def evac2(c, jj, ps, M):
            P.op("dve", lambda e: e.tensor_tensor(out=h1.ap[:, c, :], in0=ps.ap, in1=h1.ap[:, c, :], op=ALU.add),
                 reads=[ps, h1], writes=[h1])
        stream_mm(P, S, W["w2_t"], 64, 16, lambda k, jj: hid[:, k, :], 1, CH, evac2)
        rmsnorm_T(P, S, h1, gP, hn, KT_D, CH)
        pb = S.ring("pbC", 1, [128, 2, CH], MMDT)
        P.dma("pool", pb, pTv[:, :, sl])

        def evac3(c, jj, ps, M):
            psp = S.next_psum()
            for k in range(2):
                P.op("pe", lambda e, k=k: e.matmul(psp.ap, lhsT=plp.ap[:, c, k, :], rhs=pb.ap[:, k, :],
                                                   start=(k == 0), stop=(k == 1)),
                     reads=[plp, pb], writes=[psp])
            sg = S.ring("sgC", 2, [128, CH])
            P.op("act", lambda e: e.activation(out=sg.ap, in_=ps.ap, func=AF.Sigmoid), reads=[ps], writes=[sg])
            P.op("dve", lambda e: e.tensor_tensor(out=sg.ap, in0=psp.ap, in1=sg.ap, op=ALU.mult),
                 reads=[psp, sg], writes=[sg])
            P.op("pool", lambda e: e.tensor_tensor(out=h1.ap[:, c, :], in0=h1.ap[:, c, :], in1=sg.ap, op=ALU.add),
                 reads=[h1, sg], writes=[h1])
        stream_mm(P, S, W["gate_t"], KT_D, 16, lambda k, jj: hn[:, k, :], 1, CH, evac3)
        if hT_out is not None:
            P.dma("sp", hoTv[:, :, sl], h1)
        if final is not None:
            rmsnorm_T(P, S, h1, gN, h1, KT_D, CH)
            P.dma("sp", outT.re("(k p) n -> p k n", p=128)[:, :, sl], h1)
    sc2.__exit__(None, None, None)


def emit_glu_partial(P, S, actT, gluw_t, zp, T):
    nch = T // CH
    with P.scope():
        ab = P.sbuf("glu_ab", [128, 2, T], MMDT)
        av = actT.re("(k p) n -> p k n", p=128)
        for k in range(2):
            for j0 in range(0, T, 2048):
                P.dma("pool", ab[:, k, j0:j0 + 2048], av[:, k, j0:j0 + 2048])

        def evac(c, j, ps, M):
            ob = S.ring("glu_ev", 4, [128, CH])
            P.op("act", lambda e: e.copy(out=ob.ap, in_=ps.ap), reads=[ps], writes=[ob])
            P.dma("act", zp[c // 2, (c % 2) * 128:(c % 2 + 1) * 128, j * CH:(j + 1) * CH], ob)
        stream_mm(P, S, gluw_t, 2, 4, lambda k, j: ab[:, k, j * CH:(j + 1) * CH], nch, CH, evac)


def emit_outproj_partial(P, S, yT, wout_t, mp, T, glu=None):
    nch = T // CH
    half = T // 2
    with P.scope():
        yb = P.sbuf("op_yb", [128, 8, T], MMDT)
        yv = yT.re("(k p) n -> p k n", p=128)
        k0 = 0
        if glu is not None:
            zT, glub = glu
            gb = P.sbuf("op_gb", [128, 2]); load_cols(P, gb, glub, 2)
            zv = zT.re("(k p) n -> p k n", p=128)
            for k in range(2):
                for j in range(nch):
                    sl = slice(j * CH, (j + 1) * CH)
                    a = S.ring("op_a", 3, [128, CH]); z = S.ring("op_z", 3, [128, CH])
                    P.dma("sp", a, yv[:, k, sl]); P.dma("sp", z, zv[:, k, sl])
                    P.op("act", lambda e, z=z, k=k: e.activation(out=z.ap, in_=z.ap, func=AF.Sigmoid, bias=gb.ap[:, k:k + 1]), reads=[z, gb], writes=[z])
                    P.op("dve", lambda e, a=a, z=z, k=k, sl=sl: e.tensor_tensor(out=yb.ap[:, k, sl], in0=a.ap, in1=z.ap, op=ALU.mult), reads=[a, z], writes=[yb])
            k0 = 2
        for k in range(k0, 8):
            for j0 in range(0, T, 2048):
                P.dma("pool", yb[:, k, j0:j0 + 2048], yv[:, k, j0:j0 + 2048])

        def evac(c, j, ps, M):
            ob = S.ring("op_ev", 4, [128, CH])
            P.op("act", lambda e: e.copy(out=ob.ap, in_=ps.ap), reads=[ps], writes=[ob])
            t0 = j * CH
            P.dma("act", mp[c // 4, t0 // half, (c % 4) * 128:(c % 4 + 1) * 128, t0 % half:t0 % half + CH], ob)
        stream_mm(P, S, wout_t, 8, 16, lambda k, j: yb[:, k, j * CH:(j + 1) * CH], nch, CH, evac)


def emit_mlp_gate(P, S, L, hT, mixT, pT, hT_out, NT, W, final=None, outT=None):
    nch = NT // CH
    with P.scope():
        gF = P.sbuf("gF%d" % L, [128, KT_D]); load_cols(P, gF, W["nffn"], KT_D)
        gP = P.sbuf("gP%d" % L, [128, KT_D]); load_cols(P, gP, W["npl"], KT_D)
        if final is not None:
            gN = P.sbuf("gN%d" % L, [128, KT_D]); load_cols(P, gN, final, KT_D)
        hTv = hT.re("(k p) n -> p k n", p=128)
        mTv = mixT.re("(k p) n -> p k n", p=128)
        hoTv = hT_out.re("(k p) n -> p k n", p=128) if hT_out is not None else None
        plp = P.sbuf("plpC", [128, 16, 2, 128], MMDT)
        for c in range(16):
            P.dma("pool", plp[:, c, :, :], W["plp_t"][c])
        pTv = pT.re("(k p) n -> p k n", p=128)

        def tile_body(j):
            sl = slice(j * CH, (j + 1) * CH)
            h1 = S.ring("h1C", 1, [128, KT_D, CH])
            hn = S.ring("hnC", 1, [128, KT_D, CH], MMDT)
            hid = S.ring("hidC", 1, [128, 64, CH], MMDT)
            P.dma("sp", h1, hTv[:, :, sl])
            for q in range(8):
                mt = S.ring("mixC", 2, [128, 2, CH])
                P.dma("sp", mt, mTv[:, q * 2:(q + 1) * 2, sl])
                P.op("dve", lambda e, mt=mt, q=q: e.tensor_tensor(out=h1.ap[:, q * 2:(q + 1) * 2, :], in0=h1.ap[:, q * 2:(q + 1) * 2, :], in1=mt.ap, op=ALU.add),
                     reads=[h1, mt], writes=[h1])
            rmsnorm_T(P, S, h1, gF, hn, KT_D, CH)

            def evac1(c, jj, ps, M):
                rl = S.ring("rlC", 3, [128, CH])
                P.op("act", lambda e: e.activation(out=rl.ap, in_=ps.ap, func=AF.Relu), reads=[ps], writes=[rl])
                P.op("pool", lambda e: e.tensor_tensor(out=hid.ap[:, c, :], in0=rl.ap, in1=rl.ap, op=ALU.mult), reads=[rl], writes=[hid])
            stream_mm(P, S, W["w1_t"], KT_D, 64, lambda k, jj: hn[:, k, :], 1, CH, evac1)

            def evac2(c, jj, ps, M):
                P.op("dve", lambda e: e.tensor_tensor(out=h1.ap[:, c, :], in0=ps.ap, in1=h1.ap[:, c, :], op=ALU.add), reads=[ps, h1], writes=[h1])
            stream_mm(P, S, W["w2_t"], 64, 16, lambda k, jj: hid[:, k, :], 1, CH, evac2)
            rmsnorm_T(P, S, h1, gP, hn, KT_D, CH)
            pb = S.ring("pbC", 1, [128, 2, CH], MMDT)
            P.dma("pool", pb, pTv[:, :, sl])

            def evac3(c, jj, ps, M):
                psp = S.next_psum()
                for k in range(2):
                    P.op("pe", lambda e, k=k: e.matmul(psp.ap, lhsT=plp.ap[:, c, k, :], rhs=pb.ap[:, k, :], start=(k == 0), stop=(k == 1)),
                         reads=[plp, pb], writes=[psp])
                sg = S.ring("sgC", 2, [128, CH])
                P.op("act", lambda e: e.activation(out=sg.ap, in_=ps.ap, func=AF.Sigmoid), reads=[ps], writes=[sg])
                P.op("dve", lambda e: e.tensor_tensor(out=sg.ap, in0=psp.ap, in1=sg.ap, op=ALU.mult), reads=[psp, sg], writes=[sg])
                P.op("pool", lambda e: e.tensor_tensor(out=h1.ap[:, c, :], in0=h1.ap[:, c, :], in1=sg.ap, op=ALU.add), reads=[h1, sg], writes=[h1])
            stream_mm(P, S, W["gate_t"], KT_D, 16, lambda k, jj: hn[:, k, :], 1, CH, evac3)
            if hT_out is not None:
                P.dma("sp", hoTv[:, :, sl], h1)
            if final is not None:
                rmsnorm_T(P, S, h1, gN, h1, KT_D, CH)
                P.dma("sp", outT.re("(k p) n -> p k n", p=128)[:, :, sl], h1)

        for j in range(nch):
            tile_body(j)


def emit_mlp_gate_v2(P, S, L, hT, mixT, pT, hT_out, NT, W, h1d, final=None, outT=None):
    nch = NT // CH
    hTv = hT.re("(k p) n -> p k n", p=128)
    mTv = mixT.re("(k p) n -> p k n", p=128)
    h1v = h1d.re("(k p) n -> p k n", p=128)
    hreg = [[P.alias(h1d, "h1d_%d_%d" % (c, j)) for j in range(nch)] for c in range(16)]
    houts = None
    if hT_out is not None:
        houts = [[P.alias(hT_out, "hout_%d" % c)] * nch for c in range(16)]

    def norm_pass(src_v, gcol, dst_fn, add_v=None, store_v=None, tag=""):
        with P.scope():
            for j in range(nch):
                sl = slice(j * CH, (j + 1) * CH)
                ht = S.ring("npH", 2, [128, KT_D, CH])
                P.dma("sp", ht, src_v[:, :, sl])
                if add_v is not None:
                    for q in range(8):
                        mt = S.ring("npM", 2, [128, 2, CH])
                        P.dma("sp", mt, add_v[:, q * 2:(q + 1) * 2, sl])
                        P.op("dve", lambda e, mt=mt, q=q, ht=ht: e.tensor_tensor(out=ht.ap[:, q * 2:(q + 1) * 2, :], in0=ht.ap[:, q * 2:(q + 1) * 2, :],
                             in1=mt.ap, op=ALU.add), reads=[ht, mt], writes=[ht])
                if store_v is not None:
                    P.dma("sp", store_v[:, :, sl], ht)
                rmsnorm_T(P, S, ht, gcol, dst_fn(j), KT_D, CH)

    with P.scope():
        gF = P.sbuf("gF%d" % L, [128, KT_D]); load_cols(P, gF, W["nffn"], KT_D)
        gP = P.sbuf("gP%d" % L, [128, KT_D]); load_cols(P, gP, W["npl"], KT_D)
        hn = P.sbuf("hnM", [128, KT_D, NT], MMDT)
        norm_pass(hTv, gF, lambda j: hn[:, :, j * CH:(j + 1) * CH], add_v=mTv, store_v=h1v)
        with P.scope():
            hid = P.sbuf("hidM", [128, 16, NT], MMDT)
            for q in range(4):
                def evac1(c, j, ps, M):
                    rl = S.ring("rlM", 3, [128, CH])
                    P.op("act", lambda e: e.activation(out=rl.ap, in_=ps.ap, func=AF.Relu), reads=[ps], writes=[rl])
                    P.op("pool", lambda e: e.tensor_tensor(out=hid.ap[:, c, j * CH:(j + 1) * CH], in0=rl.ap, in1=rl.ap, op=ALU.mult), reads=[rl], writes=[hid])
                stream_mm(P, S, W["w1_t"][q * 16:(q + 1) * 16], KT_D, 16, lambda k, j: hn[:, k, j * CH:(j + 1) * CH], nch, CH, evac1)

                pend = {}

                def pre2(c):
                    for j in range(nch):
                        hb = S.ring("hbM", 8, [128, CH])
                        reg = V(h1d.ap[c * 128:(c + 1) * 128, j * CH:(j + 1) * CH], hreg[c][j].buf)
                        P.dma("pool", hb, reg)
                        pend[(c, j)] = (hb, reg)

                def evac2(c, j, ps, M):
                    hb, reg = pend.pop((c, j))
                    P.op("dve", lambda e: e.tensor_tensor(out=hb.ap, in0=ps.ap, in1=hb.ap, op=ALU.add), reads=[ps, hb], writes=[hb])
                    P.dma("act", reg, hb)
                stream_mm(P, S, W["w2_t"][:, :, q * 16:(q + 1) * 16, :], KT_D, 16, lambda k, j: hid[:, k, j * CH:(j + 1) * CH], nch, CH, evac2, pre=pre2)
        norm_pass(h1v, gP, lambda j: hn[:, :, j * CH:(j + 1) * CH])
        with P.scope():
            plp = P.sbuf("plpM", [128, 16, 2, 128], MMDT)
            for c in range(16):
                P.dma("pool", plp[:, c, :, :], W["plp_t"][c])
            pb = P.sbuf("pbM", [128, 2, NT], MMDT)
            pTv = pT.re("(k p) n -> p k n", p=128)
            for k in range(2):
                P.dma("pool", pb[:, k, :], pTv[:, k, :])
            dst = hT_out if hT_out is not None else h1d
            dregs = houts if hT_out is not None else hreg

            pend3 = {}

            def pre3(c):
                for j in range(nch):
                    sl = slice(j * CH, (j + 1) * CH)
                    hb = S.ring("hbG", 8, [128, CH])
                    P.dma("pool", hb, V(h1d.ap[c * 128:(c + 1) * 128, sl], hreg[c][j].buf))
                    pend3[(c, j)] = hb

            def evac3(c, j, ps, M):
                sl = slice(j * CH, (j + 1) * CH)
                psp = S.next_psum()
                for k in range(2):
                    P.op("pe", lambda e, k=k: e.matmul(psp.ap, lhsT=plp.ap[:, c, k, :], rhs=pb.ap[:, k, sl], start=(k == 0), stop=(k == 1)),
                         reads=[plp, pb], writes=[psp])
                sg = S.ring("sgM", 2, [128, CH])
                P.op("act", lambda e: e.activation(out=sg.ap, in_=ps.ap, func=AF.Sigmoid), reads=[ps], writes=[sg])
                P.op("dve", lambda e: e.tensor_tensor(out=sg.ap, in0=psp.ap, in1=sg.ap, op=ALU.mult), reads=[psp, sg], writes=[sg])
                hb = pend3.pop((c, j))
                P.op("pool", lambda e: e.tensor_tensor(out=hb.ap, in0=hb.ap, in1=sg.ap, op=ALU.add), reads=[hb, sg], writes=[hb])
                P.dma("act", V(dst.ap[c * 128:(c + 1) * 128, sl], dregs[c][j].buf), hb)
            stream_mm(P, S, W["gate_t"], KT_D, 16, lambda k, j: hn[:, k, j * CH:(j + 1) * CH], nch, CH, evac3, pre=pre3)
    if final is not None:
        with P.scope():
            gN = P.sbuf("gN%d" % L, [128, KT_D]); load_cols(P, gN, final, KT_D)
            oTv = outT.re("(k p) n -> p k n", p=128)
            for j in range(nch):
                sl = slice(j * CH, (j + 1) * CH)
                ht = S.ring("npF", 2, [128, KT_D, CH])
                P.dma("sp", ht, h1v[:, :, sl])
                rmsnorm_T(P, S, ht, gN, ht, KT_D, CH)
                P.dma("sp", oTv[:, :, sl], ht)
CH = 512
PI = math.pi
FAST32 = True


def r32(ap):
    return ap.bitcast(mybir.dt.float32r) if FAST32 else ap


def tt(P, eng, out, a, b, op):
    P.op(eng, lambda e: e.tensor_tensor(out=out.ap, in0=a.ap, in1=b.ap, op=op), reads=[a, b], writes=[out])


def ts(P, eng, out, a, s1, op0, s2=None, op1=None):
    rd = [a] + [x for x in (s1, s2) if isinstance(x, V)]
    g = lambda x: x.ap if isinstance(x, V) else x
    if op1 is None:
        P.op(eng, lambda e: e.tensor_scalar(out=out.ap, in0=a.ap, scalar1=g(s1), scalar2=None, op0=op0), reads=rd, writes=[out])
    else:
        P.op(eng, lambda e: e.tensor_scalar(out=out.ap, in0=a.ap, scalar1=g(s1), scalar2=g(s2), op0=op0, op1=op1), reads=rd, writes=[out])


def act(P, out, a, func, scale=None, bias=None):
    rd = [a] + [x for x in (scale, bias) if isinstance(x, V)]
    kw = {}
    if scale is not None:
        kw["scale"] = scale.ap if isinstance(scale, V) else scale
    if bias is not None:
        kw["bias"] = bias.ap if isinstance(bias, V) else bias
    P.op("act", lambda e: e.activation(out=out.ap, in_=a.ap, func=func, **kw), reads=rd, writes=[out])


def sin_reduced(P, tmp, out, ang, shift, zero_col):
    f1, f2, i1 = tmp
    ts(P, "dve", f1, ang, shift, ALU.add)
    ts(P, "dve", f2, f1, 1.0 / (2 * PI), ALU.mult)
    P.op("dve", lambda e: e.tensor_copy(out=i1.ap, in_=f2.ap), reads=[f2], writes=[i1])
    P.op("dve", lambda e: e.tensor_copy(out=f2.ap, in_=i1.ap), reads=[i1], writes=[f2])
    P.op("dve", lambda e: e.scalar_tensor_tensor(out=f1.ap, in0=f2.ap, scalar=-2 * PI, in1=f1.ap, op0=ALU.mult, op1=ALU.add),
         reads=[f1, f2], writes=[f1])
    ts(P, "dve", f2, f1, PI, ALU.is_gt, -2 * PI, ALU.mult)
    tt(P, "dve", f1, f1, f2, ALU.add)
    ts(P, "dve", f2, f1, -PI, ALU.is_lt, 2 * PI, ALU.mult)
    tt(P, "dve", f1, f1, f2, ALU.add)
    act(P, out, f1, AF.Sin)


def s5_abar(P, lr, li, lstep, shape, tag):
    mk = lambda n, dt=F32: P.sbuf("s5%s_%s" % (tag, n), shape, dt)
    step = mk("step"); mag = mk("mag"); ang = mk("ang"); ar = mk("ar"); ai = mk("ai")
    f1 = mk("f1"); f2 = mk("f2"); i1 = mk("i1", mybir.dt.int32)
    act(P, step, lstep, AF.Exp)
    tt(P, "dve", mag, lr, step, ALU.mult)
    act(P, mag, mag, AF.Exp)
    tt(P, "dve", ang, li, step, ALU.mult)
    sin_reduced(P, (f1, f2, i1), ai, ang, 0.0, None)
    sin_reduced(P, (f1, f2, i1), ar, ang, PI / 2, None)
    tt(P, "dve", ar, ar, mag, ALU.mult)
    tt(P, "dve", ai, ai, mag, ALU.mult)
    return ar, ai, (f1, f2)


def emit_s5(P, S, uT, prm, cst, yT, T, NG):
    nch = T // CH
    nlev = int(round(math.log2(T)))
    assert 2 ** nlev == T
    with P.scope():
        ld = lambda name, shape, src: (lambda t: (P.dma("sp", t, src), t)[1])(P.sbuf(name, shape))
        shp = [16, NG, 64]
        lr = ld("s5r_lr", shp, prm["lr_row"]); li = ld("s5r_li", shp, prm["li_row"]); ls = ld("s5r_ls", shp, prm["ls_row"])
        bre = ld("s5r_bre", shp, prm["bT_re"]); bim = ld("s5r_bim", shp, prm["bT_im"])
        ar, ai, (f1, f2) = s5_abar(P, lr, li, ls, shp, "r")
        den = P.sbuf("s5r_den", shp); cre = P.sbuf("s5r_cre", shp); cim = P.sbuf("s5r_cim", shp)
        tt(P, "dve", den, lr, lr, ALU.mult); tt(P, "dve", f1, li, li, ALU.mult); tt(P, "dve", den, den, f1, ALU.add)
        P.op("dve", lambda e: e.reciprocal(out=den.ap, in_=den.ap), reads=[den], writes=[den])
        ts(P, "dve", ar, ar, -1.0, ALU.add)
        tt(P, "dve", cre, ar, lr, ALU.mult); tt(P, "dve", f1, ai, li, ALU.mult); tt(P, "dve", cre, cre, f1, ALU.add)
        tt(P, "dve", cre, cre, den, ALU.mult)
        tt(P, "dve", cim, ai, lr, ALU.mult); tt(P, "dve", f1, ar, li, ALU.mult); tt(P, "dve", cim, cim, f1, ALU.subtract)
        tt(P, "dve", cim, cim, den, ALU.mult)
        BT = P.sbuf("s5_BT", [16, NG, 128])
        tt(P, "dve", f1, cre, bre, ALU.mult); tt(P, "dve", f2, cim, bim, ALU.mult)
        tt(P, "dve", BT[:, :, 0:64], f1, f2, ALU.subtract)
        tt(P, "dve", f1, cre, bim, ALU.mult); tt(P, "dve", f2, cim, bre, ALU.mult)
        tt(P, "dve", BT[:, :, 64:128], f1, f2, ALU.add)
        shc = [128, NG]
        lrc = ld("s5c_lr", shc, prm["lr_col"]); lic = ld("s5c_li", shc, prm["li_col"]); lsc = ld("s5c_ls", shc, prm["ls_col"])
        arc, aic, (g1, g2) = s5_abar(P, lrc, lic, lsc, shc, "c")
        sgn = ld("s5_sgn", [128, 1], cst["sgn"])
        ident = ld("s5_id", [128, 128], cst["ident"]); psw = ld("s5_psw", [128, 128], cst["psw"])
        pw = P.sbuf("s5_pw", [128, nlev, 2, NG])
        P.op("dve", lambda e: e.tensor_copy(out=pw.ap[:, 0, 0, :], in_=arc.ap), reads=[arc], writes=[pw])
        ts(P, "dve", pw[:, 0, 1, :], aic, sgn[:, 0:1], ALU.mult)
        for k in range(1, nlev):
            a0 = pw[:, k - 1, 0, :]; s0 = pw[:, k - 1, 1, :]
            tt(P, "dve", g1, a0, a0, ALU.mult); tt(P, "dve", g2, s0, s0, ALU.mult)
            tt(P, "dve", pw[:, k, 0, :], g1, g2, ALU.subtract)
            tt(P, "dve", g1, a0, s0, ALU.mult)
            ts(P, "dve", pw[:, k, 1, :], g1, 2.0, ALU.mult)
        CT = ld("s5_CT", [128, NG, 16], prm["cT"])
        ts(P, "dve", CT[64:128], CT[64:128], -1.0, ALU.mult)
        dcol = ld("s5_dcol", [16, NG], prm["d_col"])
        Xa = P.sbuf("s5_Xa", [128, T]); Xb = P.sbuf("s5_Xb", [128, T])
        Xc = [[P.alias(Xa, "s5Xa%d" % j) for j in range(nch)], [P.alias(Xb, "s5Xb%d" % j) for j in range(nch)]]
        for g in range(NG):
            ug = P.ring("s5_u", 2, [16, T])
            P.dma("sp", ug, uT[g * 16:(g + 1) * 16, :])
            Mk = P.ring("s5_M", 2, [128, nlev, 128])
            for k in range(nlev):
                P.op("pool", lambda e, Mk=Mk, k=k, g=g: e.tensor_scalar(out=r32(Mk.ap[:, k, :]), in0=ident.ap, scalar1=pw.ap[:, k, 0, g:g + 1],
                     scalar2=0.0, op0=ALU.mult, op1=ALU.add), reads=[ident, pw], writes=[Mk])
                P.op("dve", lambda e, Mk=Mk, k=k, g=g: e.scalar_tensor_tensor(out=r32(Mk.ap[:, k, :]), in0=psw.ap, scalar=pw.ap[:, k, 1, g:g + 1],
                     in1=Mk.ap[:, k, :], op0=ALU.mult, op1=ALU.add), reads=[psw, pw, Mk], writes=[Mk])
            for j in range(nch):
                ps = S.next_psum(); sl = slice(j * CH, (j + 1) * CH)
                P.op("pe", lambda e, ps=ps, ug=ug, sl=sl, g=g: e.matmul(ps.ap, lhsT=BT.ap[:, g, :], rhs=ug.ap[:, sl], start=True, stop=True),
                     reads=[BT, ug], writes=[ps])
                P.op("act", lambda e, ps=ps, sl=sl: e.copy(out=r32(Xa.ap[:, sl]), in_=ps.ap), reads=[ps], writes=[Xc[0][j]])
            for k in range(nlev):
                d = 2 ** k
                src, dst = Xc[k % 2], Xc[(k + 1) % 2]
                sa, da = (Xa, Xb) if k % 2 == 0 else (Xb, Xa)
                for j in range(nch):
                    t0 = j * CH; t1 = t0 + CH
                    lo = max(t0, d)
                    if lo > t0:
                        hi = min(lo, t1)
                        P.op("pool", lambda e, sa=sa, da=da, t0=t0, hi=hi: e.tensor_copy(out=r32(da.ap[:, t0:hi]), in_=sa.ap[:, t0:hi]),
                             reads=[src[j]], writes=[dst[j]])
                    if lo >= t1:
                        continue
                    n = t1 - lo
                    s0, s1 = lo - d, t1 - d
                    rds = [src[c] for c in range(s0 // CH, (s1 - 1) // CH + 1)]
                    ps = S.next_psum()
                    P.op("pe", lambda e, ps=ps, Mk=Mk, k=k, sa=sa, s0=s0, s1=s1, n=n: e.matmul(ps.ap[:, 0:n], lhsT=(r32(Mk.ap[:, k, :]) if (n % 2 == 0 and s0 % 2 == 0) else Mk.ap[:, k, :]), rhs=(r32(sa.ap[:, s0:s1]) if (n % 2 == 0 and s0 % 2 == 0) else sa.ap[:, s0:s1]),
                         start=True, stop=True), reads=[Mk] + rds, writes=[ps])
                    P.op("dve", lambda e, ps=ps, sa=sa, da=da, lo=lo, t1=t1, n=n: e.tensor_tensor(out=r32(da.ap[:, lo:t1]), in0=ps.ap[:, 0:n], in1=sa.ap[:, lo:t1],
                         op=ALU.add), reads=[ps, src[j]], writes=[dst[j]])
            fin = Xc[nlev % 2]; fa = Xa if nlev % 2 == 0 else Xb
            og = P.ring("s5_o", 2, [16, T])
            for j in range(nch):
                ps = S.next_psum(); sl = slice(j * CH, (j + 1) * CH)
                P.op("pe", lambda e, ps=ps, sl=sl, g=g, fa=fa: e.matmul(ps.ap[0:16, :], lhsT=CT.ap[:, g, :], rhs=fa.ap[:, sl], start=True, stop=True),
                     reads=[CT, fin[j]], writes=[ps])
                P.op("dve", lambda e, ps=ps, sl=sl, g=g, ug=ug, og=og: e.scalar_tensor_tensor(out=og.ap[:, sl], in0=ug.ap[:, sl], scalar=dcol.ap[:, g:g + 1],
                     in1=ps.ap[0:16, :], op0=ALU.mult, op1=ALU.add), reads=[ug, dcol, ps], writes=[og])
            P.op("act", lambda e, og=og: e.activation(out=og.ap, in_=og.ap, func=AF.Gelu_apprx_tanh), reads=[og], writes=[og])
            P.dma("sp", yT[g * 16:(g + 1) * 16, :], og)


def emit_ssd(P, S, zT, xsT, bT, cT, dtT, prm, cst, yT, T, NH, NG, dbg=None):
    J = NH // NG
    NP = NH // 2
    TPG = NP // NG
    L = 128
    SC = CH
    nsc = T // SC
    ncs = SC // L
    with P.scope():
        ld = lambda name, shape, src, **kw: (lambda t: (P.dma("sp", t, src, **kw), t)[1])(P.sbuf(name, shape))
        ident = ld("sd_id", [128, 128], cst["ident"]); tri = ld("sd_tri", [128, 128], cst["tri"])
        ustr = ld("sd_us", [128, 128], cst["ustr"]); sel = ld("sd_sel", [NH, NH * 64], cst["sel"])
        ones = S.ones
        cwx = P.sbuf("sd_cwx", [128, NP, 4]); cwb = P.sbuf("sd_cwb", [128, NG, 4]); cwc = P.sbuf("sd_cwc", [128, NG, 4])
        for k in range(4):
            P.dma("sp", cwx[:, :, k], prm["cw_x"][k].re("(t p) -> p t", p=128), allow_slow_non_contiguous=True)
            P.dma("sp", cwb[:, :, k], prm["cw_b"][k].re("(t p) -> p t", p=128), allow_slow_non_contiguous=True)
            P.dma("sp", cwc[:, :, k], prm["cw_c"][k].re("(t p) -> p t", p=128), allow_slow_non_contiguous=True)
        colv = lambda nm, n: ld("sd_" + nm, [128, n], prm[nm].re("(t p) -> p t", p=128), allow_slow_non_contiguous=True)
        cbx = colv("cb_x", NP); cbb = colv("cb_b", NG); cbc = colv("cb_c", NG); dch = colv("d_ch", NP); ngc = colv("ng", NP)
        dtb = ld("sd_dtb", [NH, 1], prm["dt_bias"].re("(h o) -> h o", o=1)); alog = ld("sd_alog", [NH, 1], prm["a_log"].re("(h o) -> h o", o=1))
        negA = P.sbuf("sd_negA", [NH, 1])
        act(P, negA, alog, AF.Exp)
        ts(P, "dve", negA, negA, -1.0, ALU.mult)
        one12 = P.sbuf("sd_one", [128, 1]); P.op("dve", lambda e: e.memset(one12.ap, 1.0), writes=[one12])
        prevT = P.sbuf("sd_prev", [128, NH, 64])
        P.op("dve", lambda e: e.memset(prevT.ap, 0.0), writes=[prevT])

        def conv_silu(srcT, r0, t0, cw, cb, i, dst):
            xp = P.ring("sd_xp", 2, [128, SC + 3])
            if t0 == 0:
                P.op("pool", lambda e: e.memset(xp.ap[:, 0:3], 0.0), writes=[xp])
                P.dma("sp", xp[:, 3:SC + 3], srcT[r0:r0 + 128, 0:SC])
            else:
                P.dma("sp", xp, srcT[r0:r0 + 128, t0 - 3:t0 + SC])
            P.op("dve", lambda e: e.tensor_scalar(out=dst.ap, in0=xp.ap[:, 0:SC], scalar1=cw.ap[:, i, 0:1], scalar2=cb.ap[:, i:i + 1],
                 op0=ALU.mult, op1=ALU.add), reads=[xp, cw, cb], writes=[dst])
            for k in range(1, 4):
                P.op("dve", lambda e, k=k: e.scalar_tensor_tensor(out=dst.ap, in0=xp.ap[:, k:k + SC], scalar=cw.ap[:, i, k:k + 1], in1=dst.ap,
                     op0=ALU.mult, op1=ALU.add), reads=[xp, cw, dst], writes=[dst])
            act(P, dst, dst, AF.Silu)

        def sc_body(sc):
            t0 = sc * SC
            xsc = [P.ring("sd_xsc%d" % i, 1, [128, SC]) for i in range(NP)]
            Bc = [P.ring("sd_Bc%d" % g, 1, [128, SC]) for g in range(NG)]
            Cc = [P.ring("sd_Cc%d" % g, 1, [128, SC]) for g in range(NG)]
            for i in range(NP):
                conv_silu(xsT, i * 128, t0, cwx, cbx, i, xsc[i])
            for g in range(NG):
                conv_silu(bT, g * 128, t0, cwb, cbb, g, Bc[g])
                conv_silu(cT, g * 128, t0, cwc, cbc, g, Cc[g])
            dtv = P.ring("sd_dtv", 1, [NH, SC]); da = P.ring("sd_da", 1, [NH, SC]); acT = P.ring("sd_acT", 1, [NH, SC])
            dsT = P.ring("sd_dsT", 1, [NH, SC])
            P.dma("sp", dtv, dtT[:, t0:t0 + SC])
            act(P, dtv, dtv, AF.Exp, bias=dtb[:, 0:1])
            act(P, dtv, dtv, AF.Ln, bias=one12[0:NH, 0:1])
            ts(P, "dve", da, dtv, negA[:, 0:1], ALU.mult)
            for c in range(ncs):
                cs = slice(c * L, (c + 1) * L)
                P.op("dve", lambda e, cs=cs: e.tensor_tensor_scan(out=acT.ap[:, cs], data0=one12.ap[0:NH, 0:1].to_broadcast([NH, L]), data1=da.ap[:, cs],
                     initial=0.0, op0=ALU.mult, op1=ALU.add), reads=[da, one12], writes=[acT])
            for c in range(ncs):
                cs = slice(c * L, (c + 1) * L)
                act(P, dsT[:, cs], acT[:, cs], AF.Exp, scale=-1.0, bias=acT[:, (c + 1) * L - 1:(c + 1) * L])
            tt(P, "dve", dsT, dsT, dtv, ALU.mult)
            xT = [P.ring("sd_xT%d" % i, 1, [128, SC]) for i in range(NP)]
            xdT = [P.ring("sd_xdT%d" % i, 1, [128, SC]) for i in range(NP)]
            for i in range(NP):
                for (srcrow, dst) in ((dtv, xT[i]), (dsT, xdT[i])):
                    ps = S.next_psum()
                    P.op("pe", lambda e, ps=ps, srcrow=srcrow, i=i: e.matmul(ps.ap, lhsT=sel.ap[:, i * 128:(i + 1) * 128], rhs=srcrow.ap, start=True, stop=True),
                         reads=[sel, srcrow], writes=[ps])
                    tt(P, "dve", dst, ps, xsc[i], ALU.mult)
            ysb = [P.ring("sd_ysb%d" % i, 1, [128, SC]) for i in range(NP)]
            if dbg and sc == 0:
                P.dma("sp", dbg["xsc0"], xsc[0]); P.dma("sp", dbg["dtv"], dtv); P.dma("sp", dbg["acT"], acT); P.dma("sp", dbg["dsT"], dsT)
                P.dma("sp", dbg["xT0"], xT[0]); P.dma("sp", dbg["xdT0"], xdT[0]); P.dma("sp", dbg["Bc0"], Bc[0])
            def chunk_body(c):
                cs = slice(c * L, (c + 1) * L)
                xtok = P.ring("sd_xtok", 2, [128, NP * 128]); xdtok = P.ring("sd_xdtok", 2, [128, NP * 128])
                for (srcs, dst) in ((xT, xtok), (xdT, xdtok)):
                    for i0 in range(0, NP, 4):
                        ps = S.next_psum(); n = min(4, NP - i0)
                        for i in range(i0, i0 + n):
                            P.op("pe", lambda e, ps=ps, srcs=srcs, i=i, i0=i0: e.transpose(ps.ap[:, (i - i0) * 128:(i - i0 + 1) * 128], srcs[i].ap[:, cs], ident.ap),
                                 reads=[srcs[i], ident], writes=[ps])
                        P.op("act", lambda e, ps=ps, dst=dst, i0=i0, n=n: e.copy(out=dst.ap[:, i0 * 128:(i0 + n) * 128], in_=ps.ap[:, 0:n * 128]), reads=[ps], writes=[dst])
                btok = P.ring("sd_btok", 2, [128, NG * 128])
                ps = S.next_psum()
                for g in range(NG):
                    P.op("pe", lambda e, ps=ps, g=g: e.transpose(ps.ap[:, g * 128:(g + 1) * 128], Bc[g].ap[:, cs], ident.ap), reads=[Bc[g], ident], writes=[ps])
                P.op("act", lambda e, ps=ps, btok=btok: e.copy(out=btok.ap, in_=ps.ap[:, 0:NG * 128]), reads=[ps], writes=[btok])
                datok = P.ring("sd_datok", 2, [128, NH])
                ps = S.next_psum()
                P.op("pe", lambda e, ps=ps: e.transpose(ps.ap[:, 0:NH], da.ap[:, cs], ident.ap[0:NH, 0:NH]), reads=[da, ident], writes=[ps])
                P.op("act", lambda e, ps=ps, datok=datok: e.copy(out=datok.ap, in_=ps.ap[:, 0:NH]), reads=[ps], writes=[datok])
                SM = P.ring("sd_SM", 2, [128, NG, L])
                for g in range(NG):
                    ps = S.next_psum()
                    P.op("pe", lambda e, ps=ps, g=g: e.matmul(ps.ap[:, 0:L], lhsT=Bc[g].ap[:, cs], rhs=Cc[g].ap[:, cs], start=True, stop=True),
                         reads=[Bc[g], Cc[g]], writes=[ps])
                    P.op("dve", lambda e, ps=ps, g=g, SM=SM: e.tensor_tensor(out=SM.ap[:, g, :], in0=ps.ap[:, 0:L], in1=tri.ap, op=ALU.mult),
                         reads=[ps, tri], writes=[SM])
                Vt = P.ring("sd_V", 2, [128, NH, L])
                P.op("dve", lambda e, Vt=Vt, datok=datok: e.tensor_tensor(out=Vt.ap, in0=tri.ap.unsqueeze(1).to_broadcast([128, NH, L]),
                     in1=datok.ap.unsqueeze(2).to_broadcast([128, NH, L]), op=ALU.mult), reads=[tri, datok], writes=[Vt])
                E = P.ring("sd_E", 2, [128, NH, L]); EA = P.ring("sd_EA", 2, [128, NH, L])
                for (lh, dst) in ((ustr, E), (ones, EA)):
                    for h0 in range(0, NH, 4):
                        n = min(4, NH - h0)
                        ps = S.next_psum()
                        P.op("pe", lambda e, ps=ps, lh=lh, Vt=Vt, h0=h0, n=n: e.matmul(ps.ap[:, 0:n * L], lhsT=lh.ap, rhs=Vt.ap[:, h0:h0 + n, :], start=True, stop=True),
                             reads=[lh, Vt], writes=[ps])
                        P.op("act", lambda e, ps=ps, dst=dst, h0=h0, n=n: e.activation(out=dst.ap[:, h0:h0 + n, :], in_=ps.ap[:, 0:n * L], func=AF.Exp),
                             reads=[ps], writes=[dst])
                Cs = P.ring("sd_Cs", 2, [128, NH, L])
                for g in range(NG):
                    hs = slice(g * J, (g + 1) * J)
                    P.op("dve", lambda e, E=E, SM=SM, g=g, hs=hs: e.tensor_tensor(out=E.ap[:, hs, :], in0=E.ap[:, hs, :],
                         in1=SM.ap[:, g:g + 1, :].to_broadcast([128, J, L]), op=ALU.mult), reads=[E, SM], writes=[E])
                    P.op("pool", lambda e, EA=EA, Cs=Cs, g=g, hs=hs: e.tensor_tensor(out=Cs.ap[:, hs, :], in0=EA.ap[:, hs, :],
                         in1=Cc[g].ap[:, cs].unsqueeze(1).to_broadcast([128, J, L]), op=ALU.mult), reads=[EA, Cc[g]], writes=[Cs])
                if dbg and sc == 0 and c == 1:
                    P.dma("sp", dbg["xtok"], xtok); P.dma("sp", dbg["btok"], btok); P.dma("sp", dbg["datok"], datok); P.dma("sp", dbg["SM"], SM.re("p g l -> p (g l)"))
                    P.dma("sp", dbg["E"], E.re("p g l -> p (g l)")); P.dma("sp", dbg["EA"], EA.re("p g l -> p (g l)")); P.dma("sp", dbg["Cs"], Cs.re("p g l -> p (g l)"))
                    P.dma("sp", dbg["prev1"], prevT.re("p g l -> p (g l)"))
                for i0 in range(0, NP, 4):
                    n = min(4, NP - i0)
                    ps = S.next_psum()
                    for i in range(i0, i0 + n):
                        for hh in range(2):
                            h = 2 * i + hh
                            o = ps.ap[hh * 64:(hh + 1) * 64, (i - i0) * L:(i - i0 + 1) * L]
                            P.op("pe", lambda e, o=o, h=h, xtok=xtok, E=E: e.matmul(o, lhsT=xtok.ap[:, h * 64:(h + 1) * 64], rhs=E.ap[:, h, :], start=True, stop=False),
                                 reads=[xtok, E], writes=[ps])
                            P.op("pe", lambda e, o=o, h=h, Cs=Cs: e.matmul(o, lhsT=prevT.ap[:, h, :], rhs=Cs.ap[:, h, :], start=False, stop=True),
                                 reads=[prevT, Cs], writes=[ps])
                    for i in range(i0, i0 + n):
                        P.op("dve", lambda e, ps=ps, i=i, i0=i0: e.scalar_tensor_tensor(out=ysb[i].ap[:, cs], in0=xsc[i].ap[:, cs], scalar=dch.ap[:, i:i + 1],
                             in1=ps.ap[:, (i - i0) * L:(i - i0 + 1) * L], op0=ALU.mult, op1=ALU.add), reads=[xsc[i], dch, ps], writes=[ysb[i]])
                P.op("dve", lambda e, EA=EA: e.tensor_tensor(out=prevT.ap, in0=prevT.ap, in1=EA.ap[:, :, L - 1:L].to_broadcast([128, NH, 64]), op=ALU.mult),
                     reads=[prevT, EA], writes=[prevT])
                for g in range(NG):
                    ps = S.next_psum()
                    P.op("pe", lambda e, ps=ps, g=g, btok=btok, xdtok=xdtok: e.matmul(ps.ap[:, 0:J * 64], lhsT=btok.ap[:, g * 128:(g + 1) * 128],
                         rhs=xdtok.ap[:, g * J * 64:(g + 1) * J * 64], start=True, stop=True), reads=[btok, xdtok], writes=[ps])
                    P.op("dve", lambda e, ps=ps, g=g: e.tensor_tensor(out=prevT.ap[:, g * J:(g + 1) * J, :], in0=prevT.ap[:, g * J:(g + 1) * J, :],
                         in1=ps.ap[:, 0:J * 64].rearrange("p (j d) -> p j d", d=64), op=ALU.add), reads=[ps, prevT], writes=[prevT])
            for c in range(ncs):
                chunk_body(c)
            if dbg and sc == 0:
                P.dma("sp", dbg["ysb0"], ysb[0])
            for g in range(NG):
                pss = S.next_psum()
                for ii in range(TPG):
                    i = g * TPG + ii
                    zt = P.ring("sd_z", 2, [128, SC])
                    P.dma("sp", zt, zT[i * 128:(i + 1) * 128, t0:t0 + SC])
                    act(P, zt, zt, AF.Silu)
                    tt(P, "dve", ysb[i], ysb[i], zt, ALU.mult)
                    act(P, zt, ysb[i], AF.Square)
                    P.op("pe", lambda e, pss=pss, zt=zt, ii=ii: e.matmul(pss.ap, lhsT=ones.ap, rhs=zt.ap, start=(ii == 0), stop=(ii == TPG - 1)),
                         reads=[ones, zt], writes=[pss])
                rs = P.ring("sd_rs", 2, [128, SC])
                act(P, rs, pss, AF.Sqrt, scale=1.0 / (TPG * 128), bias=S.epsc)
                P.op("dve", lambda e, rs=rs: e.reciprocal(out=rs.ap, in_=rs.ap), reads=[rs], writes=[rs])
                for ii in range(TPG):
                    i = g * TPG + ii
                    P.op("dve", lambda e, i=i, rs=rs: e.scalar_tensor_tensor(out=ysb[i].ap, in0=ysb[i].ap, scalar=ngc.ap[:, i:i + 1], in1=rs.ap,
                         op0=ALU.mult, op1=ALU.mult), reads=[ysb[i], ngc, rs], writes=[ysb[i]])
                    P.dma("sp", yT[i * 128:(i + 1) * 128, t0:t0 + SC], ysb[i])

        for sc in range(nsc):
            sc_body(sc)
CH = 512


def emit_lru(P, S, xlT, glT, prm, yT, T, ntile):
    C = ntile * 128
    nch = T // CH
    with P.scope():
        cw = P.sbuf("l_cw", [128, ntile, 4])
        for k in range(4):
            P.dma("sp", cw[:, :, k], prm["conv_w"][k].re("(t p) -> p t", p=128), allow_slow_non_contiguous=True)
        cols = {}
        for nm in ("conv_b", "b_a", "b_x", "lam"):
            cols[nm] = P.sbuf("l_" + nm, [128, ntile])
            P.dma("sp", cols[nm], prm[nm].re("(t p) -> p t", p=128), allow_slow_non_contiguous=True)
        one = P.sbuf("l_one", [128, 1])
        P.op("dve", lambda e: e.memset(one.ap, 1.0), writes=[one])
        c1 = P.sbuf("l_c1", [128, ntile])
        P.op("act", lambda e: e.activation(out=c1.ap, in_=cols["lam"].ap, func=AF.Exp, scale=-1.0), reads=[cols["lam"]], writes=[c1])
        P.op("act", lambda e: e.activation(out=c1.ap, in_=c1.ap, func=AF.Ln, bias=one.ap), reads=[c1, one], writes=[c1])
        P.op("dve", lambda e: e.tensor_scalar(out=c1.ap, in0=c1.ap, scalar1=-8.0, scalar2=None, op0=ALU.mult), reads=[c1], writes=[c1])
        wa = P.sbuf("l_wa", [128, ntile, 128]); wx = P.sbuf("l_wx", [128, ntile, 128])
        P.dma("sp", wa, prm["wa_bd"].re("t p m -> p t m"))
        P.dma("sp", wx, prm["wx_bd"].re("t p m -> p t m"))
        for i in range(ntile):
            rows = slice(i * 128, (i + 1) * 128)
            xl = P.ring("l_xl", 1, [128, T + 3])
            P.op("pool", lambda e, xl=xl: e.memset(xl.ap[:, 0:3], 0.0), writes=[xl])
            P.dma("sp", xl[:, 3:T + 3], xlT[rows, :])
            gl = P.ring("l_gl", 1, [128, T])
            P.dma("sp", gl, glT[rows, :])
            xc = P.ring("l_xc", 1, [128, T])
            P.op("dve", lambda e, xl=xl, xc=xc, i=i: e.tensor_scalar(out=xc.ap, in0=xl.ap[:, 0:T], scalar1=cw.ap[:, i, 0:1],
                 scalar2=cols["conv_b"].ap[:, i:i + 1], op0=ALU.mult, op1=ALU.add), reads=[xl, cw, cols["conv_b"]], writes=[xc])
            for k in range(1, 4):
                P.op("dve", lambda e, xl=xl, xc=xc, i=i, k=k: e.scalar_tensor_tensor(out=xc.ap, in0=xl.ap[:, k:k + T],
                     scalar=cw.ap[:, i, k:k + 1], in1=xc.ap, op0=ALU.mult, op1=ALU.add), reads=[xl, cw, xc], writes=[xc])
            ga = P.ring("l_ga", 1, [128, T]); gi = P.ring("l_gi", 1, [128, T])
            for j in range(nch):
                sl = slice(j * CH, (j + 1) * CH)
                for (wm, bn, dst) in ((wa, "b_a", ga), (wx, "b_x", gi)):
                    ps = S.next_psum()
                    P.op("pe", lambda e, ps=ps, wm=wm, xc=xc, sl=sl, i=i: e.matmul(ps.ap, lhsT=wm.ap[:, i, :], rhs=xc.ap[:, sl], start=True, stop=True),
                         reads=[wm, xc], writes=[ps])
                    P.op("act", lambda e, ps=ps, dst=dst, bn=bn, sl=sl, i=i: e.activation(out=dst.ap[:, sl], in_=ps.ap, func=AF.Sigmoid,
                         bias=cols[bn].ap[:, i:i + 1]), reads=[ps, cols[bn]], writes=[dst])
            P.op("act", lambda e, ga=ga, i=i: e.activation(out=ga.ap, in_=ga.ap, func=AF.Exp, scale=c1.ap[:, i:i + 1]), reads=[ga, c1], writes=[ga])
            mu = P.ring("l_mu", 1, [128, T])
            P.op("pool", lambda e, ga=ga, mu=mu: e.tensor_tensor(out=mu.ap, in0=ga.ap, in1=ga.ap, op=ALU.mult), reads=[ga], writes=[mu])
            P.op("act", lambda e, mu=mu: e.activation(out=mu.ap, in_=mu.ap, func=AF.Sqrt, scale=-1.0, bias=one.ap), reads=[mu, one], writes=[mu])
            P.op("pool", lambda e, mu=mu: e.memset(mu.ap[:, 0:1], 1.0), reads=[mu], writes=[mu])
            P.op("dve", lambda e, gi=gi, xc=xc: e.tensor_tensor(out=gi.ap, in0=gi.ap, in1=xc.ap, op=ALU.mult), reads=[gi, xc], writes=[gi])
            P.op("pool", lambda e, gi=gi, mu=mu: e.tensor_tensor(out=gi.ap, in0=gi.ap, in1=mu.ap, op=ALU.mult), reads=[gi, mu], writes=[gi])
            P.op("dve", lambda e, gi=gi, ga=ga, xc=xc: e.tensor_tensor_scan(out=xc.ap, data0=ga.ap, data1=gi.ap, initial=0.0, op0=ALU.mult, op1=ALU.add),
                 reads=[ga, gi], writes=[xc])
            P.op("act", lambda e, gl=gl: e.activation(out=gl.ap, in_=gl.ap, func=AF.Gelu_apprx_tanh), reads=[gl], writes=[gl])
            P.op("dve", lambda e, gl=gl, xc=xc: e.tensor_tensor(out=xc.ap, in0=xc.ap, in1=gl.ap, op=ALU.mult), reads=[gl, xc], writes=[xc])
            P.dma("sp", yT[rows, :], xc)


def _tt(P, eng, out, a, b, op, r=False):
    o = rr(out.ap) if r else out.ap
    P.op(eng, lambda e: e.tensor_tensor(out=o, in0=a.ap, in1=b.ap, op=op), reads=[a, b], writes=[out])


def _ts(P, eng, out, a, s1, op0, s2=None, op1=None, r=False):
    rd = [a] + [x for x in (s1, s2) if isinstance(x, V)]
    g = lambda x: x.ap if isinstance(x, V) else x
    o = rr(out.ap) if r else out.ap
    if op1 is None:
        P.op(eng, lambda e: e.tensor_scalar(out=o, in0=a.ap, scalar1=g(s1), scalar2=None, op0=op0), reads=rd, writes=[out])
    else:
        P.op(eng, lambda e: e.tensor_scalar(out=o, in0=a.ap, scalar1=g(s1), scalar2=g(s2), op0=op0, op1=op1), reads=rd, writes=[out])


def _act(P, out, a, func, scale=None, bias=None):
    rd = [a] + [x for x in (scale, bias) if isinstance(x, V)]
    kw = {}
    if scale is not None:
        kw["scale"] = scale.ap if isinstance(scale, V) else scale
    if bias is not None:
        kw["bias"] = bias.ap if isinstance(bias, V) else bias
    P.op("act", lambda e: e.activation(out=out.ap, in_=a.ap, func=func, **kw), reads=rd, writes=[out])


def _stt(P, out, in0, scalar, in1, op0, op1):
    rd = [in0, in1] + ([scalar] if isinstance(scalar, V) else [])
    sc = scalar.ap if isinstance(scalar, V) else scalar
    P.op("dve", lambda e: e.scalar_tensor_tensor(out=out.ap, in0=in0.ap, scalar=sc, in1=in1.ap, op0=op0, op1=op1), reads=rd, writes=[out])


R32 = True
F32R = mybir.dt.float32r


def rr(ap):
    return ap.bitcast(F32R) if R32 else ap


def _mm(P, out, lhsT, rhs, start=True, stop=True, fast=False):
    if fast and R32:
        P.op("pe", lambda e: e.matmul(out.ap, lhsT=rr(lhsT.ap), rhs=rr(rhs.ap), start=start, stop=stop), reads=[lhsT, rhs], writes=[out])
    else:
        P.op("pe", lambda e: e.matmul(out.ap, lhsT=lhsT.ap, rhs=rhs.ap, start=start, stop=stop), reads=[lhsT, rhs], writes=[out])


def _tr(P, out, in_, ident):
    P.op("pe", lambda e: e.transpose(out.ap, in_.ap, ident.ap), reads=[in_, ident], writes=[out])


def emit_rwkv(P, S, rT, kT, vT, wlT, alT, glT, prm, cst, yT, T, NP, stage=99, dbg=None):
    C = 64
    SC = 256
    NH = NP * 2
    nsc = T // SC
    ncs = SC // C
    GN_EPS = 64e-5
    with P.scope():
        ld = lambda name, shape, src, **kw: (lambda t: (P.dma("sp", t, src, **kw), t)[1])(P.sbuf(name, shape))
        ident = ld("rw_id", [128, 128], cst["ident"]); bones = ld("rw_bo", [128, 128], cst["bones"])
        mask2 = ld("rw_m2", [64, 128], cst["mask2"]); maskL = ld("rw_mL", [64, 64], cst["maskL"]); rmask = ld("rw_rm", [128, SC], cst["rmask"])
        colv = lambda nm, n, rows=128: ld("rw_" + nm, [rows, n], prm[nm].re("(t p) -> p t", p=rows), allow_slow_non_contiguous=True)
        mu = {x: colv("mu_" + x, NP) for x in "rkv"}
        mu_wl = colv("mu_wl", 1, 96); mu_al = colv("mu_al", 1, 96); mu_gl = colv("mu_gl", 2)
        w0 = colv("w0", NP); a0 = colv("a0", NP); kkc = colv("k_k", NP); kac = colv("k_a", NP); rkc = colv("r_k", NP)
        lng = colv("ln_g", NP); lnb = colv("ln_b", NP)
        wup = ld("rw_wup", [96, NP * 128], prm["w_up"]); aup = ld("rw_aup", [96, NP * 128], prm["a_up"])
        gup = ld("rw_gup", [128, 2, NP * 128], prm["g_up"].re("(k p) n -> p k n", p=128))
        def one_minus(src, name):
            t = P.sbuf(name, list(src.shape))
            _ts(P, "dve", t, src, -1.0, ALU.mult, 1.0, ALU.add)
            return t
        imu = {x: one_minus(mu[x], "rw_imu" + x) for x in "rkv"}
        imu_wl = one_minus(mu_wl, "rw_imuwl"); imu_al = one_minus(mu_al, "rw_imual"); imu_gl = one_minus(mu_gl, "rw_imugl")
        ika = one_minus(kac, "rw_ika")
        gne = P.sbuf("rw_gne", [128, 1]); P.op("dve", lambda e: e.memset(gne.ap, GN_EPS), writes=[gne])
        Hst = P.sbuf("rw_H", [64, NH, 64])
        P.op("dve", lambda e: e.memset(Hst.ap, 0.0), writes=[Hst])

        def shift_mix(srcT, r0, nr, t0, muc, imuc, dst):
            xp = P.ring("rw_xp", 3, [128, SC + 1])
            if t0 == 0:
                P.op("pool", lambda e: e.memset(xp.ap[0:nr, 0:1], 0.0), writes=[xp])
                P.dma("sp", xp[0:nr, 1:SC + 1], srcT[r0:r0 + nr, 0:SC])
            else:
                P.dma("sp", xp[0:nr], srcT[r0:r0 + nr, t0 - 1:t0 + SC])
            tmp = P.ring("rw_smt", 2, [128, SC])
            P.op("act", lambda e: e.activation(out=tmp.ap[0:nr], in_=xp.ap[0:nr, 0:SC], func=AF.Copy, scale=muc.ap),
                 reads=[xp, muc], writes=[tmp])
            P.op("dve", lambda e: e.scalar_tensor_tensor(out=dst.ap, in0=xp.ap[0:nr, 1:SC + 1], scalar=imuc.ap, in1=tmp.ap[0:nr], op0=ALU.mult, op1=ALU.add),
                 reads=[xp, imuc, tmp], writes=[dst])

        def sc_body(sc):
            t0 = sc * SC
            tw = P.ring("rw_tw", 1, [96, SC]); al = P.ring("rw_al", 1, [96, SC]); sg = P.ring("rw_sg", 1, [128, 2, SC])
            shift_mix(wlT, 0, 96, t0, mu_wl[:, 0:1], imu_wl[:, 0:1], tw)
            _act(P, tw, tw, AF.Tanh)
            shift_mix(alT, 0, 96, t0, mu_al[:, 0:1], imu_al[:, 0:1], al)
            for kx in range(2):
                shift_mix(glT, kx * 128, 128, t0, mu_gl[:, kx:kx + 1], imu_gl[:, kx:kx + 1], sg[:, kx, :])
            _act(P, sg, sg, AF.Sigmoid)
            Gc = P.ring("rw_Gc", 1, [64, NH, ncs])
            KRo = []; bho = []; kho = []
            rp = []; vp = []; k2 = []; KR = []; bh = []; kh = []; gt = []; gC = []; bon = []
            for i in range(NP):
                cols = slice(i * 128, (i + 1) * 128)
                r_ = P.ring("rw_r%d" % i, 1, [128, SC]); k_ = P.ring("rw_k%d" % i, 1, [128, SC]); v_ = P.ring("rw_v%d" % i, 1, [128, SC])
                shift_mix(rT, i * 128, 128, t0, mu["r"][:, i:i + 1], imu["r"][:, i:i + 1], r_)
                shift_mix(kT, i * 128, 128, t0, mu["k"][:, i:i + 1], imu["k"][:, i:i + 1], k_)
                shift_mix(vT, i * 128, 128, t0, mu["v"][:, i:i + 1], imu["v"][:, i:i + 1], v_)
                lw = P.ring("rw_lw", 1, [128, SC]); a_ = P.ring("rw_a", 1, [128, SC]); g_ = P.ring("rw_g%d" % i, 1, [128, SC])
                ps = S.next_psum()
                _mm(P, ps[:, 0:SC], wup[:, cols], tw)
                _act(P, lw, ps[:, 0:SC], AF.Sigmoid, bias=w0[:, i:i + 1])
                _ts(P, "dve", lw, lw, -math.exp(-0.5), ALU.mult)
                ps = S.next_psum()
                _mm(P, ps[:, 0:SC], aup[:, cols], al)
                _act(P, a_, ps[:, 0:SC], AF.Sigmoid, bias=a0[:, i:i + 1])
                ps = S.next_psum()
                _mm(P, ps[:, 0:SC], gup[:, 0, cols], sg[:, 0, :], True, False)
                _mm(P, ps[:, 0:SC], gup[:, 1, cols], sg[:, 1, :], False, True)
                P.op("act", lambda e, g_=g_, ps=ps: e.copy(out=g_.ap, in_=ps.ap[:, 0:SC]), reads=[ps], writes=[g_])
                kap = P.ring("rw_kap", 1, [128, SC]); t1 = P.ring("rw_t1", 1, [128, SC]); t2 = P.ring("rw_t2", 1, [128, SC])
                _ts(P, "dve", kap, k_, kkc[:, i:i + 1], ALU.mult)
                _tt(P, "pool", t1, kap, kap, ALU.mult)
                ps = S.next_psum()
                _mm(P, ps[:, 0:SC], bones, t1)
                _ts(P, "dve", t1, ps[:, 0:SC], 1e-24, ALU.max)
                _act(P, t1, t1, AF.Sqrt)
                P.op("dve", lambda e, t1=t1: e.reciprocal(out=t1.ap, in_=t1.ap), reads=[t1], writes=[t1])
                _tt(P, "dve", kap, kap, t1, ALU.mult)
                k2_ = P.ring("rw_k2%d" % i, 1, [128, SC])
                _ts(P, "dve", t1, a_, kac[:, i:i + 1], ALU.mult, ika[:, i:i + 1], ALU.add)
                _tt(P, "dve", k2_, k_, t1, ALU.mult)
                bet = P.ring("rw_bet", 1, [128, SC])
                _tt(P, "pool", bet, kap, a_, ALU.mult)
                cl = P.ring("rw_cl", 1, [128, SC])
                P.op("dve", lambda e, cl=cl, lw=lw: e.tensor_tensor_scan(out=cl.ap, data0=rmask.ap, data1=lw.ap, initial=0.0, op0=ALU.mult, op1=ALU.add),
                     reads=[rmask, lw], writes=[cl])
                eG = P.ring("rw_eG", 1, [128, SC]); eN = P.ring("rw_eN", 1, [128, SC])
                _act(P, eG, cl, AF.Exp)
                _act(P, eN, cl, AF.Exp, scale=-1.0)
                _tt(P, "dve", t2, cl, lw, ALU.subtract)
                _act(P, t2, t2, AF.Exp)
                KR_ = P.ring("rw_KR%d" % i, 1, [128, ncs, 2, C])
                _tt(P, "dve", KR_[:, :, 0, :], kap.re("p (c t) -> p c t", t=C), t2.re("p (c t) -> p c t", t=C), ALU.mult, r=True)
                _tt(P, "pool", KR_[:, :, 1, :], r_.re("p (c t) -> p c t", t=C), eG.re("p (c t) -> p c t", t=C), ALU.mult, r=True)
                bh_ = P.ring("rw_bh%d" % i, 1, [128, SC]); kh_ = P.ring("rw_kh%d" % i, 1, [128, SC])
                _tt(P, "dve", bh_, bet, eN, ALU.mult, r=True)
                _tt(P, "pool", kh_, k2_, eN, ALU.mult, r=True)
                gC_ = P.ring("rw_gC%d" % i, 1, [128, ncs])
                P.op("act", lambda e, gC_=gC_, eG=eG: e.copy(out=gC_.ap, in_=eG.ap.rearrange("p (c t) -> p c t", t=C)[:, :, C - 1]), reads=[eG], writes=[gC_])
                bon_ = P.ring("rw_bon%d" % i, 1, [128, SC])
                _stt(P, t1, r_, rkc[:, i:i + 1], k2_, ALU.mult, ALU.mult)
                ps = S.next_psum()
                _mm(P, ps[:, 0:SC], bones, t1)
                _tt(P, "dve", bon_, ps[:, 0:SC], v_, ALU.mult)
                KRo_ = P.ring("rw_KRo%d" % i, 1, [64, ncs, 2, C]); bho_ = P.ring("rw_bho%d" % i, 1, [64, SC]); kho_ = P.ring("rw_kho%d" % i, 1, [64, SC])
                P.dma("sp", KRo_.bitcast(F32R) if R32 else KRo_, KR_[64:128].bitcast(F32R) if R32 else KR_[64:128])
                P.dma("sp", bho_.bitcast(F32R) if R32 else bho_, bh_[64:128].bitcast(F32R) if R32 else bh_[64:128])
                P.dma("sp", kho_.bitcast(F32R) if R32 else kho_, kh_[64:128].bitcast(F32R) if R32 else kh_[64:128])
                P.op("pool", lambda e, gC_=gC_, i=i: e.tensor_copy(out=Gc.ap[:, 2 * i, :], in_=gC_.ap[0:64, :]), reads=[gC_], writes=[Gc])
                P.dma("sp", Gc[:, 2 * i + 1, :], gC_[64:128, :])
                KRo.append(KRo_); bho.append(bho_); kho.append(kho_)
                rp.append(r_); vp.append(v_); k2.append(k2_); KR.append(KR_); bh.append(bh_); kh.append(kh_); gt.append(g_); gC.append(gC_); bon.append(bon_)
            ysb = [P.ring("rw_y%d" % i, 1, [128, SC]) for i in range(NP)]
            KRh = lambda h: KR[h // 2][0:64] if h % 2 == 0 else KRo[h // 2]
            bhh = lambda h: bh[h // 2][0:64] if h % 2 == 0 else bho[h // 2]
            khh = lambda h: kh[h // 2][0:64] if h % 2 == 0 else kho[h // 2]

            def prep_chunk(c):
                cs = slice(c * C, (c + 1) * C)
                toks = {}
                for nm, srcs in (("v", vp), ("b", bh), ("k", kh)):
                    ps = S.next_psum()
                    for i in range(NP):
                        _tr(P, ps[0:C, i * 128:(i + 1) * 128], srcs[i][:, cs], ident)
                    tk = P.ring("rw_tok" + nm, 2, [64, NP * 128])
                    P.op("act", lambda e, tk=tk, ps=ps: e.copy(out=rr(tk.ap), in_=ps.ap[0:C, 0:NP * 128]), reads=[ps], writes=[tk])
                    toks[nm] = tk
                psN = S.next_psum(); psB = S.next_psum(); psK = S.next_psum(); psB2 = S.next_psum(); psK2 = S.next_psum()
                nb = 4
                for h in range(NH):
                    kapc = KRh(h)[:, c, 0, :]; krc = KRh(h)[:, c, :, :].re("p a t -> p (a t)")
                    _mm(P, psN[0:C, h * C:(h + 1) * C], kapc, bhh(h)[:, cs], fast=True)
                    pb_ = psB if h < nb else psB2
                    pk_ = psK if h < nb else psK2
                    _mm(P, pb_[0:C, (h % nb) * 128:(h % nb + 1) * 128], bhh(h)[:, cs], krc, fast=True)
                    _mm(P, pk_[0:C, (h % nb) * 128:(h % nb + 1) * 128], khh(h)[:, cs], krc, fast=True)
                Q = P.ring("rw_Q", 2, [64, NH, C]); QT = P.ring("rw_QT", 2, [64, NH, C]); R = P.ring("rw_R", 2, [64, NH, C])
                AB = P.ring("rw_AB", 2, [64, NH, 2, C]); AK = P.ring("rw_AK", 2, [64, NH, 2, C])
                P.op("dve", lambda e, Q=Q: e.scalar_tensor_tensor(out=rr(Q.ap), in0=psN.ap[0:C, 0:NH * C].rearrange("p (h t) -> p h t", t=C), scalar=-1.0,
                     in1=maskL.ap.unsqueeze(1).to_broadcast([64, NH, C]), op0=ALU.mult, op1=ALU.mult), reads=[psN, maskL], writes=[Q])
                for (pp, dst, h0) in ((psB, AB, 0), (psB2, AB, nb), (psK, AK, 0), (psK2, AK, nb)):
                    n = min(nb, NH - h0)
                    if n <= 0:
                        continue
                    P.op("dve", lambda e, pp=pp, dst=dst, h0=h0, n=n: e.tensor_tensor(out=rr(dst.ap[:, h0:h0 + n].rearrange("p h a t -> p h (a t)")),
                         in0=pp.ap[0:C, 0:n * 128].rearrange("p (h x) -> p h x", x=128), in1=mask2.ap.unsqueeze(1).to_broadcast([64, n, 128]), op=ALU.mult),
                         reads=[pp, mask2], writes=[dst])
                _ts(P, "dve", QT, AB[:, :, 0, :], -1.0, ALU.mult, r=True)
                P.op("dve", lambda e, QT=QT: e.tensor_tensor(out=rr(R.ap), in0=QT.ap, in1=ident.ap[0:64, 0:64].unsqueeze(1).to_broadcast([64, NH, C]), op=ALU.add),
                         reads=[QT, ident], writes=[R])
                nlev = 6
                yield
                for lv in range(1, nlev):
                    if lv > 1:
                        yield
                    psQ = S.next_psum(); psQT = S.next_psum(); psR = S.next_psum()
                    Qn = P.ring("rw_Q", 2, [64, NH, C]); QTn = P.ring("rw_QT", 2, [64, NH, C])
                    last = (lv == nlev - 1)
                    for h in range(NH):
                        hsl = slice(h * C, (h + 1) * C)
                        _mm(P, psQ[0:C, hsl], QT[:, h, :], Q[:, h, :], fast=True)
                        if not last:
                            _mm(P, psQT[0:C, hsl], Q[:, h, :], QT[:, h, :], fast=True)
                    P.op("act", lambda e, Qn=Qn, psQ=psQ: e.copy(out=rr(Qn.ap.rearrange("p h t -> p (h t)")), in_=psQ.ap[0:C, 0:NH * C]), reads=[psQ], writes=[Qn])
                    if not last:
                        P.op("act", lambda e, QTn=QTn, psQT=psQT: e.copy(out=rr(QTn.ap.rearrange("p h t -> p (h t)")), in_=psQT.ap[0:C, 0:NH * C]), reads=[psQT], writes=[QTn])
                    for h in range(NH):
                        _mm(P, psR[0:C, h * C:(h + 1) * C], Qn[:, h, :], R[:, h, :], fast=True)
                    P.op("dve", lambda e, psR=psR: e.tensor_tensor(out=rr(R.ap.rearrange("p h t -> p (h t)")), in0=psR.ap[0:C, 0:NH * C],
                         in1=R.ap.rearrange("p h t -> p (h t)"), op=ALU.add), reads=[psR, R], writes=[R])
                    Q, QT = Qn, QTn
                if dbg and sc == 0 and c == 0:
                    P.dma("sp", dbg["R"], R.re("p h t -> p (h t)")); P.dma("sp", dbg["AB"], AB.re("p h a t -> p (h a t)")); P.dma("sp", dbg["AK"], AK.re("p h a t -> p (h a t)"))
                    P.dma("sp", dbg["vt"], toks["v"]); P.dma("sp", dbg["bt"], toks["b"])
                return dict(toks=toks, R=R, AB=AB, AK=AK)

            def seq_chunk(c, pc):
                cs = slice(c * C, (c + 1) * C)
                toks, R, AB, AK = pc["toks"], pc["R"], pc["AB"], pc["AK"]
                vt, bt, kt = toks["v"], toks["b"], toks["k"]
                psW = S.next_psum()
                Hr = P.ring("rw_Hr", 2, [64, NH, 64])
                P.op("act", lambda e: e.copy(out=rr(Hr.ap), in_=Hst.ap), reads=[Hst], writes=[Hr])
                for h in range(NH):
                    _mm(P, psW[0:C, h * 64:(h + 1) * 64], KRh(h)[:, c, 0, :], Hr[:, h, :], True, False, fast=True)
                    _mm(P, psW[0:C, h * 64:(h + 1) * 64], AK[:, h, 0, :], vt[:, h * 64:(h + 1) * 64], False, True, fast=True)
                Wsb = P.ring("rw_W", 2, [64, NH * 64])
                P.op("act", lambda e: e.copy(out=rr(Wsb.ap), in_=psW.ap[0:C, 0:NH * 64]), reads=[psW], writes=[Wsb])
                yield
                psU = S.next_psum()
                for h in range(NH):
                    _mm(P, psU[0:C, h * 64:(h + 1) * 64], R[:, h, :], Wsb[:, h * 64:(h + 1) * 64], fast=True)
                Usb = P.ring("rw_U", 2, [64, NH * 64])
                _ts(P, "dve", Usb, psU[0:C, 0:NH * 64], -1.0, ALU.mult, r=True)
                if dbg and sc == 0 and c == 0:
                    P.dma("sp", dbg["W"], Wsb); P.dma("sp", dbg["U"], Usb)
                yield
                psY = S.next_psum(); psH = S.next_psum()
                for h in range(NH):
                    i, hh = divmod(h, 2); pr = slice(hh * 64, (hh + 1) * 64)
                    oy = psY[pr, i * C:(i + 1) * C]
                    fy = (hh == 0)
                    _mm(P, oy, Hr[:, h, :], KRh(h)[:, c, 1, :], True, False, fast=fy)
                    _mm(P, oy, Usb[:, h * 64:(h + 1) * 64], AB[:, h, 1, :], False, False, fast=fy)
                    _mm(P, oy, vt[:, h * 64:(h + 1) * 64], AK[:, h, 1, :], False, True, fast=fy)
                for h in range(NH):
                    oh = psH[0:64, h * 64:(h + 1) * 64]
                    _mm(P, oh, bt[:, h * 64:(h + 1) * 64], Usb[:, h * 64:(h + 1) * 64], True, False, fast=True)
                    _mm(P, oh, kt[:, h * 64:(h + 1) * 64], vt[:, h * 64:(h + 1) * 64], False, True, fast=True)
                yield
                for i in range(NP):
                    P.op("act", lambda e, i=i: e.copy(out=ysb[i].ap[:, cs], in_=psY.ap[:, i * C:(i + 1) * C]), reads=[psY], writes=[ysb[i]])
                P.op("dve", lambda e: e.tensor_tensor(out=Hst.ap, in0=Hst.ap, in1=psH.ap[0:64, 0:NH * 64].rearrange("p (i v) -> p i v", v=64), op=ALU.add),
                     reads=[Hst, psH], writes=[Hst])
                P.op("dve", lambda e: e.tensor_tensor(out=Hst.ap, in0=Hst.ap, in1=Gc.ap[:, :, c:c + 1].to_broadcast([64, NH, 64]), op=ALU.mult),
                     reads=[Hst, Gc], writes=[Hst])

            if dbg and sc == 0:
                P.dma("sp", dbg["KR0"], KR[0].re("p c a t -> p (c a t)")); P.dma("sp", dbg["bh0"], bh[0]); P.dma("sp", dbg["kh0"], kh[0])
                P.dma("sp", dbg["KRo0"], KRo[0].re("p c a t -> p (c a t)")); P.dma("sp", dbg["Gc"], Gc.re("p h c -> p (h c)"))
            if stage == 1:
                for i in range(NP):
                    P.dma("sp", yT[i * 128:(i + 1) * 128, t0:t0 + SC], bon[i])
                return
            def run_all(g):
                try:
                    while True:
                        next(g)
                except StopIteration as ex:
                    return ex.value
            pcs = run_all(prep_chunk(0))
            for c in range(ncs):
                g1 = prep_chunk(c + 1) if c + 1 < ncs else None
                g2 = seq_chunk(c, pcs)
                nxt = None
                d1 = g1 is None
                d2 = False
                while not (d1 and d2):
                    if not d2:
                        try:
                            next(g2)
                        except StopIteration:
                            d2 = True
                    if not d1:
                        try:
                            next(g1)
                        except StopIteration as ex:
                            nxt = ex.value
                            d1 = True
                pcs = nxt
            if dbg and sc == 0:
                P.dma("sp", dbg["y0"], ysb[0]); P.dma("sp", dbg["H"], Hst.re("p h v -> p (h v)"))
            for i in range(NP):
                y = ysb[i]
                t1 = P.ring("rw_t1", 1, [128, SC]); t2 = P.ring("rw_t2", 1, [128, SC])
                ps = S.next_psum()
                _mm(P, ps[:, 0:SC], bones, y)
                _stt(P, y, ps[:, 0:SC], -1.0 / 64, y, ALU.mult, ALU.add)
                _tt(P, "pool", t1, y, y, ALU.mult)
                ps = S.next_psum()
                _mm(P, ps[:, 0:SC], bones, t1)
                _act(P, t2, ps[:, 0:SC], AF.Sqrt, scale=1.0 / 64, bias=gne)
                P.op("dve", lambda e, t2=t2: e.reciprocal(out=t2.ap, in_=t2.ap), reads=[t2], writes=[t2])
                _stt(P, y, y, lng[:, i:i + 1], t2, ALU.mult, ALU.mult)
                _stt(P, y, y, lnb[:, i:i + 1], bon[i], ALU.add, ALU.add)
                _tt(P, "dve", y, y, gt[i], ALU.mult)
                P.dma("sp", yT[i * 128:(i + 1) * 128, t0:t0 + SC], y)

        for sc in range(nsc):
            sc_body(sc)
import numpy as _np

NT_CORE = 2048
SEQ = 4096
NB = 4


def tile_w(W):
    K, N = W.shape
    NCB = (N + 127) // 128
    Wp = _np.zeros((K, NCB * 128), _np.float32)
    Wp[:, :N] = W
    return _np.ascontiguousarray(Wp.reshape(K // 128, 128, NCB, 128).transpose(2, 1, 0, 3))


def consts_all():
    ident = _np.eye(128, dtype=_np.float32)
    psw = _np.zeros((128, 128), _np.float32)
    for k in range(128):
        psw[k, (k + 64) % 128] = 1
    sgn = _np.ones((128, 1), _np.float32); sgn[64:] = -1
    s = _np.arange(128)
    tri = (s[:, None] <= s[None, :]).astype(_np.float32)
    ustr = (s[:, None] > s[None, :]).astype(_np.float32)
    sel = _np.zeros((12, 768), _np.float32)
    for h in range(12):
        sel[h, h * 64:(h + 1) * 64] = 1
    bones = _np.zeros((128, 128), _np.float32); bones[:64, :64] = 1; bones[64:, 64:] = 1
    s6 = _np.arange(64)
    mU = (s6[:, None] < s6[None, :]).astype(_np.float32); mUi = (s6[:, None] <= s6[None, :]).astype(_np.float32)
    mask2 = _np.concatenate([mU, mUi], 1)
    maskL = (s6[None, :] < s6[:, None]).astype(_np.float32)
    rmask = _np.ones((128, 256), _np.float32); rmask[:, ::64] = 0
    return dict(ident=ident, psw=psw, sgn=sgn, tri=tri, ustr=ustr, sel=sel, bones=bones, mask2=mask2, maskL=maskL, rmask=rmask)


def s5_host(lam_re, lam_im, log_step, b_re, b_im, c_re, c_im, d):
    G = lam_re.shape[0]
    rep = lambda a: _np.ascontiguousarray(_np.broadcast_to(a[None], (16,) + a.shape)).astype(_np.float32)
    col = lambda a: _np.ascontiguousarray(_np.concatenate([a.T, a.T], 0)).astype(_np.float32)
    ls = _np.broadcast_to(log_step[:, None], (G, 64))
    return dict(lr_row=rep(lam_re), li_row=rep(lam_im), ls_row=rep(ls),
                bT_re=_np.ascontiguousarray(b_re.transpose(2, 0, 1)), bT_im=_np.ascontiguousarray(b_im.transpose(2, 0, 1)),
                lr_col=col(lam_re), li_col=col(lam_im), ls_col=col(ls),
                cT=_np.ascontiguousarray(_np.concatenate([c_re.transpose(2, 0, 1), c_im.transpose(2, 0, 1)], 0)),
                d_col=_np.ascontiguousarray(d.reshape(G, 16).T))


def even_params(inp, hh):
    g0 = hh * 16
    p = {"s5_" + k: v for k, v in s5_host(inp["s5_lam_re"][0, g0:g0 + 16], inp["s5_lam_im"][0, g0:g0 + 16], inp["s5_log_step"][0, g0:g0 + 16],
                                          inp["s5_b_re"][0, g0:g0 + 16], inp["s5_b_im"][0, g0:g0 + 16], inp["s5_c_re"][0, g0:g0 + 16],
                                          inp["s5_c_im"][0, g0:g0 + 16], inp["s5_d"][0, hh * 256:(hh + 1) * 256]).items()}
    cw = inp["ssd_conv_w"][0]; cb = inp["ssd_conv_b"][0]
    xs = slice(hh * 768, (hh + 1) * 768); bs = slice(1536 + hh * 256, 1536 + (hh + 1) * 256); cs = slice(2048 + hh * 256, 2048 + (hh + 1) * 256)
    hs = slice(hh * 12, (hh + 1) * 12)
    c = lambda a: _np.ascontiguousarray(a, dtype=_np.float32)
    p.update(sd_cw_x=c(cw[:, xs]), sd_cb_x=c(cb[xs]), sd_cw_b=c(cw[:, bs]), sd_cb_b=c(cb[bs]), sd_cw_c=c(cw[:, cs]), sd_cb_c=c(cb[cs]),
             sd_dt_bias=c(inp["ssd_dt_bias"][0, hs]), sd_a_log=c(inp["ssd_a_log"][0, hs]),
             sd_d_ch=c(_np.repeat(inp["ssd_d"][0, hs], 64)), sd_ng=c(inp["ssd_norm"][0, xs]))
    return p


def lru_bd(w, hh):
    o = _np.zeros((4, 128, 128), _np.float32)
    for bl in range(8):
        t, h = divmod(bl, 2)
        o[t, h * 64:(h + 1) * 64, h * 64:(h + 1) * 64] = w[hh * 8 + bl]
    return o


def odd_params(inp, hh):
    c = lambda a: _np.ascontiguousarray(a, dtype=_np.float32)
    mu = inp["rwkv_mu"][0]
    ch = slice(hh * 512, (hh + 1) * 512)
    p = dict(rw_mu_r=c(mu[0:1024][ch]), rw_mu_k=c(mu[1024:2048][ch]), rw_mu_v=c(mu[2048:3072][ch]), rw_mu_wl=c(mu[3072:3168]), rw_mu_al=c(mu[3168:3264]),
             rw_mu_gl=c(mu[3264:3520]), rw_w0=c(inp["rwkv_w0"][0, ch]), rw_a0=c(inp["rwkv_a0"][0, ch]), rw_k_k=c(inp["rwkv_k_k"][0, ch]),
             rw_k_a=c(inp["rwkv_k_a"][0, ch]), rw_r_k=c(inp["rwkv_r_k"][0].reshape(-1)[ch]), rw_ln_g=c(inp["rwkv_ln_g"][0, ch]), rw_ln_b=c(inp["rwkv_ln_b"][0, ch]),
             rw_w_up=c(inp["rwkv_w_up"][0][:, ch]), rw_a_up=c(inp["rwkv_a_up"][0][:, ch]), rw_g_up=c(inp["rwkv_g_up"][0][:, ch]))
    p.update(lr_conv_w=c(inp["lru_conv_w"][0][:, ch]), lr_conv_b=c(inp["lru_conv_b"][0, ch]), lr_wa_bd=lru_bd(inp["lru_w_a"][0], hh), lr_wx_bd=lru_bd(inp["lru_w_x"][0], hh),
             lr_b_a=c(inp["lru_b_a"][0].reshape(-1)[ch]), lr_b_x=c(inp["lru_b_x"][0].reshape(-1)[ch]), lr_lam=c(inp["lru_lam"][0].reshape(-1)[ch]))
    return p


def dense_w(inp, L):
    c = lambda a: _np.ascontiguousarray(a, dtype=_np.float32)
    return dict(out_t=tile_w(inp["e_out_proj" if L == 0 else "o_out_proj"][0]), w1_t=tile_w(inp["mlp_w1"][L]), w2_t=tile_w(inp["mlp_w2"][L]),
                gate_t=tile_w(inp["pl_gate"][L]), plp_t=tile_w(inp["pl_proj"][L]), nffn=c(inp["norm_ffn"][L]), npl=c(inp["norm_pl"][L]))


class Launch:
    def __init__(self):
        self.nc = bass.Bass("TRN2", target_bir_lowering=False)
        self.st = contextlib.ExitStack()
        self.P = Prog(self.nc, self.st)
        self.S = Shared(self.P)
        self.outs = []

    def inp(self, name, arr):
        return self.P.dram(name, list(arr.shape), kind="ExternalInput")

    def inps(self, d, prefix=""):
        return {k: self.inp(prefix + k, v) for k, v in d.items()}

    def out(self, name, shape):
        v = self.P.dram(name, list(shape), kind="ExternalOutput")
        self.outs.append(v)
        return v

    def run(self, in_maps):
        self.P.wait_all("sp", self.outs)
        self.P.finish()
        self.st.close()
        res = run_bass_kernel_spmd(self.nc, in_maps, core_ids=list(range(len(in_maps))))
        return res.results


def strip(d, prefix):
    return {k[len(prefix):]: v for k, v in d.items() if k.startswith(prefix)}


PAIRS = [[0, 1], [2, 3], [4, 5], [6, 7]]
ALL8 = [list(range(8))]


def pad_cols(W, n):
    o = _np.zeros((W.shape[0], n), _np.float32)
    o[:, :W.shape[1]] = W
    return o


def host_inputs(inp):
    f32c = lambda a: _np.ascontiguousarray(a, dtype=_np.float32)
    x = inp["x"]; p = inp["p"]
    cst = consts_all()
    ein = inp["e_in_proj"][0]; oin = inp["o_in_proj"][0]
    eout = inp["e_out_proj"][0]; oout = inp["o_out_proj"][0]
    shared = {}
    for L in range(2):
        shared["w1_%d" % L] = tile_w(inp["mlp_w1"][L]); shared["w2_%d" % L] = tile_w(inp["mlp_w2"][L])
        shared["gate_%d" % L] = tile_w(inp["pl_gate"][L]); shared["plp_%d" % L] = tile_w(inp["pl_proj"][L])
    per_hh = []
    for hh in range(2):
        d = {}
        cols0 = _np.concatenate([ein[:, hh * 256:(hh + 1) * 256], ein[:, 512 + hh * 768:512 + (hh + 1) * 768], ein[:, 2048 + hh * 768:2048 + (hh + 1) * 768],
                                 ein[:, 3584 + hh * 256:3584 + (hh + 1) * 256], ein[:, 4096 + hh * 256:4096 + (hh + 1) * 256],
                                 pad_cols(ein[:, 4608 + hh * 12:4608 + (hh + 1) * 12], 128)], 1)
        d["win0_t"] = tile_w(cols0)
        ch = lambda o: oin[:, o + hh * 512:o + (hh + 1) * 512]
        cols1 = _np.concatenate([ch(0), ch(1024), ch(2048), pad_cols(oin[:, 3072:3168], 128), pad_cols(oin[:, 3168:3264], 128), oin[:, 3264:3520], ch(3520), ch(4544)], 1)
        d["win1_t"] = tile_w(cols1)
        d["wout0_t"] = tile_w(_np.concatenate([eout[hh * 256:(hh + 1) * 256], eout[512 + hh * 768:512 + (hh + 1) * 768]], 0))
        d["wout1_t"] = tile_w(_np.concatenate([oout[hh * 512:(hh + 1) * 512], oout[1024 + hh * 512:1024 + (hh + 1) * 512]], 0))
        d["gluw_t"] = tile_w(inp["s5_glu_w"][0][hh * 256:(hh + 1) * 256])
        d["glub"] = f32c(inp["s5_glu_b"][0][hh * 256:(hh + 1) * 256])
        d.update(even_params(inp, hh)); d.update(odd_params(inp, hh))
        per_hh.append(d)
    ins = []
    for c in range(8):
        b, hh = divmod(c, 2)
        ts_ = slice(hh * NT_CORE, (hh + 1) * NT_CORE)
        d = dict(xT=f32c(x[b, ts_, :].T), pT0=f32c(p[0, b, ts_, :].T), pT1=f32c(p[1, b, ts_, :].T))
        for k, v in shared.items():
            d[k + "_t"] = v
        xf = x[b].T.reshape(8, 256, 2, NT_CORE).transpose(0, 2, 1, 3)
        d["xfull"] = f32c(xf)
        for L in range(2):
            d["nmix%d" % L] = f32c(inp["norm_mix"][L]); d["nffn%d" % L] = f32c(inp["norm_ffn"][L]); d["npl%d" % L] = f32c(inp["norm_pl"][L])
        d["nfin"] = f32c(inp["norm_final"])
        d.update(per_hh[hh]); d.update({"c_" + k: v for k, v in cst.items()})
        ins.append(d)
    return ins


def build_fused(ex):
    la = Launch()
    P, S = la.P, la.S
    dd = la.inps(ex)
    cs_d = strip(dd, "c_")
    T = SEQ; NT = NT_CORE
    hfull0 = dd["xfull"]
    Wf = [{}, {}]
    for L in range(2):
        for nm in ("w1", "w2", "gate", "plp"):
            Wf[L][nm + "_t"] = dd["%s_%d_t" % (nm, L)]
        Wf[L]["nffn"] = dd["nffn%d" % L]; Wf[L]["npl"] = dd["npl%d" % L]

    def rs_mix(mp, mix):
        for q in range(4):
            P.collective("ReduceScatter", mix[q * 512:(q + 1) * 512, :], mp[q].re("s f t -> (s f) t"), PAIRS, op=ALU.add)
    pin0 = P.dram("pin0", [19 * 128, T])
    for s in range(2):
        with P.scope():
            emit_dense_in(P, S, hfull0[:, s], dd["nmix0"], dd["win0_t"], 19, None, pin0[:, s * NT:(s + 1) * NT], NT)
    yT0 = P.dram("yT0", [1024, T])
    emit_s5(P, S, pin0[0:256], strip(dd, "s5_"), cs_d, yT0[0:256], T, 16)
    emit_ssd(P, S, pin0[256:1024], pin0[1024:1792], pin0[1792:2048], pin0[2048:2304], pin0[2304:2316], strip(dd, "sd_"), cs_d, yT0[256:1024], T, 12, 2)
    zp = P.dram("zp", [2, 256, T]); zr = P.dram("zr", [256, T])
    emit_glu_partial(P, S, yT0[0:256], dd["gluw_t"], zp, T)
    P.collective("ReduceScatter", zr, zp.re("s r t -> (s r) t"), PAIRS, op=ALU.add)
    mp0 = P.dram("mp0", [4, 2, 512, NT]); mix0 = P.dram("mix0", [2048, NT])
    emit_outproj_partial(P, S, yT0, dd["wout0_t"], mp0, T, glu=(zr, dd["glub"]))
    rs_mix(mp0, mix0)
    hb1 = P.dram("hb1", [2048, NT]); h1d0 = P.dram("h1d0", [2048, NT])
    emit_mlp_gate_v2(P, S, 0, dd["xT"], mix0, dd["pT0"], hb1, NT, Wf[0], h1d0)
    hfull1 = P.dram("hfull1", [8, 2, 256, NT])
    for q in range(8):
        P.collective("AllGather", hfull1[q].re("s f t -> (s f) t"), hb1[q * 256:(q + 1) * 256, :], PAIRS)
    pin1 = P.dram("pin1", [24 * 128, T])
    for s in range(2):
        with P.scope():
            emit_dense_in(P, S, hfull1[:, s], dd["nmix1"], dd["win1_t"], 24, None, pin1[:, s * NT:(s + 1) * NT], NT)
    yT1 = P.dram("yT1", [1024, T])
    emit_rwkv(P, S, pin1[0:512], pin1[512:1024], pin1[1024:1536], pin1[1536:1632], pin1[1664:1760], pin1[1792:2048], strip(dd, "rw_"), cs_d, yT1[0:512], T, 4)
    emit_lru(P, S, pin1[2048:2560], pin1[2560:3072], strip(dd, "lr_"), yT1[512:1024], T, 4)
    mp1 = P.dram("mp1", [4, 2, 512, NT]); mix1 = P.dram("mix1", [2048, NT])
    emit_outproj_partial(P, S, yT1, dd["wout1_t"], mp1, T)
    rs_mix(mp1, mix1)
    oT = la.out("outT", [2048, NT])
    h1d1 = P.dram("h1d1", [2048, NT])
    emit_mlp_gate_v2(P, S, 1, hb1, mix1, dd["pT1"], None, NT, Wf[1], h1d1, final=dd["nfin"], outT=oT)
    return la


def kernel(**inp):
    inp = {k: _np.asarray(v) for k, v in inp.items()}
    ins = host_inputs(inp)
    la = build_fused(ins[0])
    res = la.run(ins)
    out = _np.zeros((NB, SEQ, 2048), _np.float32)
    for c in range(8):
        b, hh = divmod(c, 2)
        out[b, hh * NT_CORE:(hh + 1) * NT_CORE, :] = res[c]["outT"].T
    return out
```

```python
import math, contextlib
import numpy as np
import concourse.bass as bass
import concourse.mybir as mybir
from concourse.bass_utils import run_bass_kernel_spmd

F32 = mybir.dt.float32
BF16 = mybir.dt.bfloat16
AF = mybir.ActivationFunctionType
ALU = mybir.AluOpType
AX = mybir.AxisListType

SAME_ENG_SYNC = True
CC_INC = 1


class Buf:
    __slots__ = ("name", "wconds", "rconds", "wsem", "wcount", "rsem", "rcount")

    ALL = []

    def __init__(self, name):
        Buf.ALL.append(self)
        self.name = name
        self.wconds = {}
        self.rconds = {}
        self.wsem = None
        self.wcount = 0
        self.rsem = None
        self.rcount = 0


class V:
    __slots__ = ("ap", "buf")

    def __init__(self, ap, buf):
        self.ap = ap
        self.buf = buf

    def __getitem__(self, key):
        return V(self.ap[key], self.buf)

    def re(self, s, **kw):
        return V(self.ap.rearrange(s, **kw), self.buf)

    def bc(self, shape):
        return V(self.ap.to_broadcast(shape), self.buf)

    def bitcast(self, dt):
        return V(self.ap.bitcast(dt), self.buf)

    @property
    def shape(self):
        return self.ap.shape


class Prog:
    ENG = ("pe", "dve", "act", "pool", "sp")

    def __init__(self, nc, stack):
        self.nc = nc
        Buf.ALL = []
        self.stack = stack
        self.engobj = {"pe": nc.tensor, "dve": nc.vector, "act": nc.scalar,
                       "pool": nc.gpsimd, "sp": nc.sync}
        self.q = {e: [] for e in self.ENG}
        self.cnt = {e: 0 for e in self.ENG}
        self.sems = {}
        self.nsem = 0
        for e in self.ENG:
            self.sems[("eng", e)] = self._newsem("c_" + e)
        self.known = {e: {} for e in self.ENG}
        self.uid = 0
        self.stacks = [stack]
        self.free_sems = []
        self.scope_sems = [[]]
        self.semval = {}
        self.ring_store = {}

    def _newsem(self, name):
        self.nsem += 1
        return self.stack.enter_context(self.nc.semaphore(name + "_%d" % self.nsem))

    def _dma_sem(self, key, name):
        if key not in self.sems:
            if self.free_sems:
                h, v = self.free_sems.pop()
            else:
                h, v = self._newsem("d"), 0
            self.sems[key] = h
            self.semval[key] = v
            self.scope_sems[-1].append(key)
        return self.sems[key]

    @contextlib.contextmanager
    def scope(self):
        es = contextlib.ExitStack()
        self.stacks.append(es)
        self.scope_sems.append([])
        mark = set(self.ring_store.keys())
        try:
            yield
        finally:
            self.barrier()
            for k in list(self.ring_store.keys()):
                if k not in mark:
                    del self.ring_store[k]
            for key in self.scope_sems.pop():
                self.free_sems.append((self.sems.pop(key), self.semval.pop(key)))
                for e in self.ENG:
                    self.known[e].pop(key, None)
            self.stacks.pop()
            es.close()

    def barrier(self):
        conds = {("eng", e): self.cnt[e] for e in self.ENG if self.cnt[e] > 0}
        for key, v in self.semval.items():
            if v > 0:
                conds[key] = v
        for e in self.ENG:
            waits = {}
            kn = self.known[e]
            for k, val in conds.items():
                if k == ("eng", e):
                    if e == "sp":
                        continue
                if kn.get(k, 0) < val:
                    waits[k] = val
            wl = self._emit_waits(e, waits)

            def thunk(en, wl=wl):
                for s_, v_ in wl:
                    en.wait_ge(s_, v_)
            self.q[e].append(thunk)
        for b in Buf.ALL:
            b.wconds = {}
            b.rconds = {}

    def ring(self, name, n, shape, dt=F32):
        if name not in self.ring_store:
            self.ring_store[name] = [[self.sbuf("%s_%d" % (name, i), shape, dt) for i in range(n)], 0]
        r = self.ring_store[name]
        b = r[0][r[1] % n]
        r[1] += 1
        return b

    def sbuf(self, name, shape, dt=F32):
        self.uid += 1
        t = self.stacks[-1].enter_context(self.nc.sbuf_tensor("%s_u%d" % (name, self.uid), list(shape), dt))
        return V(t.ap() if hasattr(t, "ap") and callable(getattr(t, "ap")) else t[:], Buf(name))

    def psum(self, name, shape, dt=F32):
        t = self.stack.enter_context(self.nc.psum_tensor(name, list(shape), dt))
        return V(t.ap() if hasattr(t, "ap") and callable(getattr(t, "ap")) else t[:], Buf(name))

    def dram(self, name, shape, dt=F32, kind="Internal"):
        t = self.nc.dram_tensor(name, list(shape), dt, kind=kind)
        return V(t.ap(), Buf(name))

    def alias(self, v, name):
        return V(v.ap, Buf(name))

    def _need(self, eng, conds, waits):
        kn = self.known[eng]
        for k, val in conds.items():
            if k not in self.sems:
                continue
            if k == ("eng", eng):
                if eng == "pe" or not SAME_ENG_SYNC:
                    continue
            if kn.get(k, 0) >= val:
                continue
            if waits.get(k, 0) < val:
                waits[k] = val

    def _emit_waits(self, eng, waits):
        kn = self.known[eng]
        out = []
        for k, val in waits.items():
            kn[k] = max(kn.get(k, 0), val)
            out.append((self.sems[k], val))
        return out

    def op(self, eng, fn, reads=(), writes=()):
        waits = {}
        for v in reads:
            self._need(eng, v.buf.wconds, waits)
        for v in writes:
            self._need(eng, v.buf.wconds, waits)
            self._need(eng, v.buf.rconds, waits)
        wl = self._emit_waits(eng, waits)
        self.cnt[eng] += 1
        n = self.cnt[eng]
        k = ("eng", eng)
        sem = self.sems[k]

        def thunk(e, wl=wl, fn=fn, sem=sem):
            for s, val in wl:
                e.wait_ge(s, val)
            fn(e).then_inc(sem, 1)
        self.q[eng].append(thunk)
        for v in reads:
            v.buf.rconds[k] = n
        for v in writes:
            v.buf.wconds = {k: n}
            v.buf.rconds = {}
        return n

    def dma(self, queue, out, in_, **kw):
        eng = queue
        waits = {}
        self._need(eng, in_.buf.wconds, waits)
        own_w = ("w", id(out.buf))
        for kk, val in out.buf.wconds.items():
            if kk == own_w:
                continue
            self._need(eng, {kk: val}, waits)
        self._need(eng, out.buf.rconds, waits)
        wl = self._emit_waits(eng, waits)
        b = out.buf
        key = ("w", id(b))
        sem = self._dma_sem(key, b.name)
        self.semval[key] += 16
        val = self.semval[key]
        b.wcount = val
        self._keep = getattr(self, "_keep", [])
        self._keep.append(b)
        rb = in_.buf
        rkey = None

        def thunk(e, wl=wl, sem=sem, o=out.ap, i=in_.ap, kw=kw):
            for s, v_ in wl:
                e.wait_ge(s, v_)
            e.dma_start(out=o, in_=i, **kw).then_inc(sem, 16)
        self.q[eng].append(thunk)
        if own_w in out.buf.wconds or not out.buf.wconds or True:
            newc = {key: val}
            out.buf.wconds = newc
            out.buf.rconds = {}
        in_.buf.rconds[key] = max(in_.buf.rconds.get(key, 0), val)

    def collective(self, kind, out, in_, groups, op=None):
        eng = "pool"
        waits = {}
        self._need(eng, in_.buf.wconds, waits)
        self._need(eng, out.buf.wconds, waits)
        self._need(eng, out.buf.rconds, waits)
        wl = self._emit_waits(eng, waits)
        key = ("cc", id(out.buf))
        if key not in self.sems:
            self.sems[key] = self._newsem("cc")
            self.semval[key] = 0
        self.semval[key] += CC_INC
        val = self.semval[key]
        sem = self.sems[key]
        self._keep = getattr(self, "_keep", [])
        self._keep.append(out.buf)
        op = ALU.bypass if op is None else op

        def thunk(e, wl=wl, sem=sem, o=out.ap, i=in_.ap):
            for s_, v_ in wl:
                e.wait_ge(s_, v_)
            e.collective_compute(kind, op, replica_groups=groups, ins=[i], outs=[o]).then_inc(sem, CC_INC)
        self.q[eng].append(thunk)
        out.buf.wconds = {key: val}
        out.buf.rconds = {}
        in_.buf.rconds[key] = val

    def wait_all(self, eng, views):
        waits = {}
        for v in views:
            self._need(eng, v.buf.wconds, waits)
        wl = self._emit_waits(eng, waits)

        def thunk(e, wl=wl):
            for s, v_ in wl:
                e.wait_ge(s, v_)
        self.q[eng].append(thunk)

    def finish(self):
        nc = self.nc
        with nc.Block() as block:
            @block.tensor
            def _(e):
                for t in self.q["pe"]:
                    t(e)

            @block.vector
            def _(e):
                for t in self.q["dve"]:
                    t(e)

            @block.scalar
            def _(e):
                for t in self.q["act"]:
                    t(e)

            @block.gpsimd
            def _(e):
                for t in self.q["pool"]:
                    t(e)

            @block.sync
            def _(e):
                for t in self.q["sp"]:
                    t(e)

D = 2048
KT_D = 16
EPS = 1e-6
CH = 512
MMDT = BF16


class Shared:
    def __init__(self, P):
        self.P = P
        self.ps = [P.psum("psr%d" % i, [128, 512]) for i in range(8)]
        self.pi = 0
        self.ones = P.sbuf("ones", [128, 128])
        P.op("dve", lambda e: e.memset(self.ones.ap, 1.0), writes=[self.ones])
        self.epsc = P.sbuf("epsc", [128, 1])
        P.op("dve", lambda e: e.memset(self.epsc.ap, EPS), writes=[self.epsc])
        self.rr = {}
        self.castn = 0

    def next_psum(self):
        p = self.ps[self.pi % 8]
        self.pi += 1
        return p

    def ring(self, name, n, shape, dt=F32):
        return self.P.ring(name, n, shape, dt)


def load_cols(P, dst, vec_dram, KT):
    with P.nc.allow_non_contiguous_dma(reason="tiny param vector"):
        pass
    P.dma("sp", dst, vec_dram.re("(k p) -> p k", p=128), allow_slow_non_contiguous=True)


def rmsnorm_T(P, S, src, gcol, out, KT, NT, out_scale_extra=None):
    pss = S.next_psum()
    for k in range(KT):
        sq = S.ring("sq", 3, [128, CH])
        P.op("act", lambda e, sq=sq, k=k: e.activation(out=sq.ap[:, 0:NT], in_=src.ap[:, k, :], func=AF.Square),
             reads=[src], writes=[sq])
        P.op("pe", lambda e, sq=sq, k=k: e.matmul(pss.ap[:, 0:NT], lhsT=S.ones.ap, rhs=sq.ap[:, 0:NT],
                                                 start=(k == 0), stop=(k == KT - 1)),
             reads=[S.ones, sq], writes=[pss])
    rs = S.ring("rstd", 2, [128, CH])
    P.op("act", lambda e: e.activation(out=rs.ap[:, 0:NT], in_=pss.ap[:, 0:NT], func=AF.Sqrt,
                                      bias=S.epsc.ap, scale=1.0 / (KT * 128)),
         reads=[pss, S.epsc], writes=[rs])
    P.op("dve", lambda e: e.reciprocal(out=rs.ap[:, 0:NT], in_=rs.ap[:, 0:NT]), reads=[rs], writes=[rs])
    for k in range(KT):
        P.op("dve", lambda e, k=k: e.scalar_tensor_tensor(out=out.ap[:, k, :], in0=src.ap[:, k, :],
                                                         scalar=gcol.ap[:, k:k + 1], in1=rs.ap[:, 0:NT],
                                                         op0=ALU.mult, op1=ALU.mult),
             reads=[src, gcol, rs], writes=[out])


def stream_mm(P, S, Wt, KT, NCB, rhs_fn, nchunk, NTc, evac, wname="wb", nwb=3, Ms=None, pre=None):
    wbs = {}

    def prep(c):
        wb = S.ring(wname + str(KT), nwb, [128, KT, 128], MMDT)
        for k0 in range(0, KT, 16):
            k1 = min(KT, k0 + 16)
            stg = S.ring("wstg", 3, [128, 16, 128])
            P.dma("sp", stg[:, 0:k1 - k0, :], Wt[c][:, k0:k1, :])
            P.op("dve", lambda e, stg=stg, wb=wb, k0=k0, k1=k1: e.tensor_copy(out=wb.ap[:, k0:k1, :], in_=stg.ap[:, 0:k1 - k0, :]),
                 reads=[stg], writes=[wb])
        wbs[c] = wb

    prep(0)
    for c in range(NCB):
        if c + 1 < NCB:
            prep(c + 1)
        wb = wbs.pop(c)
        M = 128 if Ms is None else Ms[c]
        if pre is not None:
            pre(c)
        for j in range(nchunk):
            ps = S.next_psum()
            for k in range(KT):
                r = rhs_fn(k, j)
                P.op("pe", lambda e, wb=wb, ps=ps, r=r, k=k, M=M: e.matmul(
                    ps.ap[0:M, 0:NTc], lhsT=wb.ap[:, k, 0:M], rhs=r.ap, start=(k == 0), stop=(k == KT - 1)),
                    reads=[wb, r], writes=[ps])
            evac(c, j, ps, M)


def emit_dense_in(P, S, hT, gvec, Wt, NCB, Ms, projT, NT):
    nch = NT // CH
    gcol = P.sbuf("gcolA", [128, KT_D])
    load_cols(P, gcol, gvec, KT_D)
    hn = P.sbuf("hnA", [128, KT_D, NT], MMDT)
    hnj = [P.alias(hn, "hnA_%d" % j) for j in range(nch)]
    for j in range(nch):
        ht = S.ring("htA", 2, [128, KT_D, CH])
        for q in range(8):
            P.dma("sp", ht[:, 2 * q:2 * q + 2, :], hT[q].re("(k p) n -> p k n", p=128)[:, :, j * CH:(j + 1) * CH])
        rmsnorm_T(P, S, ht, gcol, hnj[j][:, :, j * CH:(j + 1) * CH], KT_D, CH)

    def rhs_fn(k, j):
        return hnj[j][:, k, j * CH:(j + 1) * CH]

    def evac(c, j, ps, M):
        ob = S.ring("evA", 4, [128, CH])
        P.op("act", lambda e: e.copy(out=ob.ap[0:M, :], in_=ps.ap[0:M, :]), reads=[ps], writes=[ob])
        P.dma("act", projT[c * 128:c * 128 + M, j * CH:(j + 1) * CH], ob[0:M, :])

    stream_mm(P, S, Wt, KT_D, NCB, rhs_fn, nch, CH, evac, Ms=Ms)


def emit_dense_out(P, S, L, hT, yT, pT, hT_out, NT, W, glu=None, final=None, outT=None, h1T=None):
    nch = NT // CH
    gF = P.sbuf("gF%d" % L, [128, KT_D]); load_cols(P, gF, W["nffn"], KT_D)
    gP = P.sbuf("gP%d" % L, [128, KT_D]); load_cols(P, gP, W["npl"], KT_D)
    if final is not None:
        gN = P.sbuf("gN%d" % L, [128, KT_D]); load_cols(P, gN, final, KT_D)
    yTv = yT.re("(k p) n -> p k n", p=128)
    hTv = hT.re("(k p) n -> p k n", p=128)
    h1Tv = h1T.re("(k p) n -> p k n", p=128)
    hoTv = hT_out.re("(k p) n -> p k n", p=128) if hT_out is not None else None
    sc1 = P.scope(); sc1.__enter__()
    yb = P.sbuf("ybC", [128, KT_D, NT], MMDT)
    k_start = 0
    if glu is not None:
        gluw_t, glub = glu
        gb = P.sbuf("glub_sb", [128, 4]); load_cols(P, gb, glub, 4)
        actf = P.sbuf("actf", [128, 4, NT])
        P.dma("sp", actf, yTv[:, 0:4, :])
        actb = P.sbuf("actb", [128, 4, NT], MMDT)
        P.op("dve", lambda e: e.tensor_copy(out=actb.ap, in_=actf.ap), reads=[actf], writes=[actb])

        def evac_glu(c, j, ps, M):
            sg = S.ring("sgl", 2, [128, CH])
            P.op("act", lambda e: e.activation(out=sg.ap, in_=ps.ap, func=AF.Sigmoid, bias=gb.ap[:, c:c + 1]),
                 reads=[ps, gb], writes=[sg])
            P.op("dve", lambda e: e.tensor_tensor(out=yb.ap[:, c, j * CH:(j + 1) * CH],
                                                  in0=actf.ap[:, c, j * CH:(j + 1) * CH], in1=sg.ap, op=ALU.mult),
                 reads=[actf, sg], writes=[yb])
        stream_mm(P, S, gluw_t, 4, 4, lambda k, j: actb[:, k, j * CH:(j + 1) * CH], nch, CH, evac_glu)
        k_start = 4
    for k in range(k_start, KT_D, 4):
        P.dma("pool", yb[:, k:k + 4, :], yTv[:, k:k + 4, :])

    def pre_o(c):
        pass

    def evac_o(c, j, ps, M):
        hb = S.ring("hbC", 3, [128, CH])
        P.dma("sp", hb, hT[c * 128:(c + 1) * 128, j * CH:(j + 1) * CH])
        P.op("dve", lambda e: e.tensor_tensor(out=hb.ap, in0=ps.ap, in1=hb.ap, op=ALU.add),
             reads=[ps, hb], writes=[hb])
        P.dma("sp", h1T[c * 128:(c + 1) * 128, j * CH:(j + 1) * CH], hb)
    stream_mm(P, S, W["out_t"], KT_D, 16, lambda k, j: yb[:, k, j * CH:(j + 1) * CH], nch, CH, evac_o)
    sc1.__exit__(None, None, None)
    sc2 = P.scope(); sc2.__enter__()
    plp = P.sbuf("plpC", [128, 16, 2, 128], MMDT)
    for c in range(16):
        P.dma("pool", plp[:, c, :, :], W["plp_t"][c])
    pTv = pT.re("(k p) n -> p k n", p=128)
    for j in range(nch):
        sl = slice(j * CH, (j + 1) * CH)
        h1 = S.ring("h1C", 1, [128, KT_D, CH])
        P.dma("sp", h1, h1Tv[:, :, sl])
        hn = S.ring("hnC", 1, [128, KT_D, CH], MMDT)
        rmsnorm_T(P, S, h1, gF, hn, KT_D, CH)
        hid = S.ring("hidC", 1, [128, 64, CH], MMDT)

        def evac1(c, jj, ps, M):
            rl = S.ring("rlC", 3, [128, CH])
            P.op("act", lambda e: e.activation(out=rl.ap, in_=ps.ap, func=AF.Relu), reads=[ps], writes=[rl])
            P.op("pool", lambda e: e.tensor_tensor(out=hid.ap[:, c, :], in0=rl.ap, in1=rl.ap, op=ALU.mult),
                 reads=[rl], writes=[hid])
        stream_mm(P, S, W["w1_t"], KT_D, 64, lambda k, jj: hn[:, k, :], 1, CH, evac1)

        def evac2(c, jj, ps, M):
            P.op("dve", lambda e: e.tensor_tensor(out=h1.ap[:, c, :], in0=ps.ap, in1=h1.ap[:, c, :], op=ALU.add),
                 reads=[ps, h1], writes=[h1])
        stream_mm(P, S, W["w2_t"], 64, 16, lambda k, jj: hid[:, k, :], 1, CH, evac2)
        rmsnorm_T(P, S, h1, gP, hn, KT_D, CH)
        pb = S.ring("pbC", 1, [128, 2, CH], MMDT)
        P.dma("pool", pb, pTv[:, :, sl])

        def evac3(c, jj, ps, M):
            psp = S.next_psum()
            for k in range(2):
                P.op("pe", lambda e, k=k: e.matmul(psp.ap, lhsT=plp.ap[:, c, k, :], rhs=pb.ap[:, k, :],
                                                   start=(k == 0), stop=(k == 1)),
                     reads=[plp, pb], writes=[psp])
            sg = S.ring("sgC", 2, [128, CH])
            P.op("act", lambda e: e.activation(out=sg.ap, in_=ps.ap, func=AF.Sigmoid), reads=[ps], writes=[sg])
            P.op("dve", lambda e: e.tensor_tensor(out=sg.ap, in0=psp.ap, in1=sg.ap, op=ALU.mult),
                 reads=[psp, sg], writes=[sg])
            P.op("pool", lambda e: e.tensor_tensor(out=h1.ap[:, c, :], in0=h1.ap[:, c, :], in1=sg.ap, op=ALU.add),
                 reads=[h1, sg], writes=[h1])
        stream_mm(P, S, W["gate_t"], KT_D, 16, lambda k, jj: hn[:, k, :], 1, CH, evac3)
        if hT_out is not None:
            P.dma("sp", hoTv[:, :, sl], h1)
        if final is not None:
            rmsnorm_T(P, S, h1, gN, h1, KT_D, CH)
            P.dma("sp", outT.re("(k p) n -> p k n", p=128)[:, :, sl], h1)
    sc2.__exit__(None, None, None)


def emit_glu_partial(P, S, actT, gluw_t, zp, T):
    nch = T // CH
    with P.scope():
        ab = P.sbuf("glu_ab", [128, 2, T], MMDT)
        av = actT.re("(k p) n -> p k n", p=128)
        for k in range(2):
            for j0 in range(0, T, 2048):
                P.dma("pool", ab[:, k, j0:j0 + 2048], av[:, k, j0:j0 + 2048])

        def evac(c, j, ps, M):
            ob = S.ring("glu_ev", 4, [128, CH])
            P.op("act", lambda e: e.copy(out=ob.ap, in_=ps.ap), reads=[ps], writes=[ob])
            P.dma("act", zp[c // 2, (c % 2) * 128:(c % 2 + 1) * 128, j * CH:(j + 1) * CH], ob)
        stream_mm(P, S, gluw_t, 2, 4, lambda k, j: ab[:, k, j * CH:(j + 1) * CH], nch, CH, evac)


def emit_outproj_partial(P, S, yT, wout_t, mp, T, glu=None):
    nch = T // CH
    half = T // 2
    with P.scope():
        yb = P.sbuf("op_yb", [128, 8, T], MMDT)
        yv = yT.re("(k p) n -> p k n", p=128)
        k0 = 0
        if glu is not None:
            zT, glub = glu
            gb = P.sbuf("op_gb", [128, 2]); load_cols(P, gb, glub, 2)
            zv = zT.re("(k p) n -> p k n", p=128)
            for k in range(2):
                for j in range(nch):
                    sl = slice(j * CH, (j + 1) * CH)
                    a = S.ring("op_a", 3, [128, CH]); z = S.ring("op_z", 3, [128, CH])
                    P.dma("sp", a, yv[:, k, sl]); P.dma("sp", z, zv[:, k, sl])
                    P.op("act", lambda e, z=z, k=k: e.activation(out=z.ap, in_=z.ap, func=AF.Sigmoid, bias=gb.ap[:, k:k + 1]), reads=[z, gb], writes=[z])
                    P.op("dve", lambda e, a=a, z=z, k=k, sl=sl: e.tensor_tensor(out=yb.ap[:, k, sl], in0=a.ap, in1=z.ap, op=ALU.mult), reads=[a, z], writes=[yb])
            k0 = 2
        for k in range(k0, 8):
            for j0 in range(0, T, 2048):
                P.dma("pool", yb[:, k, j0:j0 + 2048], yv[:, k, j0:j0 + 2048])

        def evac(c, j, ps, M):
            ob = S.ring("op_ev", 4, [128, CH])
            P.op("act", lambda e: e.copy(out=ob.ap, in_=ps.ap), reads=[ps], writes=[ob])
            t0 = j * CH
            P.dma("act", mp[c // 4, t0 // half, (c % 4) * 128:(c % 4 + 1) * 128, t0 % half:t0 % half + CH], ob)
        stream_mm(P, S, wout_t, 8, 16, lambda k, j: yb[:, k, j * CH:(j + 1) * CH], nch, CH, evac)


def emit_mlp_gate(P, S, L, hT, mixT, pT, hT_out, NT, W, final=None, outT=None):
    nch = NT // CH
    with P.scope():
        gF = P.sbuf("gF%d" % L, [128, KT_D]); load_cols(P, gF, W["nffn"], KT_D)
        gP = P.sbuf("gP%d" % L, [128, KT_D]); load_cols(P, gP, W["npl"], KT_D)
        if final is not None:
            gN = P.sbuf("gN%d" % L, [128, KT_D]); load_cols(P, gN, final, KT_D)
        hTv = hT.re("(k p) n -> p k n", p=128)
        mTv = mixT.re("(k p) n -> p k n", p=128)
        hoTv = hT_out.re("(k p) n -> p k n", p=128) if hT_out is not None else None
        plp = P.sbuf("plpC", [128, 16, 2, 128], MMDT)
        for c in range(16):
            P.dma("pool", plp[:, c, :, :], W["plp_t"][c])
        pTv = pT.re("(k p) n -> p k n", p=128)

        def tile_body(j):
            sl = slice(j * CH, (j + 1) * CH)
            h1 = S.ring("h1C", 1, [128, KT_D, CH])
            hn = S.ring("hnC", 1, [128, KT_D, CH], MMDT)
            hid = S.ring("hidC", 1, [128, 64, CH], MMDT)
            P.dma("sp", h1, hTv[:, :, sl])
            for q in range(8):
                mt = S.ring("mixC", 2, [128, 2, CH])
                P.dma("sp", mt, mTv[:, q * 2:(q + 1) * 2, sl])
                P.op("dve", lambda e, mt=mt, q=q: e.tensor_tensor(out=h1.ap[:, q * 2:(q + 1) * 2, :], in0=h1.ap[:, q * 2:(q + 1) * 2, :], in1=mt.ap, op=ALU.add),
                     reads=[h1, mt], writes=[h1])
            rmsnorm_T(P, S, h1, gF, hn, KT_D, CH)

            def evac1(c, jj, ps, M):
                rl = S.ring("rlC", 3, [128, CH])
                P.op("act", lambda e: e.activation(out=rl.ap, in_=ps.ap, func=AF.Relu), reads=[ps], writes=[rl])
                P.op("pool", lambda e: e.tensor_tensor(out=hid.ap[:, c, :], in0=rl.ap, in1=rl.ap, op=ALU.mult), reads=[rl], writes=[hid])
            stream_mm(P, S, W["w1_t"], KT_D, 64, lambda k, jj: hn[:, k, :], 1, CH, evac1)

            def evac2(c, jj, ps, M):
                P.op("dve", lambda e: e.tensor_tensor(out=h1.ap[:, c, :], in0=ps.ap, in1=h1.ap[:, c, :], op=ALU.add), reads=[ps, h1], writes=[h1])
            stream_mm(P, S, W["w2_t"], 64, 16, lambda k, jj: hid[:, k, :], 1, CH, evac2)
            rmsnorm_T(P, S, h1, gP, hn, KT_D, CH)
            pb = S.ring("pbC", 1, [128, 2, CH], MMDT)
            P.dma("pool", pb, pTv[:, :, sl])

            def evac3(c, jj, ps, M):
                psp = S.next_psum()
                for k in range(2):
                    P.op("pe", lambda e, k=k: e.matmul(psp.ap, lhsT=plp.ap[:, c, k, :], rhs=pb.ap[:, k, :], start=(k == 0), stop=(k == 1)),
                         reads=[plp, pb], writes=[psp])
                sg = S.ring("sgC", 2, [128, CH])
                P.op("act", lambda e: e.activation(out=sg.ap, in_=ps.ap, func=AF.Sigmoid), reads=[ps], writes=[sg])
                P.op("dve", lambda e: e.tensor_tensor(out=sg.ap, in0=psp.ap, in1=sg.ap, op=ALU.mult), reads=[psp, sg], writes=[sg])
                P.op("pool", lambda e: e.tensor_tensor(out=h1.ap[:, c, :], in0=h1.ap[:, c, :], in1=sg.ap, op=ALU.add), reads=[h1, sg], writes=[h1])
            stream_mm(P, S, W["gate_t"], KT_D, 16, lambda k, jj: hn[:, k, :], 1, CH, evac3)
            if hT_out is not None:
                P.dma("sp", hoTv[:, :, sl], h1)
            if final is not None:
                rmsnorm_T(P, S, h1, gN, h1, KT_D, CH)
                P.dma("sp", outT.re("(k p) n -> p k n", p=128)[:, :, sl], h1)

        for j in range(nch):
            tile_body(j)


def emit_mlp_gate_v2(P, S, L, hT, mixT, pT, hT_out, NT, W, h1d, final=None, outT=None):
    nch = NT // CH
    hTv = hT.re("(k p) n -> p k n", p=128)
    mTv = mixT.re("(k p) n -> p k n", p=128)
    h1v = h1d.re("(k p) n -> p k n", p=128)
    hreg = [[P.alias(h1d, "h1d_%d_%d" % (c, j)) for j in range(nch)] for c in range(16)]
    houts = None
    if hT_out is not None:
        houts = [[P.alias(hT_out, "hout_%d" % c)] * nch for c in range(16)]

    def norm_pass(src_v, gcol, dst_fn, add_v=None, store_v=None, tag=""):
        with P.scope():
            for j in range(nch):
                sl = slice(j * CH, (j + 1) * CH)
                ht = S.ring("npH", 2, [128, KT_D, CH])
                P.dma("sp", ht, src_v[:, :, sl])
                if add_v is not None:
                    for q in range(8):
                        mt = S.ring("npM", 2, [128, 2, CH])
                        P.dma("sp", mt, add_v[:, q * 2:(q + 1) * 2, sl])
                        P.op("dve", lambda e, mt=mt, q=q, ht=ht: e.tensor_tensor(out=ht.ap[:, q * 2:(q + 1) * 2, :], in0=ht.ap[:, q * 2:(q + 1) * 2, :],
                             in1=mt.ap, op=ALU.add), reads=[ht, mt], writes=[ht])
                if store_v is not None:
                    P.dma("sp", store_v[:, :, sl], ht)
                rmsnorm_T(P, S, ht, gcol, dst_fn(j), KT_D, CH)

    with P.scope():
        gF = P.sbuf("gF%d" % L, [128, KT_D]); load_cols(P, gF, W["nffn"], KT_D)
        gP = P.sbuf("gP%d" % L, [128, KT_D]); load_cols(P, gP, W["npl"], KT_D)
        hn = P.sbuf("hnM", [128, KT_D, NT], MMDT)
        norm_pass(hTv, gF, lambda j: hn[:, :, j * CH:(j + 1) * CH], add_v=mTv, store_v=h1v)
        with P.scope():
            hid = P.sbuf("hidM", [128, 16, NT], MMDT)
            for q in range(4):
                def evac1(c, j, ps, M):
                    rl = S.ring("rlM", 3, [128, CH])
                    P.op("act", lambda e: e.activation(out=rl.ap, in_=ps.ap, func=AF.Relu), reads=[ps], writes=[rl])
                    P.op("pool", lambda e: e.tensor_tensor(out=hid.ap[:, c, j * CH:(j + 1) * CH], in0=rl.ap, in1=rl.ap, op=ALU.mult), reads=[rl], writes=[hid])
                stream_mm(P, S, W["w1_t"][q * 16:(q + 1) * 16], KT_D, 16, lambda k, j: hn[:, k, j * CH:(j + 1) * CH], nch, CH, evac1)

                pend = {}

                def pre2(c):
                    for j in range(nch):
                        hb = S.ring("hbM", 8, [128, CH])
                        reg = V(h1d.ap[c * 128:(c + 1) * 128, j * CH:(j + 1) * CH], hreg[c][j].buf)
                        P.dma("pool", hb, reg)
                        pend[(c, j)] = (hb, reg)

                def evac2(c, j, ps, M):
                    hb, reg = pend.pop((c, j))
                    P.op("dve", lambda e: e.tensor_tensor(out=hb.ap, in0=ps.ap, in1=hb.ap, op=ALU.add), reads=[ps, hb], writes=[hb])
                    P.dma("act", reg, hb)
                stream_mm(P, S, W["w2_t"][:, :, q * 16:(q + 1) * 16, :], KT_D, 16, lambda k, j: hid[:, k, j * CH:(j + 1) * CH], nch, CH, evac2, pre=pre2)
        norm_pass(h1v, gP, lambda j: hn[:, :, j * CH:(j + 1) * CH])
        with P.scope():
            plp = P.sbuf("plpM", [128, 16, 2, 128], MMDT)
            for c in range(16):
                P.dma("pool", plp[:, c, :, :], W["plp_t"][c])
            pb = P.sbuf("pbM", [128, 2, NT], MMDT)
            pTv = pT.re("(k p) n -> p k n", p=128)
            for k in range(2):
                P.dma("pool", pb[:, k, :], pTv[:, k, :])
            dst = hT_out if hT_out is not None else h1d
            dregs = houts if hT_out is not None else hreg

            pend3 = {}

            def pre3(c):
                for j in range(nch):
                    sl = slice(j * CH, (j + 1) * CH)
                    hb = S.ring("hbG", 8, [128, CH])
                    P.dma("pool", hb, V(h1d.ap[c * 128:(c + 1) * 128, sl], hreg[c][j].buf))
                    pend3[(c, j)] = hb

            def evac3(c, j, ps, M):
                sl = slice(j * CH, (j + 1) * CH)
                psp = S.next_psum()
                for k in range(2):
                    P.op("pe", lambda e, k=k: e.matmul(psp.ap, lhsT=plp.ap[:, c, k, :], rhs=pb.ap[:, k, sl], start=(k == 0), stop=(k == 1)),
                         reads=[plp, pb], writes=[psp])
                sg = S.ring("sgM", 2, [128, CH])
                P.op("act", lambda e: e.activation(out=sg.ap, in_=ps.ap, func=AF.Sigmoid), reads=[ps], writes=[sg])
                P.op("dve", lambda e: e.tensor_tensor(out=sg.ap, in0=psp.ap, in1=sg.ap, op=ALU.mult), reads=[psp, sg], writes=[sg])
                hb = pend3.pop((c, j))
                P.op("pool", lambda e: e.tensor_tensor(out=hb.ap, in0=hb.ap, in1=sg.ap, op=ALU.add), reads=[hb, sg], writes=[hb])
                P.dma("act", V(dst.ap[c * 128:(c + 1) * 128, sl], dregs[c][j].buf), hb)
            stream_mm(P, S, W["gate_t"], KT_D, 16, lambda k, j: hn[:, k, j * CH:(j + 1) * CH], nch, CH, evac3, pre=pre3)
    if final is not None:
        with P.scope():
            gN = P.sbuf("gN%d" % L, [128, KT_D]); load_cols(P, gN, final, KT_D)
            oTv = outT.re("(k p) n -> p k n", p=128)
            for j in range(nch):
                sl = slice(j * CH, (j + 1) * CH)
                ht = S.ring("npF", 2, [128, KT_D, CH])
                P.dma("sp", ht, h1v[:, :, sl])
                rmsnorm_T(P, S, ht, gN, ht, KT_D, CH)
                P.dma("sp", oTv[:, :, sl], ht)
CH = 512
PI = math.pi
FAST32 = True


def r32(ap):
    return ap.bitcast(mybir.dt.float32r) if FAST32 else ap


def tt(P, eng, out, a, b, op):
    P.op(eng, lambda e: e.tensor_tensor(out=out.ap, in0=a.ap, in1=b.ap, op=op), reads=[a, b], writes=[out])


def ts(P, eng, out, a, s1, op0, s2=None, op1=None):
    rd = [a] + [x for x in (s1, s2) if isinstance(x, V)]
    g = lambda x: x.ap if isinstance(x, V) else x
    if op1 is None:
        P.op(eng, lambda e: e.tensor_scalar(out=out.ap, in0=a.ap, scalar1=g(s1), scalar2=None, op0=op0), reads=rd, writes=[out])
    else:
        P.op(eng, lambda e: e.tensor_scalar(out=out.ap, in0=a.ap, scalar1=g(s1), scalar2=g(s2), op0=op0, op1=op1), reads=rd, writes=[out])


def act(P, out, a, func, scale=None, bias=None):
    rd = [a] + [x for x in (scale, bias) if isinstance(x, V)]
    kw = {}
    if scale is not None:
        kw["scale"] = scale.ap if isinstance(scale, V) else scale
    if bias is not None:
        kw["bias"] = bias.ap if isinstance(bias, V) else bias
    P.op("act", lambda e: e.activation(out=out.ap, in_=a.ap, func=func, **kw), reads=rd, writes=[out])


def sin_reduced(P, tmp, out, ang, shift, zero_col):
    f1, f2, i1 = tmp
    ts(P, "dve", f1, ang, shift, ALU.add)
    ts(P, "dve", f2, f1, 1.0 / (2 * PI), ALU.mult)
    P.op("dve", lambda e: e.tensor_copy(out=i1.ap, in_=f2.ap), reads=[f2], writes=[i1])
    P.op("dve", lambda e: e.tensor_copy(out=f2.ap, in_=i1.ap), reads=[i1], writes=[f2])
    P.op("dve", lambda e: e.scalar_tensor_tensor(out=f1.ap, in0=f2.ap, scalar=-2 * PI, in1=f1.ap, op0=ALU.mult, op1=ALU.add),
         reads=[f1, f2], writes=[f1])
    ts(P, "dve", f2, f1, PI, ALU.is_gt, -2 * PI, ALU.mult)
    tt(P, "dve", f1, f1, f2, ALU.add)
    ts(P, "dve", f2, f1, -PI, ALU.is_lt, 2 * PI, ALU.mult)
    tt(P, "dve", f1, f1, f2, ALU.add)
    act(P, out, f1, AF.Sin)


def s5_abar(P, lr, li, lstep, shape, tag):
    mk = lambda n, dt=F32: P.sbuf("s5%s_%s" % (tag, n), shape, dt)
    step = mk("step"); mag = mk("mag"); ang = mk("ang"); ar = mk("ar"); ai = mk("ai")
    f1 = mk("f1"); f2 = mk("f2"); i1 = mk("i1", mybir.dt.int32)
    act(P, step, lstep, AF.Exp)
    tt(P, "dve", mag, lr, step, ALU.mult)
    act(P, mag, mag, AF.Exp)
    tt(P, "dve", ang, li, step, ALU.mult)
    sin_reduced(P, (f1, f2, i1), ai, ang, 0.0, None)
    sin_reduced(P, (f1, f2, i1), ar, ang, PI / 2, None)
    tt(P, "dve", ar, ar, mag, ALU.mult)
    tt(P, "dve", ai, ai, mag, ALU.mult)
    return ar, ai, (f1, f2)


def emit_s5(P, S, uT, prm, cst, yT, T, NG):
    nch = T // CH
    nlev = int(round(math.log2(T)))
    assert 2 ** nlev == T
    with P.scope():
        ld = lambda name, shape, src: (lambda t: (P.dma("sp", t, src), t)[1])(P.sbuf(name, shape))
        shp = [16, NG, 64]
        lr = ld("s5r_lr", shp, prm["lr_row"]); li = ld("s5r_li", shp, prm["li_row"]); ls = ld("s5r_ls", shp, prm["ls_row"])
        bre = ld("s5r_bre", shp, prm["bT_re"]); bim = ld("s5r_bim", shp, prm["bT_im"])
        ar, ai, (f1, f2) = s5_abar(P, lr, li, ls, shp, "r")
        den = P.sbuf("s5r_den", shp); cre = P.sbuf("s5r_cre", shp); cim = P.sbuf("s5r_cim", shp)
        tt(P, "dve", den, lr, lr, ALU.mult); tt(P, "dve", f1, li, li, ALU.mult); tt(P, "dve", den, den, f1, ALU.add)
        P.op("dve", lambda e: e.reciprocal(out=den.ap, in_=den.ap), reads=[den], writes=[den])
        ts(P, "dve", ar, ar, -1.0, ALU.add)
        tt(P, "dve", cre, ar, lr, ALU.mult); tt(P, "dve", f1, ai, li, ALU.mult); tt(P, "dve", cre, cre, f1, ALU.add)
        tt(P, "dve", cre, cre, den, ALU.mult)
        tt(P, "dve", cim, ai, lr, ALU.mult); tt(P, "dve", f1, ar, li, ALU.mult); tt(P, "dve", cim, cim, f1, ALU.subtract)
        tt(P, "dve", cim, cim, den, ALU.mult)
        BT = P.sbuf("s5_BT", [16, NG, 128])
        tt(P, "dve", f1, cre, bre, ALU.mult); tt(P, "dve", f2, cim, bim, ALU.mult)
        tt(P, "dve", BT[:, :, 0:64], f1, f2, ALU.subtract)
        tt(P, "dve", f1, cre, bim, ALU.mult); tt(P, "dve", f2, cim, bre, ALU.mult)
        tt(P, "dve", BT[:, :, 64:128], f1, f2, ALU.add)
        shc = [128, NG]
        lrc = ld("s5c_lr", shc, prm["lr_col"]); lic = ld("s5c_li", shc, prm["li_col"]); lsc = ld("s5c_ls", shc, prm["ls_col"])
        arc, aic, (g1, g2) = s5_abar(P, lrc, lic, lsc, shc, "c")
        sgn = ld("s5_sgn", [128, 1], cst["sgn"])
        ident = ld("s5_id", [128, 128], cst["ident"]); psw = ld("s5_psw", [128, 128], cst["psw"])
        pw = P.sbuf("s5_pw", [128, nlev, 2, NG])
        P.op("dve", lambda e: e.tensor_copy(out=pw.ap[:, 0, 0, :], in_=arc.ap), reads=[arc], writes=[pw])
        ts(P, "dve", pw[:, 0, 1, :], aic, sgn[:, 0:1], ALU.mult)
        for k in range(1, nlev):
            a0 = pw[:, k - 1, 0, :]; s0 = pw[:, k - 1, 1, :]
            tt(P, "dve", g1, a0, a0, ALU.mult); tt(P, "dve", g2, s0, s0, ALU.mult)
            tt(P, "dve", pw[:, k, 0, :], g1, g2, ALU.subtract)
            tt(P, "dve", g1, a0, s0, ALU.mult)
            ts(P, "dve", pw[:, k, 1, :], g1, 2.0, ALU.mult)
        CT = ld("s5_CT", [128, NG, 16], prm["cT"])
        ts(P, "dve", CT[64:128], CT[64:128], -1.0, ALU.mult)
        dcol = ld("s5_dcol", [16, NG], prm["d_col"])
        Xa = P.sbuf("s5_Xa", [128, T]); Xb = P.sbuf("s5_Xb", [128, T])
        Xc = [[P.alias(Xa, "s5Xa%d" % j) for j in range(nch)], [P.alias(Xb, "s5Xb%d" % j) for j in range(nch)]]
        for g in range(NG):
            ug = P.ring("s5_u", 2, [16, T])
            P.dma("sp", ug, uT[g * 16:(g + 1) * 16, :])
            Mk = P.ring("s5_M", 2, [128, nlev, 128])
            for k in range(nlev):
                P.op("pool", lambda e, Mk=Mk, k=k, g=g: e.tensor_scalar(out=r32(Mk.ap[:, k, :]), in0=ident.ap, scalar1=pw.ap[:, k, 0, g:g + 1],
                     scalar2=0.0, op0=ALU.mult, op1=ALU.add), reads=[ident, pw], writes=[Mk])
                P.op("dve", lambda e, Mk=Mk, k=k, g=g: e.scalar_tensor_tensor(out=r32(Mk.ap[:, k, :]), in0=psw.ap, scalar=pw.ap[:, k, 1, g:g + 1],
                     in1=Mk.ap[:, k, :], op0=ALU.mult, op1=ALU.add), reads=[psw, pw, Mk], writes=[Mk])
            for j in range(nch):
                ps = S.next_psum(); sl = slice(j * CH, (j + 1) * CH)
                P.op("pe", lambda e, ps=ps, ug=ug, sl=sl, g=g: e.matmul(ps.ap, lhsT=BT.ap[:, g, :], rhs=ug.ap[:, sl], start=True, stop=True),
                     reads=[BT, ug], writes=[ps])
                P.op("act", lambda e, ps=ps, sl=sl: e.copy(out=r32(Xa.ap[:, sl]), in_=ps.ap), reads=[ps], writes=[Xc[0][j]])
            for k in range(nlev):
                d = 2 ** k
                src, dst = Xc[k % 2], Xc[(k + 1) % 2]
                sa, da = (Xa, Xb) if k % 2 == 0 else (Xb, Xa)
                for j in range(nch):
                    t0 = j * CH; t1 = t0 + CH
                    lo = max(t0, d)
                    if lo > t0:
                        hi = min(lo, t1)
                        P.op("pool", lambda e, sa=sa, da=da, t0=t0, hi=hi: e.tensor_copy(out=r32(da.ap[:, t0:hi]), in_=sa.ap[:, t0:hi]),
                             reads=[src[j]], writes=[dst[j]])
                    if lo >= t1:
                        continue
                    n = t1 - lo
                    s0, s1 = lo - d, t1 - d
                    rds = [src[c] for c in range(s0 // CH, (s1 - 1) // CH + 1)]
                    ps = S.next_psum()
                    P.op("pe", lambda e, ps=ps, Mk=Mk, k=k, sa=sa, s0=s0, s1=s1, n=n: e.matmul(ps.ap[:, 0:n], lhsT=(r32(Mk.ap[:, k, :]) if (n % 2 == 0 and s0 % 2 == 0) else Mk.ap[:, k, :]), rhs=(r32(sa.ap[:, s0:s1]) if (n % 2 == 0 and s0 % 2 == 0) else sa.ap[:, s0:s1]),
                         start=True, stop=True), reads=[Mk] + rds, writes=[ps])
                    P.op("dve", lambda e, ps=ps, sa=sa, da=da, lo=lo, t1=t1, n=n: e.tensor_tensor(out=r32(da.ap[:, lo:t1]), in0=ps.ap[:, 0:n], in1=sa.ap[:, lo:t1],
                         op=ALU.add), reads=[ps, src[j]], writes=[dst[j]])
            fin = Xc[nlev % 2]; fa = Xa if nlev % 2 == 0 else Xb
            og = P.ring("s5_o", 2, [16, T])
            for j in range(nch):
                ps = S.next_psum(); sl = slice(j * CH, (j + 1) * CH)
                P.op("pe", lambda e, ps=ps, sl=sl, g=g, fa=fa: e.matmul(ps.ap[0:16, :], lhsT=CT.ap[:, g, :], rhs=fa.ap[:, sl], start=True, stop=True),
                     reads=[CT, fin[j]], writes=[ps])
                P.op("dve", lambda e, ps=ps, sl=sl, g=g, ug=ug, og=og: e.scalar_tensor_tensor(out=og.ap[:, sl], in0=ug.ap[:, sl], scalar=dcol.ap[:, g:g + 1],
                     in1=ps.ap[0:16, :], op0=ALU.mult, op1=ALU.add), reads=[ug, dcol, ps], writes=[og])
            P.op("act", lambda e, og=og: e.activation(out=og.ap, in_=og.ap, func=AF.Gelu_apprx_tanh), reads=[og], writes=[og])
            P.dma("sp", yT[g * 16:(g + 1) * 16, :], og)


def emit_ssd(P, S, zT, xsT, bT, cT, dtT, prm, cst, yT, T, NH, NG, dbg=None):
    J = NH // NG
    NP = NH // 2
    TPG = NP // NG
    L = 128
    SC = CH
    nsc = T // SC
    ncs = SC // L
    with P.scope():
        ld = lambda name, shape, src, **kw: (lambda t: (P.dma("sp", t, src, **kw), t)[1])(P.sbuf(name, shape))
        ident = ld("sd_id", [128, 128], cst["ident"]); tri = ld("sd_tri", [128, 128], cst["tri"])
        ustr = ld("sd_us", [128, 128], cst["ustr"]); sel = ld("sd_sel", [NH, NH * 64], cst["sel"])
        ones = S.ones
        cwx = P.sbuf("sd_cwx", [128, NP, 4]); cwb = P.sbuf("sd_cwb", [128, NG, 4]); cwc = P.sbuf("sd_cwc", [128, NG, 4])
        for k in range(4):
            P.dma("sp", cwx[:, :, k], prm["cw_x"][k].re("(t p) -> p t", p=128), allow_slow_non_contiguous=True)
            P.dma("sp", cwb[:, :, k], prm["cw_b"][k].re("(t p) -> p t", p=128), allow_slow_non_contiguous=True)
            P.dma("sp", cwc[:, :, k], prm["cw_c"][k].re("(t p) -> p t", p=128), allow_slow_non_contiguous=True)
        colv = lambda nm, n: ld("sd_" + nm, [128, n], prm[nm].re("(t p) -> p t", p=128), allow_slow_non_contiguous=True)
        cbx = colv("cb_x", NP); cbb = colv("cb_b", NG); cbc = colv("cb_c", NG); dch = colv("d_ch", NP); ngc = colv("ng", NP)
        dtb = ld("sd_dtb", [NH, 1], prm["dt_bias"].re("(h o) -> h o", o=1)); alog = ld("sd_alog", [NH, 1], prm["a_log"].re("(h o) -> h o", o=1))
        negA = P.sbuf("sd_negA", [NH, 1])
        act(P, negA, alog, AF.Exp)
        ts(P, "dve", negA, negA, -1.0, ALU.mult)
        one12 = P.sbuf("sd_one", [128, 1]); P.op("dve", lambda e: e.memset(one12.ap, 1.0), writes=[one12])
        prevT = P.sbuf("sd_prev", [128, NH, 64])
        P.op("dve", lambda e: e.memset(prevT.ap, 0.0), writes=[prevT])

        def conv_silu(srcT, r0, t0, cw, cb, i, dst):
            xp = P.ring("sd_xp", 2, [128, SC + 3])
            if t0 == 0:
                P.op("pool", lambda e: e.memset(xp.ap[:, 0:3], 0.0), writes=[xp])
                P.dma("sp", xp[:, 3:SC + 3], srcT[r0:r0 + 128, 0:SC])
            else:
                P.dma("sp", xp, srcT[r0:r0 + 128, t0 - 3:t0 + SC])
            P.op("dve", lambda e: e.tensor_scalar(out=dst.ap, in0=xp.ap[:, 0:SC], scalar1=cw.ap[:, i, 0:1], scalar2=cb.ap[:, i:i + 1],
                 op0=ALU.mult, op1=ALU.add), reads=[xp, cw, cb], writes=[dst])
            for k in range(1, 4):
                P.op("dve", lambda e, k=k: e.scalar_tensor_tensor(out=dst.ap, in0=xp.ap[:, k:k + SC], scalar=cw.ap[:, i, k:k + 1], in1=dst.ap,
                     op0=ALU.mult, op1=ALU.add), reads=[xp, cw, dst], writes=[dst])
            act(P, dst, dst, AF.Silu)

        def sc_body(sc):
            t0 = sc * SC
            xsc = [P.ring("sd_xsc%d" % i, 1, [128, SC]) for i in range(NP)]
            Bc = [P.ring("sd_Bc%d" % g, 1, [128, SC]) for g in range(NG)]
            Cc = [P.ring("sd_Cc%d" % g, 1, [128, SC]) for g in range(NG)]
            for i in range(NP):
                conv_silu(xsT, i * 128, t0, cwx, cbx, i, xsc[i])
            for g in range(NG):
                conv_silu(bT, g * 128, t0, cwb, cbb, g, Bc[g])
                conv_silu(cT, g * 128, t0, cwc, cbc, g, Cc[g])
            dtv = P.ring("sd_dtv", 1, [NH, SC]); da = P.ring("sd_da", 1, [NH, SC]); acT = P.ring("sd_acT", 1, [NH, SC])
            dsT = P.ring("sd_dsT", 1, [NH, SC])
            P.dma("sp", dtv, dtT[:, t0:t0 + SC])
            act(P, dtv, dtv, AF.Exp, bias=dtb[:, 0:1])
            act(P, dtv, dtv, AF.Ln, bias=one12[0:NH, 0:1])
            ts(P, "dve", da, dtv, negA[:, 0:1], ALU.mult)
            for c in range(ncs):
                cs = slice(c * L, (c + 1) * L)
                P.op("dve", lambda e, cs=cs: e.tensor_tensor_scan(out=acT.ap[:, cs], data0=one12.ap[0:NH, 0:1].to_broadcast([NH, L]), data1=da.ap[:, cs],
                     initial=0.0, op0=ALU.mult, op1=ALU.add), reads=[da, one12], writes=[acT])
            for c in range(ncs):
                cs = slice(c * L, (c + 1) * L)
                act(P, dsT[:, cs], acT[:, cs], AF.Exp, scale=-1.0, bias=acT[:, (c + 1) * L - 1:(c + 1) * L])
            tt(P, "dve", dsT, dsT, dtv, ALU.mult)
            xT = [P.ring("sd_xT%d" % i, 1, [128, SC]) for i in range(NP)]
            xdT = [P.ring("sd_xdT%d" % i, 1, [128, SC]) for i in range(NP)]
            for i in range(NP):
                for (srcrow, dst) in ((dtv, xT[i]), (dsT, xdT[i])):
                    ps = S.next_psum()
                    P.op("pe", lambda e, ps=ps, srcrow=srcrow, i=i: e.matmul(ps.ap, lhsT=sel.ap[:, i * 128:(i + 1) * 128], rhs=srcrow.ap, start=True, stop=True),
                         reads=[sel, srcrow], writes=[ps])
                    tt(P, "dve", dst, ps, xsc[i], ALU.mult)
            ysb = [P.ring("sd_ysb%d" % i, 1, [128, SC]) for i in range(NP)]
            if dbg and sc == 0:
                P.dma("sp", dbg["xsc0"], xsc[0]); P.dma("sp", dbg["dtv"], dtv); P.dma("sp", dbg["acT"], acT); P.dma("sp", dbg["dsT"], dsT)
                P.dma("sp", dbg["xT0"], xT[0]); P.dma("sp", dbg["xdT0"], xdT[0]); P.dma("sp", dbg["Bc0"], Bc[0])
            def chunk_prep(c):
                cs = slice(c * L, (c + 1) * L)
                xtok = P.ring("sd_xtok", 2, [128, NP * 128]); xdtok = P.ring("sd_xdtok", 2, [128, NP * 128])
                for (srcs, dst) in ((xT, xtok), (xdT, xdtok)):
                    for i0 in range(0, NP, 4):
                        ps = S.next_psum(); n = min(4, NP - i0)
                        for i in range(i0, i0 + n):
                            P.op("pe", lambda e, ps=ps, srcs=srcs, i=i, i0=i0: e.transpose(ps.ap[:, (i - i0) * 128:(i - i0 + 1) * 128], srcs[i].ap[:, cs], ident.ap),
                                 reads=[srcs[i], ident], writes=[ps])
                        P.op("act", lambda e, ps=ps, dst=dst, i0=i0, n=n: e.copy(out=dst.ap[:, i0 * 128:(i0 + n) * 128], in_=ps.ap[:, 0:n * 128]), reads=[ps], writes=[dst])
                btok = P.ring("sd_btok", 2, [128, NG * 128])
                ps = S.next_psum()
                for g in range(NG):
                    P.op("pe", lambda e, ps=ps, g=g: e.transpose(ps.ap[:, g * 128:(g + 1) * 128], Bc[g].ap[:, cs], ident.ap), reads=[Bc[g], ident], writes=[ps])
                P.op("act", lambda e, ps=ps, btok=btok: e.copy(out=btok.ap, in_=ps.ap[:, 0:NG * 128]), reads=[ps], writes=[btok])
                datok = P.ring("sd_datok", 2, [128, NH])
                ps = S.next_psum()
                P.op("pe", lambda e, ps=ps: e.transpose(ps.ap[:, 0:NH], da.ap[:, cs], ident.ap[0:NH, 0:NH]), reads=[da, ident], writes=[ps])
                P.op("act", lambda e, ps=ps, datok=datok: e.copy(out=datok.ap, in_=ps.ap[:, 0:NH]), reads=[ps], writes=[datok])
                yield
                SM = P.ring("sd_SM", 2, [128, NG, L])
                for g in range(NG):
                    ps = S.next_psum()
                    P.op("pe", lambda e, ps=ps, g=g: e.matmul(ps.ap[:, 0:L], lhsT=Bc[g].ap[:, cs], rhs=Cc[g].ap[:, cs], start=True, stop=True),
                         reads=[Bc[g], Cc[g]], writes=[ps])
                    P.op("dve", lambda e, ps=ps, g=g, SM=SM: e.tensor_tensor(out=SM.ap[:, g, :], in0=ps.ap[:, 0:L], in1=tri.ap, op=ALU.mult),
                         reads=[ps, tri], writes=[SM])
                Vt = P.ring("sd_V", 2, [128, NH, L])
                P.op("dve", lambda e, Vt=Vt, datok=datok: e.tensor_tensor(out=Vt.ap, in0=tri.ap.unsqueeze(1).to_broadcast([128, NH, L]),
                     in1=datok.ap.unsqueeze(2).to_broadcast([128, NH, L]), op=ALU.mult), reads=[tri, datok], writes=[Vt])
                E = P.ring("sd_E", 2, [128, NH, L]); EA = P.ring("sd_EA", 2, [128, NH, L])
                for (lh, dst) in ((ustr, E), (ones, EA)):
                    for h0 in range(0, NH, 4):
                        n = min(4, NH - h0)
                        ps = S.next_psum()
                        P.op("pe", lambda e, ps=ps, lh=lh, Vt=Vt, h0=h0, n=n: e.matmul(ps.ap[:, 0:n * L], lhsT=lh.ap, rhs=Vt.ap[:, h0:h0 + n, :], start=True, stop=True),
                             reads=[lh, Vt], writes=[ps])
                        P.op("act", lambda e, ps=ps, dst=dst, h0=h0, n=n: e.activation(out=dst.ap[:, h0:h0 + n, :], in_=ps.ap[:, 0:n * L], func=AF.Exp),
                             reads=[ps], writes=[dst])
                yield
                Cs = P.ring("sd_Cs", 2, [128, NH, L])
                for g in range(NG):
                    hs = slice(g * J, (g + 1) * J)
                    P.op("dve", lambda e, E=E, SM=SM, g=g, hs=hs: e.tensor_tensor(out=E.ap[:, hs, :], in0=E.ap[:, hs, :],
                         in1=SM.ap[:, g:g + 1, :].to_broadcast([128, J, L]), op=ALU.mult), reads=[E, SM], writes=[E])
                    P.op("pool", lambda e, EA=EA, Cs=Cs, g=g, hs=hs: e.tensor_tensor(out=Cs.ap[:, hs, :], in0=EA.ap[:, hs, :],
                         in1=Cc[g].ap[:, cs].unsqueeze(1).to_broadcast([128, J, L]), op=ALU.mult), reads=[EA, Cc[g]], writes=[Cs])
                if dbg and sc == 0 and c == 1:
                    P.dma("sp", dbg["xtok"], xtok); P.dma("sp", dbg["btok"], btok); P.dma("sp", dbg["datok"], datok); P.dma("sp", dbg["SM"], SM.re("p g l -> p (g l)"))
                    P.dma("sp", dbg["E"], E.re("p g l -> p (g l)")); P.dma("sp", dbg["EA"], EA.re("p g l -> p (g l)")); P.dma("sp", dbg["Cs"], Cs.re("p g l -> p (g l)"))
                    P.dma("sp", dbg["prev1"], prevT.re("p g l -> p (g l)"))
                return (xtok, xdtok, btok, E, EA, Cs)

            def chunk_seq(c, tl):
                cs = slice(c * L, (c + 1) * L)
                xtok, xdtok, btok, E, EA, Cs = tl
                for i0 in range(0, NP, 4):
                    n = min(4, NP - i0)
                    ps = S.next_psum()
                    for i in range(i0, i0 + n):
                        for hh in range(2):
                            h = 2 * i + hh
                            o = ps.ap[hh * 64:(hh + 1) * 64, (i - i0) * L:(i - i0 + 1) * L]
                            P.op("pe", lambda e, o=o, h=h, xtok=xtok, E=E: e.matmul(o, lhsT=xtok.ap[:, h * 64:(h + 1) * 64], rhs=E.ap[:, h, :], start=True, stop=False),
                                 reads=[xtok, E], writes=[ps])
                            P.op("pe", lambda e, o=o, h=h, Cs=Cs: e.matmul(o, lhsT=prevT.ap[:, h, :], rhs=Cs.ap[:, h, :], start=False, stop=True),
                                 reads=[prevT, Cs], writes=[ps])
                    for i in range(i0, i0 + n):
                        P.op("dve", lambda e, ps=ps, i=i, i0=i0: e.scalar_tensor_tensor(out=ysb[i].ap[:, cs], in0=xsc[i].ap[:, cs], scalar=dch.ap[:, i:i + 1],
                             in1=ps.ap[:, (i - i0) * L:(i - i0 + 1) * L], op0=ALU.mult, op1=ALU.add), reads=[xsc[i], dch, ps], writes=[ysb[i]])
                yield
                P.op("dve", lambda e, EA=EA: e.tensor_tensor(out=prevT.ap, in0=prevT.ap, in1=EA.ap[:, :, L - 1:L].to_broadcast([128, NH, 64]), op=ALU.mult),
                     reads=[prevT, EA], writes=[prevT])
                for g in range(NG):
                    ps = S.next_psum()
                    P.op("pe", lambda e, ps=ps, g=g, btok=btok, xdtok=xdtok: e.matmul(ps.ap[:, 0:J * 64], lhsT=btok.ap[:, g * 128:(g + 1) * 128],
                         rhs=xdtok.ap[:, g * J * 64:(g + 1) * J * 64], start=True, stop=True), reads=[btok, xdtok], writes=[ps])
                    P.op("dve", lambda e, ps=ps, g=g: e.tensor_tensor(out=prevT.ap[:, g * J:(g + 1) * J, :], in0=prevT.ap[:, g * J:(g + 1) * J, :],
                         in1=ps.ap[:, 0:J * 64].rearrange("p (j d) -> p j d", d=64), op=ALU.add), reads=[ps, prevT], writes=[prevT])
            def run_all(g):
                try:
                    while True:
                        next(g)
                except StopIteration as ex:
                    return ex.value
            tl = run_all(chunk_prep(0))
            for c in range(ncs):
                g1 = chunk_prep(c + 1) if c + 1 < ncs else None
                g2 = chunk_seq(c, tl)
                nxt = None
                d1 = g1 is None
                d2 = False
                while not (d1 and d2):
                    if not d2:
                        try:
                            next(g2)
                        except StopIteration:
                            d2 = True
                    if not d1:
                        try:
                            next(g1)
                        except StopIteration as ex:
                            nxt = ex.value
                            d1 = True
                tl = nxt
            if dbg and sc == 0:
                P.dma("sp", dbg["ysb0"], ysb[0])
            for g in range(NG):
                pss = S.next_psum()
                for ii in range(TPG):
                    i = g * TPG + ii
                    zt = P.ring("sd_z", 2, [128, SC])
                    P.dma("sp", zt, zT[i * 128:(i + 1) * 128, t0:t0 + SC])
                    act(P, zt, zt, AF.Silu)
                    tt(P, "dve", ysb[i], ysb[i], zt, ALU.mult)
                    act(P, zt, ysb[i], AF.Square)
                    P.op("pe", lambda e, pss=pss, zt=zt, ii=ii: e.matmul(pss.ap, lhsT=ones.ap, rhs=zt.ap, start=(ii == 0), stop=(ii == TPG - 1)),
                         reads=[ones, zt], writes=[pss])
                rs = P.ring("sd_rs", 2, [128, SC])
                act(P, rs, pss, AF.Sqrt, scale=1.0 / (TPG * 128), bias=S.epsc)
                P.op("dve", lambda e, rs=rs: e.reciprocal(out=rs.ap, in_=rs.ap), reads=[rs], writes=[rs])
                for ii in range(TPG):
                    i = g * TPG + ii
                    P.op("dve", lambda e, i=i, rs=rs: e.scalar_tensor_tensor(out=ysb[i].ap, in0=ysb[i].ap, scalar=ngc.ap[:, i:i + 1], in1=rs.ap,
                         op0=ALU.mult, op1=ALU.mult), reads=[ysb[i], ngc, rs], writes=[ysb[i]])
                    P.dma("sp", yT[i * 128:(i + 1) * 128, t0:t0 + SC], ysb[i])

        for sc in range(nsc):
            sc_body(sc)
CH = 512


def emit_lru(P, S, xlT, glT, prm, yT, T, ntile):
    C = ntile * 128
    nch = T // CH
    with P.scope():
        cw = P.sbuf("l_cw", [128, ntile, 4])
        for k in range(4):
            P.dma("sp", cw[:, :, k], prm["conv_w"][k].re("(t p) -> p t", p=128), allow_slow_non_contiguous=True)
        cols = {}
        for nm in ("conv_b", "b_a", "b_x", "lam"):
            cols[nm] = P.sbuf("l_" + nm, [128, ntile])
            P.dma("sp", cols[nm], prm[nm].re("(t p) -> p t", p=128), allow_slow_non_contiguous=True)
        one = P.sbuf("l_one", [128, 1])
        P.op("dve", lambda e: e.memset(one.ap, 1.0), writes=[one])
        c1 = P.sbuf("l_c1", [128, ntile])
        P.op("act", lambda e: e.activation(out=c1.ap, in_=cols["lam"].ap, func=AF.Exp, scale=-1.0), reads=[cols["lam"]], writes=[c1])
        P.op("act", lambda e: e.activation(out=c1.ap, in_=c1.ap, func=AF.Ln, bias=one.ap), reads=[c1, one], writes=[c1])
        P.op("dve", lambda e: e.tensor_scalar(out=c1.ap, in0=c1.ap, scalar1=-8.0, scalar2=None, op0=ALU.mult), reads=[c1], writes=[c1])
        wa = P.sbuf("l_wa", [128, ntile, 128]); wx = P.sbuf("l_wx", [128, ntile, 128])
        P.dma("sp", wa, prm["wa_bd"].re("t p m -> p t m"))
        P.dma("sp", wx, prm["wx_bd"].re("t p m -> p t m"))
        for i in range(ntile):
            rows = slice(i * 128, (i + 1) * 128)
            xl = P.ring("l_xl", 1, [128, T + 3])
            P.op("pool", lambda e, xl=xl: e.memset(xl.ap[:, 0:3], 0.0), writes=[xl])
            P.dma("sp", xl[:, 3:T + 3], xlT[rows, :])
            gl = P.ring("l_gl", 1, [128, T])
            P.dma("sp", gl, glT[rows, :])
            xc = P.ring("l_xc", 1, [128, T])
            P.op("dve", lambda e, xl=xl, xc=xc, i=i: e.tensor_scalar(out=xc.ap, in0=xl.ap[:, 0:T], scalar1=cw.ap[:, i, 0:1],
                 scalar2=cols["conv_b"].ap[:, i:i + 1], op0=ALU.mult, op1=ALU.add), reads=[xl, cw, cols["conv_b"]], writes=[xc])
            for k in range(1, 4):
                P.op("dve", lambda e, xl=xl, xc=xc, i=i, k=k: e.scalar_tensor_tensor(out=xc.ap, in0=xl.ap[:, k:k + T],
                     scalar=cw.ap[:, i, k:k + 1], in1=xc.ap, op0=ALU.mult, op1=ALU.add), reads=[xl, cw, xc], writes=[xc])
            ga = P.ring("l_ga", 1, [128, T]); gi = P.ring("l_gi", 1, [128, T])
            for j in range(nch):
                sl = slice(j * CH, (j + 1) * CH)
                for (wm, bn, dst) in ((wa, "b_a", ga), (wx, "b_x", gi)):
                    ps = S.next_psum()
                    P.op("pe", lambda e, ps=ps, wm=wm, xc=xc, sl=sl, i=i: e.matmul(ps.ap, lhsT=wm.ap[:, i, :], rhs=xc.ap[:, sl], start=True, stop=True),
                         reads=[wm, xc], writes=[ps])
                    P.op("act", lambda e, ps=ps, dst=dst, bn=bn, sl=sl, i=i: e.activation(out=dst.ap[:, sl], in_=ps.ap, func=AF.Sigmoid,
                         bias=cols[bn].ap[:, i:i + 1]), reads=[ps, cols[bn]], writes=[dst])
            P.op("act", lambda e, ga=ga, i=i: e.activation(out=ga.ap, in_=ga.ap, func=AF.Exp, scale=c1.ap[:, i:i + 1]), reads=[ga, c1], writes=[ga])
            mu = P.ring("l_mu", 1, [128, T])
            P.op("pool", lambda e, ga=ga, mu=mu: e.tensor_tensor(out=mu.ap, in0=ga.ap, in1=ga.ap, op=ALU.mult), reads=[ga], writes=[mu])
            P.op("act", lambda e, mu=mu: e.activation(out=mu.ap, in_=mu.ap, func=AF.Sqrt, scale=-1.0, bias=one.ap), reads=[mu, one], writes=[mu])
            P.op("pool", lambda e, mu=mu: e.memset(mu.ap[:, 0:1], 1.0), reads=[mu], writes=[mu])
            P.op("dve", lambda e, gi=gi, xc=xc: e.tensor_tensor(out=gi.ap, in0=gi.ap, in1=xc.ap, op=ALU.mult), reads=[gi, xc], writes=[gi])
            P.op("pool", lambda e, gi=gi, mu=mu: e.tensor_tensor(out=gi.ap, in0=gi.ap, in1=mu.ap, op=ALU.mult), reads=[gi, mu], writes=[gi])
            P.op("dve", lambda e, gi=gi, ga=ga, xc=xc: e.tensor_tensor_scan(out=xc.ap, data0=ga.ap, data1=gi.ap, initial=0.0, op0=ALU.mult, op1=ALU.add),
                 reads=[ga, gi], writes=[xc])
            P.op("act", lambda e, gl=gl: e.activation(out=gl.ap, in_=gl.ap, func=AF.Gelu_apprx_tanh), reads=[gl], writes=[gl])
            P.op("dve", lambda e, gl=gl, xc=xc: e.tensor_tensor(out=xc.ap, in0=xc.ap, in1=gl.ap, op=ALU.mult), reads=[gl, xc], writes=[xc])
            P.dma("sp", yT[rows, :], xc)


def _tt(P, eng, out, a, b, op, r=False):
    o = rr(out.ap) if r else out.ap
    P.op(eng, lambda e: e.tensor_tensor(out=o, in0=a.ap, in1=b.ap, op=op), reads=[a, b], writes=[out])


def _ts(P, eng, out, a, s1, op0, s2=None, op1=None, r=False):
    rd = [a] + [x for x in (s1, s2) if isinstance(x, V)]
    g = lambda x: x.ap if isinstance(x, V) else x
    o = rr(out.ap) if r else out.ap
    if op1 is None:
        P.op(eng, lambda e: e.tensor_scalar(out=o, in0=a.ap, scalar1=g(s1), scalar2=None, op0=op0), reads=rd, writes=[out])
    else:
        P.op(eng, lambda e: e.tensor_scalar(out=o, in0=a.ap, scalar1=g(s1), scalar2=g(s2), op0=op0, op1=op1), reads=rd, writes=[out])


def _act(P, out, a, func, scale=None, bias=None):
    rd = [a] + [x for x in (scale, bias) if isinstance(x, V)]
    kw = {}
    if scale is not None:
        kw["scale"] = scale.ap if isinstance(scale, V) else scale
    if bias is not None:
        kw["bias"] = bias.ap if isinstance(bias, V) else bias
    P.op("act", lambda e: e.activation(out=out.ap, in_=a.ap, func=func, **kw), reads=rd, writes=[out])


def _stt(P, out, in0, scalar, in1, op0, op1):
    rd = [in0, in1] + ([scalar] if isinstance(scalar, V) else [])
    sc = scalar.ap if isinstance(scalar, V) else scalar
    P.op("dve", lambda e: e.scalar_tensor_tensor(out=out.ap, in0=in0.ap, scalar=sc, in1=in1.ap, op0=op0, op1=op1), reads=rd, writes=[out])


R32 = True
F32R = mybir.dt.float32r


def rr(ap):
    return ap.bitcast(F32R) if R32 else ap


def _mm(P, out, lhsT, rhs, start=True, stop=True, fast=False):
    if fast and R32:
        P.op("pe", lambda e: e.matmul(out.ap, lhsT=rr(lhsT.ap), rhs=rr(rhs.ap), start=start, stop=stop), reads=[lhsT, rhs], writes=[out])
    else:
        P.op("pe", lambda e: e.matmul(out.ap, lhsT=lhsT.ap, rhs=rhs.ap, start=start, stop=stop), reads=[lhsT, rhs], writes=[out])


def _tr(P, out, in_, ident):
    P.op("pe", lambda e: e.transpose(out.ap, in_.ap, ident.ap), reads=[in_, ident], writes=[out])


def emit_rwkv(P, S, rT, kT, vT, wlT, alT, glT, prm, cst, yT, T, NP, stage=99, dbg=None):
    C = 64
    SC = 256
    NH = NP * 2
    nsc = T // SC
    ncs = SC // C
    GN_EPS = 64e-5
    with P.scope():
        ld = lambda name, shape, src, **kw: (lambda t: (P.dma("sp", t, src, **kw), t)[1])(P.sbuf(name, shape))
        ident = ld("rw_id", [128, 128], cst["ident"]); bones = ld("rw_bo", [128, 128], cst["bones"])
        mask2 = ld("rw_m2", [64, 128], cst["mask2"]); maskL = ld("rw_mL", [64, 64], cst["maskL"]); rmask = ld("rw_rm", [128, SC], cst["rmask"])
        colv = lambda nm, n, rows=128: ld("rw_" + nm, [rows, n], prm[nm].re("(t p) -> p t", p=rows), allow_slow_non_contiguous=True)
        mu = {x: colv("mu_" + x, NP) for x in "rkv"}
        mu_wl = colv("mu_wl", 1, 96); mu_al = colv("mu_al", 1, 96); mu_gl = colv("mu_gl", 2)
        w0 = colv("w0", NP); a0 = colv("a0", NP); kkc = colv("k_k", NP); kac = colv("k_a", NP); rkc = colv("r_k", NP)
        lng = colv("ln_g", NP); lnb = colv("ln_b", NP)
        wup = ld("rw_wup", [96, NP * 128], prm["w_up"]); aup = ld("rw_aup", [96, NP * 128], prm["a_up"])
        gup = ld("rw_gup", [128, 2, NP * 128], prm["g_up"].re("(k p) n -> p k n", p=128))
        def one_minus(src, name):
            t = P.sbuf(name, list(src.shape))
            _ts(P, "dve", t, src, -1.0, ALU.mult, 1.0, ALU.add)
            return t
        imu = {x: one_minus(mu[x], "rw_imu" + x) for x in "rkv"}
        imu_wl = one_minus(mu_wl, "rw_imuwl"); imu_al = one_minus(mu_al, "rw_imual"); imu_gl = one_minus(mu_gl, "rw_imugl")
        ika = one_minus(kac, "rw_ika")
        gne = P.sbuf("rw_gne", [128, 1]); P.op("dve", lambda e: e.memset(gne.ap, GN_EPS), writes=[gne])
        Hst = P.sbuf("rw_H", [64, NH, 64])
        P.op("dve", lambda e: e.memset(Hst.ap, 0.0), writes=[Hst])

        def shift_mix(srcT, r0, nr, t0, muc, imuc, dst):
            xp = P.ring("rw_xp", 3, [128, SC + 1])
            if t0 == 0:
                P.op("pool", lambda e: e.memset(xp.ap[0:nr, 0:1], 0.0), writes=[xp])
                P.dma("sp", xp[0:nr, 1:SC + 1], srcT[r0:r0 + nr, 0:SC])
            else:
                P.dma("sp", xp[0:nr], srcT[r0:r0 + nr, t0 - 1:t0 + SC])
            tmp = P.ring("rw_smt", 2, [128, SC])
            P.op("act", lambda e: e.activation(out=tmp.ap[0:nr], in_=xp.ap[0:nr, 0:SC], func=AF.Copy, scale=muc.ap),
                 reads=[xp, muc], writes=[tmp])
            P.op("dve", lambda e: e.scalar_tensor_tensor(out=dst.ap, in0=xp.ap[0:nr, 1:SC + 1], scalar=imuc.ap, in1=tmp.ap[0:nr], op0=ALU.mult, op1=ALU.add),
                 reads=[xp, imuc, tmp], writes=[dst])

        def sc_body(sc):
            t0 = sc * SC
            tw = P.ring("rw_tw", 1, [96, SC]); al = P.ring("rw_al", 1, [96, SC]); sg = P.ring("rw_sg", 1, [128, 2, SC])
            shift_mix(wlT, 0, 96, t0, mu_wl[:, 0:1], imu_wl[:, 0:1], tw)
            _act(P, tw, tw, AF.Tanh)
            shift_mix(alT, 0, 96, t0, mu_al[:, 0:1], imu_al[:, 0:1], al)
            for kx in range(2):
                shift_mix(glT, kx * 128, 128, t0, mu_gl[:, kx:kx + 1], imu_gl[:, kx:kx + 1], sg[:, kx, :])
            _act(P, sg, sg, AF.Sigmoid)
            Gc = P.ring("rw_Gc", 1, [64, NH, ncs])
            KRo = []; bho = []; kho = []
            rp = []; vp = []; k2 = []; KR = []; bh = []; kh = []; gt = []; gC = []; bon = []
            for i in range(NP):
                cols = slice(i * 128, (i + 1) * 128)
                r_ = P.ring("rw_r%d" % i, 1, [128, SC]); k_ = P.ring("rw_k%d" % i, 1, [128, SC]); v_ = P.ring("rw_v%d" % i, 1, [128, SC])
                shift_mix(rT, i * 128, 128, t0, mu["r"][:, i:i + 1], imu["r"][:, i:i + 1], r_)
                shift_mix(kT, i * 128, 128, t0, mu["k"][:, i:i + 1], imu["k"][:, i:i + 1], k_)
                shift_mix(vT, i * 128, 128, t0, mu["v"][:, i:i + 1], imu["v"][:, i:i + 1], v_)
                lw = P.ring("rw_lw", 1, [128, SC]); a_ = P.ring("rw_a", 1, [128, SC]); g_ = P.ring("rw_g%d" % i, 1, [128, SC])
                ps = S.next_psum()
                _mm(P, ps[:, 0:SC], wup[:, cols], tw)
                _act(P, lw, ps[:, 0:SC], AF.Sigmoid, bias=w0[:, i:i + 1])
                _ts(P, "dve", lw, lw, -math.exp(-0.5), ALU.mult)
                ps = S.next_psum()
                _mm(P, ps[:, 0:SC], aup[:, cols], al)
                _act(P, a_, ps[:, 0:SC], AF.Sigmoid, bias=a0[:, i:i + 1])
                ps = S.next_psum()
                _mm(P, ps[:, 0:SC], gup[:, 0, cols], sg[:, 0, :], True, False)
                _mm(P, ps[:, 0:SC], gup[:, 1, cols], sg[:, 1, :], False, True)
                P.op("act", lambda e, g_=g_, ps=ps: e.copy(out=g_.ap, in_=ps.ap[:, 0:SC]), reads=[ps], writes=[g_])
                kap = P.ring("rw_kap", 1, [128, SC]); t1 = P.ring("rw_t1", 1, [128, SC]); t2 = P.ring("rw_t2", 1, [128, SC])
                _ts(P, "dve", kap, k_, kkc[:, i:i + 1], ALU.mult)
                _tt(P, "pool", t1, kap, kap, ALU.mult)
                ps = S.next_psum()
                _mm(P, ps[:, 0:SC], bones, t1)
                _ts(P, "dve", t1, ps[:, 0:SC], 1e-24, ALU.max)
                _act(P, t1, t1, AF.Sqrt)
                P.op("dve", lambda e, t1=t1: e.reciprocal(out=t1.ap, in_=t1.ap), reads=[t1], writes=[t1])
                _tt(P, "dve", kap, kap, t1, ALU.mult)
                k2_ = P.ring("rw_k2%d" % i, 1, [128, SC])
                _ts(P, "dve", t1, a_, kac[:, i:i + 1], ALU.mult, ika[:, i:i + 1], ALU.add)
                _tt(P, "dve", k2_, k_, t1, ALU.mult)
                bet = P.ring("rw_bet", 1, [128, SC])
                _tt(P, "pool", bet, kap, a_, ALU.mult)
                cl = P.ring("rw_cl", 1, [128, SC])
                P.op("dve", lambda e, cl=cl, lw=lw: e.tensor_tensor_scan(out=cl.ap, data0=rmask.ap, data1=lw.ap, initial=0.0, op0=ALU.mult, op1=ALU.add),
                     reads=[rmask, lw], writes=[cl])
                eG = P.ring("rw_eG", 1, [128, SC]); eN = P.ring("rw_eN", 1, [128, SC])
                _act(P, eG, cl, AF.Exp)
                _act(P, eN, cl, AF.Exp, scale=-1.0)
                _tt(P, "dve", t2, cl, lw, ALU.subtract)
                _act(P, t2, t2, AF.Exp)
                KR_ = P.ring("rw_KR%d" % i, 1, [128, ncs, 2, C])
                _tt(P, "dve", KR_[:, :, 0, :], kap.re("p (c t) -> p c t", t=C), t2.re("p (c t) -> p c t", t=C), ALU.mult, r=True)
                _tt(P, "pool", KR_[:, :, 1, :], r_.re("p (c t) -> p c t", t=C), eG.re("p (c t) -> p c t", t=C), ALU.mult, r=True)
                bh_ = P.ring("rw_bh%d" % i, 1, [128, SC]); kh_ = P.ring("rw_kh%d" % i, 1, [128, SC])
                _tt(P, "dve", bh_, bet, eN, ALU.mult, r=True)
                _tt(P, "pool", kh_, k2_, eN, ALU.mult, r=True)
                gC_ = P.ring("rw_gC%d" % i, 1, [128, ncs])
                P.op("act", lambda e, gC_=gC_, eG=eG: e.copy(out=gC_.ap, in_=eG.ap.rearrange("p (c t) -> p c t", t=C)[:, :, C - 1]), reads=[eG], writes=[gC_])
                bon_ = P.ring("rw_bon%d" % i, 1, [128, SC])
                _stt(P, t1, r_, rkc[:, i:i + 1], k2_, ALU.mult, ALU.mult)
                ps = S.next_psum()
                _mm(P, ps[:, 0:SC], bones, t1)
                _tt(P, "dve", bon_, ps[:, 0:SC], v_, ALU.mult)
                KRo_ = P.ring("rw_KRo%d" % i, 1, [64, ncs, 2, C]); bho_ = P.ring("rw_bho%d" % i, 1, [64, SC]); kho_ = P.ring("rw_kho%d" % i, 1, [64, SC])
                P.dma("sp", KRo_.bitcast(F32R) if R32 else KRo_, KR_[64:128].bitcast(F32R) if R32 else KR_[64:128])
                P.dma("sp", bho_.bitcast(F32R) if R32 else bho_, bh_[64:128].bitcast(F32R) if R32 else bh_[64:128])
                P.dma("sp", kho_.bitcast(F32R) if R32 else kho_, kh_[64:128].bitcast(F32R) if R32 else kh_[64:128])
                P.op("pool", lambda e, gC_=gC_, i=i: e.tensor_copy(out=Gc.ap[:, 2 * i, :], in_=gC_.ap[0:64, :]), reads=[gC_], writes=[Gc])
                P.dma("sp", Gc[:, 2 * i + 1, :], gC_[64:128, :])
                KRo.append(KRo_); bho.append(bho_); kho.append(kho_)
                rp.append(r_); vp.append(v_); k2.append(k2_); KR.append(KR_); bh.append(bh_); kh.append(kh_); gt.append(g_); gC.append(gC_); bon.append(bon_)
            ysb = [P.ring("rw_y%d" % i, 1, [128, SC]) for i in range(NP)]
            KRh = lambda h: KR[h // 2][0:64] if h % 2 == 0 else KRo[h // 2]
            bhh = lambda h: bh[h // 2][0:64] if h % 2 == 0 else bho[h // 2]
            khh = lambda h: kh[h // 2][0:64] if h % 2 == 0 else kho[h // 2]

            def prep_chunk(c):
                cs = slice(c * C, (c + 1) * C)
                toks = {}
                for nm, srcs in (("v", vp), ("b", bh), ("k", kh)):
                    ps = S.next_psum()
                    for i in range(NP):
                        _tr(P, ps[0:C, i * 128:(i + 1) * 128], srcs[i][:, cs], ident)
                    tk = P.ring("rw_tok" + nm, 2, [64, NP * 128])
                    P.op("act", lambda e, tk=tk, ps=ps: e.copy(out=rr(tk.ap), in_=ps.ap[0:C, 0:NP * 128]), reads=[ps], writes=[tk])
                    toks[nm] = tk
                psN = S.next_psum(); psB = S.next_psum(); psK = S.next_psum(); psB2 = S.next_psum(); psK2 = S.next_psum()
                nb = 4
                for h in range(NH):
                    kapc = KRh(h)[:, c, 0, :]; krc = KRh(h)[:, c, :, :].re("p a t -> p (a t)")
                    _mm(P, psN[0:C, h * C:(h + 1) * C], kapc, bhh(h)[:, cs], fast=True)
                    pb_ = psB if h < nb else psB2
                    pk_ = psK if h < nb else psK2
                    _mm(P, pb_[0:C, (h % nb) * 128:(h % nb + 1) * 128], bhh(h)[:, cs], krc, fast=True)
                    _mm(P, pk_[0:C, (h % nb) * 128:(h % nb + 1) * 128], khh(h)[:, cs], krc, fast=True)
                Q = P.ring("rw_Q", 2, [64, NH, C]); QT = P.ring("rw_QT", 2, [64, NH, C]); R = P.ring("rw_R", 2, [64, NH, C])
                AB = P.ring("rw_AB", 2, [64, NH, 2, C]); AK = P.ring("rw_AK", 2, [64, NH, 2, C])
                P.op("dve", lambda e, Q=Q: e.scalar_tensor_tensor(out=rr(Q.ap), in0=psN.ap[0:C, 0:NH * C].rearrange("p (h t) -> p h t", t=C), scalar=-1.0,
                     in1=maskL.ap.unsqueeze(1).to_broadcast([64, NH, C]), op0=ALU.mult, op1=ALU.mult), reads=[psN, maskL], writes=[Q])
                for (pp, dst, h0) in ((psB, AB, 0), (psB2, AB, nb), (psK, AK, 0), (psK2, AK, nb)):
                    n = min(nb, NH - h0)
                    if n <= 0:
                        continue
                    P.op("dve", lambda e, pp=pp, dst=dst, h0=h0, n=n: e.tensor_tensor(out=rr(dst.ap[:, h0:h0 + n].rearrange("p h a t -> p h (a t)")),
                         in0=pp.ap[0:C, 0:n * 128].rearrange("p (h x) -> p h x", x=128), in1=mask2.ap.unsqueeze(1).to_broadcast([64, n, 128]), op=ALU.mult),
                         reads=[pp, mask2], writes=[dst])
                _ts(P, "dve", QT, AB[:, :, 0, :], -1.0, ALU.mult, r=True)
                P.op("dve", lambda e, QT=QT: e.tensor_tensor(out=rr(R.ap), in0=QT.ap, in1=ident.ap[0:64, 0:64].unsqueeze(1).to_broadcast([64, NH, C]), op=ALU.add),
                         reads=[QT, ident], writes=[R])
                nlev = 6
                yield
                for lv in range(1, nlev):
                    if lv > 1:
                        yield
                    psQ = S.next_psum(); psQT = S.next_psum(); psR = S.next_psum()
                    Qn = P.ring("rw_Q", 2, [64, NH, C]); QTn = P.ring("rw_QT", 2, [64, NH, C])
                    last = (lv == nlev - 1)
                    for h in range(NH):
                        hsl = slice(h * C, (h + 1) * C)
                        _mm(P, psQ[0:C, hsl], QT[:, h, :], Q[:, h, :], fast=True)
                        if not last:
                            _mm(P, psQT[0:C, hsl], Q[:, h, :], QT[:, h, :], fast=True)
                    P.op("act", lambda e, Qn=Qn, psQ=psQ: e.copy(out=rr(Qn.ap.rearrange("p h t -> p (h t)")), in_=psQ.ap[0:C, 0:NH * C]), reads=[psQ], writes=[Qn])
                    if not last:
                        P.op("act", lambda e, QTn=QTn, psQT=psQT: e.copy(out=rr(QTn.ap.rearrange("p h t -> p (h t)")), in_=psQT.ap[0:C, 0:NH * C]), reads=[psQT], writes=[QTn])
                    for h in range(NH):
                        _mm(P, psR[0:C, h * C:(h + 1) * C], Qn[:, h, :], R[:, h, :], fast=True)
                    P.op("dve", lambda e, psR=psR: e.tensor_tensor(out=rr(R.ap.rearrange("p h t -> p (h t)")), in0=psR.ap[0:C, 0:NH * C],
                         in1=R.ap.rearrange("p h t -> p (h t)"), op=ALU.add), reads=[psR, R], writes=[R])
                    Q, QT = Qn, QTn
                if dbg and sc == 0 and c == 0:
                    P.dma("sp", dbg["R"], R.re("p h t -> p (h t)")); P.dma("sp", dbg["AB"], AB.re("p h a t -> p (h a t)")); P.dma("sp", dbg["AK"], AK.re("p h a t -> p (h a t)"))
                    P.dma("sp", dbg["vt"], toks["v"]); P.dma("sp", dbg["bt"], toks["b"])
                return dict(toks=toks, R=R, AB=AB, AK=AK)

            def seq_chunk(c, pc):
                cs = slice(c * C, (c + 1) * C)
                toks, R, AB, AK = pc["toks"], pc["R"], pc["AB"], pc["AK"]
                vt, bt, kt = toks["v"], toks["b"], toks["k"]
                psW = S.next_psum()
                Hr = P.ring("rw_Hr", 2, [64, NH, 64])
                P.op("act", lambda e: e.copy(out=rr(Hr.ap), in_=Hst.ap), reads=[Hst], writes=[Hr])
                for h in range(NH):
                    _mm(P, psW[0:C, h * 64:(h + 1) * 64], KRh(h)[:, c, 0, :], Hr[:, h, :], True, False, fast=True)
                    _mm(P, psW[0:C, h * 64:(h + 1) * 64], AK[:, h, 0, :], vt[:, h * 64:(h + 1) * 64], False, True, fast=True)
                Wsb = P.ring("rw_W", 2, [64, NH * 64])
                P.op("act", lambda e: e.copy(out=rr(Wsb.ap), in_=psW.ap[0:C, 0:NH * 64]), reads=[psW], writes=[Wsb])
                yield
                psU = S.next_psum()
                for h in range(NH):
                    _mm(P, psU[0:C, h * 64:(h + 1) * 64], R[:, h, :], Wsb[:, h * 64:(h + 1) * 64], fast=True)
                Usb = P.ring("rw_U", 2, [64, NH * 64])
                _ts(P, "dve", Usb, psU[0:C, 0:NH * 64], -1.0, ALU.mult, r=True)
                if dbg and sc == 0 and c == 0:
                    P.dma("sp", dbg["W"], Wsb); P.dma("sp", dbg["U"], Usb)
                yield
                psY = S.next_psum(); psH = S.next_psum()
                for h in range(NH):
                    i, hh = divmod(h, 2); pr = slice(hh * 64, (hh + 1) * 64)
                    oy = psY[pr, i * C:(i + 1) * C]
                    fy = (hh == 0)
                    _mm(P, oy, Hr[:, h, :], KRh(h)[:, c, 1, :], True, False, fast=fy)
                    _mm(P, oy, Usb[:, h * 64:(h + 1) * 64], AB[:, h, 1, :], False, False, fast=fy)
                    _mm(P, oy, vt[:, h * 64:(h + 1) * 64], AK[:, h, 1, :], False, True, fast=fy)
                for h in range(NH):
                    oh = psH[0:64, h * 64:(h + 1) * 64]
                    _mm(P, oh, bt[:, h * 64:(h + 1) * 64], Usb[:, h * 64:(h + 1) * 64], True, False, fast=True)
                    _mm(P, oh, kt[:, h * 64:(h + 1) * 64], vt[:, h * 64:(h + 1) * 64], False, True, fast=True)
                yield
                for i in range(NP):
                    P.op("act", lambda e, i=i: e.copy(out=ysb[i].ap[:, cs], in_=psY.ap[:, i * C:(i + 1) * C]), reads=[psY], writes=[ysb[i]])
                P.op("dve", lambda e: e.tensor_tensor(out=Hst.ap, in0=Hst.ap, in1=psH.ap[0:64, 0:NH * 64].rearrange("p (i v) -> p i v", v=64), op=ALU.add),
                     reads=[Hst, psH], writes=[Hst])
                P.op("dve", lambda e: e.tensor_tensor(out=Hst.ap, in0=Hst.ap, in1=Gc.ap[:, :, c:c + 1].to_broadcast([64, NH, 64]), op=ALU.mult),
                     reads=[Hst, Gc], writes=[Hst])

            if dbg and sc == 0:
                P.dma("sp", dbg["KR0"], KR[0].re("p c a t -> p (c a t)")); P.dma("sp", dbg["bh0"], bh[0]); P.dma("sp", dbg["kh0"], kh[0])
                P.dma("sp", dbg["KRo0"], KRo[0].re("p c a t -> p (c a t)")); P.dma("sp", dbg["Gc"], Gc.re("p h c -> p (h c)"))
            if stage == 1:
                for i in range(NP):
                    P.dma("sp", yT[i * 128:(i + 1) * 128, t0:t0 + SC], bon[i])
                return
            def run_all(g):
                try:
                    while True:
                        next(g)
                except StopIteration as ex:
                    return ex.value
            pcs = run_all(prep_chunk(0))
            for c in range(ncs):
                g1 = prep_chunk(c + 1) if c + 1 < ncs else None
                g2 = seq_chunk(c, pcs)
                nxt = None
                d1 = g1 is None
                d2 = False
                while not (d1 and d2):
                    if not d2:
                        try:
                            next(g2)
                        except StopIteration:
                            d2 = True
                    if not d1:
                        try:
                            next(g1)
                        except StopIteration as ex:
                            nxt = ex.value
                            d1 = True
                pcs = nxt
            if dbg and sc == 0:
                P.dma("sp", dbg["y0"], ysb[0]); P.dma("sp", dbg["H"], Hst.re("p h v -> p (h v)"))
            for i in range(NP):
                y = ysb[i]
                t1 = P.ring("rw_t1", 1, [128, SC]); t2 = P.ring("rw_t2", 1, [128, SC])
                ps = S.next_psum()
                _mm(P, ps[:, 0:SC], bones, y)
                _stt(P, y, ps[:, 0:SC], -1.0 / 64, y, ALU.mult, ALU.add)
                _tt(P, "pool", t1, y, y, ALU.mult)
                ps = S.next_psum()
                _mm(P, ps[:, 0:SC], bones, t1)
                _act(P, t2, ps[:, 0:SC], AF.Sqrt, scale=1.0 / 64, bias=gne)
                P.op("dve", lambda e, t2=t2: e.reciprocal(out=t2.ap, in_=t2.ap), reads=[t2], writes=[t2])
                _stt(P, y, y, lng[:, i:i + 1], t2, ALU.mult, ALU.mult)
                _stt(P, y, y, lnb[:, i:i + 1], bon[i], ALU.add, ALU.add)
                _tt(P, "dve", y, y, gt[i], ALU.mult)
                P.dma("sp", yT[i * 128:(i + 1) * 128, t0:t0 + SC], y)

        for sc in range(nsc):
            sc_body(sc)
import numpy as _np

NT_CORE = 2048
SEQ = 4096
NB = 4


def tile_w(W):
    K, N = W.shape
    NCB = (N + 127) // 128
    Wp = _np.zeros((K, NCB * 128), _np.float32)
    Wp[:, :N] = W
    return _np.ascontiguousarray(Wp.reshape(K // 128, 128, NCB, 128).transpose(2, 1, 0, 3))


def consts_all():
    ident = _np.eye(128, dtype=_np.float32)
    psw = _np.zeros((128, 128), _np.float32)
    for k in range(128):
        psw[k, (k + 64) % 128] = 1
    sgn = _np.ones((128, 1), _np.float32); sgn[64:] = -1
    s = _np.arange(128)
    tri = (s[:, None] <= s[None, :]).astype(_np.float32)
    ustr = (s[:, None] > s[None, :]).astype(_np.float32)
    sel = _np.zeros((12, 768), _np.float32)
    for h in range(12):
        sel[h, h * 64:(h + 1) * 64] = 1
    bones = _np.zeros((128, 128), _np.float32); bones[:64, :64] = 1; bones[64:, 64:] = 1
    s6 = _np.arange(64)
    mU = (s6[:, None] < s6[None, :]).astype(_np.float32); mUi = (s6[:, None] <= s6[None, :]).astype(_np.float32)
    mask2 = _np.concatenate([mU, mUi], 1)
    maskL = (s6[None, :] < s6[:, None]).astype(_np.float32)
    rmask = _np.ones((128, 256), _np.float32); rmask[:, ::64] = 0
    return dict(ident=ident, psw=psw, sgn=sgn, tri=tri, ustr=ustr, sel=sel, bones=bones, mask2=mask2, maskL=maskL, rmask=rmask)


def s5_host(lam_re, lam_im, log_step, b_re, b_im, c_re, c_im, d):
    G = lam_re.shape[0]
    rep = lambda a: _np.ascontiguousarray(_np.broadcast_to(a[None], (16,) + a.shape)).astype(_np.float32)
    col = lambda a: _np.ascontiguousarray(_np.concatenate([a.T, a.T], 0)).astype(_np.float32)
    ls = _np.broadcast_to(log_step[:, None], (G, 64))
    return dict(lr_row=rep(lam_re), li_row=rep(lam_im), ls_row=rep(ls),
                bT_re=_np.ascontiguousarray(b_re.transpose(2, 0, 1)), bT_im=_np.ascontiguousarray(b_im.transpose(2, 0, 1)),
                lr_col=col(lam_re), li_col=col(lam_im), ls_col=col(ls),
                cT=_np.ascontiguousarray(_np.concatenate([c_re.transpose(2, 0, 1), c_im.transpose(2, 0, 1)], 0)),
                d_col=_np.ascontiguousarray(d.reshape(G, 16).T))


def even_params(inp, hh):
    g0 = hh * 16
    p = {"s5_" + k: v for k, v in s5_host(inp["s5_lam_re"][0, g0:g0 + 16], inp["s5_lam_im"][0, g0:g0 + 16], inp["s5_log_step"][0, g0:g0 + 16],
                                          inp["s5_b_re"][0, g0:g0 + 16], inp["s5_b_im"][0, g0:g0 + 16], inp["s5_c_re"][0, g0:g0 + 16],
                                          inp["s5_c_im"][0, g0:g0 + 16], inp["s5_d"][0, hh * 256:(hh + 1) * 256]).items()}
    cw = inp["ssd_conv_w"][0]; cb = inp["ssd_conv_b"][0]
    xs = slice(hh * 768, (hh + 1) * 768); bs = slice(1536 + hh * 256, 1536 + (hh + 1) * 256); cs = slice(2048 + hh * 256, 2048 + (hh + 1) * 256)
    hs = slice(hh * 12, (hh + 1) * 12)
    c = lambda a: _np.ascontiguousarray(a, dtype=_np.float32)
    p.update(sd_cw_x=c(cw[:, xs]), sd_cb_x=c(cb[xs]), sd_cw_b=c(cw[:, bs]), sd_cb_b=c(cb[bs]), sd_cw_c=c(cw[:, cs]), sd_cb_c=c(cb[cs]),
             sd_dt_bias=c(inp["ssd_dt_bias"][0, hs]), sd_a_log=c(inp["ssd_a_log"][0, hs]),
             sd_d_ch=c(_np.repeat(inp["ssd_d"][0, hs], 64)), sd_ng=c(inp["ssd_norm"][0, xs]))
    return p


def lru_bd(w, hh):
    o = _np.zeros((4, 128, 128), _np.float32)
    for bl in range(8):
        t, h = divmod(bl, 2)
        o[t, h * 64:(h + 1) * 64, h * 64:(h + 1) * 64] = w[hh * 8 + bl]
    return o


def odd_params(inp, hh):
    c = lambda a: _np.ascontiguousarray(a, dtype=_np.float32)
    mu = inp["rwkv_mu"][0]
    ch = slice(hh * 512, (hh + 1) * 512)
    p = dict(rw_mu_r=c(mu[0:1024][ch]), rw_mu_k=c(mu[1024:2048][ch]), rw_mu_v=c(mu[2048:3072][ch]), rw_mu_wl=c(mu[3072:3168]), rw_mu_al=c(mu[3168:3264]),
             rw_mu_gl=c(mu[3264:3520]), rw_w0=c(inp["rwkv_w0"][0, ch]), rw_a0=c(inp["rwkv_a0"][0, ch]), rw_k_k=c(inp["rwkv_k_k"][0, ch]),
             rw_k_a=c(inp["rwkv_k_a"][0, ch]), rw_r_k=c(inp["rwkv_r_k"][0].reshape(-1)[ch]), rw_ln_g=c(inp["rwkv_ln_g"][0, ch]), rw_ln_b=c(inp["rwkv_ln_b"][0, ch]),
             rw_w_up=c(inp["rwkv_w_up"][0][:, ch]), rw_a_up=c(inp["rwkv_a_up"][0][:, ch]), rw_g_up=c(inp["rwkv_g_up"][0][:, ch]))
    p.update(lr_conv_w=c(inp["lru_conv_w"][0][:, ch]), lr_conv_b=c(inp["lru_conv_b"][0, ch]), lr_wa_bd=lru_bd(inp["lru_w_a"][0], hh), lr_wx_bd=lru_bd(inp["lru_w_x"][0], hh),
             lr_b_a=c(inp["lru_b_a"][0].reshape(-1)[ch]), lr_b_x=c(inp["lru_b_x"][0].reshape(-1)[ch]), lr_lam=c(inp["lru_lam"][0].reshape(-1)[ch]))
    return p


def dense_w(inp, L):
    c = lambda a: _np.ascontiguousarray(a, dtype=_np.float32)
    return dict(out_t=tile_w(inp["e_out_proj" if L == 0 else "o_out_proj"][0]), w1_t=tile_w(inp["mlp_w1"][L]), w2_t=tile_w(inp["mlp_w2"][L]),
                gate_t=tile_w(inp["pl_gate"][L]), plp_t=tile_w(inp["pl_proj"][L]), nffn=c(inp["norm_ffn"][L]), npl=c(inp["norm_pl"][L]))


class Launch:
    def __init__(self):
        self.nc = bass.Bass("TRN2", target_bir_lowering=False)
        self.st = contextlib.ExitStack()
        self.P = Prog(self.nc, self.st)
        self.S = Shared(self.P)
        self.outs = []

    def inp(self, name, arr):
        return self.P.dram(name, list(arr.shape), kind="ExternalInput")

    def inps(self, d, prefix=""):
        return {k: self.inp(prefix + k, v) for k, v in d.items()}

    def out(self, name, shape):
        v = self.P.dram(name, list(shape), kind="ExternalOutput")
        self.outs.append(v)
        return v

    def run(self, in_maps):
        self.P.wait_all("sp", self.outs)
        self.P.finish()
        self.st.close()
        res = run_bass_kernel_spmd(self.nc, in_maps, core_ids=list(range(len(in_maps))))
        return res.results


def strip(d, prefix):
    return {k[len(prefix):]: v for k, v in d.items() if k.startswith(prefix)}


PAIRS = [[0, 1], [2, 3], [4, 5], [6, 7]]
ALL8 = [list(range(8))]


def pad_cols(W, n):
    o = _np.zeros((W.shape[0], n), _np.float32)
    o[:, :W.shape[1]] = W
    return o


def host_inputs(inp):
    f32c = lambda a: _np.ascontiguousarray(a, dtype=_np.float32)
    x = inp["x"]; p = inp["p"]
    cst = consts_all()
    ein = inp["e_in_proj"][0]; oin = inp["o_in_proj"][0]
    eout = inp["e_out_proj"][0]; oout = inp["o_out_proj"][0]
    shared = {}
    for L in range(2):
        shared["w1_%d" % L] = tile_w(inp["mlp_w1"][L]); shared["w2_%d" % L] = tile_w(inp["mlp_w2"][L])
        shared["gate_%d" % L] = tile_w(inp["pl_gate"][L]); shared["plp_%d" % L] = tile_w(inp["pl_proj"][L])
    per_hh = []
    for hh in range(2):
        d = {}
        cols0 = _np.concatenate([ein[:, hh * 256:(hh + 1) * 256], ein[:, 512 + hh * 768:512 + (hh + 1) * 768], ein[:, 2048 + hh * 768:2048 + (hh + 1) * 768],
                                 ein[:, 3584 + hh * 256:3584 + (hh + 1) * 256], ein[:, 4096 + hh * 256:4096 + (hh + 1) * 256],
                                 pad_cols(ein[:, 4608 + hh * 12:4608 + (hh + 1) * 12], 128)], 1)
        d["win0_t"] = tile_w(cols0)
        ch = lambda o: oin[:, o + hh * 512:o + (hh + 1) * 512]
        cols1 = _np.concatenate([ch(0), ch(1024), ch(2048), pad_cols(oin[:, 3072:3168], 128), pad_cols(oin[:, 3168:3264], 128), oin[:, 3264:3520], ch(3520), ch(4544)], 1)
        d["win1_t"] = tile_w(cols1)
        d["wout0_t"] = tile_w(_np.concatenate([eout[hh * 256:(hh + 1) * 256], eout[512 + hh * 768:512 + (hh + 1) * 768]], 0))
        d["wout1_t"] = tile_w(_np.concatenate([oout[hh * 512:(hh + 1) * 512], oout[1024 + hh * 512:1024 + (hh + 1) * 512]], 0))
        d["gluw_t"] = tile_w(inp["s5_glu_w"][0][hh * 256:(hh + 1) * 256])
        d["glub"] = f32c(inp["s5_glu_b"][0][hh * 256:(hh + 1) * 256])
        d.update(even_params(inp, hh)); d.update(odd_params(inp, hh))
        per_hh.append(d)
    ins = []
    for c in range(8):
        b, hh = divmod(c, 2)
        ts_ = slice(hh * NT_CORE, (hh + 1) * NT_CORE)
        d = dict(xT=f32c(x[b, ts_, :].T), pT0=f32c(p[0, b, ts_, :].T), pT1=f32c(p[1, b, ts_, :].T))
        for k, v in shared.items():
            d[k + "_t"] = v
        xf = x[b].T.reshape(8, 256, 2, NT_CORE).transpose(0, 2, 1, 3)
        d["xfull"] = f32c(xf)
        for L in range(2):
            d["nmix%d" % L] = f32c(inp["norm_mix"][L]); d["nffn%d" % L] = f32c(inp["norm_ffn"][L]); d["npl%d" % L] = f32c(inp["norm_pl"][L])
        d["nfin"] = f32c(inp["norm_final"])
        d.update(per_hh[hh]); d.update({"c_" + k: v for k, v in cst.items()})
        ins.append(d)
    return ins


def build_fused(ex):
    la = Launch()
    P, S = la.P, la.S
    dd = la.inps(ex)
    cs_d = strip(dd, "c_")
    T = SEQ; NT = NT_CORE
    hfull0 = dd["xfull"]
    Wf = [{}, {}]
    for L in range(2):
        for nm in ("w1", "w2", "gate", "plp"):
            Wf[L][nm + "_t"] = dd["%s_%d_t" % (nm, L)]
        Wf[L]["nffn"] = dd["nffn%d" % L]; Wf[L]["npl"] = dd["npl%d" % L]

    def rs_mix(mp, mix):
        for q in range(4):
            P.collective("ReduceScatter", mix[q * 512:(q + 1) * 512, :], mp[q].re("s f t -> (s f) t"), PAIRS, op=ALU.add)
    pin0 = P.dram("pin0", [19 * 128, T])
    for s in range(2):
        with P.scope():
            emit_dense_in(P, S, hfull0[:, s], dd["nmix0"], dd["win0_t"], 19, None, pin0[:, s * NT:(s + 1) * NT], NT)
    yT0 = P.dram("yT0", [1024, T])
    emit_s5(P, S, pin0[0:256], strip(dd, "s5_"), cs_d, yT0[0:256], T, 16)
    emit_ssd(P, S, pin0[256:1024], pin0[1024:1792], pin0[1792:2048], pin0[2048:2304], pin0[2304:2316], strip(dd, "sd_"), cs_d, yT0[256:1024], T, 12, 2)
    zp = P.dram("zp", [2, 256, T]); zr = P.dram("zr", [256, T])
    emit_glu_partial(P, S, yT0[0:256], dd["gluw_t"], zp, T)
    P.collective("ReduceScatter", zr, zp.re("s r t -> (s r) t"), PAIRS, op=ALU.add)
    mp0 = P.dram("mp0", [4, 2, 512, NT]); mix0 = P.dram("mix0", [2048, NT])
    emit_outproj_partial(P, S, yT0, dd["wout0_t"], mp0, T, glu=(zr, dd["glub"]))
    rs_mix(mp0, mix0)
    hb1 = P.dram("hb1", [2048, NT]); h1d0 = P.dram("h1d0", [2048, NT])
    emit_mlp_gate_v2(P, S, 0, dd["xT"], mix0, dd["pT0"], hb1, NT, Wf[0], h1d0)
    hfull1 = P.dram("hfull1", [8, 2, 256, NT])
    for q in range(8):
        P.collective("AllGather", hfull1[q].re("s f t -> (s f) t"), hb1[q * 256:(q + 1) * 256, :], PAIRS)
    pin1 = P.dram("pin1", [24 * 128, T])
    for s in range(2):
        with P.scope():
            emit_dense_in(P, S, hfull1[:, s], dd["nmix1"], dd["win1_t"], 24, None, pin1[:, s * NT:(s + 1) * NT], NT)
    yT1 = P.dram("yT1", [1024, T])
    emit_rwkv(P, S, pin1[0:512], pin1[512:1024], pin1[1024:1536], pin1[1536:1632], pin1[1664:1760], pin1[1792:2048], strip(dd, "rw_"), cs_d, yT1[0:512], T, 4)
    emit_lru(P, S, pin1[2048:2560], pin1[2560:3072], strip(dd, "lr_"), yT1[512:1024], T, 4)
    mp1 = P.dram("mp1", [4, 2, 512, NT]); mix1 = P.dram("mix1", [2048, NT])
    emit_outproj_partial(P, S, yT1, dd["wout1_t"], mp1, T)
    rs_mix(mp1, mix1)
    oT = la.out("outT", [2048, NT])
    h1d1 = P.dram("h1d1", [2048, NT])
    emit_mlp_gate_v2(P, S, 1, hb1, mix1, dd["pT1"], None, NT, Wf[1], h1d1, final=dd["nfin"], outT=oT)
    return la


def kernel(**inp):
    inp = {k: _np.asarray(v) for k, v in inp.items()}
    ins = host_inputs(inp)
    la = build_fused(ins[0])
    res = la.run(ins)
    out = _np.zeros((NB, SEQ, 2048), _np.float32)
    for c in range(8):
        b, hh = divmod(c, 2)
        out[b, hh * NT_CORE:(hh + 1) * NT_CORE, :] = res[c]["outT"].T
    return out
```

```python
import math, contextlib
import numpy as np
import concourse.bass as bass
import concourse.mybir as mybir
from concourse.bass_utils import run_bass_kernel_spmd

F32 = mybir.dt.float32
BF16 = mybir.dt.bfloat16
AF = mybir.ActivationFunctionType
ALU = mybir.AluOpType
AX = mybir.AxisListType

SAME_ENG_SYNC = True
CC_INC = 1


class Buf:
    __slots__ = ("name", "wconds", "rconds", "wsem", "wcount", "rsem", "rcount")

    ALL = []

    def __init__(self, name):
        Buf.ALL.append(self)
        self.name = name
        self.wconds = {}
        self.rconds = {}
        self.wsem = None
        self.wcount = 0
        self.rsem = None
        self.rcount = 0


class V:
    __slots__ = ("ap", "buf")

    def __init__(self, ap, buf):
        self.ap = ap
        self.buf = buf

    def __getitem__(self, key):
        return V(self.ap[key], self.buf)

    def re(self, s, **kw):
        return V(self.ap.rearrange(s, **kw), self.buf)

    def bc(self, shape):
        return V(self.ap.to_broadcast(shape), self.buf)

    def bitcast(self, dt):
        return V(self.ap.bitcast(dt), self.buf)

    @property
    def shape(self):
        return self.ap.shape


class Prog:
    ENG = ("pe", "dve", "act", "pool", "sp")

    def __init__(self, nc, stack):
        self.nc = nc
        Buf.ALL = []
        self.stack = stack
        self.engobj = {"pe": nc.tensor, "dve": nc.vector, "act": nc.scalar,
                       "pool": nc.gpsimd, "sp": nc.sync}
        self.q = {e: [] for e in self.ENG}
        self.cnt = {e: 0 for e in self.ENG}
        self.sems = {}
        self.nsem = 0
        for e in self.ENG:
            self.sems[("eng", e)] = self._newsem("c_" + e)
        self.known = {e: {} for e in self.ENG}
        self.uid = 0
        self.stacks = [stack]
        self.free_sems = []
        self.scope_sems = [[]]
        self.semval = {}
        self.ring_store = {}

    def _newsem(self, name):
        self.nsem += 1
        return self.stack.enter_context(self.nc.semaphore(name + "_%d" % self.nsem))

    def _dma_sem(self, key, name):
        if key not in self.sems:
            if self.free_sems:
                h, v = self.free_sems.pop()
            else:
                h, v = self._newsem("d"), 0
            self.sems[key] = h
            self.semval[key] = v
            self.scope_sems[-1].append(key)
        return self.sems[key]

    @contextlib.contextmanager
    def scope(self):
        es = contextlib.ExitStack()
        self.stacks.append(es)
        self.scope_sems.append([])
        mark = set(self.ring_store.keys())
        try:
            yield
        finally:
            self.barrier()
            for k in list(self.ring_store.keys()):
                if k not in mark:
                    del self.ring_store[k]
            for key in self.scope_sems.pop():
                self.free_sems.append((self.sems.pop(key), self.semval.pop(key)))
                for e in self.ENG:
                    self.known[e].pop(key, None)
            self.stacks.pop()
            es.close()

    def barrier(self):
        conds = {("eng", e): self.cnt[e] for e in self.ENG if self.cnt[e] > 0}
        for key, v in self.semval.items():
            if v > 0:
                conds[key] = v
        for e in self.ENG:
            waits = {}
            kn = self.known[e]
            for k, val in conds.items():
                if k == ("eng", e):
                    if e == "sp":
                        continue
                if kn.get(k, 0) < val:
                    waits[k] = val
            wl = self._emit_waits(e, waits)

            def thunk(en, wl=wl):
                for s_, v_ in wl:
                    en.wait_ge(s_, v_)
            self.q[e].append(thunk)
        for b in Buf.ALL:
            b.wconds = {}
            b.rconds = {}

    def ring(self, name, n, shape, dt=F32):
        if name not in self.ring_store:
            self.ring_store[name] = [[self.sbuf("%s_%d" % (name, i), shape, dt) for i in range(n)], 0]
        r = self.ring_store[name]
        b = r[0][r[1] % n]
        r[1] += 1
        return b

    def sbuf(self, name, shape, dt=F32):
        self.uid += 1
        t = self.stacks[-1].enter_context(self.nc.sbuf_tensor("%s_u%d" % (name, self.uid), list(shape), dt))
        return V(t.ap() if hasattr(t, "ap") and callable(getattr(t, "ap")) else t[:], Buf(name))

    def psum(self, name, shape, dt=F32):
        t = self.stack.enter_context(self.nc.psum_tensor(name, list(shape), dt))
        return V(t.ap() if hasattr(t, "ap") and callable(getattr(t, "ap")) else t[:], Buf(name))

    def dram(self, name, shape, dt=F32, kind="Internal"):
        t = self.nc.dram_tensor(name, list(shape), dt, kind=kind)
        return V(t.ap(), Buf(name))

    def alias(self, v, name):
        return V(v.ap, Buf(name))

    def _need(self, eng, conds, waits):
        kn = self.known[eng]
        for k, val in conds.items():
            if k not in self.sems:
                continue
            if k == ("eng", eng):
                if eng == "pe" or not SAME_ENG_SYNC:
                    continue
            if kn.get(k, 0) >= val:
                continue
            if waits.get(k, 0) < val:
                waits[k] = val

    def _emit_waits(self, eng, waits):
        kn = self.known[eng]
        out = []
        for k, val in waits.items():
            kn[k] = max(kn.get(k, 0), val)
            out.append((self.sems[k], val))
        return out

    def op(self, eng, fn, reads=(), writes=()):
        waits = {}
        for v in reads:
            self._need(eng, v.buf.wconds, waits)
        for v in writes:
            self._need(eng, v.buf.wconds, waits)
            self._need(eng, v.buf.rconds, waits)
        wl = self._emit_waits(eng, waits)
        self.cnt[eng] += 1
        n = self.cnt[eng]
        k = ("eng", eng)
        sem = self.sems[k]

        def thunk(e, wl=wl, fn=fn, sem=sem):
            for s, val in wl:
                e.wait_ge(s, val)
            fn(e).then_inc(sem, 1)
        self.q[eng].append(thunk)
        for v in reads:
            v.buf.rconds[k] = n
        for v in writes:
            v.buf.wconds = {k: n}
            v.buf.rconds = {}
        return n

    def dma(self, queue, out, in_, **kw):
        eng = queue
        waits = {}
        self._need(eng, in_.buf.wconds, waits)
        own_w = ("w", id(out.buf))
        for kk, val in out.buf.wconds.items():
            if kk == own_w:
                continue
            self._need(eng, {kk: val}, waits)
        self._need(eng, out.buf.rconds, waits)
        wl = self._emit_waits(eng, waits)
        b = out.buf
        key = ("w", id(b))
        sem = self._dma_sem(key, b.name)
        self.semval[key] += 16
        val = self.semval[key]
        b.wcount = val
        self._keep = getattr(self, "_keep", [])
        self._keep.append(b)
        rb = in_.buf
        rkey = None

        def thunk(e, wl=wl, sem=sem, o=out.ap, i=in_.ap, kw=kw):
            for s, v_ in wl:
                e.wait_ge(s, v_)
            e.dma_start(out=o, in_=i, **kw).then_inc(sem, 16)
        self.q[eng].append(thunk)
        if own_w in out.buf.wconds or not out.buf.wconds or True:
            newc = {key: val}
            out.buf.wconds = newc
            out.buf.rconds = {}
        in_.buf.rconds[key] = max(in_.buf.rconds.get(key, 0), val)

    def collective(self, kind, out, in_, groups, op=None):
        eng = "pool"
        waits = {}
        self._need(eng, in_.buf.wconds, waits)
        self._need(eng, out.buf.wconds, waits)
        self._need(eng, out.buf.rconds, waits)
        wl = self._emit_waits(eng, waits)
        key = ("cc", id(out.buf))
        if key not in self.sems:
            self.sems[key] = self._newsem("cc")
            self.semval[key] = 0
        self.semval[key] += CC_INC
        val = self.semval[key]
        sem = self.sems[key]
        self._keep = getattr(self, "_keep", [])
        self._keep.append(out.buf)
        op = ALU.bypass if op is None else op

        def thunk(e, wl=wl, sem=sem, o=out.ap, i=in_.ap):
            for s_, v_ in wl:
                e.wait_ge(s_, v_)
            e.collective_compute(kind, op, replica_groups=groups, ins=[i], outs=[o]).then_inc(sem, CC_INC)
        self.q[eng].append(thunk)
        out.buf.wconds = {key: val}
        out.buf.rconds = {}
        in_.buf.rconds[key] = val

    def wait_all(self, eng, views):
        waits = {}
        for v in views:
            self._need(eng, v.buf.wconds, waits)
        wl = self._emit_waits(eng, waits)

        def thunk(e, wl=wl):
            for s, v_ in wl:
                e.wait_ge(s, v_)
        self.q[eng].append(thunk)

    def finish(self):
        nc = self.nc
        with nc.Block() as block:
            @block.tensor
            def _(e):
                for t in self.q["pe"]:
                    t(e)

            @block.vector
            def _(e):
                for t in self.q["dve"]:
                    t(e)

            @block.scalar
            def _(e):
                for t in self.q["act"]:
                    t(e)

            @block.gpsimd
            def _(e):
                for t in self.q["pool"]:
                    t(e)

            @block.sync
            def _(e):
                for t in self.q["sp"]:
                    t(e)

D = 2048
KT_D = 16
EPS = 1e-6
CH = 512
MMDT = BF16


class Shared:
    def __init__(self, P):
        self.P = P
        self.ps = [P.psum("psr%d" % i, [128, 512]) for i in range(8)]
        self.pi = 0
        self.ones = P.sbuf("ones", [128, 128])
        P.op("dve", lambda e: e.memset(self.ones.ap, 1.0), writes=[self.ones])
        self.ones_r = P.sbuf("ones_r", [128, 128])
        P.op("dve", lambda e: e.tensor_copy(out=self.ones_r.ap.bitcast(mybir.dt.float32r), in_=self.ones.ap), reads=[self.ones], writes=[self.ones_r])
        self.epsc = P.sbuf("epsc", [128, 1])
        P.op("dve", lambda e: e.memset(self.epsc.ap, EPS), writes=[self.epsc])
        self.rr = {}
        self.castn = 0

    def next_psum(self):
        p = self.ps[self.pi % 8]
        self.pi += 1
        return p

    def ring(self, name, n, shape, dt=F32):
        return self.P.ring(name, n, shape, dt)


def load_cols(P, dst, vec_dram, KT):
    with P.nc.allow_non_contiguous_dma(reason="tiny param vector"):
        pass
    P.dma("sp", dst, vec_dram.re("(k p) -> p k", p=128), allow_slow_non_contiguous=True)


def rmsnorm_T(P, S, src, gcol, out, KT, NT, out_scale_extra=None):
    pss = S.next_psum()
    for k in range(KT):
        sq = S.ring("sq", 3, [128, CH])
        P.op("act", lambda e, sq=sq, k=k: e.activation(out=sq.ap[:, 0:NT].bitcast(mybir.dt.float32r), in_=src.ap[:, k, :], func=AF.Square),
             reads=[src], writes=[sq])
        P.op("pe", lambda e, sq=sq, k=k: e.matmul(pss.ap[:, 0:NT], lhsT=S.ones_r.ap.bitcast(mybir.dt.float32r),
                                                 rhs=sq.ap[:, 0:NT].bitcast(mybir.dt.float32r),
                                                 start=(k == 0), stop=(k == KT - 1)),
             reads=[S.ones_r, sq], writes=[pss])
    rs = S.ring("rstd", 2, [128, CH])
    P.op("act", lambda e: e.activation(out=rs.ap[:, 0:NT], in_=pss.ap[:, 0:NT], func=AF.Sqrt,
                                      bias=S.epsc.ap, scale=1.0 / (KT * 128)),
         reads=[pss, S.epsc], writes=[rs])
    P.op("dve", lambda e: e.reciprocal(out=rs.ap[:, 0:NT], in_=rs.ap[:, 0:NT]), reads=[rs], writes=[rs])
    for k in range(KT):
        P.op("dve", lambda e, k=k: e.scalar_tensor_tensor(out=out.ap[:, k, :], in0=src.ap[:, k, :],
                                                         scalar=gcol.ap[:, k:k + 1], in1=rs.ap[:, 0:NT],
                                                         op0=ALU.mult, op1=ALU.mult),
             reads=[src, gcol, rs], writes=[out])


def stream_mm(P, S, Wt, KT, NCB, rhs_fn, nchunk, NTc, evac, wname="wb", nwb=3, Ms=None, pre=None):
    wbs = {}

    def prep(c):
        wb = S.ring(wname + str(KT), nwb, [128, KT, 128], MMDT)
        for k0 in range(0, KT, 16):
            k1 = min(KT, k0 + 16)
            stg = S.ring("wstg", 3, [128, 16, 128])
            P.dma("sp", stg[:, 0:k1 - k0, :], Wt[c][:, k0:k1, :])
            P.op("dve", lambda e, stg=stg, wb=wb, k0=k0, k1=k1: e.tensor_copy(out=wb.ap[:, k0:k1, :], in_=stg.ap[:, 0:k1 - k0, :]),
                 reads=[stg], writes=[wb])
        wbs[c] = wb

    prep(0)
    for c in range(NCB):
        if c + 1 < NCB:
            prep(c + 1)
        wb = wbs.pop(c)
        M = 128 if Ms is None else Ms[c]
        if pre is not None:
            pre(c)
        for j in range(nchunk):
            ps = S.next_psum()
            for k in range(KT):
                r = rhs_fn(k, j)
                P.op("pe", lambda e, wb=wb, ps=ps, r=r, k=k, M=M: e.matmul(
                    ps.ap[0:M, 0:NTc], lhsT=wb.ap[:, k, 0:M], rhs=r.ap, start=(k == 0), stop=(k == KT - 1)),
                    reads=[wb, r], writes=[ps])
            evac(c, j, ps, M)


def emit_dense_in(P, S, hT, gvec, Wt, NCB, Ms, projT, NT):
    nch = NT // CH
    gcol = P.sbuf("gcolA", [128, KT_D])
    load_cols(P, gcol, gvec, KT_D)
    hn = P.sbuf("hnA", [128, KT_D, NT], MMDT)
    hnj = [P.alias(hn, "hnA_%d" % j) for j in range(nch)]
    for j in range(nch):
        ht = S.ring("htA", 2, [128, KT_D, CH])
        for q in range(8):
            P.dma("sp", ht[:, 2 * q:2 * q + 2, :], hT[q].re("(k p) n -> p k n", p=128)[:, :, j * CH:(j + 1) * CH])
        rmsnorm_T(P, S, ht, gcol, hnj[j][:, :, j * CH:(j + 1) * CH], KT_D, CH)

    def rhs_fn(k, j):
        return hnj[j][:, k, j * CH:(j + 1) * CH]

    def evac(c, j, ps, M):
        ob = S.ring("evA", 4, [128, CH])
        P.op("act", lambda e: e.copy(out=ob.ap[0:M, :], in_=ps.ap[0:M, :]), reads=[ps], writes=[ob])
        P.dma("act", projT[c * 128:c * 128 + M, j * CH:(j + 1) * CH], ob[0:M, :])

    stream_mm(P, S, Wt, KT_D, NCB, rhs_fn, nch, CH, evac, Ms=Ms)


def emit_dense_out(P, S, L, hT, yT, pT, hT_out, NT, W, glu=None, final=None, outT=None, h1T=None):
    nch = NT // CH
    gF = P.sbuf("gF%d" % L, [128, KT_D]); load_cols(P, gF, W["nffn"], KT_D)
    gP = P.sbuf("gP%d" % L, [128, KT_D]); load_cols(P, gP, W["npl"], KT_D)
    if final is not None:
        gN = P.sbuf("gN%d" % L, [128, KT_D]); load_cols(P, gN, final, KT_D)
    yTv = yT.re("(k p) n -> p k n", p=128)
    hTv = hT.re("(k p) n -> p k n", p=128)
    h1Tv = h1T.re("(k p) n -> p k n", p=128)
    hoTv = hT_out.re("(k p) n -> p k n", p=128) if hT_out is not None else None
    sc1 = P.scope(); sc1.__enter__()
    yb = P.sbuf("ybC", [128, KT_D, NT], MMDT)
    k_start = 0
    if glu is not None:
        gluw_t, glub = glu
        gb = P.sbuf("glub_sb", [128, 4]); load_cols(P, gb, glub, 4)
        actf = P.sbuf("actf", [128, 4, NT])
        P.dma("sp", actf, yTv[:, 0:4, :])
        actb = P.sbuf("actb", [128, 4, NT], MMDT)
        P.op("dve", lambda e: e.tensor_copy(out=actb.ap, in_=actf.ap), reads=[actf], writes=[actb])

        def evac_glu(c, j, ps, M):
            sg = S.ring("sgl", 2, [128, CH])
            P.op("act", lambda e: e.activation(out=sg.ap, in_=ps.ap, func=AF.Sigmoid, bias=gb.ap[:, c:c + 1]),
                 reads=[ps, gb], writes=[sg])
            P.op("dve", lambda e: e.tensor_tensor(out=yb.ap[:, c, j * CH:(j + 1) * CH],
                                                  in0=actf.ap[:, c, j * CH:(j + 1) * CH], in1=sg.ap, op=ALU.mult),
                 reads=[actf, sg], writes=[yb])
        stream_mm(P, S, gluw_t, 4, 4, lambda k, j: actb[:, k, j * CH:(j + 1) * CH], nch, CH, evac_glu)
        k_start = 4
    for k in range(k_start, KT_D, 4):
        P.dma("pool", yb[:, k:k + 4, :], yTv[:, k:k + 4, :])

    def pre_o(c):
        pass

    def evac_o(c, j, ps, M):
        hb = S.ring("hbC", 3, [128, CH])
        P.dma("sp", hb, hT[c * 128:(c + 1) * 128, j * CH:(j + 1) * CH])
        P.op("dve", lambda e: e.tensor_tensor(out=hb.ap, in0=ps.ap, in1=hb.ap, op=ALU.add),
             reads=[ps, hb], writes=[hb])
        P.dma("sp", h1T[c * 128:(c + 1) * 128, j * CH:(j + 1) * CH], hb)
    stream_mm(P, S, W["out_t"], KT_D, 16, lambda k, j: yb[:, k, j * CH:(j + 1) * CH], nch, CH, evac_o)
    sc1.__exit__(None, None, None)
    sc2 = P.scope(); sc2.__enter__()
    plp = P.sbuf("plpC", [128, 16, 2, 128], MMDT)
    for c in range(16):
        P.dma("pool", plp[:, c, :, :], W["plp_t"][c])
    pTv = pT.re("(k p) n -> p k n", p=128)
    for j in range(nch):
        sl = slice(j * CH, (j + 1) * CH)
        h1 = S.ring("h1C", 1, [128, KT_D, CH])
        P.dma("sp", h1, h1Tv[:, :, sl])
        hn = S.ring("hnC", 1, [128, KT_D, CH], MMDT)
        rmsnorm_T(P, S, h1, gF, hn, KT_D, CH)
        hid = S.ring("hidC", 1, [128, 64, CH], MMDT)

        def evac1(c, jj, ps, M):
            rl = S.ring("rlC", 3, [128, CH])
            P.op("act", lambda e: e.activation(out=rl.ap, in_=ps.ap, func=AF.Relu), reads=[ps], writes=[rl])
            P.op("pool", lambda e: e.tensor_tensor(out=hid.ap[:, c, :], in0=rl.ap, in1=rl.ap, op=ALU.mult),
                 reads=[rl], writes=[hid])
        stream_mm(P, S, W["w1_t"], KT_D, 64, lambda k, jj: hn[:, k, :], 1, CH, evac1)

        def evac2(c, jj, ps, M):
            P.op("dve", lambda e: e.tensor_tensor(out=h1.ap[:, c, :], in0=ps.ap, in1=h1.ap[:, c, :], op=ALU.add),
                 reads=[ps, h1], writes=[h1])
        stream_mm(P, S, W["w2_t"], 64, 16, lambda k, jj: hid[:, k, :], 1, CH, evac2)
        rmsnorm_T(P, S, h1, gP, hn, KT_D, CH)
        pb = S.ring("pbC", 1, [128, 2, CH], MMDT)
        P.dma("pool", pb, pTv[:, :, sl])

        def evac3(c, jj, ps, M):
            psp = S.next_psum()
            for k in range(2):
                P.op("pe", lambda e, k=k: e.matmul(psp.ap, lhsT=plp.ap[:, c, k, :], rhs=pb.ap[:, k, :],
                                                   start=(k == 0), stop=(k == 1)),
                     reads=[plp, pb], writes=[psp])
            sg = S.ring("sgC", 2, [128, CH])
            P.op("act", lambda e: e.activation(out=sg.ap, in_=ps.ap, func=AF.Sigmoid), reads=[ps], writes=[sg])
            P.op("dve", lambda e: e.tensor_tensor(out=sg.ap, in0=psp.ap, in1=sg.ap, op=ALU.mult),
                 reads=[psp, sg], writes=[sg])
            P.op("pool", lambda e: e.tensor_tensor(out=h1.ap[:, c, :], in0=h1.ap[:, c, :], in1=sg.ap, op=ALU.add),
                 reads=[h1, sg], writes=[h1])
        stream_mm(P, S, W["gate_t"], KT_D, 16, lambda k, jj: hn[:, k, :], 1, CH, evac3)
        if hT_out is not None:
            P.dma("sp", hoTv[:, :, sl], h1)
        if final is not None:
            rmsnorm_T(P, S, h1, gN, h1, KT_D, CH)
            P.dma("sp", outT.re("(k p) n -> p k n", p=128)[:, :, sl], h1)
    sc2.__exit__(None, None, None)


def emit_glu_partial(P, S, actT, gluw_t, zp, T):
    nch = T // CH
    with P.scope():
        ab = P.sbuf("glu_ab", [128, 2, T], MMDT)
        av = actT.re("(k p) n -> p k n", p=128)
        for k in range(2):
            for j0 in range(0, T, 2048):
                P.dma("pool", ab[:, k, j0:j0 + 2048], av[:, k, j0:j0 + 2048])

        def evac(c, j, ps, M):
            ob = S.ring("glu_ev", 4, [128, CH])
            P.op("act", lambda e: e.copy(out=ob.ap, in_=ps.ap), reads=[ps], writes=[ob])
            P.dma("act", zp[c // 2, (c % 2) * 128:(c % 2 + 1) * 128, j * CH:(j + 1) * CH], ob)
        stream_mm(P, S, gluw_t, 2, 4, lambda k, j: ab[:, k, j * CH:(j + 1) * CH], nch, CH, evac)


def emit_outproj_partial(P, S, yT, wout_t, mp, T, glu=None):
    nch = T // CH
    half = T // 2
    with P.scope():
        yb = P.sbuf("op_yb", [128, 8, T], MMDT)
        yv = yT.re("(k p) n -> p k n", p=128)
        k0 = 0
        if glu is not None:
            zT, glub = glu
            gb = P.sbuf("op_gb", [128, 2]); load_cols(P, gb, glub, 2)
            zv = zT.re("(k p) n -> p k n", p=128)
            for k in range(2):
                for j in range(nch):
                    sl = slice(j * CH, (j + 1) * CH)
                    a = S.ring("op_a", 3, [128, CH]); z = S.ring("op_z", 3, [128, CH])
                    P.dma("sp", a, yv[:, k, sl]); P.dma("sp", z, zv[:, k, sl])
                    P.op("act", lambda e, z=z, k=k: e.activation(out=z.ap, in_=z.ap, func=AF.Sigmoid, bias=gb.ap[:, k:k + 1]), reads=[z, gb], writes=[z])
                    P.op("dve", lambda e, a=a, z=z, k=k, sl=sl: e.tensor_tensor(out=yb.ap[:, k, sl], in0=a.ap, in1=z.ap, op=ALU.mult), reads=[a, z], writes=[yb])
            k0 = 2
        for k in range(k0, 8):
            for j0 in range(0, T, 2048):
                P.dma("pool", yb[:, k, j0:j0 + 2048], yv[:, k, j0:j0 + 2048])

        def evac(c, j, ps, M):
            ob = S.ring("op_ev", 4, [128, CH])
            P.op("act", lambda e: e.copy(out=ob.ap, in_=ps.ap), reads=[ps], writes=[ob])
            t0 = j * CH
            P.dma("act", mp[c // 4, t0 // half, (c % 4) * 128:(c % 4 + 1) * 128, t0 % half:t0 % half + CH], ob)
        stream_mm(P, S, wout_t, 8, 16, lambda k, j: yb[:, k, j * CH:(j + 1) * CH], nch, CH, evac)


def emit_mlp_gate(P, S, L, hT, mixT, pT, hT_out, NT, W, final=None, outT=None):
    nch = NT // CH
    with P.scope():
        gF = P.sbuf("gF%d" % L, [128, KT_D]); load_cols(P, gF, W["nffn"], KT_D)
        gP = P.sbuf("gP%d" % L, [128, KT_D]); load_cols(P, gP, W["npl"], KT_D)
        if final is not None:
            gN = P.sbuf("gN%d" % L, [128, KT_D]); load_cols(P, gN, final, KT_D)
        hTv = hT.re("(k p) n -> p k n", p=128)
        mTv = mixT.re("(k p) n -> p k n", p=128)
        hoTv = hT_out.re("(k p) n -> p k n", p=128) if hT_out is not None else None
        plp = P.sbuf("plpC", [128, 16, 2, 128], MMDT)
        for c in range(16):
            P.dma("pool", plp[:, c, :, :], W["plp_t"][c])
        pTv = pT.re("(k p) n -> p k n", p=128)

        def tile_body(j):
            sl = slice(j * CH, (j + 1) * CH)
            h1 = S.ring("h1C", 1, [128, KT_D, CH])
            hn = S.ring("hnC", 1, [128, KT_D, CH], MMDT)
            hid = S.ring("hidC", 1, [128, 64, CH], MMDT)
            P.dma("sp", h1, hTv[:, :, sl])
            for q in range(8):
                mt = S.ring("mixC", 2, [128, 2, CH])
                P.dma("sp", mt, mTv[:, q * 2:(q + 1) * 2, sl])
                P.op("dve", lambda e, mt=mt, q=q: e.tensor_tensor(out=h1.ap[:, q * 2:(q + 1) * 2, :], in0=h1.ap[:, q * 2:(q + 1) * 2, :], in1=mt.ap, op=ALU.add),
                     reads=[h1, mt], writes=[h1])
            rmsnorm_T(P, S, h1, gF, hn, KT_D, CH)

            def evac1(c, jj, ps, M):
                rl = S.ring("rlC", 3, [128, CH])
                P.op("act", lambda e: e.activation(out=rl.ap, in_=ps.ap, func=AF.Relu), reads=[ps], writes=[rl])
                P.op("pool", lambda e: e.tensor_tensor(out=hid.ap[:, c, :], in0=rl.ap, in1=rl.ap, op=ALU.mult), reads=[rl], writes=[hid])
            stream_mm(P, S, W["w1_t"], KT_D, 64, lambda k, jj: hn[:, k, :], 1, CH, evac1)

            def evac2(c, jj, ps, M):
                P.op("dve", lambda e: e.tensor_tensor(out=h1.ap[:, c, :], in0=ps.ap, in1=h1.ap[:, c, :], op=ALU.add), reads=[ps, h1], writes=[h1])
            stream_mm(P, S, W["w2_t"], 64, 16, lambda k, jj: hid[:, k, :], 1, CH, evac2)
            rmsnorm_T(P, S, h1, gP, hn, KT_D, CH)
            pb = S.ring("pbC", 1, [128, 2, CH], MMDT)
            P.dma("pool", pb, pTv[:, :, sl])

            def evac3(c, jj, ps, M):
                psp = S.next_psum()
                for k in range(2):
                    P.op("pe", lambda e, k=k: e.matmul(psp.ap, lhsT=plp.ap[:, c, k, :], rhs=pb.ap[:, k, :], start=(k == 0), stop=(k == 1)),
                         reads=[plp, pb], writes=[psp])
                sg = S.ring("sgC", 2, [128, CH])
                P.op("act", lambda e: e.activation(out=sg.ap, in_=ps.ap, func=AF.Sigmoid), reads=[ps], writes=[sg])
                P.op("dve", lambda e: e.tensor_tensor(out=sg.ap, in0=psp.ap, in1=sg.ap, op=ALU.mult), reads=[psp, sg], writes=[sg])
                P.op("pool", lambda e: e.tensor_tensor(out=h1.ap[:, c, :], in0=h1.ap[:, c, :], in1=sg.ap, op=ALU.add), reads=[h1, sg], writes=[h1])
            stream_mm(P, S, W["gate_t"], KT_D, 16, lambda k, jj: hn[:, k, :], 1, CH, evac3)
            if hT_out is not None:
                P.dma("sp", hoTv[:, :, sl], h1)
            if final is not None:
                rmsnorm_T(P, S, h1, gN, h1, KT_D, CH)
                P.dma("sp", outT.re("(k p) n -> p k n", p=128)[:, :, sl], h1)

        for j in range(nch):
            tile_body(j)


def emit_mlp_gate_v2(P, S, L, hT, mixT, pT, hT_out, NT, W, h1d, final=None, outT=None):
    nch = NT // CH
    hTv = hT.re("(k p) n -> p k n", p=128)
    mTv = mixT.re("(k p) n -> p k n", p=128)
    h1v = h1d.re("(k p) n -> p k n", p=128)
    hreg = [[P.alias(h1d, "h1d_%d_%d" % (c, j)) for j in range(nch)] for c in range(16)]
    houts = None
    if hT_out is not None:
        houts = [[P.alias(hT_out, "hout_%d" % c)] * nch for c in range(16)]

    def norm_pass(src_v, gcol, dst_fn, add_v=None, store_v=None, tag=""):
        with P.scope():
            for j in range(nch):
                sl = slice(j * CH, (j + 1) * CH)
                ht = S.ring("npH", 2, [128, KT_D, CH])
                P.dma("sp", ht, src_v[:, :, sl])
                if add_v is not None:
                    for q in range(8):
                        mt = S.ring("npM", 2, [128, 2, CH])
                        P.dma("sp", mt, add_v[:, q * 2:(q + 1) * 2, sl])
                        P.op("dve", lambda e, mt=mt, q=q, ht=ht: e.tensor_tensor(out=ht.ap[:, q * 2:(q + 1) * 2, :], in0=ht.ap[:, q * 2:(q + 1) * 2, :],
                             in1=mt.ap, op=ALU.add), reads=[ht, mt], writes=[ht])
                if store_v is not None:
                    P.dma("sp", store_v[:, :, sl], ht)
                rmsnorm_T(P, S, ht, gcol, dst_fn(j), KT_D, CH)

    with P.scope():
        gF = P.sbuf("gF%d" % L, [128, KT_D]); load_cols(P, gF, W["nffn"], KT_D)
        gP = P.sbuf("gP%d" % L, [128, KT_D]); load_cols(P, gP, W["npl"], KT_D)
        hn = P.sbuf("hnM", [128, KT_D, NT], MMDT)
        norm_pass(hTv, gF, lambda j: hn[:, :, j * CH:(j + 1) * CH], add_v=mTv, store_v=h1v)
        with P.scope():
            hid = P.sbuf("hidM", [128, 16, NT], MMDT)
            for q in range(4):
                def evac1(c, j, ps, M):
                    rl = S.ring("rlM", 3, [128, CH])
                    P.op("act", lambda e: e.activation(out=rl.ap, in_=ps.ap, func=AF.Relu), reads=[ps], writes=[rl])
                    P.op("pool", lambda e: e.tensor_tensor(out=hid.ap[:, c, j * CH:(j + 1) * CH], in0=rl.ap, in1=rl.ap, op=ALU.mult), reads=[rl], writes=[hid])
                stream_mm(P, S, W["w1_t"][q * 16:(q + 1) * 16], KT_D, 16, lambda k, j: hn[:, k, j * CH:(j + 1) * CH], nch, CH, evac1)

                pend = {}

                def pre2(c):
                    for j in range(nch):
                        hb = S.ring("hbM", 8, [128, CH])
                        reg = V(h1d.ap[c * 128:(c + 1) * 128, j * CH:(j + 1) * CH], hreg[c][j].buf)
                        P.dma("pool", hb, reg)
                        pend[(c, j)] = (hb, reg)

                def evac2(c, j, ps, M):
                    hb, reg = pend.pop((c, j))
                    P.op("dve", lambda e: e.tensor_tensor(out=hb.ap, in0=ps.ap, in1=hb.ap, op=ALU.add), reads=[ps, hb], writes=[hb])
                    P.dma("act", reg, hb)
                stream_mm(P, S, W["w2_t"][:, :, q * 16:(q + 1) * 16, :], KT_D, 16, lambda k, j: hid[:, k, j * CH:(j + 1) * CH], nch, CH, evac2, pre=pre2)
        norm_pass(h1v, gP, lambda j: hn[:, :, j * CH:(j + 1) * CH])
        with P.scope():
            plp = P.sbuf("plpM", [128, 16, 2, 128], MMDT)
            for c in range(16):
                P.dma("pool", plp[:, c, :, :], W["plp_t"][c])
            pb = P.sbuf("pbM", [128, 2, NT], MMDT)
            pTv = pT.re("(k p) n -> p k n", p=128)
            for k in range(2):
                P.dma("pool", pb[:, k, :], pTv[:, k, :])
            dst = hT_out if hT_out is not None else h1d
            dregs = houts if hT_out is not None else hreg

            pend3 = {}

            def pre3(c):
                for j in range(nch):
                    sl = slice(j * CH, (j + 1) * CH)
                    hb = S.ring("hbG", 8, [128, CH])
                    P.dma("pool", hb, V(h1d.ap[c * 128:(c + 1) * 128, sl], hreg[c][j].buf))
                    pend3[(c, j)] = hb

            def evac3(c, j, ps, M):
                sl = slice(j * CH, (j + 1) * CH)
                psp = S.next_psum()
                for k in range(2):
                    P.op("pe", lambda e, k=k: e.matmul(psp.ap, lhsT=plp.ap[:, c, k, :], rhs=pb.ap[:, k, sl], start=(k == 0), stop=(k == 1)),
                         reads=[plp, pb], writes=[psp])
                sg = S.ring("sgM", 2, [128, CH])
                P.op("act", lambda e: e.activation(out=sg.ap, in_=ps.ap, func=AF.Sigmoid), reads=[ps], writes=[sg])
                P.op("dve", lambda e: e.tensor_tensor(out=sg.ap, in0=psp.ap, in1=sg.ap, op=ALU.mult), reads=[psp, sg], writes=[sg])
                hb = pend3.pop((c, j))
                P.op("pool", lambda e: e.tensor_tensor(out=hb.ap, in0=hb.ap, in1=sg.ap, op=ALU.add), reads=[hb, sg], writes=[hb])
                P.dma("act", V(dst.ap[c * 128:(c + 1) * 128, sl], dregs[c][j].buf), hb)
            stream_mm(P, S, W["gate_t"], KT_D, 16, lambda k, j: hn[:, k, j * CH:(j + 1) * CH], nch, CH, evac3, pre=pre3)
    if final is not None:
        with P.scope():
            gN = P.sbuf("gN%d" % L, [128, KT_D]); load_cols(P, gN, final, KT_D)
            oTv = outT.re("(k p) n -> p k n", p=128)
            for j in range(nch):
                sl = slice(j * CH, (j + 1) * CH)
                ht = S.ring("npF", 2, [128, KT_D, CH])
                P.dma("sp", ht, h1v[:, :, sl])
                rmsnorm_T(P, S, ht, gN, ht, KT_D, CH)
                P.dma("sp", oTv[:, :, sl], ht)
CH = 512
PI = math.pi
FAST32 = True


def r32(ap):
    return ap.bitcast(mybir.dt.float32r) if FAST32 else ap


def tt(P, eng, out, a, b, op):
    P.op(eng, lambda e: e.tensor_tensor(out=out.ap, in0=a.ap, in1=b.ap, op=op), reads=[a, b], writes=[out])


def ts(P, eng, out, a, s1, op0, s2=None, op1=None):
    rd = [a] + [x for x in (s1, s2) if isinstance(x, V)]
    g = lambda x: x.ap if isinstance(x, V) else x
    if op1 is None:
        P.op(eng, lambda e: e.tensor_scalar(out=out.ap, in0=a.ap, scalar1=g(s1), scalar2=None, op0=op0), reads=rd, writes=[out])
    else:
        P.op(eng, lambda e: e.tensor_scalar(out=out.ap, in0=a.ap, scalar1=g(s1), scalar2=g(s2), op0=op0, op1=op1), reads=rd, writes=[out])


def act(P, out, a, func, scale=None, bias=None):
    rd = [a] + [x for x in (scale, bias) if isinstance(x, V)]
    kw = {}
    if scale is not None:
        kw["scale"] = scale.ap if isinstance(scale, V) else scale
    if bias is not None:
        kw["bias"] = bias.ap if isinstance(bias, V) else bias
    P.op("act", lambda e: e.activation(out=out.ap, in_=a.ap, func=func, **kw), reads=rd, writes=[out])


def sin_reduced(P, tmp, out, ang, shift, zero_col):
    f1, f2, i1 = tmp
    ts(P, "dve", f1, ang, shift, ALU.add)
    ts(P, "dve", f2, f1, 1.0 / (2 * PI), ALU.mult)
    P.op("dve", lambda e: e.tensor_copy(out=i1.ap, in_=f2.ap), reads=[f2], writes=[i1])
    P.op("dve", lambda e: e.tensor_copy(out=f2.ap, in_=i1.ap), reads=[i1], writes=[f2])
    P.op("dve", lambda e: e.scalar_tensor_tensor(out=f1.ap, in0=f2.ap, scalar=-2 * PI, in1=f1.ap, op0=ALU.mult, op1=ALU.add),
         reads=[f1, f2], writes=[f1])
    ts(P, "dve", f2, f1, PI, ALU.is_gt, -2 * PI, ALU.mult)
    tt(P, "dve", f1, f1, f2, ALU.add)
    ts(P, "dve", f2, f1, -PI, ALU.is_lt, 2 * PI, ALU.mult)
    tt(P, "dve", f1, f1, f2, ALU.add)
    act(P, out, f1, AF.Sin)


def s5_abar(P, lr, li, lstep, shape, tag):
    mk = lambda n, dt=F32: P.sbuf("s5%s_%s" % (tag, n), shape, dt)
    step = mk("step"); mag = mk("mag"); ang = mk("ang"); ar = mk("ar"); ai = mk("ai")
    f1 = mk("f1"); f2 = mk("f2"); i1 = mk("i1", mybir.dt.int32)
    act(P, step, lstep, AF.Exp)
    tt(P, "dve", mag, lr, step, ALU.mult)
    act(P, mag, mag, AF.Exp)
    tt(P, "dve", ang, li, step, ALU.mult)
    sin_reduced(P, (f1, f2, i1), ai, ang, 0.0, None)
    sin_reduced(P, (f1, f2, i1), ar, ang, PI / 2, None)
    tt(P, "dve", ar, ar, mag, ALU.mult)
    tt(P, "dve", ai, ai, mag, ALU.mult)
    return ar, ai, (f1, f2)


def emit_s5(P, S, uT, prm, cst, yT, T, NG):
    nch = T // CH
    nlev = int(round(math.log2(T)))
    assert 2 ** nlev == T
    with P.scope():
        ld = lambda name, shape, src: (lambda t: (P.dma("sp", t, src), t)[1])(P.sbuf(name, shape))
        shp = [16, NG, 64]
        lr = ld("s5r_lr", shp, prm["lr_row"]); li = ld("s5r_li", shp, prm["li_row"]); ls = ld("s5r_ls", shp, prm["ls_row"])
        bre = ld("s5r_bre", shp, prm["bT_re"]); bim = ld("s5r_bim", shp, prm["bT_im"])
        ar, ai, (f1, f2) = s5_abar(P, lr, li, ls, shp, "r")
        den = P.sbuf("s5r_den", shp); cre = P.sbuf("s5r_cre", shp); cim = P.sbuf("s5r_cim", shp)
        tt(P, "dve", den, lr, lr, ALU.mult); tt(P, "dve", f1, li, li, ALU.mult); tt(P, "dve", den, den, f1, ALU.add)
        P.op("dve", lambda e: e.reciprocal(out=den.ap, in_=den.ap), reads=[den], writes=[den])
        ts(P, "dve", ar, ar, -1.0, ALU.add)
        tt(P, "dve", cre, ar, lr, ALU.mult); tt(P, "dve", f1, ai, li, ALU.mult); tt(P, "dve", cre, cre, f1, ALU.add)
        tt(P, "dve", cre, cre, den, ALU.mult)
        tt(P, "dve", cim, ai, lr, ALU.mult); tt(P, "dve", f1, ar, li, ALU.mult); tt(P, "dve", cim, cim, f1, ALU.subtract)
        tt(P, "dve", cim, cim, den, ALU.mult)
        BT = P.sbuf("s5_BT", [16, NG, 128])
        tt(P, "dve", f1, cre, bre, ALU.mult); tt(P, "dve", f2, cim, bim, ALU.mult)
        tt(P, "dve", BT[:, :, 0:64], f1, f2, ALU.subtract)
        tt(P, "dve", f1, cre, bim, ALU.mult); tt(P, "dve", f2, cim, bre, ALU.mult)
        tt(P, "dve", BT[:, :, 64:128], f1, f2, ALU.add)
        shc = [128, NG]
        lrc = ld("s5c_lr", shc, prm["lr_col"]); lic = ld("s5c_li", shc, prm["li_col"]); lsc = ld("s5c_ls", shc, prm["ls_col"])
        arc, aic, (g1, g2) = s5_abar(P, lrc, lic, lsc, shc, "c")
        sgn = ld("s5_sgn", [128, 1], cst["sgn"])
        ident = ld("s5_id", [128, 128], cst["ident"]); psw = ld("s5_psw", [128, 128], cst["psw"])
        pw = P.sbuf("s5_pw", [128, nlev, 2, NG])
        P.op("dve", lambda e: e.tensor_copy(out=pw.ap[:, 0, 0, :], in_=arc.ap), reads=[arc], writes=[pw])
        ts(P, "dve", pw[:, 0, 1, :], aic, sgn[:, 0:1], ALU.mult)
        for k in range(1, nlev):
            a0 = pw[:, k - 1, 0, :]; s0 = pw[:, k - 1, 1, :]
            tt(P, "dve", g1, a0, a0, ALU.mult); tt(P, "dve", g2, s0, s0, ALU.mult)
            tt(P, "dve", pw[:, k, 0, :], g1, g2, ALU.subtract)
            tt(P, "dve", g1, a0, s0, ALU.mult)
            ts(P, "dve", pw[:, k, 1, :], g1, 2.0, ALU.mult)
        CT = ld("s5_CT", [128, NG, 16], prm["cT"])
        ts(P, "dve", CT[64:128], CT[64:128], -1.0, ALU.mult)
        dcol = ld("s5_dcol", [16, NG], prm["d_col"])
        Xa = P.sbuf("s5_Xa", [128, T]); Xb = P.sbuf("s5_Xb", [128, T])
        Xc = [[P.alias(Xa, "s5Xa%d" % j) for j in range(nch)], [P.alias(Xb, "s5Xb%d" % j) for j in range(nch)]]
        for g in range(NG):
            ug = P.ring("s5_u", 2, [16, T])
            P.dma("sp", ug, uT[g * 16:(g + 1) * 16, :])
            Mk = P.ring("s5_M", 2, [128, nlev, 128])
            for k in range(nlev):
                P.op("pool", lambda e, Mk=Mk, k=k, g=g: e.tensor_scalar(out=r32(Mk.ap[:, k, :]), in0=ident.ap, scalar1=pw.ap[:, k, 0, g:g + 1],
                     scalar2=0.0, op0=ALU.mult, op1=ALU.add), reads=[ident, pw], writes=[Mk])
                P.op("dve", lambda e, Mk=Mk, k=k, g=g: e.scalar_tensor_tensor(out=r32(Mk.ap[:, k, :]), in0=psw.ap, scalar=pw.ap[:, k, 1, g:g + 1],
                     in1=Mk.ap[:, k, :], op0=ALU.mult, op1=ALU.add), reads=[psw, pw, Mk], writes=[Mk])
            for j in range(nch):
                ps = S.next_psum(); sl = slice(j * CH, (j + 1) * CH)
                P.op("pe", lambda e, ps=ps, ug=ug, sl=sl, g=g: e.matmul(ps.ap, lhsT=BT.ap[:, g, :], rhs=ug.ap[:, sl], start=True, stop=True),
                     reads=[BT, ug], writes=[ps])
                P.op("act", lambda e, ps=ps, sl=sl: e.copy(out=r32(Xa.ap[:, sl]), in_=ps.ap), reads=[ps], writes=[Xc[0][j]])
            for k in range(nlev):
                d = 2 ** k
                src, dst = Xc[k % 2], Xc[(k + 1) % 2]
                sa, da = (Xa, Xb) if k % 2 == 0 else (Xb, Xa)
                for j in range(nch):
                    t0 = j * CH; t1 = t0 + CH
                    lo = max(t0, d)
                    if lo > t0:
                        hi = min(lo, t1)
                        P.op("pool", lambda e, sa=sa, da=da, t0=t0, hi=hi: e.tensor_copy(out=r32(da.ap[:, t0:hi]), in_=sa.ap[:, t0:hi]),
                             reads=[src[j]], writes=[dst[j]])
                    if lo >= t1:
                        continue
                    n = t1 - lo
                    s0, s1 = lo - d, t1 - d
                    rds = [src[c] for c in range(s0 // CH, (s1 - 1) // CH + 1)]
                    ps = S.next_psum()
                    P.op("pe", lambda e, ps=ps, Mk=Mk, k=k, sa=sa, s0=s0, s1=s1, n=n: e.matmul(ps.ap[:, 0:n], lhsT=(r32(Mk.ap[:, k, :]) if (n % 2 == 0 and s0 % 2 == 0) else Mk.ap[:, k, :]), rhs=(r32(sa.ap[:, s0:s1]) if (n % 2 == 0 and s0 % 2 == 0) else sa.ap[:, s0:s1]),
                         start=True, stop=True), reads=[Mk] + rds, writes=[ps])
                    P.op("dve", lambda e, ps=ps, sa=sa, da=da, lo=lo, t1=t1, n=n: e.tensor_tensor(out=r32(da.ap[:, lo:t1]), in0=ps.ap[:, 0:n], in1=sa.ap[:, lo:t1],
                         op=ALU.add), reads=[ps, src[j]], writes=[dst[j]])
            fin = Xc[nlev % 2]; fa = Xa if nlev % 2 == 0 else Xb
            og = P.ring("s5_o", 2, [16, T])
            for j in range(nch):
                ps = S.next_psum(); sl = slice(j * CH, (j + 1) * CH)
                P.op("pe", lambda e, ps=ps, sl=sl, g=g, fa=fa: e.matmul(ps.ap[0:16, :], lhsT=CT.ap[:, g, :], rhs=fa.ap[:, sl], start=True, stop=True),
                     reads=[CT, fin[j]], writes=[ps])
                P.op("dve", lambda e, ps=ps, sl=sl, g=g, ug=ug, og=og: e.scalar_tensor_tensor(out=og.ap[:, sl], in0=ug.ap[:, sl], scalar=dcol.ap[:, g:g + 1],
                     in1=ps.ap[0:16, :], op0=ALU.mult, op1=ALU.add), reads=[ug, dcol, ps], writes=[og])
            P.op("act", lambda e, og=og: e.activation(out=og.ap, in_=og.ap, func=AF.Gelu_apprx_tanh), reads=[og], writes=[og])
            P.dma("sp", yT[g * 16:(g + 1) * 16, :], og)


def emit_ssd(P, S, zT, xsT, bT, cT, dtT, prm, cst, yT, T, NH, NG, dbg=None):
    J = NH // NG
    NP = NH // 2
    TPG = NP // NG
    L = 128
    SC = CH
    nsc = T // SC
    ncs = SC // L
    with P.scope():
        ld = lambda name, shape, src, **kw: (lambda t: (P.dma("sp", t, src, **kw), t)[1])(P.sbuf(name, shape))
        ident = ld("sd_id", [128, 128], cst["ident"]); tri = ld("sd_tri", [128, 128], cst["tri"])
        ustr = ld("sd_us", [128, 128], cst["ustr"]); sel = ld("sd_sel", [NH, NH * 64], cst["sel"])
        ones = S.ones
        cwx = P.sbuf("sd_cwx", [128, NP, 4]); cwb = P.sbuf("sd_cwb", [128, NG, 4]); cwc = P.sbuf("sd_cwc", [128, NG, 4])
        for k in range(4):
            P.dma("sp", cwx[:, :, k], prm["cw_x"][k].re("(t p) -> p t", p=128), allow_slow_non_contiguous=True)
            P.dma("sp", cwb[:, :, k], prm["cw_b"][k].re("(t p) -> p t", p=128), allow_slow_non_contiguous=True)
            P.dma("sp", cwc[:, :, k], prm["cw_c"][k].re("(t p) -> p t", p=128), allow_slow_non_contiguous=True)
        colv = lambda nm, n: ld("sd_" + nm, [128, n], prm[nm].re("(t p) -> p t", p=128), allow_slow_non_contiguous=True)
        cbx = colv("cb_x", NP); cbb = colv("cb_b", NG); cbc = colv("cb_c", NG); dch = colv("d_ch", NP); ngc = colv("ng", NP)
        dtb = ld("sd_dtb", [NH, 1], prm["dt_bias"].re("(h o) -> h o", o=1)); alog = ld("sd_alog", [NH, 1], prm["a_log"].re("(h o) -> h o", o=1))
        negA = P.sbuf("sd_negA", [NH, 1])
        act(P, negA, alog, AF.Exp)
        ts(P, "dve", negA, negA, -1.0, ALU.mult)
        one12 = P.sbuf("sd_one", [128, 1]); P.op("dve", lambda e: e.memset(one12.ap, 1.0), writes=[one12])
        prevT = P.sbuf("sd_prev", [128, NH, 64])
        P.op("dve", lambda e: e.memset(prevT.ap, 0.0), writes=[prevT])

        def conv_silu(srcT, r0, t0, cw, cb, i, dst):
            xp = P.ring("sd_xp", 2, [128, SC + 3])
            if t0 == 0:
                P.op("pool", lambda e: e.memset(xp.ap[:, 0:3], 0.0), writes=[xp])
                P.dma("sp", xp[:, 3:SC + 3], srcT[r0:r0 + 128, 0:SC])
            else:
                P.dma("sp", xp, srcT[r0:r0 + 128, t0 - 3:t0 + SC])
            P.op("dve", lambda e: e.tensor_scalar(out=dst.ap, in0=xp.ap[:, 0:SC], scalar1=cw.ap[:, i, 0:1], scalar2=cb.ap[:, i:i + 1],
                 op0=ALU.mult, op1=ALU.add), reads=[xp, cw, cb], writes=[dst])
            for k in range(1, 4):
                P.op("dve", lambda e, k=k: e.scalar_tensor_tensor(out=dst.ap, in0=xp.ap[:, k:k + SC], scalar=cw.ap[:, i, k:k + 1], in1=dst.ap,
                     op0=ALU.mult, op1=ALU.add), reads=[xp, cw, dst], writes=[dst])
            act(P, dst, dst, AF.Silu)

        def sc_body(sc):
            t0 = sc * SC
            xsc = [P.ring("sd_xsc%d" % i, 1, [128, SC]) for i in range(NP)]
            Bc = [P.ring("sd_Bc%d" % g, 1, [128, SC]) for g in range(NG)]
            Cc = [P.ring("sd_Cc%d" % g, 1, [128, SC]) for g in range(NG)]
            for i in range(NP):
                conv_silu(xsT, i * 128, t0, cwx, cbx, i, xsc[i])
            for g in range(NG):
                conv_silu(bT, g * 128, t0, cwb, cbb, g, Bc[g])
                conv_silu(cT, g * 128, t0, cwc, cbc, g, Cc[g])
            dtv = P.ring("sd_dtv", 1, [NH, SC]); da = P.ring("sd_da", 1, [NH, SC]); acT = P.ring("sd_acT", 1, [NH, SC])
            dsT = P.ring("sd_dsT", 1, [NH, SC])
            P.dma("sp", dtv, dtT[:, t0:t0 + SC])
            act(P, dtv, dtv, AF.Exp, bias=dtb[:, 0:1])
            act(P, dtv, dtv, AF.Ln, bias=one12[0:NH, 0:1])
            ts(P, "dve", da, dtv, negA[:, 0:1], ALU.mult)
            for c in range(ncs):
                cs = slice(c * L, (c + 1) * L)
                P.op("dve", lambda e, cs=cs: e.tensor_tensor_scan(out=acT.ap[:, cs], data0=one12.ap[0:NH, 0:1].to_broadcast([NH, L]), data1=da.ap[:, cs],
                     initial=0.0, op0=ALU.mult, op1=ALU.add), reads=[da, one12], writes=[acT])
            for c in range(ncs):
                cs = slice(c * L, (c + 1) * L)
                act(P, dsT[:, cs], acT[:, cs], AF.Exp, scale=-1.0, bias=acT[:, (c + 1) * L - 1:(c + 1) * L])
            tt(P, "dve", dsT, dsT, dtv, ALU.mult)
            xT = [P.ring("sd_xT%d" % i, 1, [128, SC]) for i in range(NP)]
            xdT = [P.ring("sd_xdT%d" % i, 1, [128, SC]) for i in range(NP)]
            for i in range(NP):
                for (srcrow, dst) in ((dtv, xT[i]), (dsT, xdT[i])):
                    ps = S.next_psum()
                    P.op("pe", lambda e, ps=ps, srcrow=srcrow, i=i: e.matmul(ps.ap, lhsT=sel.ap[:, i * 128:(i + 1) * 128], rhs=srcrow.ap, start=True, stop=True),
                         reads=[sel, srcrow], writes=[ps])
                    tt(P, "dve", dst, ps, xsc[i], ALU.mult)
            ysb = [P.ring("sd_ysb%d" % i, 1, [128, SC]) for i in range(NP)]
            if dbg and sc == 0:
                P.dma("sp", dbg["xsc0"], xsc[0]); P.dma("sp", dbg["dtv"], dtv); P.dma("sp", dbg["acT"], acT); P.dma("sp", dbg["dsT"], dsT)
                P.dma("sp", dbg["xT0"], xT[0]); P.dma("sp", dbg["xdT0"], xdT[0]); P.dma("sp", dbg["Bc0"], Bc[0])
            def chunk_body(c):
                cs = slice(c * L, (c + 1) * L)
                xtok = P.ring("sd_xtok", 2, [128, NP * 128]); xdtok = P.ring("sd_xdtok", 2, [128, NP * 128])
                for (srcs, dst) in ((xT, xtok), (xdT, xdtok)):
                    for i0 in range(0, NP, 4):
                        ps = S.next_psum(); n = min(4, NP - i0)
                        for i in range(i0, i0 + n):
                            P.op("pe", lambda e, ps=ps, srcs=srcs, i=i, i0=i0: e.transpose(ps.ap[:, (i - i0) * 128:(i - i0 + 1) * 128], srcs[i].ap[:, cs], ident.ap),
                                 reads=[srcs[i], ident], writes=[ps])
                        P.op("act", lambda e, ps=ps, dst=dst, i0=i0, n=n: e.copy(out=dst.ap[:, i0 * 128:(i0 + n) * 128], in_=ps.ap[:, 0:n * 128]), reads=[ps], writes=[dst])
                btok = P.ring("sd_btok", 2, [128, NG * 128])
                ps = S.next_psum()
                for g in range(NG):
                    P.op("pe", lambda e, ps=ps, g=g: e.transpose(ps.ap[:, g * 128:(g + 1) * 128], Bc[g].ap[:, cs], ident.ap), reads=[Bc[g], ident], writes=[ps])
                P.op("act", lambda e, ps=ps, btok=btok: e.copy(out=btok.ap, in_=ps.ap[:, 0:NG * 128]), reads=[ps], writes=[btok])
                datok = P.ring("sd_datok", 2, [128, NH])
                ps = S.next_psum()
                P.op("pe", lambda e, ps=ps: e.transpose(ps.ap[:, 0:NH], da.ap[:, cs], ident.ap[0:NH, 0:NH]), reads=[da, ident], writes=[ps])
                P.op("act", lambda e, ps=ps, datok=datok: e.copy(out=datok.ap, in_=ps.ap[:, 0:NH]), reads=[ps], writes=[datok])
                SM = P.ring("sd_SM", 2, [128, NG, L])
                for g in range(NG):
                    ps = S.next_psum()
                    P.op("pe", lambda e, ps=ps, g=g: e.matmul(ps.ap[:, 0:L], lhsT=Bc[g].ap[:, cs], rhs=Cc[g].ap[:, cs], start=True, stop=True),
                         reads=[Bc[g], Cc[g]], writes=[ps])
                    P.op("dve", lambda e, ps=ps, g=g, SM=SM: e.tensor_tensor(out=SM.ap[:, g, :], in0=ps.ap[:, 0:L], in1=tri.ap, op=ALU.mult),
                         reads=[ps, tri], writes=[SM])
                Vt = P.ring("sd_V", 2, [128, NH, L])
                P.op("dve", lambda e, Vt=Vt, datok=datok: e.tensor_tensor(out=Vt.ap, in0=tri.ap.unsqueeze(1).to_broadcast([128, NH, L]),
                     in1=datok.ap.unsqueeze(2).to_broadcast([128, NH, L]), op=ALU.mult), reads=[tri, datok], writes=[Vt])
                E = P.ring("sd_E", 2, [128, NH, L]); EA = P.ring("sd_EA", 2, [128, NH, L])
                for (lh, dst) in ((ustr, E), (ones, EA)):
                    for h0 in range(0, NH, 4):
                        n = min(4, NH - h0)
                        ps = S.next_psum()
                        P.op("pe", lambda e, ps=ps, lh=lh, Vt=Vt, h0=h0, n=n: e.matmul(ps.ap[:, 0:n * L], lhsT=lh.ap, rhs=Vt.ap[:, h0:h0 + n, :], start=True, stop=True),
                             reads=[lh, Vt], writes=[ps])
                        P.op("act", lambda e, ps=ps, dst=dst, h0=h0, n=n: e.activation(out=dst.ap[:, h0:h0 + n, :], in_=ps.ap[:, 0:n * L], func=AF.Exp),
                             reads=[ps], writes=[dst])
                Cs = P.ring("sd_Cs", 2, [128, NH, L])
                for g in range(NG):
                    hs = slice(g * J, (g + 1) * J)
                    P.op("dve", lambda e, E=E, SM=SM, g=g, hs=hs: e.tensor_tensor(out=E.ap[:, hs, :], in0=E.ap[:, hs, :],
                         in1=SM.ap[:, g:g + 1, :].to_broadcast([128, J, L]), op=ALU.mult), reads=[E, SM], writes=[E])
                    P.op("pool", lambda e, EA=EA, Cs=Cs, g=g, hs=hs: e.tensor_tensor(out=Cs.ap[:, hs, :], in0=EA.ap[:, hs, :],
                         in1=Cc[g].ap[:, cs].unsqueeze(1).to_broadcast([128, J, L]), op=ALU.mult), reads=[EA, Cc[g]], writes=[Cs])
                if dbg and sc == 0 and c == 1:
                    P.dma("sp", dbg["xtok"], xtok); P.dma("sp", dbg["btok"], btok); P.dma("sp", dbg["datok"], datok); P.dma("sp", dbg["SM"], SM.re("p g l -> p (g l)"))
                    P.dma("sp", dbg["E"], E.re("p g l -> p (g l)")); P.dma("sp", dbg["EA"], EA.re("p g l -> p (g l)")); P.dma("sp", dbg["Cs"], Cs.re("p g l -> p (g l)"))
                    P.dma("sp", dbg["prev1"], prevT.re("p g l -> p (g l)"))
                for i0 in range(0, NP, 4):
                    n = min(4, NP - i0)
                    ps = S.next_psum()
                    for i in range(i0, i0 + n):
                        for hh in range(2):
                            h = 2 * i + hh
                            o = ps.ap[hh * 64:(hh + 1) * 64, (i - i0) * L:(i - i0 + 1) * L]
                            P.op("pe", lambda e, o=o, h=h, xtok=xtok, E=E: e.matmul(o, lhsT=xtok.ap[:, h * 64:(h + 1) * 64], rhs=E.ap[:, h, :], start=True, stop=False),
                                 reads=[xtok, E], writes=[ps])
                            P.op("pe", lambda e, o=o, h=h, Cs=Cs: e.matmul(o, lhsT=prevT.ap[:, h, :], rhs=Cs.ap[:, h, :], start=False, stop=True),
                                 reads=[prevT, Cs], writes=[ps])
                    for i in range(i0, i0 + n):
                        P.op("dve", lambda e, ps=ps, i=i, i0=i0: e.scalar_tensor_tensor(out=ysb[i].ap[:, cs], in0=xsc[i].ap[:, cs], scalar=dch.ap[:, i:i + 1],
                             in1=ps.ap[:, (i - i0) * L:(i - i0 + 1) * L], op0=ALU.mult, op1=ALU.add), reads=[xsc[i], dch, ps], writes=[ysb[i]])
                P.op("dve", lambda e, EA=EA: e.tensor_tensor(out=prevT.ap, in0=prevT.ap, in1=EA.ap[:, :, L - 1:L].to_broadcast([128, NH, 64]), op=ALU.mult),
                     reads=[prevT, EA], writes=[prevT])
                for g in range(NG):
                    ps = S.next_psum()
                    P.op("pe", lambda e, ps=ps, g=g, btok=btok, xdtok=xdtok: e.matmul(ps.ap[:, 0:J * 64], lhsT=btok.ap[:, g * 128:(g + 1) * 128],
                         rhs=xdtok.ap[:, g * J * 64:(g + 1) * J * 64], start=True, stop=True), reads=[btok, xdtok], writes=[ps])
                    P.op("dve", lambda e, ps=ps, g=g: e.tensor_tensor(out=prevT.ap[:, g * J:(g + 1) * J, :], in0=prevT.ap[:, g * J:(g + 1) * J, :],
                         in1=ps.ap[:, 0:J * 64].rearrange("p (j d) -> p j d", d=64), op=ALU.add), reads=[ps, prevT], writes=[prevT])
            for c in range(ncs):
                chunk_body(c)
            if dbg and sc == 0:
                P.dma("sp", dbg["ysb0"], ysb[0])
            for g in range(NG):
                pss = S.next_psum()
                for ii in range(TPG):
                    i = g * TPG + ii
                    zt = P.ring("sd_z", 2, [128, SC])
                    P.dma("sp", zt, zT[i * 128:(i + 1) * 128, t0:t0 + SC])
                    act(P, zt, zt, AF.Silu)
                    tt(P, "dve", ysb[i], ysb[i], zt, ALU.mult)
                    act(P, zt, ysb[i], AF.Square)
                    P.op("pe", lambda e, pss=pss, zt=zt, ii=ii: e.matmul(pss.ap, lhsT=ones.ap, rhs=zt.ap, start=(ii == 0), stop=(ii == TPG - 1)),
                         reads=[ones, zt], writes=[pss])
                rs = P.ring("sd_rs", 2, [128, SC])
                act(P, rs, pss, AF.Sqrt, scale=1.0 / (TPG * 128), bias=S.epsc)
                P.op("dve", lambda e, rs=rs: e.reciprocal(out=rs.ap, in_=rs.ap), reads=[rs], writes=[rs])
                for ii in range(TPG):
                    i = g * TPG + ii
                    P.op("dve", lambda e, i=i, rs=rs: e.scalar_tensor_tensor(out=ysb[i].ap, in0=ysb[i].ap, scalar=ngc.ap[:, i:i + 1], in1=rs.ap,
                         op0=ALU.mult, op1=ALU.mult), reads=[ysb[i], ngc, rs], writes=[ysb[i]])
                    P.dma("sp", yT[i * 128:(i + 1) * 128, t0:t0 + SC], ysb[i])

        for sc in range(nsc):
            sc_body(sc)
CH = 512


def emit_lru(P, S, xlT, glT, prm, yT, T, ntile):
    C = ntile * 128
    nch = T // CH
    with P.scope():
        cw = P.sbuf("l_cw", [128, ntile, 4])
        for k in range(4):
            P.dma("sp", cw[:, :, k], prm["conv_w"][k].re("(t p) -> p t", p=128), allow_slow_non_contiguous=True)
        cols = {}
        for nm in ("conv_b", "b_a", "b_x", "lam"):
            cols[nm] = P.sbuf("l_" + nm, [128, ntile])
            P.dma("sp", cols[nm], prm[nm].re("(t p) -> p t", p=128), allow_slow_non_contiguous=True)
        one = P.sbuf("l_one", [128, 1])
        P.op("dve", lambda e: e.memset(one.ap, 1.0), writes=[one])
        c1 = P.sbuf("l_c1", [128, ntile])
        P.op("act", lambda e: e.activation(out=c1.ap, in_=cols["lam"].ap, func=AF.Exp, scale=-1.0), reads=[cols["lam"]], writes=[c1])
        P.op("act", lambda e: e.activation(out=c1.ap, in_=c1.ap, func=AF.Ln, bias=one.ap), reads=[c1, one], writes=[c1])
        P.op("dve", lambda e: e.tensor_scalar(out=c1.ap, in0=c1.ap, scalar1=-8.0, scalar2=None, op0=ALU.mult), reads=[c1], writes=[c1])
        wa = P.sbuf("l_wa", [128, ntile, 128]); wx = P.sbuf("l_wx", [128, ntile, 128])
        P.dma("sp", wa, prm["wa_bd"].re("t p m -> p t m"))
        P.dma("sp", wx, prm["wx_bd"].re("t p m -> p t m"))
        for i in range(ntile):
            rows = slice(i * 128, (i + 1) * 128)
            xl = P.ring("l_xl", 1, [128, T + 3])
            P.op("pool", lambda e, xl=xl: e.memset(xl.ap[:, 0:3], 0.0), writes=[xl])
            P.dma("sp", xl[:, 3:T + 3], xlT[rows, :])
            gl = P.ring("l_gl", 1, [128, T])
            P.dma("sp", gl, glT[rows, :])
            xc = P.ring("l_xc", 1, [128, T])
            P.op("dve", lambda e, xl=xl, xc=xc, i=i: e.tensor_scalar(out=xc.ap, in0=xl.ap[:, 0:T], scalar1=cw.ap[:, i, 0:1],
                 scalar2=cols["conv_b"].ap[:, i:i + 1], op0=ALU.mult, op1=ALU.add), reads=[xl, cw, cols["conv_b"]], writes=[xc])
            for k in range(1, 4):
                P.op("dve", lambda e, xl=xl, xc=xc, i=i, k=k: e.scalar_tensor_tensor(out=xc.ap, in0=xl.ap[:, k:k + T],
                     scalar=cw.ap[:, i, k:k + 1], in1=xc.ap, op0=ALU.mult, op1=ALU.add), reads=[xl, cw, xc], writes=[xc])
            ga = P.ring("l_ga", 1, [128, T]); gi = P.ring("l_gi", 1, [128, T])
            for j in range(nch):
                sl = slice(j * CH, (j + 1) * CH)
                for (wm, bn, dst) in ((wa, "b_a", ga), (wx, "b_x", gi)):
                    ps = S.next_psum()
                    P.op("pe", lambda e, ps=ps, wm=wm, xc=xc, sl=sl, i=i: e.matmul(ps.ap, lhsT=wm.ap[:, i, :], rhs=xc.ap[:, sl], start=True, stop=True),
                         reads=[wm, xc], writes=[ps])
                    P.op("act", lambda e, ps=ps, dst=dst, bn=bn, sl=sl, i=i: e.activation(out=dst.ap[:, sl], in_=ps.ap, func=AF.Sigmoid,
                         bias=cols[bn].ap[:, i:i + 1]), reads=[ps, cols[bn]], writes=[dst])
            P.op("act", lambda e, ga=ga, i=i: e.activation(out=ga.ap, in_=ga.ap, func=AF.Exp, scale=c1.ap[:, i:i + 1]), reads=[ga, c1], writes=[ga])
            mu = P.ring("l_mu", 1, [128, T])
            P.op("pool", lambda e, ga=ga, mu=mu: e.tensor_tensor(out=mu.ap, in0=ga.ap, in1=ga.ap, op=ALU.mult), reads=[ga], writes=[mu])
            P.op("act", lambda e, mu=mu: e.activation(out=mu.ap, in_=mu.ap, func=AF.Sqrt, scale=-1.0, bias=one.ap), reads=[mu, one], writes=[mu])
            P.op("pool", lambda e, mu=mu: e.memset(mu.ap[:, 0:1], 1.0), reads=[mu], writes=[mu])
            P.op("dve", lambda e, gi=gi, xc=xc: e.tensor_tensor(out=gi.ap, in0=gi.ap, in1=xc.ap, op=ALU.mult), reads=[gi, xc], writes=[gi])
            P.op("pool", lambda e, gi=gi, mu=mu: e.tensor_tensor(out=gi.ap, in0=gi.ap, in1=mu.ap, op=ALU.mult), reads=[gi, mu], writes=[gi])
            P.op("dve", lambda e, gi=gi, ga=ga, xc=xc: e.tensor_tensor_scan(out=xc.ap, data0=ga.ap, data1=gi.ap, initial=0.0, op0=ALU.mult, op1=ALU.add),
                 reads=[ga, gi], writes=[xc])
            P.op("act", lambda e, gl=gl: e.activation(out=gl.ap, in_=gl.ap, func=AF.Gelu_apprx_tanh), reads=[gl], writes=[gl])
            P.op("dve", lambda e, gl=gl, xc=xc: e.tensor_tensor(out=xc.ap, in0=xc.ap, in1=gl.ap, op=ALU.mult), reads=[gl, xc], writes=[xc])
            P.dma("sp", yT[rows, :], xc)


def _tt(P, eng, out, a, b, op, r=False):
    o = rr(out.ap) if r else out.ap
    P.op(eng, lambda e: e.tensor_tensor(out=o, in0=a.ap, in1=b.ap, op=op), reads=[a, b], writes=[out])


def _ts(P, eng, out, a, s1, op0, s2=None, op1=None, r=False):
    rd = [a] + [x for x in (s1, s2) if isinstance(x, V)]
    g = lambda x: x.ap if isinstance(x, V) else x
    o = rr(out.ap) if r else out.ap
    if op1 is None:
        P.op(eng, lambda e: e.tensor_scalar(out=o, in0=a.ap, scalar1=g(s1), scalar2=None, op0=op0), reads=rd, writes=[out])
    else:
        P.op(eng, lambda e: e.tensor_scalar(out=o, in0=a.ap, scalar1=g(s1), scalar2=g(s2), op0=op0, op1=op1), reads=rd, writes=[out])


def _act(P, out, a, func, scale=None, bias=None):
    rd = [a] + [x for x in (scale, bias) if isinstance(x, V)]
    kw = {}
    if scale is not None:
        kw["scale"] = scale.ap if isinstance(scale, V) else scale
    if bias is not None:
        kw["bias"] = bias.ap if isinstance(bias, V) else bias
    P.op("act", lambda e: e.activation(out=out.ap, in_=a.ap, func=func, **kw), reads=rd, writes=[out])


def _stt(P, out, in0, scalar, in1, op0, op1):
    rd = [in0, in1] + ([scalar] if isinstance(scalar, V) else [])
    sc = scalar.ap if isinstance(scalar, V) else scalar
    P.op("dve", lambda e: e.scalar_tensor_tensor(out=out.ap, in0=in0.ap, scalar=sc, in1=in1.ap, op0=op0, op1=op1), reads=rd, writes=[out])


R32 = True
F32R = mybir.dt.float32r


def rr(ap):
    return ap.bitcast(F32R) if R32 else ap


def _mm(P, out, lhsT, rhs, start=True, stop=True, fast=False):
    if fast and R32:
        P.op("pe", lambda e: e.matmul(out.ap, lhsT=rr(lhsT.ap), rhs=rr(rhs.ap), start=start, stop=stop), reads=[lhsT, rhs], writes=[out])
    else:
        P.op("pe", lambda e: e.matmul(out.ap, lhsT=lhsT.ap, rhs=rhs.ap, start=start, stop=stop), reads=[lhsT, rhs], writes=[out])


def _tr(P, out, in_, ident):
    P.op("pe", lambda e: e.transpose(out.ap, in_.ap, ident.ap), reads=[in_, ident], writes=[out])


def emit_rwkv(P, S, rT, kT, vT, wlT, alT, glT, prm, cst, yT, T, NP, stage=99, dbg=None):
    C = 64
    SC = 256
    NH = NP * 2
    nsc = T // SC
    ncs = SC // C
    GN_EPS = 64e-5
    with P.scope():
        ld = lambda name, shape, src, **kw: (lambda t: (P.dma("sp", t, src, **kw), t)[1])(P.sbuf(name, shape))
        ident = ld("rw_id", [128, 128], cst["ident"]); bones = ld("rw_bo", [128, 128], cst["bones"])
        mask2 = ld("rw_m2", [64, 128], cst["mask2"]); maskL = ld("rw_mL", [64, 64], cst["maskL"]); rmask = ld("rw_rm", [128, SC], cst["rmask"])
        colv = lambda nm, n, rows=128: ld("rw_" + nm, [rows, n], prm[nm].re("(t p) -> p t", p=rows), allow_slow_non_contiguous=True)
        mu = {x: colv("mu_" + x, NP) for x in "rkv"}
        mu_wl = colv("mu_wl", 1, 96); mu_al = colv("mu_al", 1, 96); mu_gl = colv("mu_gl", 2)
        w0 = colv("w0", NP); a0 = colv("a0", NP); kkc = colv("k_k", NP); kac = colv("k_a", NP); rkc = colv("r_k", NP)
        lng = colv("ln_g", NP); lnb = colv("ln_b", NP)
        wup = ld("rw_wup", [96, NP * 128], prm["w_up"]); aup = ld("rw_aup", [96, NP * 128], prm["a_up"])
        gup = ld("rw_gup", [128, 2, NP * 128], prm["g_up"].re("(k p) n -> p k n", p=128))
        def one_minus(src, name):
            t = P.sbuf(name, list(src.shape))
            _ts(P, "dve", t, src, -1.0, ALU.mult, 1.0, ALU.add)
            return t
        imu = {x: one_minus(mu[x], "rw_imu" + x) for x in "rkv"}
        imu_wl = one_minus(mu_wl, "rw_imuwl"); imu_al = one_minus(mu_al, "rw_imual"); imu_gl = one_minus(mu_gl, "rw_imugl")
        ika = one_minus(kac, "rw_ika")
        gne = P.sbuf("rw_gne", [128, 1]); P.op("dve", lambda e: e.memset(gne.ap, GN_EPS), writes=[gne])
        Hst = P.sbuf("rw_H", [64, NH, 64])
        P.op("dve", lambda e: e.memset(Hst.ap, 0.0), writes=[Hst])

        def shift_mix(srcT, r0, nr, t0, muc, imuc, dst):
            xp = P.ring("rw_xp", 3, [128, SC + 1])
            if t0 == 0:
                P.op("pool", lambda e: e.memset(xp.ap[0:nr, 0:1], 0.0), writes=[xp])
                P.dma("sp", xp[0:nr, 1:SC + 1], srcT[r0:r0 + nr, 0:SC])
            else:
                P.dma("sp", xp[0:nr], srcT[r0:r0 + nr, t0 - 1:t0 + SC])
            tmp = P.ring("rw_smt", 2, [128, SC])
            P.op("act", lambda e: e.activation(out=tmp.ap[0:nr], in_=xp.ap[0:nr, 0:SC], func=AF.Copy, scale=muc.ap),
                 reads=[xp, muc], writes=[tmp])
            P.op("dve", lambda e: e.scalar_tensor_tensor(out=dst.ap, in0=xp.ap[0:nr, 1:SC + 1], scalar=imuc.ap, in1=tmp.ap[0:nr], op0=ALU.mult, op1=ALU.add),
                 reads=[xp, imuc, tmp], writes=[dst])

        def sc_body(sc):
            t0 = sc * SC
            tw = P.ring("rw_tw", 1, [96, SC]); al = P.ring("rw_al", 1, [96, SC]); sg = P.ring("rw_sg", 1, [128, 2, SC])
            shift_mix(wlT, 0, 96, t0, mu_wl[:, 0:1], imu_wl[:, 0:1], tw)
            _act(P, tw, tw, AF.Tanh)
            shift_mix(alT, 0, 96, t0, mu_al[:, 0:1], imu_al[:, 0:1], al)
            for kx in range(2):
                shift_mix(glT, kx * 128, 128, t0, mu_gl[:, kx:kx + 1], imu_gl[:, kx:kx + 1], sg[:, kx, :])
            _act(P, sg, sg, AF.Sigmoid)
            Gc = P.ring("rw_Gc", 1, [64, NH, ncs])
            KRo = []; bho = []; kho = []
            rp = []; vp = []; k2 = []; KR = []; bh = []; kh = []; gt = []; gC = []; bon = []
            for i in range(NP):
                cols = slice(i * 128, (i + 1) * 128)
                r_ = P.ring("rw_r%d" % i, 1, [128, SC]); k_ = P.ring("rw_k%d" % i, 1, [128, SC]); v_ = P.ring("rw_v%d" % i, 1, [128, SC])
                shift_mix(rT, i * 128, 128, t0, mu["r"][:, i:i + 1], imu["r"][:, i:i + 1], r_)
                shift_mix(kT, i * 128, 128, t0, mu["k"][:, i:i + 1], imu["k"][:, i:i + 1], k_)
                shift_mix(vT, i * 128, 128, t0, mu["v"][:, i:i + 1], imu["v"][:, i:i + 1], v_)
                lw = P.ring("rw_lw", 1, [128, SC]); a_ = P.ring("rw_a", 1, [128, SC]); g_ = P.ring("rw_g%d" % i, 1, [128, SC])
                ps = S.next_psum()
                _mm(P, ps[:, 0:SC], wup[:, cols], tw)
                _act(P, lw, ps[:, 0:SC], AF.Sigmoid, bias=w0[:, i:i + 1])
                _ts(P, "dve", lw, lw, -math.exp(-0.5), ALU.mult)
                ps = S.next_psum()
                _mm(P, ps[:, 0:SC], aup[:, cols], al)
                _act(P, a_, ps[:, 0:SC], AF.Sigmoid, bias=a0[:, i:i + 1])
                ps = S.next_psum()
                _mm(P, ps[:, 0:SC], gup[:, 0, cols], sg[:, 0, :], True, False)
                _mm(P, ps[:, 0:SC], gup[:, 1, cols], sg[:, 1, :], False, True)
                P.op("act", lambda e, g_=g_, ps=ps: e.copy(out=g_.ap, in_=ps.ap[:, 0:SC]), reads=[ps], writes=[g_])
                kap = P.ring("rw_kap", 1, [128, SC]); t1 = P.ring("rw_t1", 1, [128, SC]); t2 = P.ring("rw_t2", 1, [128, SC])
                _ts(P, "dve", kap, k_, kkc[:, i:i + 1], ALU.mult)
                _tt(P, "pool", t1, kap, kap, ALU.mult)
                ps = S.next_psum()
                _mm(P, ps[:, 0:SC], bones, t1)
                _ts(P, "dve", t1, ps[:, 0:SC], 1e-24, ALU.max)
                _act(P, t1, t1, AF.Sqrt)
                P.op("dve", lambda e, t1=t1: e.reciprocal(out=t1.ap, in_=t1.ap), reads=[t1], writes=[t1])
                _tt(P, "dve", kap, kap, t1, ALU.mult)
                k2_ = P.ring("rw_k2%d" % i, 1, [128, SC])
                _ts(P, "dve", t1, a_, kac[:, i:i + 1], ALU.mult, ika[:, i:i + 1], ALU.add)
                _tt(P, "dve", k2_, k_, t1, ALU.mult)
                bet = P.ring("rw_bet", 1, [128, SC])
                _tt(P, "pool", bet, kap, a_, ALU.mult)
                cl = P.ring("rw_cl", 1, [128, SC])
                P.op("dve", lambda e, cl=cl, lw=lw: e.tensor_tensor_scan(out=cl.ap, data0=rmask.ap, data1=lw.ap, initial=0.0, op0=ALU.mult, op1=ALU.add),
                     reads=[rmask, lw], writes=[cl])
                eG = P.ring("rw_eG", 1, [128, SC]); eN = P.ring("rw_eN", 1, [128, SC])
                _act(P, eG, cl, AF.Exp)
                _act(P, eN, cl, AF.Exp, scale=-1.0)
                _tt(P, "dve", t2, cl, lw, ALU.subtract)
                _act(P, t2, t2, AF.Exp)
                KR_ = P.ring("rw_KR%d" % i, 1, [128, ncs, 2, C])
                _tt(P, "dve", KR_[:, :, 0, :], kap.re("p (c t) -> p c t", t=C), t2.re("p (c t) -> p c t", t=C), ALU.mult, r=True)
                _tt(P, "pool", KR_[:, :, 1, :], r_.re("p (c t) -> p c t", t=C), eG.re("p (c t) -> p c t", t=C), ALU.mult, r=True)
                bh_ = P.ring("rw_bh%d" % i, 1, [128, SC]); kh_ = P.ring("rw_kh%d" % i, 1, [128, SC])
                _tt(P, "dve", bh_, bet, eN, ALU.mult, r=True)
                _tt(P, "pool", kh_, k2_, eN, ALU.mult, r=True)
                gC_ = P.ring("rw_gC%d" % i, 1, [128, ncs])
                P.op("act", lambda e, gC_=gC_, eG=eG: e.copy(out=gC_.ap, in_=eG.ap.rearrange("p (c t) -> p c t", t=C)[:, :, C - 1]), reads=[eG], writes=[gC_])
                bon_ = P.ring("rw_bon%d" % i, 1, [128, SC])
                _stt(P, t1, r_, rkc[:, i:i + 1], k2_, ALU.mult, ALU.mult)
                ps = S.next_psum()
                _mm(P, ps[:, 0:SC], bones, t1)
                _tt(P, "dve", bon_, ps[:, 0:SC], v_, ALU.mult)
                KRo_ = P.ring("rw_KRo%d" % i, 1, [64, ncs, 2, C]); bho_ = P.ring("rw_bho%d" % i, 1, [64, SC]); kho_ = P.ring("rw_kho%d" % i, 1, [64, SC])
                P.dma("sp", KRo_.bitcast(F32R) if R32 else KRo_, KR_[64:128].bitcast(F32R) if R32 else KR_[64:128])
                P.dma("sp", bho_.bitcast(F32R) if R32 else bho_, bh_[64:128].bitcast(F32R) if R32 else bh_[64:128])
                P.dma("sp", kho_.bitcast(F32R) if R32 else kho_, kh_[64:128].bitcast(F32R) if R32 else kh_[64:128])
                P.op("pool", lambda e, gC_=gC_, i=i: e.tensor_copy(out=Gc.ap[:, 2 * i, :], in_=gC_.ap[0:64, :]), reads=[gC_], writes=[Gc])
                P.dma("sp", Gc[:, 2 * i + 1, :], gC_[64:128, :])
                KRo.append(KRo_); bho.append(bho_); kho.append(kho_)
                rp.append(r_); vp.append(v_); k2.append(k2_); KR.append(KR_); bh.append(bh_); kh.append(kh_); gt.append(g_); gC.append(gC_); bon.append(bon_)
            ysb = [P.ring("rw_y%d" % i, 1, [128, SC]) for i in range(NP)]
            KRh = lambda h: KR[h // 2][0:64] if h % 2 == 0 else KRo[h // 2]
            bhh = lambda h: bh[h // 2][0:64] if h % 2 == 0 else bho[h // 2]
            khh = lambda h: kh[h // 2][0:64] if h % 2 == 0 else kho[h // 2]

            def prep_chunk(c):
                cs = slice(c * C, (c + 1) * C)
                toks = {}
                for nm, srcs in (("v", vp), ("b", bh), ("k", kh)):
                    ps = S.next_psum()
                    for i in range(NP):
                        _tr(P, ps[0:C, i * 128:(i + 1) * 128], srcs[i][:, cs], ident)
                    tk = P.ring("rw_tok" + nm, 2, [64, NP * 128])
                    P.op("act", lambda e, tk=tk, ps=ps: e.copy(out=rr(tk.ap), in_=ps.ap[0:C, 0:NP * 128]), reads=[ps], writes=[tk])
                    toks[nm] = tk
                psN = S.next_psum(); psB = S.next_psum(); psK = S.next_psum(); psB2 = S.next_psum(); psK2 = S.next_psum()
                nb = 4
                for h in range(NH):
                    kapc = KRh(h)[:, c, 0, :]; krc = KRh(h)[:, c, :, :].re("p a t -> p (a t)")
                    _mm(P, psN[0:C, h * C:(h + 1) * C], kapc, bhh(h)[:, cs], fast=True)
                    pb_ = psB if h < nb else psB2
                    pk_ = psK if h < nb else psK2
                    _mm(P, pb_[0:C, (h % nb) * 128:(h % nb + 1) * 128], bhh(h)[:, cs], krc, fast=True)
                    _mm(P, pk_[0:C, (h % nb) * 128:(h % nb + 1) * 128], khh(h)[:, cs], krc, fast=True)
                Q = P.ring("rw_Q", 2, [64, NH, C]); QT = P.ring("rw_QT", 2, [64, NH, C]); R = P.ring("rw_R", 2, [64, NH, C])
                AB = P.ring("rw_AB", 2, [64, NH, 2, C]); AK = P.ring("rw_AK", 2, [64, NH, 2, C])
                P.op("dve", lambda e, Q=Q: e.scalar_tensor_tensor(out=rr(Q.ap), in0=psN.ap[0:C, 0:NH * C].rearrange("p (h t) -> p h t", t=C), scalar=-1.0,
                     in1=maskL.ap.unsqueeze(1).to_broadcast([64, NH, C]), op0=ALU.mult, op1=ALU.mult), reads=[psN, maskL], writes=[Q])
                for (pp, dst, h0) in ((psB, AB, 0), (psB2, AB, nb), (psK, AK, 0), (psK2, AK, nb)):
                    n = min(nb, NH - h0)
                    if n <= 0:
                        continue
                    P.op("dve", lambda e, pp=pp, dst=dst, h0=h0, n=n: e.tensor_tensor(out=rr(dst.ap[:, h0:h0 + n].rearrange("p h a t -> p h (a t)")),
                         in0=pp.ap[0:C, 0:n * 128].rearrange("p (h x) -> p h x", x=128), in1=mask2.ap.unsqueeze(1).to_broadcast([64, n, 128]), op=ALU.mult),
                         reads=[pp, mask2], writes=[dst])
                _ts(P, "dve", QT, AB[:, :, 0, :], -1.0, ALU.mult, r=True)
                P.op("dve", lambda e, QT=QT: e.tensor_tensor(out=rr(R.ap), in0=QT.ap, in1=ident.ap[0:64, 0:64].unsqueeze(1).to_broadcast([64, NH, C]), op=ALU.add),
                         reads=[QT, ident], writes=[R])
                nlev = 6
                yield
                for lv in range(1, nlev):
                    if lv > 1:
                        yield
                    psQ = S.next_psum(); psQT = S.next_psum(); psR = S.next_psum()
                    Qn = P.ring("rw_Q", 2, [64, NH, C]); QTn = P.ring("rw_QT", 2, [64, NH, C])
                    last = (lv == nlev - 1)
                    for h in range(NH):
                        hsl = slice(h * C, (h + 1) * C)
                        _mm(P, psQ[0:C, hsl], QT[:, h, :], Q[:, h, :], fast=True)
                        if not last:
                            _mm(P, psQT[0:C, hsl], Q[:, h, :], QT[:, h, :], fast=True)
                    P.op("act", lambda e, Qn=Qn, psQ=psQ: e.copy(out=rr(Qn.ap.rearrange("p h t -> p (h t)")), in_=psQ.ap[0:C, 0:NH * C]), reads=[psQ], writes=[Qn])
                    if not last:
                        P.op("act", lambda e, QTn=QTn, psQT=psQT: e.copy(out=rr(QTn.ap.rearrange("p h t -> p (h t)")), in_=psQT.ap[0:C, 0:NH * C]), reads=[psQT], writes=[QTn])
                    for h in range(NH):
                        _mm(P, psR[0:C, h * C:(h + 1) * C], Qn[:, h, :], R[:, h, :], fast=True)
                    P.op("dve", lambda e, psR=psR: e.tensor_tensor(out=rr(R.ap.rearrange("p h t -> p (h t)")), in0=psR.ap[0:C, 0:NH * C],
                         in1=R.ap.rearrange("p h t -> p (h t)"), op=ALU.add), reads=[psR, R], writes=[R])
                    Q, QT = Qn, QTn
                if dbg and sc == 0 and c == 0:
                    P.dma("sp", dbg["R"], R.re("p h t -> p (h t)")); P.dma("sp", dbg["AB"], AB.re("p h a t -> p (h a t)")); P.dma("sp", dbg["AK"], AK.re("p h a t -> p (h a t)"))
                    P.dma("sp", dbg["vt"], toks["v"]); P.dma("sp", dbg["bt"], toks["b"])
                return dict(toks=toks, R=R, AB=AB, AK=AK)

            def seq_chunk(c, pc):
                cs = slice(c * C, (c + 1) * C)
                toks, R, AB, AK = pc["toks"], pc["R"], pc["AB"], pc["AK"]
                vt, bt, kt = toks["v"], toks["b"], toks["k"]
                psW = S.next_psum()
                Hr = P.ring("rw_Hr", 2, [64, NH, 64])
                P.op("act", lambda e: e.copy(out=rr(Hr.ap), in_=Hst.ap), reads=[Hst], writes=[Hr])
                for h in range(NH):
                    _mm(P, psW[0:C, h * 64:(h + 1) * 64], KRh(h)[:, c, 0, :], Hr[:, h, :], True, False, fast=True)
                    _mm(P, psW[0:C, h * 64:(h + 1) * 64], AK[:, h, 0, :], vt[:, h * 64:(h + 1) * 64], False, True, fast=True)
                Wsb = P.ring("rw_W", 2, [64, NH * 64])
                P.op("act", lambda e: e.copy(out=rr(Wsb.ap), in_=psW.ap[0:C, 0:NH * 64]), reads=[psW], writes=[Wsb])
                yield
                psU = S.next_psum()
                for h in range(NH):
                    _mm(P, psU[0:C, h * 64:(h + 1) * 64], R[:, h, :], Wsb[:, h * 64:(h + 1) * 64], fast=True)
                Usb = P.ring("rw_U", 2, [64, NH * 64])
                _ts(P, "dve", Usb, psU[0:C, 0:NH * 64], -1.0, ALU.mult, r=True)
                if dbg and sc == 0 and c == 0:
                    P.dma("sp", dbg["W"], Wsb); P.dma("sp", dbg["U"], Usb)
                yield
                psY = S.next_psum(); psH = S.next_psum()
                for h in range(NH):
                    i, hh = divmod(h, 2); pr = slice(hh * 64, (hh + 1) * 64)
                    oy = psY[pr, i * C:(i + 1) * C]
                    fy = (hh == 0)
                    _mm(P, oy, Hr[:, h, :], KRh(h)[:, c, 1, :], True, False, fast=fy)
                    _mm(P, oy, Usb[:, h * 64:(h + 1) * 64], AB[:, h, 1, :], False, False, fast=fy)
                    _mm(P, oy, vt[:, h * 64:(h + 1) * 64], AK[:, h, 1, :], False, True, fast=fy)
                for h in range(NH):
                    oh = psH[0:64, h * 64:(h + 1) * 64]
                    _mm(P, oh, bt[:, h * 64:(h + 1) * 64], Usb[:, h * 64:(h + 1) * 64], True, False, fast=True)
                    _mm(P, oh, kt[:, h * 64:(h + 1) * 64], vt[:, h * 64:(h + 1) * 64], False, True, fast=True)
                yield
                for i in range(NP):
                    P.op("act", lambda e, i=i: e.copy(out=ysb[i].ap[:, cs], in_=psY.ap[:, i * C:(i + 1) * C]), reads=[psY], writes=[ysb[i]])
                P.op("dve", lambda e: e.tensor_tensor(out=Hst.ap, in0=Hst.ap, in1=psH.ap[0:64, 0:NH * 64].rearrange("p (i v) -> p i v", v=64), op=ALU.add),
                     reads=[Hst, psH], writes=[Hst])
                P.op("dve", lambda e: e.tensor_tensor(out=Hst.ap, in0=Hst.ap, in1=Gc.ap[:, :, c:c + 1].to_broadcast([64, NH, 64]), op=ALU.mult),
                     reads=[Hst, Gc], writes=[Hst])

            if dbg and sc == 0:
                P.dma("sp", dbg["KR0"], KR[0].re("p c a t -> p (c a t)")); P.dma("sp", dbg["bh0"], bh[0]); P.dma("sp", dbg["kh0"], kh[0])
                P.dma("sp", dbg["KRo0"], KRo[0].re("p c a t -> p (c a t)")); P.dma("sp", dbg["Gc"], Gc.re("p h c -> p (h c)"))
            if stage == 1:
                for i in range(NP):
                    P.dma("sp", yT[i * 128:(i + 1) * 128, t0:t0 + SC], bon[i])
                return
            def run_all(g):
                try:
                    while True:
                        next(g)
                except StopIteration as ex:
                    return ex.value
            pcs = run_all(prep_chunk(0))
            for c in range(ncs):
                g1 = prep_chunk(c + 1) if c + 1 < ncs else None
                g2 = seq_chunk(c, pcs)
                nxt = None
                d1 = g1 is None
                d2 = False
                while not (d1 and d2):
                    if not d2:
                        try:
                            next(g2)
                        except StopIteration:
                            d2 = True
                    if not d1:
                        try:
                            next(g1)
                        except StopIteration as ex:
                            nxt = ex.value
                            d1 = True
                pcs = nxt
            if dbg and sc == 0:
                P.dma("sp", dbg["y0"], ysb[0]); P.dma("sp", dbg["H"], Hst.re("p h v -> p (h v)"))
            for i in range(NP):
                y = ysb[i]
                t1 = P.ring("rw_t1", 1, [128, SC]); t2 = P.ring("rw_t2", 1, [128, SC])
                ps = S.next_psum()
                _mm(P, ps[:, 0:SC], bones, y)
                _stt(P, y, ps[:, 0:SC], -1.0 / 64, y, ALU.mult, ALU.add)
                _tt(P, "pool", t1, y, y, ALU.mult)
                ps = S.next_psum()
                _mm(P, ps[:, 0:SC], bones, t1)
                _act(P, t2, ps[:, 0:SC], AF.Sqrt, scale=1.0 / 64, bias=gne)
                P.op("dve", lambda e, t2=t2: e.reciprocal(out=t2.ap, in_=t2.ap), reads=[t2], writes=[t2])
                _stt(P, y, y, lng[:, i:i + 1], t2, ALU.mult, ALU.mult)
                _stt(P, y, y, lnb[:, i:i + 1], bon[i], ALU.add, ALU.add)
                _tt(P, "dve", y, y, gt[i], ALU.mult)
                P.dma("sp", yT[i * 128:(i + 1) * 128, t0:t0 + SC], y)

        for sc in range(nsc):
            sc_body(sc)
import numpy as _np

NT_CORE = 2048
SEQ = 4096
NB = 4


def tile_w(W):
    K, N = W.shape
    NCB = (N + 127) // 128
    Wp = _np.zeros((K, NCB * 128), _np.float32)
    Wp[:, :N] = W
    return _np.ascontiguousarray(Wp.reshape(K // 128, 128, NCB, 128).transpose(2, 1, 0, 3))


def consts_all():
    ident = _np.eye(128, dtype=_np.float32)
    psw = _np.zeros((128, 128), _np.float32)
    for k in range(128):
        psw[k, (k + 64) % 128] = 1
    sgn = _np.ones((128, 1), _np.float32); sgn[64:] = -1
    s = _np.arange(128)
    tri = (s[:, None] <= s[None, :]).astype(_np.float32)
    ustr = (s[:, None] > s[None, :]).astype(_np.float32)
    sel = _np.zeros((12, 768), _np.float32)
    for h in range(12):
        sel[h, h * 64:(h + 1) * 64] = 1
    bones = _np.zeros((128, 128), _np.float32); bones[:64, :64] = 1; bones[64:, 64:] = 1
    s6 = _np.arange(64)
    mU = (s6[:, None] < s6[None, :]).astype(_np.float32); mUi = (s6[:, None] <= s6[None, :]).astype(_np.float32)
    mask2 = _np.concatenate([mU, mUi], 1)
    maskL = (s6[None, :] < s6[:, None]).astype(_np.float32)
    rmask = _np.ones((128, 256), _np.float32); rmask[:, ::64] = 0
    return dict(ident=ident, psw=psw, sgn=sgn, tri=tri, ustr=ustr, sel=sel, bones=bones, mask2=mask2, maskL=maskL, rmask=rmask)


def s5_host(lam_re, lam_im, log_step, b_re, b_im, c_re, c_im, d):
    G = lam_re.shape[0]
    rep = lambda a: _np.ascontiguousarray(_np.broadcast_to(a[None], (16,) + a.shape)).astype(_np.float32)
    col = lambda a: _np.ascontiguousarray(_np.concatenate([a.T, a.T], 0)).astype(_np.float32)
    ls = _np.broadcast_to(log_step[:, None], (G, 64))
    return dict(lr_row=rep(lam_re), li_row=rep(lam_im), ls_row=rep(ls),
                bT_re=_np.ascontiguousarray(b_re.transpose(2, 0, 1)), bT_im=_np.ascontiguousarray(b_im.transpose(2, 0, 1)),
                lr_col=col(lam_re), li_col=col(lam_im), ls_col=col(ls),
                cT=_np.ascontiguousarray(_np.concatenate([c_re.transpose(2, 0, 1), c_im.transpose(2, 0, 1)], 0)),
                d_col=_np.ascontiguousarray(d.reshape(G, 16).T))


def even_params(inp, hh):
    g0 = hh * 16
    p = {"s5_" + k: v for k, v in s5_host(inp["s5_lam_re"][0, g0:g0 + 16], inp["s5_lam_im"][0, g0:g0 + 16], inp["s5_log_step"][0, g0:g0 + 16],
                                          inp["s5_b_re"][0, g0:g0 + 16], inp["s5_b_im"][0, g0:g0 + 16], inp["s5_c_re"][0, g0:g0 + 16],
                                          inp["s5_c_im"][0, g0:g0 + 16], inp["s5_d"][0, hh * 256:(hh + 1) * 256]).items()}
    cw = inp["ssd_conv_w"][0]; cb = inp["ssd_conv_b"][0]
    xs = slice(hh * 768, (hh + 1) * 768); bs = slice(1536 + hh * 256, 1536 + (hh + 1) * 256); cs = slice(2048 + hh * 256, 2048 + (hh + 1) * 256)
    hs = slice(hh * 12, (hh + 1) * 12)
    c = lambda a: _np.ascontiguousarray(a, dtype=_np.float32)
    p.update(sd_cw_x=c(cw[:, xs]), sd_cb_x=c(cb[xs]), sd_cw_b=c(cw[:, bs]), sd_cb_b=c(cb[bs]), sd_cw_c=c(cw[:, cs]), sd_cb_c=c(cb[cs]),
             sd_dt_bias=c(inp["ssd_dt_bias"][0, hs]), sd_a_log=c(inp["ssd_a_log"][0, hs]),
             sd_d_ch=c(_np.repeat(inp["ssd_d"][0, hs], 64)), sd_ng=c(inp["ssd_norm"][0, xs]))
    return p


def lru_bd(w, hh):
    o = _np.zeros((4, 128, 128), _np.float32)
    for bl in range(8):
        t, h = divmod(bl, 2)
        o[t, h * 64:(h + 1) * 64, h * 64:(h + 1) * 64] = w[hh * 8 + bl]
    return o


def odd_params(inp, hh):
    c = lambda a: _np.ascontiguousarray(a, dtype=_np.float32)
    mu = inp["rwkv_mu"][0]
    ch = slice(hh * 512, (hh + 1) * 512)
    p = dict(rw_mu_r=c(mu[0:1024][ch]), rw_mu_k=c(mu[1024:2048][ch]), rw_mu_v=c(mu[2048:3072][ch]), rw_mu_wl=c(mu[3072:3168]), rw_mu_al=c(mu[3168:3264]),
             rw_mu_gl=c(mu[3264:3520]), rw_w0=c(inp["rwkv_w0"][0, ch]), rw_a0=c(inp["rwkv_a0"][0, ch]), rw_k_k=c(inp["rwkv_k_k"][0, ch]),
             rw_k_a=c(inp["rwkv_k_a"][0, ch]), rw_r_k=c(inp["rwkv_r_k"][0].reshape(-1)[ch]), rw_ln_g=c(inp["rwkv_ln_g"][0, ch]), rw_ln_b=c(inp["rwkv_ln_b"][0, ch]),
             rw_w_up=c(inp["rwkv_w_up"][0][:, ch]), rw_a_up=c(inp["rwkv_a_up"][0][:, ch]), rw_g_up=c(inp["rwkv_g_up"][0][:, ch]))
    p.update(lr_conv_w=c(inp["lru_conv_w"][0][:, ch]), lr_conv_b=c(inp["lru_conv_b"][0, ch]), lr_wa_bd=lru_bd(inp["lru_w_a"][0], hh), lr_wx_bd=lru_bd(inp["lru_w_x"][0], hh),
             lr_b_a=c(inp["lru_b_a"][0].reshape(-1)[ch]), lr_b_x=c(inp["lru_b_x"][0].reshape(-1)[ch]), lr_lam=c(inp["lru_lam"][0].reshape(-1)[ch]))
    return p


def dense_w(inp, L):
    c = lambda a: _np.ascontiguousarray(a, dtype=_np.float32)
    return dict(out_t=tile_w(inp["e_out_proj" if L == 0 else "o_out_proj"][0]), w1_t=tile_w(inp["mlp_w1"][L]), w2_t=tile_w(inp["mlp_w2"][L]),
                gate_t=tile_w(inp["pl_gate"][L]), plp_t=tile_w(inp["pl_proj"][L]), nffn=c(inp["norm_ffn"][L]), npl=c(inp["norm_pl"][L]))


class Launch:
    def __init__(self):
        self.nc = bass.Bass("TRN2", target_bir_lowering=False)
        self.st = contextlib.ExitStack()
        self.P = Prog(self.nc, self.st)
        self.S = Shared(self.P)
        self.outs = []

    def inp(self, name, arr):
        return self.P.dram(name, list(arr.shape), kind="ExternalInput")

    def inps(self, d, prefix=""):
        return {k: self.inp(prefix + k, v) for k, v in d.items()}

    def out(self, name, shape):
        v = self.P.dram(name, list(shape), kind="ExternalOutput")
        self.outs.append(v)
        return v

    def run(self, in_maps):
        self.P.wait_all("sp", self.outs)
        self.P.finish()
        self.st.close()
        res = run_bass_kernel_spmd(self.nc, in_maps, core_ids=list(range(len(in_maps))))
        return res.results


def strip(d, prefix):
    return {k[len(prefix):]: v for k, v in d.items() if k.startswith(prefix)}


PAIRS = [[0, 1], [2, 3], [4, 5], [6, 7]]
ALL8 = [list(range(8))]


def pad_cols(W, n):
    o = _np.zeros((W.shape[0], n), _np.float32)
    o[:, :W.shape[1]] = W
    return o


def host_inputs(inp):
    f32c = lambda a: _np.ascontiguousarray(a, dtype=_np.float32)
    x = inp["x"]; p = inp["p"]
    cst = consts_all()
    ein = inp["e_in_proj"][0]; oin = inp["o_in_proj"][0]
    eout = inp["e_out_proj"][0]; oout = inp["o_out_proj"][0]
    shared = {}
    for L in range(2):
        shared["w1_%d" % L] = tile_w(inp["mlp_w1"][L]); shared["w2_%d" % L] = tile_w(inp["mlp_w2"][L])
        shared["gate_%d" % L] = tile_w(inp["pl_gate"][L]); shared["plp_%d" % L] = tile_w(inp["pl_proj"][L])
    per_hh = []
    for hh in range(2):
        d = {}
        cols0 = _np.concatenate([ein[:, hh * 256:(hh + 1) * 256], ein[:, 512 + hh * 768:512 + (hh + 1) * 768], ein[:, 2048 + hh * 768:2048 + (hh + 1) * 768],
                                 ein[:, 3584 + hh * 256:3584 + (hh + 1) * 256], ein[:, 4096 + hh * 256:4096 + (hh + 1) * 256],
                                 pad_cols(ein[:, 4608 + hh * 12:4608 + (hh + 1) * 12], 128)], 1)
        d["win0_t"] = tile_w(cols0)
        ch = lambda o: oin[:, o + hh * 512:o + (hh + 1) * 512]
        cols1 = _np.concatenate([ch(0), ch(1024), ch(2048), pad_cols(oin[:, 3072:3168], 128), pad_cols(oin[:, 3168:3264], 128), oin[:, 3264:3520], ch(3520), ch(4544)], 1)
        d["win1_t"] = tile_w(cols1)
        d["wout0_t"] = tile_w(_np.concatenate([eout[hh * 256:(hh + 1) * 256], eout[512 + hh * 768:512 + (hh + 1) * 768]], 0))
        d["wout1_t"] = tile_w(_np.concatenate([oout[hh * 512:(hh + 1) * 512], oout[1024 + hh * 512:1024 + (hh + 1) * 512]], 0))
        d["gluw_t"] = tile_w(inp["s5_glu_w"][0][hh * 256:(hh + 1) * 256])
        d["glub"] = f32c(inp["s5_glu_b"][0][hh * 256:(hh + 1) * 256])
        d.update(even_params(inp, hh)); d.update(odd_params(inp, hh))
        per_hh.append(d)
    ins = []
    for c in range(8):
        b, hh = divmod(c, 2)
        ts_ = slice(hh * NT_CORE, (hh + 1) * NT_CORE)
        d = dict(xT=f32c(x[b, ts_, :].T), pT0=f32c(p[0, b, ts_, :].T), pT1=f32c(p[1, b, ts_, :].T))
        for k, v in shared.items():
            d[k + "_t"] = v
        xf = x[b].T.reshape(8, 256, 2, NT_CORE).transpose(0, 2, 1, 3)
        d["xfull"] = f32c(xf)
        for L in range(2):
            d["nmix%d" % L] = f32c(inp["norm_mix"][L]); d["nffn%d" % L] = f32c(inp["norm_ffn"][L]); d["npl%d" % L] = f32c(inp["norm_pl"][L])
        d["nfin"] = f32c(inp["norm_final"])
        d.update(per_hh[hh]); d.update({"c_" + k: v for k, v in cst.items()})
        ins.append(d)
    return ins


def build_fused(ex):
    la = Launch()
    P, S = la.P, la.S
    dd = la.inps(ex)
    cs_d = strip(dd, "c_")
    T = SEQ; NT = NT_CORE
    hfull0 = dd["xfull"]
    Wf = [{}, {}]
    for L in range(2):
        for nm in ("w1", "w2", "gate", "plp"):
            Wf[L][nm + "_t"] = dd["%s_%d_t" % (nm, L)]
        Wf[L]["nffn"] = dd["nffn%d" % L]; Wf[L]["npl"] = dd["npl%d" % L]

    def rs_mix(mp, mix):
        for q in range(4):
            P.collective("ReduceScatter", mix[q * 512:(q + 1) * 512, :], mp[q].re("s f t -> (s f) t"), PAIRS, op=ALU.add)
    pin0 = P.dram("pin0", [19 * 128, T])
    for s in range(2):
        with P.scope():
            emit_dense_in(P, S, hfull0[:, s], dd["nmix0"], dd["win0_t"], 19, None, pin0[:, s * NT:(s + 1) * NT], NT)
    yT0 = P.dram("yT0", [1024, T])
    emit_s5(P, S, pin0[0:256], strip(dd, "s5_"), cs_d, yT0[0:256], T, 16)
    emit_ssd(P, S, pin0[256:1024], pin0[1024:1792], pin0[1792:2048], pin0[2048:2304], pin0[2304:2316], strip(dd, "sd_"), cs_d, yT0[256:1024], T, 12, 2)
    zp = P.dram("zp", [2, 256, T]); zr = P.dram("zr", [256, T])
    emit_glu_partial(P, S, yT0[0:256], dd["gluw_t"], zp, T)
    P.collective("ReduceScatter", zr, zp.re("s r t -> (s r) t"), PAIRS, op=ALU.add)
    mp0 = P.dram("mp0", [4, 2, 512, NT]); mix0 = P.dram("mix0", [2048, NT])
    emit_outproj_partial(P, S, yT0, dd["wout0_t"], mp0, T, glu=(zr, dd["glub"]))
    rs_mix(mp0, mix0)
    hb1 = P.dram("hb1", [2048, NT]); h1d0 = P.dram("h1d0", [2048, NT])
    emit_mlp_gate_v2(P, S, 0, dd["xT"], mix0, dd["pT0"], hb1, NT, Wf[0], h1d0)
    hfull1 = P.dram("hfull1", [8, 2, 256, NT])
    for q in range(8):
        P.collective("AllGather", hfull1[q].re("s f t -> (s f) t"), hb1[q * 256:(q + 1) * 256, :], PAIRS)
    pin1 = P.dram("pin1", [24 * 128, T])
    for s in range(2):
        with P.scope():
            emit_dense_in(P, S, hfull1[:, s], dd["nmix1"], dd["win1_t"], 24, None, pin1[:, s * NT:(s + 1) * NT], NT)
    yT1 = P.dram("yT1", [1024, T])
    emit_rwkv(P, S, pin1[0:512], pin1[512:1024], pin1[1024:1536], pin1[1536:1632], pin1[1664:1760], pin1[1792:2048], strip(dd, "rw_"), cs_d, yT1[0:512], T, 4)
    emit_lru(P, S, pin1[2048:2560], pin1[2560:3072], strip(dd, "lr_"), yT1[512:1024], T, 4)
    mp1 = P.dram("mp1", [4, 2, 512, NT]); mix1 = P.dram("mix1", [2048, NT])
    emit_outproj_partial(P, S, yT1, dd["wout1_t"], mp1, T)
    rs_mix(mp1, mix1)
    oT = la.out("outT", [2048, NT])
    h1d1 = P.dram("h1d1", [2048, NT])
    emit_mlp_gate_v2(P, S, 1, hb1, mix1, dd["pT1"], None, NT, Wf[1], h1d1, final=dd["nfin"], outT=oT)
    return la


def kernel(**inp):
    inp = {k: _np.asarray(v) for k, v in inp.items()}
    ins = host_inputs(inp)
    la = build_fused(ins[0])
    res = la.run(ins)
    out = _np.zeros((NB, SEQ, 2048), _np.float32)
    for c in range(8):
        b, hh = divmod(c, 2)
        out[b, hh * NT_CORE:(hh + 1) * NT_CORE, :] = res[c]["outT"].T
    return out
```

```python
import math, contextlib
import numpy as np
import concourse.bass as bass
import concourse.mybir as mybir
from concourse.bass_utils import run_bass_kernel_spmd

F32 = mybir.dt.float32
BF16 = mybir.dt.bfloat16
AF = mybir.ActivationFunctionType
ALU = mybir.AluOpType
AX = mybir.AxisListType

SAME_ENG_SYNC = True
CC_INC = 1


class Buf:
    __slots__ = ("name", "wconds", "rconds", "wsem", "wcount", "rsem", "rcount")

    ALL = []

    def __init__(self, name):
        Buf.ALL.append(self)
        self.name = name
        self.wconds = {}
        self.rconds = {}
        self.wsem = None
        self.wcount = 0
        self.rsem = None
        self.rcount = 0


class V:
    __slots__ = ("ap", "buf")

    def __init__(self, ap, buf):
        self.ap = ap
        self.buf = buf

    def __getitem__(self, key):
        return V(self.ap[key], self.buf)

    def re(self, s, **kw):
        return V(self.ap.rearrange(s, **kw), self.buf)

    def bc(self, shape):
        return V(self.ap.to_broadcast(shape), self.buf)

    def bitcast(self, dt):
        return V(self.ap.bitcast(dt), self.buf)

    @property
    def shape(self):
        return self.ap.shape


class Prog:
    ENG = ("pe", "dve", "act", "pool", "sp")

    def __init__(self, nc, stack):
        self.nc = nc
        Buf.ALL = []
        self.stack = stack
        self.engobj = {"pe": nc.tensor, "dve": nc.vector, "act": nc.scalar,
                       "pool": nc.gpsimd, "sp": nc.sync}
        self.q = {e: [] for e in self.ENG}
        self.cnt = {e: 0 for e in self.ENG}
        self.sems = {}
        self.nsem = 0
        for e in self.ENG:
            self.sems[("eng", e)] = self._newsem("c_" + e)
        self.known = {e: {} for e in self.ENG}
        self.uid = 0
        self.stacks = [stack]
        self.free_sems = []
        self.scope_sems = [[]]
        self.semval = {}
        self.ring_store = {}

    def _newsem(self, name):
        self.nsem += 1
        return self.stack.enter_context(self.nc.semaphore(name + "_%d" % self.nsem))

    def _dma_sem(self, key, name):
        if key not in self.sems:
            if self.free_sems:
                h, v = self.free_sems.pop()
            else:
                h, v = self._newsem("d"), 0
            self.sems[key] = h
            self.semval[key] = v
            self.scope_sems[-1].append(key)
        return self.sems[key]

    @contextlib.contextmanager
    def scope(self):
        es = contextlib.ExitStack()
        self.stacks.append(es)
        self.scope_sems.append([])
        mark = set(self.ring_store.keys())
        try:
            yield
        finally:
            self.barrier()
            for k in list(self.ring_store.keys()):
                if k not in mark:
                    del self.ring_store[k]
            for key in self.scope_sems.pop():
                self.free_sems.append((self.sems.pop(key), self.semval.pop(key)))
                for e in self.ENG:
                    self.known[e].pop(key, None)
            self.stacks.pop()
            es.close()

    def barrier(self):
        conds = {("eng", e): self.cnt[e] for e in self.ENG if self.cnt[e] > 0}
        for key, v in self.semval.items():
            if v > 0:
                conds[key] = v
        for e in self.ENG:
            waits = {}
            kn = self.known[e]
            for k, val in conds.items():
                if k == ("eng", e):
                    if e == "sp":
                        continue
                if kn.get(k, 0) < val:
                    waits[k] = val
            wl = self._emit_waits(e, waits)

            def thunk(en, wl=wl):
                for s_, v_ in wl:
                    en.wait_ge(s_, v_)
            self.q[e].append(thunk)
        for b in Buf.ALL:
            b.wconds = {}
            b.rconds = {}

    def ring(self, name, n, shape, dt=F32):
        if name not in self.ring_store:
            self.ring_store[name] = [[self.sbuf("%s_%d" % (name, i), shape, dt) for i in range(n)], 0]
        r = self.ring_store[name]
        b = r[0][r[1] % n]
        r[1] += 1
        return b

    def sbuf(self, name, shape, dt=F32):
        self.uid += 1
        t = self.stacks[-1].enter_context(self.nc.sbuf_tensor("%s_u%d" % (name, self.uid), list(shape), dt))
        return V(t.ap() if hasattr(t, "ap") and callable(getattr(t, "ap")) else t[:], Buf(name))

    def psum(self, name, shape, dt=F32):
        t = self.stack.enter_context(self.nc.psum_tensor(name, list(shape), dt))
        return V(t.ap() if hasattr(t, "ap") and callable(getattr(t, "ap")) else t[:], Buf(name))

    def dram(self, name, shape, dt=F32, kind="Internal"):
        t = self.nc.dram_tensor(name, list(shape), dt, kind=kind)
        return V(t.ap(), Buf(name))

    def alias(self, v, name):
        return V(v.ap, Buf(name))

    def _need(self, eng, conds, waits, war=False):
        kn = self.known[eng]
        for k, val in conds.items():
            if k not in self.sems:
                continue
            if k == ("eng", eng):
                if eng == "pe" or not SAME_ENG_SYNC or war:
                    continue
            if kn.get(k, 0) >= val:
                continue
            if waits.get(k, 0) < val:
                waits[k] = val

    def _emit_waits(self, eng, waits):
        kn = self.known[eng]
        out = []
        for k, val in waits.items():
            kn[k] = max(kn.get(k, 0), val)
            out.append((self.sems[k], val))
        return out

    def op(self, eng, fn, reads=(), writes=()):
        waits = {}
        for v in reads:
            self._need(eng, v.buf.wconds, waits)
        for v in writes:
            self._need(eng, v.buf.wconds, waits)
            self._need(eng, v.buf.rconds, waits, war=True)
        wl = self._emit_waits(eng, waits)
        self.cnt[eng] += 1
        n = self.cnt[eng]
        k = ("eng", eng)
        sem = self.sems[k]

        def thunk(e, wl=wl, fn=fn, sem=sem):
            for s, val in wl:
                e.wait_ge(s, val)
            fn(e).then_inc(sem, 1)
        self.q[eng].append(thunk)
        for v in reads:
            v.buf.rconds[k] = n
        for v in writes:
            v.buf.wconds = {k: n}
            v.buf.rconds = {}
        return n

    def dma(self, queue, out, in_, **kw):
        eng = queue
        waits = {}
        self._need(eng, in_.buf.wconds, waits)
        own_w = ("w", id(out.buf))
        for kk, val in out.buf.wconds.items():
            if kk == own_w:
                continue
            self._need(eng, {kk: val}, waits)
        self._need(eng, out.buf.rconds, waits)
        wl = self._emit_waits(eng, waits)
        b = out.buf
        key = ("w", id(b))
        sem = self._dma_sem(key, b.name)
        self.semval[key] += 16
        val = self.semval[key]
        b.wcount = val
        self._keep = getattr(self, "_keep", [])
        self._keep.append(b)
        rb = in_.buf
        rkey = None

        def thunk(e, wl=wl, sem=sem, o=out.ap, i=in_.ap, kw=kw):
            for s, v_ in wl:
                e.wait_ge(s, v_)
            e.dma_start(out=o, in_=i, **kw).then_inc(sem, 16)
        self.q[eng].append(thunk)
        if own_w in out.buf.wconds or not out.buf.wconds or True:
            newc = {key: val}
            out.buf.wconds = newc
            out.buf.rconds = {}
        in_.buf.rconds[key] = max(in_.buf.rconds.get(key, 0), val)

    def collective(self, kind, out, in_, groups, op=None):
        eng = "pool"
        waits = {}
        self._need(eng, in_.buf.wconds, waits)
        self._need(eng, out.buf.wconds, waits)
        self._need(eng, out.buf.rconds, waits)
        wl = self._emit_waits(eng, waits)
        key = ("cc", id(out.buf))
        if key not in self.sems:
            self.sems[key] = self._newsem("cc")
            self.semval[key] = 0
        self.semval[key] += CC_INC
        val = self.semval[key]
        sem = self.sems[key]
        self._keep = getattr(self, "_keep", [])
        self._keep.append(out.buf)
        op = ALU.bypass if op is None else op

        def thunk(e, wl=wl, sem=sem, o=out.ap, i=in_.ap):
            for s_, v_ in wl:
                e.wait_ge(s_, v_)
            e.collective_compute(kind, op, replica_groups=groups, ins=[i], outs=[o]).then_inc(sem, CC_INC)
        self.q[eng].append(thunk)
        out.buf.wconds = {key: val}
        out.buf.rconds = {}
        in_.buf.rconds[key] = val

    def wait_all(self, eng, views):
        waits = {}
        for v in views:
            self._need(eng, v.buf.wconds, waits)
        wl = self._emit_waits(eng, waits)

        def thunk(e, wl=wl):
            for s, v_ in wl:
                e.wait_ge(s, v_)
        self.q[eng].append(thunk)

    def finish(self):
        nc = self.nc
        with nc.Block() as block:
            @block.tensor
            def _(e):
                for t in self.q["pe"]:
                    t(e)

            @block.vector
            def _(e):
                for t in self.q["dve"]:
                    t(e)

            @block.scalar
            def _(e):
                for t in self.q["act"]:
                    t(e)

            @block.gpsimd
            def _(e):
                for t in self.q["pool"]:
                    t(e)

            @block.sync
            def _(e):
                for t in self.q["sp"]:
                    t(e)

D = 2048
KT_D = 16
EPS = 1e-6
CH = 512
MMDT = BF16


class Shared:
    def __init__(self, P):
        self.P = P
        self.ps = [P.psum("psr%d" % i, [128, 512]) for i in range(8)]
        self.pi = 0
        self.ones = P.sbuf("ones", [128, 128])
        P.op("dve", lambda e: e.memset(self.ones.ap, 1.0), writes=[self.ones])
        self.ones_r = P.sbuf("ones_r", [128, 128])
        P.op("dve", lambda e: e.tensor_copy(out=self.ones_r.ap.bitcast(mybir.dt.float32r), in_=self.ones.ap), reads=[self.ones], writes=[self.ones_r])
        self.epsc = P.sbuf("epsc", [128, 1])
        P.op("dve", lambda e: e.memset(self.epsc.ap, EPS), writes=[self.epsc])
        self.rr = {}
        self.castn = 0

    def next_psum(self):
        p = self.ps[self.pi % 8]
        self.pi += 1
        return p

    def ring(self, name, n, shape, dt=F32):
        return self.P.ring(name, n, shape, dt)


def load_cols(P, dst, vec_dram, KT):
    with P.nc.allow_non_contiguous_dma(reason="tiny param vector"):
        pass
    P.dma("sp", dst, vec_dram.re("(k p) -> p k", p=128), allow_slow_non_contiguous=True)


def rmsnorm_T(P, S, src, gcol, out, KT, NT, out_scale_extra=None):
    pss = S.next_psum()
    for k in range(KT):
        sq = S.ring("sq", 3, [128, CH])
        P.op("act", lambda e, sq=sq, k=k: e.activation(out=sq.ap[:, 0:NT].bitcast(mybir.dt.float32r), in_=src.ap[:, k, :], func=AF.Square),
             reads=[src], writes=[sq])
        P.op("pe", lambda e, sq=sq, k=k: e.matmul(pss.ap[:, 0:NT], lhsT=S.ones_r.ap.bitcast(mybir.dt.float32r),
                                                 rhs=sq.ap[:, 0:NT].bitcast(mybir.dt.float32r),
                                                 start=(k == 0), stop=(k == KT - 1)),
             reads=[S.ones_r, sq], writes=[pss])
    rs = S.ring("rstd", 2, [128, CH])
    P.op("act", lambda e: e.activation(out=rs.ap[:, 0:NT], in_=pss.ap[:, 0:NT], func=AF.Sqrt,
                                      bias=S.epsc.ap, scale=1.0 / (KT * 128)),
         reads=[pss, S.epsc], writes=[rs])
    P.op("dve", lambda e: e.reciprocal(out=rs.ap[:, 0:NT], in_=rs.ap[:, 0:NT]), reads=[rs], writes=[rs])
    for k in range(KT):
        P.op("dve", lambda e, k=k: e.scalar_tensor_tensor(out=out.ap[:, k, :], in0=src.ap[:, k, :],
                                                         scalar=gcol.ap[:, k:k + 1], in1=rs.ap[:, 0:NT],
                                                         op0=ALU.mult, op1=ALU.mult),
             reads=[src, gcol, rs], writes=[out])


def stream_mm(P, S, Wt, KT, NCB, rhs_fn, nchunk, NTc, evac, wname="wb", nwb=3, Ms=None, pre=None):
    wbs = {}

    def prep(c):
        wb = S.ring(wname + str(KT), nwb, [128, KT, 128], MMDT)
        for k0 in range(0, KT, 16):
            k1 = min(KT, k0 + 16)
            stg = S.ring("wstg", 3, [128, 16, 128])
            P.dma("sp", stg[:, 0:k1 - k0, :], Wt[c][:, k0:k1, :])
            P.op("dve", lambda e, stg=stg, wb=wb, k0=k0, k1=k1: e.tensor_copy(out=wb.ap[:, k0:k1, :], in_=stg.ap[:, 0:k1 - k0, :]),
                 reads=[stg], writes=[wb])
        wbs[c] = wb

    prep(0)
    for c in range(NCB):
        if c + 1 < NCB:
            prep(c + 1)
        wb = wbs.pop(c)
        M = 128 if Ms is None else Ms[c]
        if pre is not None:
            pre(c)
        for j in range(nchunk):
            ps = S.next_psum()
            for k in range(KT):
                r = rhs_fn(k, j)
                P.op("pe", lambda e, wb=wb, ps=ps, r=r, k=k, M=M: e.matmul(
                    ps.ap[0:M, 0:NTc], lhsT=wb.ap[:, k, 0:M], rhs=r.ap, start=(k == 0), stop=(k == KT - 1)),
                    reads=[wb, r], writes=[ps])
            evac(c, j, ps, M)


def emit_dense_in(P, S, hT, gvec, Wt, NCB, Ms, projT, NT):
    nch = NT // CH
    gcol = P.sbuf("gcolA", [128, KT_D])
    load_cols(P, gcol, gvec, KT_D)
    hn = P.sbuf("hnA", [128, KT_D, NT], MMDT)
    hnj = [P.alias(hn, "hnA_%d" % j) for j in range(nch)]
    for j in range(nch):
        ht = S.ring("htA", 2, [128, KT_D, CH])
        for q in range(8):
            P.dma("sp", ht[:, 2 * q:2 * q + 2, :], hT[q].re("(k p) n -> p k n", p=128)[:, :, j * CH:(j + 1) * CH])
        rmsnorm_T(P, S, ht, gcol, hnj[j][:, :, j * CH:(j + 1) * CH], KT_D, CH)

    def rhs_fn(k, j):
        return hnj[j][:, k, j * CH:(j + 1) * CH]

    def evac(c, j, ps, M):
        ob = S.ring("evA", 4, [128, CH])
        P.op("act", lambda e: e.copy(out=ob.ap[0:M, :], in_=ps.ap[0:M, :]), reads=[ps], writes=[ob])
        P.dma("act", projT[c * 128:c * 128 + M, j * CH:(j + 1) * CH], ob[0:M, :])

    stream_mm(P, S, Wt, KT_D, NCB, rhs_fn, nch, CH, evac, Ms=Ms)


def emit_dense_out(P, S, L, hT, yT, pT, hT_out, NT, W, glu=None, final=None, outT=None, h1T=None):
    nch = NT // CH
    gF = P.sbuf("gF%d" % L, [128, KT_D]); load_cols(P, gF, W["nffn"], KT_D)
    gP = P.sbuf("gP%d" % L, [128, KT_D]); load_cols(P, gP, W["npl"], KT_D)
    if final is not None:
        gN = P.sbuf("gN%d" % L, [128, KT_D]); load_cols(P, gN, final, KT_D)
    yTv = yT.re("(k p) n -> p k n", p=128)
    hTv = hT.re("(k p) n -> p k n", p=128)
    h1Tv = h1T.re("(k p) n -> p k n", p=128)
    hoTv = hT_out.re("(k p) n -> p k n", p=128) if hT_out is not None else None
    sc1 = P.scope(); sc1.__enter__()
    yb = P.sbuf("ybC", [128, KT_D, NT], MMDT)
    k_start = 0
    if glu is not None:
        gluw_t, glub = glu
        gb = P.sbuf("glub_sb", [128, 4]); load_cols(P, gb, glub, 4)
        actf = P.sbuf("actf", [128, 4, NT])
        P.dma("sp", actf, yTv[:, 0:4, :])
        actb = P.sbuf("actb", [128, 4, NT], MMDT)
        P.op("dve", lambda e: e.tensor_copy(out=actb.ap, in_=actf.ap), reads=[actf], writes=[actb])

        def evac_glu(c, j, ps, M):
            sg = S.ring("sgl", 2, [128, CH])
            P.op("act", lambda e: e.activation(out=sg.ap, in_=ps.ap, func=AF.Sigmoid, bias=gb.ap[:, c:c + 1]),
                 reads=[ps, gb], writes=[sg])
            P.op("dve", lambda e: e.tensor_tensor(out=yb.ap[:, c, j * CH:(j + 1) * CH],
                                                  in0=actf.ap[:, c, j * CH:(j + 1) * CH], in1=sg.ap, op=ALU.mult),
                 reads=[actf, sg], writes=[yb])
        stream_mm(P, S, gluw_t, 4, 4, lambda k, j: actb[:, k, j * CH:(j + 1) * CH], nch, CH, evac_glu)
        k_start = 4
    for k in range(k_start, KT_D, 4):
        P.dma("pool", yb[:, k:k + 4, :], yTv[:, k:k + 4, :])

    def pre_o(c):
        pass

    def evac_o(c, j, ps, M):
        hb = S.ring("hbC", 3, [128, CH])
        P.dma("sp", hb, hT[c * 128:(c + 1) * 128, j * CH:(j + 1) * CH])
        P.op("dve", lambda e: e.tensor_tensor(out=hb.ap, in0=ps.ap, in1=hb.ap, op=ALU.add),
             reads=[ps, hb], writes=[hb])
        P.dma("sp", h1T[c * 128:(c + 1) * 128, j * CH:(j + 1) * CH], hb)
    stream_mm(P, S, W["out_t"], KT_D, 16, lambda k, j: yb[:, k, j * CH:(j + 1) * CH], nch, CH, evac_o)
    sc1.__exit__(None, None, None)
    sc2 = P.scope(); sc2.__enter__()
    plp = P.sbuf("plpC", [128, 16, 2, 128], MMDT)
    for c in range(16):
        P.dma("pool", plp[:, c, :, :], W["plp_t"][c])
    pTv = pT.re("(k p) n -> p k n", p=128)
    for j in range(nch):
        sl = slice(j * CH, (j + 1) * CH)
        h1 = S.ring("h1C", 1, [128, KT_D, CH])
        P.dma("sp", h1, h1Tv[:, :, sl])
        hn = S.ring("hnC", 1, [128, KT_D, CH], MMDT)
        rmsnorm_T(P, S, h1, gF, hn, KT_D, CH)
        hid = S.ring("hidC", 1, [128, 64, CH], MMDT)

        def evac1(c, jj, ps, M):
            rl = S.ring("rlC", 3, [128, CH])
            P.op("act", lambda e: e.activation(out=rl.ap, in_=ps.ap, func=AF.Relu), reads=[ps], writes=[rl])
            P.op("pool", lambda e: e.tensor_tensor(out=hid.ap[:, c, :], in0=rl.ap, in1=rl.ap, op=ALU.mult),
                 reads=[rl], writes=[hid])
        stream_mm(P, S, W["w1_t"], KT_D, 64, lambda k, jj: hn[:, k, :], 1, CH, evac1)

        def evac2(c, jj, ps, M):
            P.op("dve", lambda e: e.tensor_tensor(out=h1.ap[:, c, :], in0=ps.ap, in1=h1.ap[:, c, :], op=ALU.add),
                 reads=[ps, h1], writes=[h1])
        stream_mm(P, S, W["w2_t"], 64, 16, lambda k, jj: hid[:, k, :], 1, CH, evac2)
        rmsnorm_T(P, S, h1, gP, hn, KT_D, CH)
        pb = S.ring("pbC", 1, [128, 2, CH], MMDT)
        P.dma("pool", pb, pTv[:, :, sl])

        def evac3(c, jj, ps, M):
            psp = S.next_psum()
            for k in range(2):
                P.op("pe", lambda e, k=k: e.matmul(psp.ap, lhsT=plp.ap[:, c, k, :], rhs=pb.ap[:, k, :],
                                                   start=(k == 0), stop=(k == 1)),
                     reads=[plp, pb], writes=[psp])
            sg = S.ring("sgC", 2, [128, CH])
            P.op("act", lambda e: e.activation(out=sg.ap, in_=ps.ap, func=AF.Sigmoid), reads=[ps], writes=[sg])
            P.op("dve", lambda e: e.tensor_tensor(out=sg.ap, in0=psp.ap, in1=sg.ap, op=ALU.mult),
                 reads=[psp, sg], writes=[sg])
            P.op("pool", lambda e: e.tensor_tensor(out=h1.ap[:, c, :], in0=h1.ap[:, c, :], in1=sg.ap, op=ALU.add),
                 reads=[h1, sg], writes=[h1])
        stream_mm(P, S, W["gate_t"], KT_D, 16, lambda k, jj: hn[:, k, :], 1, CH, evac3)
        if hT_out is not None:
            P.dma("sp", hoTv[:, :, sl], h1)
        if final is not None:
            rmsnorm_T(P, S, h1, gN, h1, KT_D, CH)
            P.dma("sp", outT.re("(k p) n -> p k n", p=128)[:, :, sl], h1)
    sc2.__exit__(None, None, None)


def emit_glu_partial(P, S, actT, gluw_t, zp, T):
    nch = T // CH
    with P.scope():
        ab = P.sbuf("glu_ab", [128, 2, T], MMDT)
        av = actT.re("(k p) n -> p k n", p=128)
        for k in range(2):
            for j0 in range(0, T, 2048):
                P.dma("pool", ab[:, k, j0:j0 + 2048], av[:, k, j0:j0 + 2048])

        def evac(c, j, ps, M):
            ob = S.ring("glu_ev", 4, [128, CH])
            P.op("act", lambda e: e.copy(out=ob.ap, in_=ps.ap), reads=[ps], writes=[ob])
            P.dma("act", zp[c // 2, (c % 2) * 128:(c % 2 + 1) * 128, j * CH:(j + 1) * CH], ob)
        stream_mm(P, S, gluw_t, 2, 4, lambda k, j: ab[:, k, j * CH:(j + 1) * CH], nch, CH, evac)


def emit_outproj_partial(P, S, yT, wout_t, mp, T, glu=None):
    nch = T // CH
    half = T // 2
    with P.scope():
        yb = P.sbuf("op_yb", [128, 8, T], MMDT)
        yv = yT.re("(k p) n -> p k n", p=128)
        k0 = 0
        if glu is not None:
            zT, glub = glu
            gb = P.sbuf("op_gb", [128, 2]); load_cols(P, gb, glub, 2)
            zv = zT.re("(k p) n -> p k n", p=128)
            for k in range(2):
                for j in range(nch):
                    sl = slice(j * CH, (j + 1) * CH)
                    a = S.ring("op_a", 3, [128, CH]); z = S.ring("op_z", 3, [128, CH])
                    P.dma("sp", a, yv[:, k, sl]); P.dma("sp", z, zv[:, k, sl])
                    P.op("act", lambda e, z=z, k=k: e.activation(out=z.ap, in_=z.ap, func=AF.Sigmoid, bias=gb.ap[:, k:k + 1]), reads=[z, gb], writes=[z])
                    P.op("dve", lambda e, a=a, z=z, k=k, sl=sl: e.tensor_tensor(out=yb.ap[:, k, sl], in0=a.ap, in1=z.ap, op=ALU.mult), reads=[a, z], writes=[yb])
            k0 = 2
        for k in range(k0, 8):
            for j0 in range(0, T, 2048):
                P.dma("pool", yb[:, k, j0:j0 + 2048], yv[:, k, j0:j0 + 2048])

        def evac(c, j, ps, M):
            ob = S.ring("op_ev", 4, [128, CH])
            P.op("act", lambda e: e.copy(out=ob.ap, in_=ps.ap), reads=[ps], writes=[ob])
            t0 = j * CH
            P.dma("act", mp[c // 4, t0 // half, (c % 4) * 128:(c % 4 + 1) * 128, t0 % half:t0 % half + CH], ob)
        stream_mm(P, S, wout_t, 8, 16, lambda k, j: yb[:, k, j * CH:(j + 1) * CH], nch, CH, evac)


def emit_mlp_gate(P, S, L, hT, mixT, pT, hT_out, NT, W, final=None, outT=None):
    nch = NT // CH
    with P.scope():
        gF = P.sbuf("gF%d" % L, [128, KT_D]); load_cols(P, gF, W["nffn"], KT_D)
        gP = P.sbuf("gP%d" % L, [128, KT_D]); load_cols(P, gP, W["npl"], KT_D)
        if final is not None:
            gN = P.sbuf("gN%d" % L, [128, KT_D]); load_cols(P, gN, final, KT_D)
        hTv = hT.re("(k p) n -> p k n", p=128)
        mTv = mixT.re("(k p) n -> p k n", p=128)
        hoTv = hT_out.re("(k p) n -> p k n", p=128) if hT_out is not None else None
        plp = P.sbuf("plpC", [128, 16, 2, 128], MMDT)
        for c in range(16):
            P.dma("pool", plp[:, c, :, :], W["plp_t"][c])
        pTv = pT.re("(k p) n -> p k n", p=128)

        def tile_body(j):
            sl = slice(j * CH, (j + 1) * CH)
            h1 = S.ring("h1C", 1, [128, KT_D, CH])
            hn = S.ring("hnC", 1, [128, KT_D, CH], MMDT)
            hid = S.ring("hidC", 1, [128, 64, CH], MMDT)
            P.dma("sp", h1, hTv[:, :, sl])
            for q in range(8):
                mt = S.ring("mixC", 2, [128, 2, CH])
                P.dma("sp", mt, mTv[:, q * 2:(q + 1) * 2, sl])
                P.op("dve", lambda e, mt=mt, q=q: e.tensor_tensor(out=h1.ap[:, q * 2:(q + 1) * 2, :], in0=h1.ap[:, q * 2:(q + 1) * 2, :], in1=mt.ap, op=ALU.add),
                     reads=[h1, mt], writes=[h1])
            rmsnorm_T(P, S, h1, gF, hn, KT_D, CH)

            def evac1(c, jj, ps, M):
                rl = S.ring("rlC", 3, [128, CH])
                P.op("act", lambda e: e.activation(out=rl.ap, in_=ps.ap, func=AF.Relu), reads=[ps], writes=[rl])
                P.op("pool", lambda e: e.tensor_tensor(out=hid.ap[:, c, :], in0=rl.ap, in1=rl.ap, op=ALU.mult), reads=[rl], writes=[hid])
            stream_mm(P, S, W["w1_t"], KT_D, 64, lambda k, jj: hn[:, k, :], 1, CH, evac1)

            def evac2(c, jj, ps, M):
                P.op("dve", lambda e: e.tensor_tensor(out=h1.ap[:, c, :], in0=ps.ap, in1=h1.ap[:, c, :], op=ALU.add), reads=[ps, h1], writes=[h1])
            stream_mm(P, S, W["w2_t"], 64, 16, lambda k, jj: hid[:, k, :], 1, CH, evac2)
            rmsnorm_T(P, S, h1, gP, hn, KT_D, CH)
            pb = S.ring("pbC", 1, [128, 2, CH], MMDT)
            P.dma("pool", pb, pTv[:, :, sl])

            def evac3(c, jj, ps, M):
                psp = S.next_psum()
                for k in range(2):
                    P.op("pe", lambda e, k=k: e.matmul(psp.ap, lhsT=plp.ap[:, c, k, :], rhs=pb.ap[:, k, :], start=(k == 0), stop=(k == 1)),
                         reads=[plp, pb], writes=[psp])
                sg = S.ring("sgC", 2, [128, CH])
                P.op("act", lambda e: e.activation(out=sg.ap, in_=ps.ap, func=AF.Sigmoid), reads=[ps], writes=[sg])
                P.op("dve", lambda e: e.tensor_tensor(out=sg.ap, in0=psp.ap, in1=sg.ap, op=ALU.mult), reads=[psp, sg], writes=[sg])
                P.op("pool", lambda e: e.tensor_tensor(out=h1.ap[:, c, :], in0=h1.ap[:, c, :], in1=sg.ap, op=ALU.add), reads=[h1, sg], writes=[h1])
            stream_mm(P, S, W["gate_t"], KT_D, 16, lambda k, jj: hn[:, k, :], 1, CH, evac3)
            if hT_out is not None:
                P.dma("sp", hoTv[:, :, sl], h1)
            if final is not None:
                rmsnorm_T(P, S, h1, gN, h1, KT_D, CH)
                P.dma("sp", outT.re("(k p) n -> p k n", p=128)[:, :, sl], h1)

        for j in range(nch):
            tile_body(j)


def emit_mlp_gate_v2(P, S, L, hT, mixT, pT, hT_out, NT, W, h1d, final=None, outT=None):
    nch = NT // CH
    hTv = hT.re("(k p) n -> p k n", p=128)
    mTv = mixT.re("(k p) n -> p k n", p=128)
    h1v = h1d.re("(k p) n -> p k n", p=128)
    hreg = [[P.alias(h1d, "h1d_%d_%d" % (c, j)) for j in range(nch)] for c in range(16)]
    houts = None
    if hT_out is not None:
        houts = [[P.alias(hT_out, "hout_%d" % c)] * nch for c in range(16)]

    def norm_pass(src_v, gcol, dst_fn, add_v=None, store_v=None, tag=""):
        with P.scope():
            for j in range(nch):
                sl = slice(j * CH, (j + 1) * CH)
                ht = S.ring("npH", 2, [128, KT_D, CH])
                P.dma("sp", ht, src_v[:, :, sl])
                if add_v is not None:
                    for q in range(8):
                        mt = S.ring("npM", 2, [128, 2, CH])
                        P.dma("sp", mt, add_v[:, q * 2:(q + 1) * 2, sl])
                        P.op("dve", lambda e, mt=mt, q=q, ht=ht: e.tensor_tensor(out=ht.ap[:, q * 2:(q + 1) * 2, :], in0=ht.ap[:, q * 2:(q + 1) * 2, :],
                             in1=mt.ap, op=ALU.add), reads=[ht, mt], writes=[ht])
                if store_v is not None:
                    P.dma("sp", store_v[:, :, sl], ht)
                rmsnorm_T(P, S, ht, gcol, dst_fn(j), KT_D, CH)

    with P.scope():
        gF = P.sbuf("gF%d" % L, [128, KT_D]); load_cols(P, gF, W["nffn"], KT_D)
        gP = P.sbuf("gP%d" % L, [128, KT_D]); load_cols(P, gP, W["npl"], KT_D)
        hn = P.sbuf("hnM", [128, KT_D, NT], MMDT)
        norm_pass(hTv, gF, lambda j: hn[:, :, j * CH:(j + 1) * CH], add_v=mTv, store_v=h1v)
        with P.scope():
            hid = P.sbuf("hidM", [128, 16, NT], MMDT)
            for q in range(4):
                def evac1(c, j, ps, M):
                    rl = S.ring("rlM", 3, [128, CH])
                    P.op("act", lambda e: e.activation(out=rl.ap, in_=ps.ap, func=AF.Relu), reads=[ps], writes=[rl])
                    P.op("pool", lambda e: e.tensor_tensor(out=hid.ap[:, c, j * CH:(j + 1) * CH], in0=rl.ap, in1=rl.ap, op=ALU.mult), reads=[rl], writes=[hid])
                stream_mm(P, S, W["w1_t"][q * 16:(q + 1) * 16], KT_D, 16, lambda k, j: hn[:, k, j * CH:(j + 1) * CH], nch, CH, evac1)

                pend = {}

                def pre2(c):
                    for j in range(nch):
                        hb = S.ring("hbM", 8, [128, CH])
                        reg = V(h1d.ap[c * 128:(c + 1) * 128, j * CH:(j + 1) * CH], hreg[c][j].buf)
                        P.dma("pool", hb, reg)
                        pend[(c, j)] = (hb, reg)

                def evac2(c, j, ps, M):
                    hb, reg = pend.pop((c, j))
                    P.op("dve", lambda e: e.tensor_tensor(out=hb.ap, in0=ps.ap, in1=hb.ap, op=ALU.add), reads=[ps, hb], writes=[hb])
                    P.dma("act", reg, hb)
                stream_mm(P, S, W["w2_t"][:, :, q * 16:(q + 1) * 16, :], KT_D, 16, lambda k, j: hid[:, k, j * CH:(j + 1) * CH], nch, CH, evac2, pre=pre2)
        norm_pass(h1v, gP, lambda j: hn[:, :, j * CH:(j + 1) * CH])
        with P.scope():
            plp = P.sbuf("plpM", [128, 16, 2, 128], MMDT)
            for c in range(16):
                P.dma("pool", plp[:, c, :, :], W["plp_t"][c])
            pb = P.sbuf("pbM", [128, 2, NT], MMDT)
            pTv = pT.re("(k p) n -> p k n", p=128)
            for k in range(2):
                P.dma("pool", pb[:, k, :], pTv[:, k, :])
            dst = hT_out if hT_out is not None else h1d
            dregs = houts if hT_out is not None else hreg

            pend3 = {}

            def pre3(c):
                for j in range(nch):
                    sl = slice(j * CH, (j + 1) * CH)
                    hb = S.ring("hbG", 8, [128, CH])
                    P.dma("pool", hb, V(h1d.ap[c * 128:(c + 1) * 128, sl], hreg[c][j].buf))
                    pend3[(c, j)] = hb

            def evac3(c, j, ps, M):
                sl = slice(j * CH, (j + 1) * CH)
                psp = S.next_psum()
                for k in range(2):
                    P.op("pe", lambda e, k=k: e.matmul(psp.ap, lhsT=plp.ap[:, c, k, :], rhs=pb.ap[:, k, sl], start=(k == 0), stop=(k == 1)),
                         reads=[plp, pb], writes=[psp])
                sg = S.ring("sgM", 2, [128, CH])
                P.op("act", lambda e: e.activation(out=sg.ap, in_=ps.ap, func=AF.Sigmoid), reads=[ps], writes=[sg])
                P.op("dve", lambda e: e.tensor_tensor(out=sg.ap, in0=psp.ap, in1=sg.ap, op=ALU.mult), reads=[psp, sg], writes=[sg])
                hb = pend3.pop((c, j))
                P.op("pool", lambda e: e.tensor_tensor(out=hb.ap, in0=hb.ap, in1=sg.ap, op=ALU.add), reads=[hb, sg], writes=[hb])
                P.dma("act", V(dst.ap[c * 128:(c + 1) * 128, sl], dregs[c][j].buf), hb)
            stream_mm(P, S, W["gate_t"], KT_D, 16, lambda k, j: hn[:, k, j * CH:(j + 1) * CH], nch, CH, evac3, pre=pre3)
    if final is not None:
        with P.scope():
            gN = P.sbuf("gN%d" % L, [128, KT_D]); load_cols(P, gN, final, KT_D)
            oTv = outT.re("(k p) n -> p k n", p=128)
            for j in range(nch):
                sl = slice(j * CH, (j + 1) * CH)
                ht = S.ring("npF", 2, [128, KT_D, CH])
                P.dma("sp", ht, h1v[:, :, sl])
                rmsnorm_T(P, S, ht, gN, ht, KT_D, CH)
                P.dma("sp", oTv[:, :, sl], ht)
CH = 512
PI = math.pi
FAST32 = True


def r32(ap):
    return ap.bitcast(mybir.dt.float32r) if FAST32 else ap


def tt(P, eng, out, a, b, op):
    P.op(eng, lambda e: e.tensor_tensor(out=out.ap, in0=a.ap, in1=b.ap, op=op), reads=[a, b], writes=[out])


def ts(P, eng, out, a, s1, op0, s2=None, op1=None):
    rd = [a] + [x for x in (s1, s2) if isinstance(x, V)]
    g = lambda x: x.ap if isinstance(x, V) else x
    if op1 is None:
        P.op(eng, lambda e: e.tensor_scalar(out=out.ap, in0=a.ap, scalar1=g(s1), scalar2=None, op0=op0), reads=rd, writes=[out])
    else:
        P.op(eng, lambda e: e.tensor_scalar(out=out.ap, in0=a.ap, scalar1=g(s1), scalar2=g(s2), op0=op0, op1=op1), reads=rd, writes=[out])


def act(P, out, a, func, scale=None, bias=None):
    rd = [a] + [x for x in (scale, bias) if isinstance(x, V)]
    kw = {}
    if scale is not None:
        kw["scale"] = scale.ap if isinstance(scale, V) else scale
    if bias is not None:
        kw["bias"] = bias.ap if isinstance(bias, V) else bias
    P.op("act", lambda e: e.activation(out=out.ap, in_=a.ap, func=func, **kw), reads=rd, writes=[out])


def sin_reduced(P, tmp, out, ang, shift, zero_col):
    f1, f2, i1 = tmp
    ts(P, "dve", f1, ang, shift, ALU.add)
    ts(P, "dve", f2, f1, 1.0 / (2 * PI), ALU.mult)
    P.op("dve", lambda e: e.tensor_copy(out=i1.ap, in_=f2.ap), reads=[f2], writes=[i1])
    P.op("dve", lambda e: e.tensor_copy(out=f2.ap, in_=i1.ap), reads=[i1], writes=[f2])
    P.op("dve", lambda e: e.scalar_tensor_tensor(out=f1.ap, in0=f2.ap, scalar=-2 * PI, in1=f1.ap, op0=ALU.mult, op1=ALU.add),
         reads=[f1, f2], writes=[f1])
    ts(P, "dve", f2, f1, PI, ALU.is_gt, -2 * PI, ALU.mult)
    tt(P, "dve", f1, f1, f2, ALU.add)
    ts(P, "dve", f2, f1, -PI, ALU.is_lt, 2 * PI, ALU.mult)
    tt(P, "dve", f1, f1, f2, ALU.add)
    act(P, out, f1, AF.Sin)


def s5_abar(P, lr, li, lstep, shape, tag):
    mk = lambda n, dt=F32: P.sbuf("s5%s_%s" % (tag, n), shape, dt)
    step = mk("step"); mag = mk("mag"); ang = mk("ang"); ar = mk("ar"); ai = mk("ai")
    f1 = mk("f1"); f2 = mk("f2"); i1 = mk("i1", mybir.dt.int32)
    act(P, step, lstep, AF.Exp)
    tt(P, "dve", mag, lr, step, ALU.mult)
    act(P, mag, mag, AF.Exp)
    tt(P, "dve", ang, li, step, ALU.mult)
    sin_reduced(P, (f1, f2, i1), ai, ang, 0.0, None)
    sin_reduced(P, (f1, f2, i1), ar, ang, PI / 2, None)
    tt(P, "dve", ar, ar, mag, ALU.mult)
    tt(P, "dve", ai, ai, mag, ALU.mult)
    return ar, ai, (f1, f2)


def emit_s5(P, S, uT, prm, cst, yT, T, NG):
    nch = T // CH
    nlev = int(round(math.log2(T)))
    assert 2 ** nlev == T
    with P.scope():
        ld = lambda name, shape, src: (lambda t: (P.dma("sp", t, src), t)[1])(P.sbuf(name, shape))
        shp = [16, NG, 64]
        lr = ld("s5r_lr", shp, prm["lr_row"]); li = ld("s5r_li", shp, prm["li_row"]); ls = ld("s5r_ls", shp, prm["ls_row"])
        bre = ld("s5r_bre", shp, prm["bT_re"]); bim = ld("s5r_bim", shp, prm["bT_im"])
        ar, ai, (f1, f2) = s5_abar(P, lr, li, ls, shp, "r")
        den = P.sbuf("s5r_den", shp); cre = P.sbuf("s5r_cre", shp); cim = P.sbuf("s5r_cim", shp)
        tt(P, "dve", den, lr, lr, ALU.mult); tt(P, "dve", f1, li, li, ALU.mult); tt(P, "dve", den, den, f1, ALU.add)
        P.op("dve", lambda e: e.reciprocal(out=den.ap, in_=den.ap), reads=[den], writes=[den])
        ts(P, "dve", ar, ar, -1.0, ALU.add)
        tt(P, "dve", cre, ar, lr, ALU.mult); tt(P, "dve", f1, ai, li, ALU.mult); tt(P, "dve", cre, cre, f1, ALU.add)
        tt(P, "dve", cre, cre, den, ALU.mult)
        tt(P, "dve", cim, ai, lr, ALU.mult); tt(P, "dve", f1, ar, li, ALU.mult); tt(P, "dve", cim, cim, f1, ALU.subtract)
        tt(P, "dve", cim, cim, den, ALU.mult)
        BT = P.sbuf("s5_BT", [16, NG, 128])
        tt(P, "dve", f1, cre, bre, ALU.mult); tt(P, "dve", f2, cim, bim, ALU.mult)
        tt(P, "dve", BT[:, :, 0:64], f1, f2, ALU.subtract)
        tt(P, "dve", f1, cre, bim, ALU.mult); tt(P, "dve", f2, cim, bre, ALU.mult)
        tt(P, "dve", BT[:, :, 64:128], f1, f2, ALU.add)
        shc = [128, NG]
        lrc = ld("s5c_lr", shc, prm["lr_col"]); lic = ld("s5c_li", shc, prm["li_col"]); lsc = ld("s5c_ls", shc, prm["ls_col"])
        arc, aic, (g1, g2) = s5_abar(P, lrc, lic, lsc, shc, "c")
        sgn = ld("s5_sgn", [128, 1], cst["sgn"])
        ident = ld("s5_id", [128, 128], cst["ident"]); psw = ld("s5_psw", [128, 128], cst["psw"])
        pw = P.sbuf("s5_pw", [128, nlev, 2, NG])
        P.op("dve", lambda e: e.tensor_copy(out=pw.ap[:, 0, 0, :], in_=arc.ap), reads=[arc], writes=[pw])
        ts(P, "dve", pw[:, 0, 1, :], aic, sgn[:, 0:1], ALU.mult)
        for k in range(1, nlev):
            a0 = pw[:, k - 1, 0, :]; s0 = pw[:, k - 1, 1, :]
            tt(P, "dve", g1, a0, a0, ALU.mult); tt(P, "dve", g2, s0, s0, ALU.mult)
            tt(P, "dve", pw[:, k, 0, :], g1, g2, ALU.subtract)
            tt(P, "dve", g1, a0, s0, ALU.mult)
            ts(P, "dve", pw[:, k, 1, :], g1, 2.0, ALU.mult)
        CT = ld("s5_CT", [128, NG, 16], prm["cT"])
        ts(P, "dve", CT[64:128], CT[64:128], -1.0, ALU.mult)
        dcol = ld("s5_dcol", [16, NG], prm["d_col"])
        Xa = P.sbuf("s5_Xa", [128, T]); Xb = P.sbuf("s5_Xb", [128, T])
        Xc = [[P.alias(Xa, "s5Xa%d" % j) for j in range(nch)], [P.alias(Xb, "s5Xb%d" % j) for j in range(nch)]]
        for g in range(NG):
            ug = P.ring("s5_u", 2, [16, T])
            P.dma("sp", ug, uT[g * 16:(g + 1) * 16, :])
            Mk = P.ring("s5_M", 2, [128, nlev, 128])
            for k in range(nlev):
                P.op("pool", lambda e, Mk=Mk, k=k, g=g: e.tensor_scalar(out=r32(Mk.ap[:, k, :]), in0=ident.ap, scalar1=pw.ap[:, k, 0, g:g + 1],
                     scalar2=0.0, op0=ALU.mult, op1=ALU.add), reads=[ident, pw], writes=[Mk])
                P.op("dve", lambda e, Mk=Mk, k=k, g=g: e.scalar_tensor_tensor(out=r32(Mk.ap[:, k, :]), in0=psw.ap, scalar=pw.ap[:, k, 1, g:g + 1],
                     in1=Mk.ap[:, k, :], op0=ALU.mult, op1=ALU.add), reads=[psw, pw, Mk], writes=[Mk])
            for j in range(nch):
                ps = S.next_psum(); sl = slice(j * CH, (j + 1) * CH)
                P.op("pe", lambda e, ps=ps, ug=ug, sl=sl, g=g: e.matmul(ps.ap, lhsT=BT.ap[:, g, :], rhs=ug.ap[:, sl], start=True, stop=True),
                     reads=[BT, ug], writes=[ps])
                P.op("act", lambda e, ps=ps, sl=sl: e.copy(out=r32(Xa.ap[:, sl]), in_=ps.ap), reads=[ps], writes=[Xc[0][j]])
            for k in range(nlev):
                d = 2 ** k
                src, dst = Xc[k % 2], Xc[(k + 1) % 2]
                sa, da = (Xa, Xb) if k % 2 == 0 else (Xb, Xa)
                for j in range(nch):
                    t0 = j * CH; t1 = t0 + CH
                    lo = max(t0, d)
                    if lo > t0:
                        hi = min(lo, t1)
                        P.op("pool", lambda e, sa=sa, da=da, t0=t0, hi=hi: e.tensor_copy(out=r32(da.ap[:, t0:hi]), in_=sa.ap[:, t0:hi]),
                             reads=[src[j]], writes=[dst[j]])
                    if lo >= t1:
                        continue
                    n = t1 - lo
                    s0, s1 = lo - d, t1 - d
                    rds = [src[c] for c in range(s0 // CH, (s1 - 1) // CH + 1)]
                    ps = S.next_psum()
                    P.op("pe", lambda e, ps=ps, Mk=Mk, k=k, sa=sa, s0=s0, s1=s1, n=n: e.matmul(ps.ap[:, 0:n], lhsT=(r32(Mk.ap[:, k, :]) if (n % 2 == 0 and s0 % 2 == 0) else Mk.ap[:, k, :]), rhs=(r32(sa.ap[:, s0:s1]) if (n % 2 == 0 and s0 % 2 == 0) else sa.ap[:, s0:s1]),
                         start=True, stop=True), reads=[Mk] + rds, writes=[ps])
                    P.op("dve", lambda e, ps=ps, sa=sa, da=da, lo=lo, t1=t1, n=n: e.tensor_tensor(out=r32(da.ap[:, lo:t1]), in0=ps.ap[:, 0:n], in1=sa.ap[:, lo:t1],
                         op=ALU.add), reads=[ps, src[j]], writes=[dst[j]])
            fin = Xc[nlev % 2]; fa = Xa if nlev % 2 == 0 else Xb
            og = P.ring("s5_o", 2, [16, T])
            for j in range(nch):
                ps = S.next_psum(); sl = slice(j * CH, (j + 1) * CH)
                P.op("pe", lambda e, ps=ps, sl=sl, g=g, fa=fa: e.matmul(ps.ap[0:16, :], lhsT=CT.ap[:, g, :], rhs=fa.ap[:, sl], start=True, stop=True),
                     reads=[CT, fin[j]], writes=[ps])
                P.op("dve", lambda e, ps=ps, sl=sl, g=g, ug=ug, og=og: e.scalar_tensor_tensor(out=og.ap[:, sl], in0=ug.ap[:, sl], scalar=dcol.ap[:, g:g + 1],
                     in1=ps.ap[0:16, :], op0=ALU.mult, op1=ALU.add), reads=[ug, dcol, ps], writes=[og])
            P.op("act", lambda e, og=og: e.activation(out=og.ap, in_=og.ap, func=AF.Gelu_apprx_tanh), reads=[og], writes=[og])
            P.dma("sp", yT[g * 16:(g + 1) * 16, :], og)


def emit_ssd(P, S, zT, xsT, bT, cT, dtT, prm, cst, yT, T, NH, NG, dbg=None):
    J = NH // NG
    NP = NH // 2
    TPG = NP // NG
    L = 128
    SC = CH
    nsc = T // SC
    ncs = SC // L
    with P.scope():
        ld = lambda name, shape, src, **kw: (lambda t: (P.dma("sp", t, src, **kw), t)[1])(P.sbuf(name, shape))
        ident = ld("sd_id", [128, 128], cst["ident"]); tri = ld("sd_tri", [128, 128], cst["tri"])
        ustr = ld("sd_us", [128, 128], cst["ustr"]); sel = ld("sd_sel", [NH, NH * 64], cst["sel"])
        ones = S.ones
        cwx = P.sbuf("sd_cwx", [128, NP, 4]); cwb = P.sbuf("sd_cwb", [128, NG, 4]); cwc = P.sbuf("sd_cwc", [128, NG, 4])
        for k in range(4):
            P.dma("sp", cwx[:, :, k], prm["cw_x"][k].re("(t p) -> p t", p=128), allow_slow_non_contiguous=True)
            P.dma("sp", cwb[:, :, k], prm["cw_b"][k].re("(t p) -> p t", p=128), allow_slow_non_contiguous=True)
            P.dma("sp", cwc[:, :, k], prm["cw_c"][k].re("(t p) -> p t", p=128), allow_slow_non_contiguous=True)
        colv = lambda nm, n: ld("sd_" + nm, [128, n], prm[nm].re("(t p) -> p t", p=128), allow_slow_non_contiguous=True)
        cbx = colv("cb_x", NP); cbb = colv("cb_b", NG); cbc = colv("cb_c", NG); dch = colv("d_ch", NP); ngc = colv("ng", NP)
        dtb = ld("sd_dtb", [NH, 1], prm["dt_bias"].re("(h o) -> h o", o=1)); alog = ld("sd_alog", [NH, 1], prm["a_log"].re("(h o) -> h o", o=1))
        negA = P.sbuf("sd_negA", [NH, 1])
        act(P, negA, alog, AF.Exp)
        ts(P, "dve", negA, negA, -1.0, ALU.mult)
        one12 = P.sbuf("sd_one", [128, 1]); P.op("dve", lambda e: e.memset(one12.ap, 1.0), writes=[one12])
        prevT = P.sbuf("sd_prev", [128, NH, 64])
        P.op("dve", lambda e: e.memset(prevT.ap, 0.0), writes=[prevT])

        def conv_silu(srcT, r0, t0, cw, cb, i, dst):
            xp = P.ring("sd_xp", 2, [128, SC + 3])
            if t0 == 0:
                P.op("pool", lambda e: e.memset(xp.ap[:, 0:3], 0.0), writes=[xp])
                P.dma("sp", xp[:, 3:SC + 3], srcT[r0:r0 + 128, 0:SC])
            else:
                P.dma("sp", xp, srcT[r0:r0 + 128, t0 - 3:t0 + SC])
            P.op("dve", lambda e: e.tensor_scalar(out=dst.ap, in0=xp.ap[:, 0:SC], scalar1=cw.ap[:, i, 0:1], scalar2=cb.ap[:, i:i + 1],
                 op0=ALU.mult, op1=ALU.add), reads=[xp, cw, cb], writes=[dst])
            for k in range(1, 4):
                P.op("dve", lambda e, k=k: e.scalar_tensor_tensor(out=dst.ap, in0=xp.ap[:, k:k + SC], scalar=cw.ap[:, i, k:k + 1], in1=dst.ap,
                     op0=ALU.mult, op1=ALU.add), reads=[xp, cw, dst], writes=[dst])
            act(P, dst, dst, AF.Silu)

        def sc_body(sc):
            t0 = sc * SC
            xsc = [P.ring("sd_xsc%d" % i, 1, [128, SC]) for i in range(NP)]
            Bc = [P.ring("sd_Bc%d" % g, 1, [128, SC]) for g in range(NG)]
            Cc = [P.ring("sd_Cc%d" % g, 1, [128, SC]) for g in range(NG)]
            for i in range(NP):
                conv_silu(xsT, i * 128, t0, cwx, cbx, i, xsc[i])
            for g in range(NG):
                conv_silu(bT, g * 128, t0, cwb, cbb, g, Bc[g])
                conv_silu(cT, g * 128, t0, cwc, cbc, g, Cc[g])
            dtv = P.ring("sd_dtv", 1, [NH, SC]); da = P.ring("sd_da", 1, [NH, SC]); acT = P.ring("sd_acT", 1, [NH, SC])
            dsT = P.ring("sd_dsT", 1, [NH, SC])
            P.dma("sp", dtv, dtT[:, t0:t0 + SC])
            act(P, dtv, dtv, AF.Exp, bias=dtb[:, 0:1])
            act(P, dtv, dtv, AF.Ln, bias=one12[0:NH, 0:1])
            ts(P, "dve", da, dtv, negA[:, 0:1], ALU.mult)
            for c in range(ncs):
                cs = slice(c * L, (c + 1) * L)
                P.op("dve", lambda e, cs=cs: e.tensor_tensor_scan(out=acT.ap[:, cs], data0=one12.ap[0:NH, 0:1].to_broadcast([NH, L]), data1=da.ap[:, cs],
                     initial=0.0, op0=ALU.mult, op1=ALU.add), reads=[da, one12], writes=[acT])
            for c in range(ncs):
                cs = slice(c * L, (c + 1) * L)
                act(P, dsT[:, cs], acT[:, cs], AF.Exp, scale=-1.0, bias=acT[:, (c + 1) * L - 1:(c + 1) * L])
            tt(P, "dve", dsT, dsT, dtv, ALU.mult)
            xT = [P.ring("sd_xT%d" % i, 1, [128, SC]) for i in range(NP)]
            xdT = [P.ring("sd_xdT%d" % i, 1, [128, SC]) for i in range(NP)]
            for i in range(NP):
                for (srcrow, dst) in ((dtv, xT[i]), (dsT, xdT[i])):
                    ps = S.next_psum()
                    P.op("pe", lambda e, ps=ps, srcrow=srcrow, i=i: e.matmul(ps.ap, lhsT=sel.ap[:, i * 128:(i + 1) * 128], rhs=srcrow.ap, start=True, stop=True),
                         reads=[sel, srcrow], writes=[ps])
                    tt(P, "dve", dst, ps, xsc[i], ALU.mult)
            ysb = [P.ring("sd_ysb%d" % i, 1, [128, SC]) for i in range(NP)]
            if dbg and sc == 0:
                P.dma("sp", dbg["xsc0"], xsc[0]); P.dma("sp", dbg["dtv"], dtv); P.dma("sp", dbg["acT"], acT); P.dma("sp", dbg["dsT"], dsT)
                P.dma("sp", dbg["xT0"], xT[0]); P.dma("sp", dbg["xdT0"], xdT[0]); P.dma("sp", dbg["Bc0"], Bc[0])
            def chunk_body(c):
                cs = slice(c * L, (c + 1) * L)
                xtok = P.ring("sd_xtok", 2, [128, NP * 128]); xdtok = P.ring("sd_xdtok", 2, [128, NP * 128])
                for (srcs, dst) in ((xT, xtok), (xdT, xdtok)):
                    for i0 in range(0, NP, 4):
                        ps = S.next_psum(); n = min(4, NP - i0)
                        for i in range(i0, i0 + n):
                            P.op("pe", lambda e, ps=ps, srcs=srcs, i=i, i0=i0: e.transpose(ps.ap[:, (i - i0) * 128:(i - i0 + 1) * 128], srcs[i].ap[:, cs], ident.ap),
                                 reads=[srcs[i], ident], writes=[ps])
                        P.op("act", lambda e, ps=ps, dst=dst, i0=i0, n=n: e.copy(out=dst.ap[:, i0 * 128:(i0 + n) * 128], in_=ps.ap[:, 0:n * 128]), reads=[ps], writes=[dst])
                btok = P.ring("sd_btok", 2, [128, NG * 128])
                ps = S.next_psum()
                for g in range(NG):
                    P.op("pe", lambda e, ps=ps, g=g: e.transpose(ps.ap[:, g * 128:(g + 1) * 128], Bc[g].ap[:, cs], ident.ap), reads=[Bc[g], ident], writes=[ps])
                P.op("act", lambda e, ps=ps, btok=btok: e.copy(out=btok.ap, in_=ps.ap[:, 0:NG * 128]), reads=[ps], writes=[btok])
                datok = P.ring("sd_datok", 2, [128, NH])
                ps = S.next_psum()
                P.op("pe", lambda e, ps=ps: e.transpose(ps.ap[:, 0:NH], da.ap[:, cs], ident.ap[0:NH, 0:NH]), reads=[da, ident], writes=[ps])
                P.op("act", lambda e, ps=ps, datok=datok: e.copy(out=datok.ap, in_=ps.ap[:, 0:NH]), reads=[ps], writes=[datok])
                SM = P.ring("sd_SM", 2, [128, NG, L])
                for g in range(NG):
                    ps = S.next_psum()
                    P.op("pe", lambda e, ps=ps, g=g: e.matmul(ps.ap[:, 0:L], lhsT=Bc[g].ap[:, cs], rhs=Cc[g].ap[:, cs], start=True, stop=True),
                         reads=[Bc[g], Cc[g]], writes=[ps])
                    P.op("dve", lambda e, ps=ps, g=g, SM=SM: e.tensor_tensor(out=SM.ap[:, g, :], in0=ps.ap[:, 0:L], in1=tri.ap, op=ALU.mult),
                         reads=[ps, tri], writes=[SM])
                Vt = P.ring("sd_V", 2, [128, NH, L])
                P.op("dve", lambda e, Vt=Vt, datok=datok: e.tensor_tensor(out=Vt.ap, in0=tri.ap.unsqueeze(1).to_broadcast([128, NH, L]),
                     in1=datok.ap.unsqueeze(2).to_broadcast([128, NH, L]), op=ALU.mult), reads=[tri, datok], writes=[Vt])
                E = P.ring("sd_E", 2, [128, NH, L]); EA = P.ring("sd_EA", 2, [128, NH, L])
                for (lh, dst) in ((ustr, E), (ones, EA)):
                    for h0 in range(0, NH, 4):
                        n = min(4, NH - h0)
                        ps = S.next_psum()
                        P.op("pe", lambda e, ps=ps, lh=lh, Vt=Vt, h0=h0, n=n: e.matmul(ps.ap[:, 0:n * L], lhsT=lh.ap, rhs=Vt.ap[:, h0:h0 + n, :], start=True, stop=True),
                             reads=[lh, Vt], writes=[ps])
                        P.op("act", lambda e, ps=ps, dst=dst, h0=h0, n=n: e.activation(out=dst.ap[:, h0:h0 + n, :], in_=ps.ap[:, 0:n * L], func=AF.Exp),
                             reads=[ps], writes=[dst])
                Cs = P.ring("sd_Cs", 2, [128, NH, L])
                for g in range(NG):
                    hs = slice(g * J, (g + 1) * J)
                    P.op("dve", lambda e, E=E, SM=SM, g=g, hs=hs: e.tensor_tensor(out=E.ap[:, hs, :], in0=E.ap[:, hs, :],
                         in1=SM.ap[:, g:g + 1, :].to_broadcast([128, J, L]), op=ALU.mult), reads=[E, SM], writes=[E])
                    P.op("pool", lambda e, EA=EA, Cs=Cs, g=g, hs=hs: e.tensor_tensor(out=Cs.ap[:, hs, :], in0=EA.ap[:, hs, :],
                         in1=Cc[g].ap[:, cs].unsqueeze(1).to_broadcast([128, J, L]), op=ALU.mult), reads=[EA, Cc[g]], writes=[Cs])
                if dbg and sc == 0 and c == 1:
                    P.dma("sp", dbg["xtok"], xtok); P.dma("sp", dbg["btok"], btok); P.dma("sp", dbg["datok"], datok); P.dma("sp", dbg["SM"], SM.re("p g l -> p (g l)"))
                    P.dma("sp", dbg["E"], E.re("p g l -> p (g l)")); P.dma("sp", dbg["EA"], EA.re("p g l -> p (g l)")); P.dma("sp", dbg["Cs"], Cs.re("p g l -> p (g l)"))
                    P.dma("sp", dbg["prev1"], prevT.re("p g l -> p (g l)"))
                for i0 in range(0, NP, 4):
                    n = min(4, NP - i0)
                    ps = S.next_psum()
                    for i in range(i0, i0 + n):
                        for hh in range(2):
                            h = 2 * i + hh
                            o = ps.ap[hh * 64:(hh + 1) * 64, (i - i0) * L:(i - i0 + 1) * L]
                            P.op("pe", lambda e, o=o, h=h, xtok=xtok, E=E: e.matmul(o, lhsT=xtok.ap[:, h * 64:(h + 1) * 64], rhs=E.ap[:, h, :], start=True, stop=False),
                                 reads=[xtok, E], writes=[ps])
                            P.op("pe", lambda e, o=o, h=h, Cs=Cs: e.matmul(o, lhsT=prevT.ap[:, h, :], rhs=Cs.ap[:, h, :], start=False, stop=True),
                                 reads=[prevT, Cs], writes=[ps])
                    for i in range(i0, i0 + n):
                        P.op("dve", lambda e, ps=ps, i=i, i0=i0: e.scalar_tensor_tensor(out=ysb[i].ap[:, cs], in0=xsc[i].ap[:, cs], scalar=dch.ap[:, i:i + 1],
                             in1=ps.ap[:, (i - i0) * L:(i - i0 + 1) * L], op0=ALU.mult, op1=ALU.add), reads=[xsc[i], dch, ps], writes=[ysb[i]])
                P.op("dve", lambda e, EA=EA: e.tensor_tensor(out=prevT.ap, in0=prevT.ap, in1=EA.ap[:, :, L - 1:L].to_broadcast([128, NH, 64]), op=ALU.mult),
                     reads=[prevT, EA], writes=[prevT])
                for g in range(NG):
                    ps = S.next_psum()
                    P.op("pe", lambda e, ps=ps, g=g, btok=btok, xdtok=xdtok: e.matmul(ps.ap[:, 0:J * 64], lhsT=btok.ap[:, g * 128:(g + 1) * 128],
                         rhs=xdtok.ap[:, g * J * 64:(g + 1) * J * 64], start=True, stop=True), reads=[btok, xdtok], writes=[ps])
                    P.op("dve", lambda e, ps=ps, g=g: e.tensor_tensor(out=prevT.ap[:, g * J:(g + 1) * J, :], in0=prevT.ap[:, g * J:(g + 1) * J, :],
                         in1=ps.ap[:, 0:J * 64].rearrange("p (j d) -> p j d", d=64), op=ALU.add), reads=[ps, prevT], writes=[prevT])
            for c in range(ncs):
                chunk_body(c)
            if dbg and sc == 0:
                P.dma("sp", dbg["ysb0"], ysb[0])
            for g in range(NG):
                pss = S.next_psum()
                for ii in range(TPG):
                    i = g * TPG + ii
                    zt = P.ring("sd_z", 2, [128, SC])
                    P.dma("sp", zt, zT[i * 128:(i + 1) * 128, t0:t0 + SC])
                    act(P, zt, zt, AF.Silu)
                    tt(P, "dve", ysb[i], ysb[i], zt, ALU.mult)
                    act(P, zt, ysb[i], AF.Square)
                    P.op("pe", lambda e, pss=pss, zt=zt, ii=ii: e.matmul(pss.ap, lhsT=ones.ap, rhs=zt.ap, start=(ii == 0), stop=(ii == TPG - 1)),
                         reads=[ones, zt], writes=[pss])
                rs = P.ring("sd_rs", 2, [128, SC])
                act(P, rs, pss, AF.Sqrt, scale=1.0 / (TPG * 128), bias=S.epsc)
                P.op("dve", lambda e, rs=rs: e.reciprocal(out=rs.ap, in_=rs.ap), reads=[rs], writes=[rs])
                for ii in range(TPG):
                    i = g * TPG + ii
                    P.op("dve", lambda e, i=i, rs=rs: e.scalar_tensor_tensor(out=ysb[i].ap, in0=ysb[i].ap, scalar=ngc.ap[:, i:i + 1], in1=rs.ap,
                         op0=ALU.mult, op1=ALU.mult), reads=[ysb[i], ngc, rs], writes=[ysb[i]])
                    P.dma("sp", yT[i * 128:(i + 1) * 128, t0:t0 + SC], ysb[i])

        for sc in range(nsc):
            sc_body(sc)
CH = 512


def emit_lru(P, S, xlT, glT, prm, yT, T, ntile):
    C = ntile * 128
    nch = T // CH
    with P.scope():
        cw = P.sbuf("l_cw", [128, ntile, 4])
        for k in range(4):
            P.dma("sp", cw[:, :, k], prm["conv_w"][k].re("(t p) -> p t", p=128), allow_slow_non_contiguous=True)
        cols = {}
        for nm in ("conv_b", "b_a", "b_x", "lam"):
            cols[nm] = P.sbuf("l_" + nm, [128, ntile])
            P.dma("sp", cols[nm], prm[nm].re("(t p) -> p t", p=128), allow_slow_non_contiguous=True)
        one = P.sbuf("l_one", [128, 1])
        P.op("dve", lambda e: e.memset(one.ap, 1.0), writes=[one])
        c1 = P.sbuf("l_c1", [128, ntile])
        P.op("act", lambda e: e.activation(out=c1.ap, in_=cols["lam"].ap, func=AF.Exp, scale=-1.0), reads=[cols["lam"]], writes=[c1])
        P.op("act", lambda e: e.activation(out=c1.ap, in_=c1.ap, func=AF.Ln, bias=one.ap), reads=[c1, one], writes=[c1])
        P.op("dve", lambda e: e.tensor_scalar(out=c1.ap, in0=c1.ap, scalar1=-8.0, scalar2=None, op0=ALU.mult), reads=[c1], writes=[c1])
        wa = P.sbuf("l_wa", [128, ntile, 128]); wx = P.sbuf("l_wx", [128, ntile, 128])
        P.dma("sp", wa, prm["wa_bd"].re("t p m -> p t m"))
        P.dma("sp", wx, prm["wx_bd"].re("t p m -> p t m"))
        for i in range(ntile):
            rows = slice(i * 128, (i + 1) * 128)
            xl = P.ring("l_xl", 1, [128, T + 3])
            P.op("pool", lambda e, xl=xl: e.memset(xl.ap[:, 0:3], 0.0), writes=[xl])
            P.dma("sp", xl[:, 3:T + 3], xlT[rows, :])
            gl = P.ring("l_gl", 1, [128, T])
            P.dma("sp", gl, glT[rows, :])
            xc = P.ring("l_xc", 1, [128, T])
            P.op("dve", lambda e, xl=xl, xc=xc, i=i: e.tensor_scalar(out=xc.ap, in0=xl.ap[:, 0:T], scalar1=cw.ap[:, i, 0:1],
                 scalar2=cols["conv_b"].ap[:, i:i + 1], op0=ALU.mult, op1=ALU.add), reads=[xl, cw, cols["conv_b"]], writes=[xc])
            for k in range(1, 4):
                P.op("dve", lambda e, xl=xl, xc=xc, i=i, k=k: e.scalar_tensor_tensor(out=xc.ap, in0=xl.ap[:, k:k + T],
                     scalar=cw.ap[:, i, k:k + 1], in1=xc.ap, op0=ALU.mult, op1=ALU.add), reads=[xl, cw, xc], writes=[xc])
            ga = P.ring("l_ga", 1, [128, T]); gi = P.ring("l_gi", 1, [128, T])
            for j in range(nch):
                sl = slice(j * CH, (j + 1) * CH)
                for (wm, bn, dst) in ((wa, "b_a", ga), (wx, "b_x", gi)):
                    ps = S.next_psum()
                    P.op("pe", lambda e, ps=ps, wm=wm, xc=xc, sl=sl, i=i: e.matmul(ps.ap, lhsT=wm.ap[:, i, :], rhs=xc.ap[:, sl], start=True, stop=True),
                         reads=[wm, xc], writes=[ps])
                    P.op("act", lambda e, ps=ps, dst=dst, bn=bn, sl=sl, i=i: e.activation(out=dst.ap[:, sl], in_=ps.ap, func=AF.Sigmoid,
                         bias=cols[bn].ap[:, i:i + 1]), reads=[ps, cols[bn]], writes=[dst])
            P.op("act", lambda e, ga=ga, i=i: e.activation(out=ga.ap, in_=ga.ap, func=AF.Exp, scale=c1.ap[:, i:i + 1]), reads=[ga, c1], writes=[ga])
            mu = P.ring("l_mu", 1, [128, T])
            P.op("pool", lambda e, ga=ga, mu=mu: e.tensor_tensor(out=mu.ap, in0=ga.ap, in1=ga.ap, op=ALU.mult), reads=[ga], writes=[mu])
            P.op("act", lambda e, mu=mu: e.activation(out=mu.ap, in_=mu.ap, func=AF.Sqrt, scale=-1.0, bias=one.ap), reads=[mu, one], writes=[mu])
            P.op("pool", lambda e, mu=mu: e.memset(mu.ap[:, 0:1], 1.0), reads=[mu], writes=[mu])
            P.op("dve", lambda e, gi=gi, xc=xc: e.tensor_tensor(out=gi.ap, in0=gi.ap, in1=xc.ap, op=ALU.mult), reads=[gi, xc], writes=[gi])
            P.op("pool", lambda e, gi=gi, mu=mu: e.tensor_tensor(out=gi.ap, in0=gi.ap, in1=mu.ap, op=ALU.mult), reads=[gi, mu], writes=[gi])
            P.op("dve", lambda e, gi=gi, ga=ga, xc=xc: e.tensor_tensor_scan(out=xc.ap, data0=ga.ap, data1=gi.ap, initial=0.0, op0=ALU.mult, op1=ALU.add),
                 reads=[ga, gi], writes=[xc])
            P.op("act", lambda e, gl=gl: e.activation(out=gl.ap, in_=gl.ap, func=AF.Gelu_apprx_tanh), reads=[gl], writes=[gl])
            P.op("dve", lambda e, gl=gl, xc=xc: e.tensor_tensor(out=xc.ap, in0=xc.ap, in1=gl.ap, op=ALU.mult), reads=[gl, xc], writes=[xc])
            P.dma("sp", yT[rows, :], xc)


def _tt(P, eng, out, a, b, op, r=False):
    o = rr(out.ap) if r else out.ap
    P.op(eng, lambda e: e.tensor_tensor(out=o, in0=a.ap, in1=b.ap, op=op), reads=[a, b], writes=[out])


def _ts(P, eng, out, a, s1, op0, s2=None, op1=None, r=False):
    rd = [a] + [x for x in (s1, s2) if isinstance(x, V)]
    g = lambda x: x.ap if isinstance(x, V) else x
    o = rr(out.ap) if r else out.ap
    if op1 is None:
        P.op(eng, lambda e: e.tensor_scalar(out=o, in0=a.ap, scalar1=g(s1), scalar2=None, op0=op0), reads=rd, writes=[out])
    else:
        P.op(eng, lambda e: e.tensor_scalar(out=o, in0=a.ap, scalar1=g(s1), scalar2=g(s2), op0=op0, op1=op1), reads=rd, writes=[out])


def _act(P, out, a, func, scale=None, bias=None):
    rd = [a] + [x for x in (scale, bias) if isinstance(x, V)]
    kw = {}
    if scale is not None:
        kw["scale"] = scale.ap if isinstance(scale, V) else scale
    if bias is not None:
        kw["bias"] = bias.ap if isinstance(bias, V) else bias
    P.op("act", lambda e: e.activation(out=out.ap, in_=a.ap, func=func, **kw), reads=rd, writes=[out])


def _stt(P, out, in0, scalar, in1, op0, op1):
    rd = [in0, in1] + ([scalar] if isinstance(scalar, V) else [])
    sc = scalar.ap if isinstance(scalar, V) else scalar
    P.op("dve", lambda e: e.scalar_tensor_tensor(out=out.ap, in0=in0.ap, scalar=sc, in1=in1.ap, op0=op0, op1=op1), reads=rd, writes=[out])


R32 = True
F32R = mybir.dt.float32r


def rr(ap):
    return ap.bitcast(F32R) if R32 else ap


def _mm(P, out, lhsT, rhs, start=True, stop=True, fast=False):
    if fast and R32:
        P.op("pe", lambda e: e.matmul(out.ap, lhsT=rr(lhsT.ap), rhs=rr(rhs.ap), start=start, stop=stop), reads=[lhsT, rhs], writes=[out])
    else:
        P.op("pe", lambda e: e.matmul(out.ap, lhsT=lhsT.ap, rhs=rhs.ap, start=start, stop=stop), reads=[lhsT, rhs], writes=[out])


def _tr(P, out, in_, ident):
    P.op("pe", lambda e: e.transpose(out.ap, in_.ap, ident.ap), reads=[in_, ident], writes=[out])


def emit_rwkv(P, S, rT, kT, vT, wlT, alT, glT, prm, cst, yT, T, NP, stage=99, dbg=None):
    C = 64
    SC = 256
    NH = NP * 2
    nsc = T // SC
    ncs = SC // C
    GN_EPS = 64e-5
    with P.scope():
        ld = lambda name, shape, src, **kw: (lambda t: (P.dma("sp", t, src, **kw), t)[1])(P.sbuf(name, shape))
        ident = ld("rw_id", [128, 128], cst["ident"]); bones = ld("rw_bo", [128, 128], cst["bones"])
        mask2 = ld("rw_m2", [64, 128], cst["mask2"]); maskL = ld("rw_mL", [64, 64], cst["maskL"]); rmask = ld("rw_rm", [128, SC], cst["rmask"])
        colv = lambda nm, n, rows=128: ld("rw_" + nm, [rows, n], prm[nm].re("(t p) -> p t", p=rows), allow_slow_non_contiguous=True)
        mu = {x: colv("mu_" + x, NP) for x in "rkv"}
        mu_wl = colv("mu_wl", 1, 96); mu_al = colv("mu_al", 1, 96); mu_gl = colv("mu_gl", 2)
        w0 = colv("w0", NP); a0 = colv("a0", NP); kkc = colv("k_k", NP); kac = colv("k_a", NP); rkc = colv("r_k", NP)
        lng = colv("ln_g", NP); lnb = colv("ln_b", NP)
        wup = ld("rw_wup", [96, NP * 128], prm["w_up"]); aup = ld("rw_aup", [96, NP * 128], prm["a_up"])
        gup = ld("rw_gup", [128, 2, NP * 128], prm["g_up"].re("(k p) n -> p k n", p=128))
        def one_minus(src, name):
            t = P.sbuf(name, list(src.shape))
            _ts(P, "dve", t, src, -1.0, ALU.mult, 1.0, ALU.add)
            return t
        imu = {x: one_minus(mu[x], "rw_imu" + x) for x in "rkv"}
        imu_wl = one_minus(mu_wl, "rw_imuwl"); imu_al = one_minus(mu_al, "rw_imual"); imu_gl = one_minus(mu_gl, "rw_imugl")
        ika = one_minus(kac, "rw_ika")
        gne = P.sbuf("rw_gne", [128, 1]); P.op("dve", lambda e: e.memset(gne.ap, GN_EPS), writes=[gne])
        Hst = P.sbuf("rw_H", [64, NH, 64])
        P.op("dve", lambda e: e.memset(Hst.ap, 0.0), writes=[Hst])

        def shift_mix(srcT, r0, nr, t0, muc, imuc, dst):
            xp = P.ring("rw_xp", 3, [128, SC + 1])
            if t0 == 0:
                P.op("pool", lambda e: e.memset(xp.ap[0:nr, 0:1], 0.0), writes=[xp])
                P.dma("sp", xp[0:nr, 1:SC + 1], srcT[r0:r0 + nr, 0:SC])
            else:
                P.dma("sp", xp[0:nr], srcT[r0:r0 + nr, t0 - 1:t0 + SC])
            tmp = P.ring("rw_smt", 2, [128, SC])
            P.op("act", lambda e: e.activation(out=tmp.ap[0:nr], in_=xp.ap[0:nr, 0:SC], func=AF.Copy, scale=muc.ap),
                 reads=[xp, muc], writes=[tmp])
            P.op("dve", lambda e: e.scalar_tensor_tensor(out=dst.ap, in0=xp.ap[0:nr, 1:SC + 1], scalar=imuc.ap, in1=tmp.ap[0:nr], op0=ALU.mult, op1=ALU.add),
                 reads=[xp, imuc, tmp], writes=[dst])

        def sc_body(sc):
            t0 = sc * SC
            tw = P.ring("rw_tw", 1, [96, SC]); al = P.ring("rw_al", 1, [96, SC]); sg = P.ring("rw_sg", 1, [128, 2, SC])
            shift_mix(wlT, 0, 96, t0, mu_wl[:, 0:1], imu_wl[:, 0:1], tw)
            _act(P, tw, tw, AF.Tanh)
            shift_mix(alT, 0, 96, t0, mu_al[:, 0:1], imu_al[:, 0:1], al)
            for kx in range(2):
                shift_mix(glT, kx * 128, 128, t0, mu_gl[:, kx:kx + 1], imu_gl[:, kx:kx + 1], sg[:, kx, :])
            _act(P, sg, sg, AF.Sigmoid)
            Gc = P.ring("rw_Gc", 1, [64, NH, ncs])
            KRo = []; bho = []; kho = []
            rp = []; vp = []; k2 = []; KR = []; bh = []; kh = []; gt = []; gC = []; bon = []
            for i in range(NP):
                cols = slice(i * 128, (i + 1) * 128)
                r_ = P.ring("rw_r%d" % i, 1, [128, SC]); k_ = P.ring("rw_k%d" % i, 1, [128, SC]); v_ = P.ring("rw_v%d" % i, 1, [128, SC])
                shift_mix(rT, i * 128, 128, t0, mu["r"][:, i:i + 1], imu["r"][:, i:i + 1], r_)
                shift_mix(kT, i * 128, 128, t0, mu["k"][:, i:i + 1], imu["k"][:, i:i + 1], k_)
                shift_mix(vT, i * 128, 128, t0, mu["v"][:, i:i + 1], imu["v"][:, i:i + 1], v_)
                lw = P.ring("rw_lw", 1, [128, SC]); a_ = P.ring("rw_a", 1, [128, SC]); g_ = P.ring("rw_g%d" % i, 1, [128, SC])
                ps = S.next_psum()
                _mm(P, ps[:, 0:SC], wup[:, cols], tw)
                _act(P, lw, ps[:, 0:SC], AF.Sigmoid, bias=w0[:, i:i + 1])
                _ts(P, "dve", lw, lw, -math.exp(-0.5), ALU.mult)
                ps = S.next_psum()
                _mm(P, ps[:, 0:SC], aup[:, cols], al)
                _act(P, a_, ps[:, 0:SC], AF.Sigmoid, bias=a0[:, i:i + 1])
                ps = S.next_psum()
                _mm(P, ps[:, 0:SC], gup[:, 0, cols], sg[:, 0, :], True, False)
                _mm(P, ps[:, 0:SC], gup[:, 1, cols], sg[:, 1, :], False, True)
                P.op("act", lambda e, g_=g_, ps=ps: e.copy(out=g_.ap, in_=ps.ap[:, 0:SC]), reads=[ps], writes=[g_])
                kap = P.ring("rw_kap", 1, [128, SC]); t1 = P.ring("rw_t1", 1, [128, SC]); t2 = P.ring("rw_t2", 1, [128, SC])
                _ts(P, "dve", kap, k_, kkc[:, i:i + 1], ALU.mult)
                _tt(P, "pool", t1, kap, kap, ALU.mult)
                ps = S.next_psum()
                _mm(P, ps[:, 0:SC], bones, t1)
                _ts(P, "dve", t1, ps[:, 0:SC], 1e-24, ALU.max)
                _act(P, t1, t1, AF.Sqrt)
                P.op("dve", lambda e, t1=t1: e.reciprocal(out=t1.ap, in_=t1.ap), reads=[t1], writes=[t1])
                _tt(P, "dve", kap, kap, t1, ALU.mult)
                k2_ = P.ring("rw_k2%d" % i, 1, [128, SC])
                _ts(P, "dve", t1, a_, kac[:, i:i + 1], ALU.mult, ika[:, i:i + 1], ALU.add)
                _tt(P, "dve", k2_, k_, t1, ALU.mult)
                bet = P.ring("rw_bet", 1, [128, SC])
                _tt(P, "pool", bet, kap, a_, ALU.mult)
                cl = P.ring("rw_cl", 1, [128, SC])
                P.op("dve", lambda e, cl=cl, lw=lw: e.tensor_tensor_scan(out=cl.ap, data0=rmask.ap, data1=lw.ap, initial=0.0, op0=ALU.mult, op1=ALU.add),
                     reads=[rmask, lw], writes=[cl])
                eG = P.ring("rw_eG", 1, [128, SC]); eN = P.ring("rw_eN", 1, [128, SC])
                _act(P, eG, cl, AF.Exp)
                _act(P, eN, cl, AF.Exp, scale=-1.0)
                _tt(P, "dve", t2, cl, lw, ALU.subtract)
                _act(P, t2, t2, AF.Exp)
                KR_ = P.ring("rw_KR%d" % i, 1, [128, ncs, 2, C])
                _tt(P, "dve", KR_[:, :, 0, :], kap.re("p (c t) -> p c t", t=C), t2.re("p (c t) -> p c t", t=C), ALU.mult, r=True)
                _tt(P, "pool", KR_[:, :, 1, :], r_.re("p (c t) -> p c t", t=C), eG.re("p (c t) -> p c t", t=C), ALU.mult, r=True)
                bh_ = P.ring("rw_bh%d" % i, 1, [128, SC]); kh_ = P.ring("rw_kh%d" % i, 1, [128, SC])
                _tt(P, "dve", bh_, bet, eN, ALU.mult, r=True)
                _tt(P, "pool", kh_, k2_, eN, ALU.mult, r=True)
                gC_ = P.ring("rw_gC%d" % i, 1, [128, ncs])
                P.op("act", lambda e, gC_=gC_, eG=eG: e.copy(out=gC_.ap, in_=eG.ap.rearrange("p (c t) -> p c t", t=C)[:, :, C - 1]), reads=[eG], writes=[gC_])
                bon_ = P.ring("rw_bon%d" % i, 1, [128, SC])
                _stt(P, t1, r_, rkc[:, i:i + 1], k2_, ALU.mult, ALU.mult)
                ps = S.next_psum()
                _mm(P, ps[:, 0:SC], bones, t1)
                _tt(P, "dve", bon_, ps[:, 0:SC], v_, ALU.mult)
                KRo_ = P.ring("rw_KRo%d" % i, 1, [64, ncs, 2, C]); bho_ = P.ring("rw_bho%d" % i, 1, [64, SC]); kho_ = P.ring("rw_kho%d" % i, 1, [64, SC])
                P.dma("sp", KRo_.bitcast(F32R) if R32 else KRo_, KR_[64:128].bitcast(F32R) if R32 else KR_[64:128])
                P.dma("sp", bho_.bitcast(F32R) if R32 else bho_, bh_[64:128].bitcast(F32R) if R32 else bh_[64:128])
                P.dma("sp", kho_.bitcast(F32R) if R32 else kho_, kh_[64:128].bitcast(F32R) if R32 else kh_[64:128])
                P.op("pool", lambda e, gC_=gC_, i=i: e.tensor_copy(out=Gc.ap[:, 2 * i, :], in_=gC_.ap[0:64, :]), reads=[gC_], writes=[Gc])
                P.dma("sp", Gc[:, 2 * i + 1, :], gC_[64:128, :])
                KRo.append(KRo_); bho.append(bho_); kho.append(kho_)
                rp.append(r_); vp.append(v_); k2.append(k2_); KR.append(KR_); bh.append(bh_); kh.append(kh_); gt.append(g_); gC.append(gC_); bon.append(bon_)
            ysb = [P.ring("rw_y%d" % i, 1, [128, SC]) for i in range(NP)]
            KRh = lambda h: KR[h // 2][0:64] if h % 2 == 0 else KRo[h // 2]
            bhh = lambda h: bh[h // 2][0:64] if h % 2 == 0 else bho[h // 2]
            khh = lambda h: kh[h // 2][0:64] if h % 2 == 0 else kho[h // 2]

            def prep_chunk(c):
                cs = slice(c * C, (c + 1) * C)
                toks = {}
                for nm, srcs in (("v", vp), ("b", bh), ("k", kh)):
                    ps = S.next_psum()
                    for i in range(NP):
                        _tr(P, ps[0:C, i * 128:(i + 1) * 128], srcs[i][:, cs], ident)
                    tk = P.ring("rw_tok" + nm, 2, [64, NP * 128])
                    P.op("act", lambda e, tk=tk, ps=ps: e.copy(out=rr(tk.ap), in_=ps.ap[0:C, 0:NP * 128]), reads=[ps], writes=[tk])
                    toks[nm] = tk
                psN = S.next_psum(); psB = S.next_psum(); psK = S.next_psum(); psB2 = S.next_psum(); psK2 = S.next_psum()
                nb = 4
                for h in range(NH):
                    kapc = KRh(h)[:, c, 0, :]; krc = KRh(h)[:, c, :, :].re("p a t -> p (a t)")
                    _mm(P, psN[0:C, h * C:(h + 1) * C], kapc, bhh(h)[:, cs], fast=True)
                    pb_ = psB if h < nb else psB2
                    pk_ = psK if h < nb else psK2
                    _mm(P, pb_[0:C, (h % nb) * 128:(h % nb + 1) * 128], bhh(h)[:, cs], krc, fast=True)
                    _mm(P, pk_[0:C, (h % nb) * 128:(h % nb + 1) * 128], khh(h)[:, cs], krc, fast=True)
                Q = P.ring("rw_Q", 2, [64, NH, C]); QT = P.ring("rw_QT", 2, [64, NH, C]); R = P.ring("rw_R", 2, [64, NH, C])
                AB = P.ring("rw_AB", 2, [64, NH, 2, C]); AK = P.ring("rw_AK", 2, [64, NH, 2, C])
                P.op("dve", lambda e, Q=Q: e.scalar_tensor_tensor(out=rr(Q.ap), in0=psN.ap[0:C, 0:NH * C].rearrange("p (h t) -> p h t", t=C), scalar=-1.0,
                     in1=maskL.ap.unsqueeze(1).to_broadcast([64, NH, C]), op0=ALU.mult, op1=ALU.mult), reads=[psN, maskL], writes=[Q])
                for (pp, dst, h0) in ((psB, AB, 0), (psB2, AB, nb), (psK, AK, 0), (psK2, AK, nb)):
                    n = min(nb, NH - h0)
                    if n <= 0:
                        continue
                    P.op("dve", lambda e, pp=pp, dst=dst, h0=h0, n=n: e.tensor_tensor(out=rr(dst.ap[:, h0:h0 + n].rearrange("p h a t -> p h (a t)")),
                         in0=pp.ap[0:C, 0:n * 128].rearrange("p (h x) -> p h x", x=128), in1=mask2.ap.unsqueeze(1).to_broadcast([64, n, 128]), op=ALU.mult),
                         reads=[pp, mask2], writes=[dst])
                _ts(P, "dve", QT, AB[:, :, 0, :], -1.0, ALU.mult, r=True)
                P.op("dve", lambda e, QT=QT: e.tensor_tensor(out=rr(R.ap), in0=QT.ap, in1=ident.ap[0:64, 0:64].unsqueeze(1).to_broadcast([64, NH, C]), op=ALU.add),
                         reads=[QT, ident], writes=[R])
                nlev = 6
                yield
                for lv in range(1, nlev):
                    if lv > 1:
                        yield
                    psQ = S.next_psum(); psQT = S.next_psum(); psR = S.next_psum()
                    Qn = P.ring("rw_Q", 2, [64, NH, C]); QTn = P.ring("rw_QT", 2, [64, NH, C])
                    last = (lv == nlev - 1)
                    for h in range(NH):
                        hsl = slice(h * C, (h + 1) * C)
                        _mm(P, psQ[0:C, hsl], QT[:, h, :], Q[:, h, :], fast=True)
                        if not last:
                            _mm(P, psQT[0:C, hsl], Q[:, h, :], QT[:, h, :], fast=True)
                    P.op("act", lambda e, Qn=Qn, psQ=psQ: e.copy(out=rr(Qn.ap.rearrange("p h t -> p (h t)")), in_=psQ.ap[0:C, 0:NH * C]), reads=[psQ], writes=[Qn])
                    if not last:
                        P.op("act", lambda e, QTn=QTn, psQT=psQT: e.copy(out=rr(QTn.ap.rearrange("p h t -> p (h t)")), in_=psQT.ap[0:C, 0:NH * C]), reads=[psQT], writes=[QTn])
                    for h in range(NH):
                        _mm(P, psR[0:C, h * C:(h + 1) * C], Qn[:, h, :], R[:, h, :], fast=True)
                    P.op("dve", lambda e, psR=psR: e.tensor_tensor(out=rr(R.ap.rearrange("p h t -> p (h t)")), in0=psR.ap[0:C, 0:NH * C],
                         in1=R.ap.rearrange("p h t -> p (h t)"), op=ALU.add), reads=[psR, R], writes=[R])
                    Q, QT = Qn, QTn
                if dbg and sc == 0 and c == 0:
                    P.dma("sp", dbg["R"], R.re("p h t -> p (h t)")); P.dma("sp", dbg["AB"], AB.re("p h a t -> p (h a t)")); P.dma("sp", dbg["AK"], AK.re("p h a t -> p (h a t)"))
                    P.dma("sp", dbg["vt"], toks["v"]); P.dma("sp", dbg["bt"], toks["b"])
                return dict(toks=toks, R=R, AB=AB, AK=AK)

            def seq_chunk(c, pc):
                cs = slice(c * C, (c + 1) * C)
                toks, R, AB, AK = pc["toks"], pc["R"], pc["AB"], pc["AK"]
                vt, bt, kt = toks["v"], toks["b"], toks["k"]
                psW = S.next_psum()
                Hr = P.ring("rw_Hr", 2, [64, NH, 64])
                P.op("act", lambda e: e.copy(out=rr(Hr.ap), in_=Hst.ap), reads=[Hst], writes=[Hr])
                for h in range(NH):
                    _mm(P, psW[0:C, h * 64:(h + 1) * 64], KRh(h)[:, c, 0, :], Hr[:, h, :], True, False, fast=True)
                    _mm(P, psW[0:C, h * 64:(h + 1) * 64], AK[:, h, 0, :], vt[:, h * 64:(h + 1) * 64], False, True, fast=True)
                Wsb = P.ring("rw_W", 2, [64, NH * 64])
                P.op("act", lambda e: e.copy(out=rr(Wsb.ap), in_=psW.ap[0:C, 0:NH * 64]), reads=[psW], writes=[Wsb])
                yield
                psU = S.next_psum()
                for h in range(NH):
                    _mm(P, psU[0:C, h * 64:(h + 1) * 64], R[:, h, :], Wsb[:, h * 64:(h + 1) * 64], fast=True)
                Usb = P.ring("rw_U", 2, [64, NH * 64])
                _ts(P, "dve", Usb, psU[0:C, 0:NH * 64], -1.0, ALU.mult, r=True)
                if dbg and sc == 0 and c == 0:
                    P.dma("sp", dbg["W"], Wsb); P.dma("sp", dbg["U"], Usb)
                yield
                psY = S.next_psum(); psH = S.next_psum()
                for h in range(NH):
                    i, hh = divmod(h, 2); pr = slice(hh * 64, (hh + 1) * 64)
                    oy = psY[pr, i * C:(i + 1) * C]
                    fy = (hh == 0)
                    _mm(P, oy, Hr[:, h, :], KRh(h)[:, c, 1, :], True, False, fast=fy)
                    _mm(P, oy, Usb[:, h * 64:(h + 1) * 64], AB[:, h, 1, :], False, False, fast=fy)
                    _mm(P, oy, vt[:, h * 64:(h + 1) * 64], AK[:, h, 1, :], False, True, fast=fy)
                for h in range(NH):
                    oh = psH[0:64, h * 64:(h + 1) * 64]
                    _mm(P, oh, bt[:, h * 64:(h + 1) * 64], Usb[:, h * 64:(h + 1) * 64], True, False, fast=True)
                    _mm(P, oh, kt[:, h * 64:(h + 1) * 64], vt[:, h * 64:(h + 1) * 64], False, True, fast=True)
                yield
                for i in range(NP):
                    P.op("act", lambda e, i=i: e.copy(out=ysb[i].ap[:, cs], in_=psY.ap[:, i * C:(i + 1) * C]), reads=[psY], writes=[ysb[i]])
                P.op("dve", lambda e: e.tensor_tensor(out=Hst.ap, in0=Hst.ap, in1=psH.ap[0:64, 0:NH * 64].rearrange("p (i v) -> p i v", v=64), op=ALU.add),
                     reads=[Hst, psH], writes=[Hst])
                P.op("dve", lambda e: e.tensor_tensor(out=Hst.ap, in0=Hst.ap, in1=Gc.ap[:, :, c:c + 1].to_broadcast([64, NH, 64]), op=ALU.mult),
                     reads=[Hst, Gc], writes=[Hst])

            if dbg and sc == 0:
                P.dma("sp", dbg["KR0"], KR[0].re("p c a t -> p (c a t)")); P.dma("sp", dbg["bh0"], bh[0]); P.dma("sp", dbg["kh0"], kh[0])
                P.dma("sp", dbg["KRo0"], KRo[0].re("p c a t -> p (c a t)")); P.dma("sp", dbg["Gc"], Gc.re("p h c -> p (h c)"))
            if stage == 1:
                for i in range(NP):
                    P.dma("sp", yT[i * 128:(i + 1) * 128, t0:t0 + SC], bon[i])
                return
            def run_all(g):
                try:
                    while True:
                        next(g)
                except StopIteration as ex:
                    return ex.value
            pcs = run_all(prep_chunk(0))
            for c in range(ncs):
                g1 = prep_chunk(c + 1) if c + 1 < ncs else None
                g2 = seq_chunk(c, pcs)
                nxt = None
                d1 = g1 is None
                d2 = False
                while not (d1 and d2):
                    if not d2:
                        try:
                            next(g2)
                        except StopIteration:
                            d2 = True
                    if not d1:
                        try:
                            next(g1)
                        except StopIteration as ex:
                            nxt = ex.value
                            d1 = True
                pcs = nxt
            if dbg and sc == 0:
                P.dma("sp", dbg["y0"], ysb[0]); P.dma("sp", dbg["H"], Hst.re("p h v -> p (h v)"))
            for i in range(NP):
                y = ysb[i]
                t1 = P.ring("rw_t1", 1, [128, SC]); t2 = P.ring("rw_t2", 1, [128, SC])
                ps = S.next_psum()
                _mm(P, ps[:, 0:SC], bones, y)
                _stt(P, y, ps[:, 0:SC], -1.0 / 64, y, ALU.mult, ALU.add)
                _tt(P, "pool", t1, y, y, ALU.mult)
                ps = S.next_psum()
                _mm(P, ps[:, 0:SC], bones, t1)
                _act(P, t2, ps[:, 0:SC], AF.Sqrt, scale=1.0 / 64, bias=gne)
                P.op("dve", lambda e, t2=t2: e.reciprocal(out=t2.ap, in_=t2.ap), reads=[t2], writes=[t2])
                _stt(P, y, y, lng[:, i:i + 1], t2, ALU.mult, ALU.mult)
                _stt(P, y, y, lnb[:, i:i + 1], bon[i], ALU.add, ALU.add)
                _tt(P, "dve", y, y, gt[i], ALU.mult)
                P.dma("sp", yT[i * 128:(i + 1) * 128, t0:t0 + SC], y)

        for sc in range(nsc):
            sc_body(sc)
import numpy as _np

NT_CORE = 2048
SEQ = 4096
NB = 4


def tile_w(W):
    K, N = W.shape
    NCB = (N + 127) // 128
    Wp = _np.zeros((K, NCB * 128), _np.float32)
    Wp[:, :N] = W
    return _np.ascontiguousarray(Wp.reshape(K // 128, 128, NCB, 128).transpose(2, 1, 0, 3))


def consts_all():
    ident = _np.eye(128, dtype=_np.float32)
    psw = _np.zeros((128, 128), _np.float32)
    for k in range(128):
        psw[k, (k + 64) % 128] = 1
    sgn = _np.ones((128, 1), _np.float32); sgn[64:] = -1
    s = _np.arange(128)
    tri = (s[:, None] <= s[None, :]).astype(_np.float32)
    ustr = (s[:, None] > s[None, :]).astype(_np.float32)
    sel = _np.zeros((12, 768), _np.float32)
    for h in range(12):
        sel[h, h * 64:(h + 1) * 64] = 1
    bones = _np.zeros((128, 128), _np.float32); bones[:64, :64] = 1; bones[64:, 64:] = 1
    s6 = _np.arange(64)
    mU = (s6[:, None] < s6[None, :]).astype(_np.float32); mUi = (s6[:, None] <= s6[None, :]).astype(_np.float32)
    mask2 = _np.concatenate([mU, mUi], 1)
    maskL = (s6[None, :] < s6[:, None]).astype(_np.float32)
    rmask = _np.ones((128, 256), _np.float32); rmask[:, ::64] = 0
    return dict(ident=ident, psw=psw, sgn=sgn, tri=tri, ustr=ustr, sel=sel, bones=bones, mask2=mask2, maskL=maskL, rmask=rmask)


def s5_host(lam_re, lam_im, log_step, b_re, b_im, c_re, c_im, d):
    G = lam_re.shape[0]
    rep = lambda a: _np.ascontiguousarray(_np.broadcast_to(a[None], (16,) + a.shape)).astype(_np.float32)
    col = lambda a: _np.ascontiguousarray(_np.concatenate([a.T, a.T], 0)).astype(_np.float32)
    ls = _np.broadcast_to(log_step[:, None], (G, 64))
    return dict(lr_row=rep(lam_re), li_row=rep(lam_im), ls_row=rep(ls),
                bT_re=_np.ascontiguousarray(b_re.transpose(2, 0, 1)), bT_im=_np.ascontiguousarray(b_im.transpose(2, 0, 1)),
                lr_col=col(lam_re), li_col=col(lam_im), ls_col=col(ls),
                cT=_np.ascontiguousarray(_np.concatenate([c_re.transpose(2, 0, 1), c_im.transpose(2, 0, 1)], 0)),
                d_col=_np.ascontiguousarray(d.reshape(G, 16).T))


def even_params(inp, hh):
    g0 = hh * 16
    p = {"s5_" + k: v for k, v in s5_host(inp["s5_lam_re"][0, g0:g0 + 16], inp["s5_lam_im"][0, g0:g0 + 16], inp["s5_log_step"][0, g0:g0 + 16],
                                          inp["s5_b_re"][0, g0:g0 + 16], inp["s5_b_im"][0, g0:g0 + 16], inp["s5_c_re"][0, g0:g0 + 16],
                                          inp["s5_c_im"][0, g0:g0 + 16], inp["s5_d"][0, hh * 256:(hh + 1) * 256]).items()}
    cw = inp["ssd_conv_w"][0]; cb = inp["ssd_conv_b"][0]
    xs = slice(hh * 768, (hh + 1) * 768); bs = slice(1536 + hh * 256, 1536 + (hh + 1) * 256); cs = slice(2048 + hh * 256, 2048 + (hh + 1) * 256)
    hs = slice(hh * 12, (hh + 1) * 12)
    c = lambda a: _np.ascontiguousarray(a, dtype=_np.float32)
    p.update(sd_cw_x=c(cw[:, xs]), sd_cb_x=c(cb[xs]), sd_cw_b=c(cw[:, bs]), sd_cb_b=c(cb[bs]), sd_cw_c=c(cw[:, cs]), sd_cb_c=c(cb[cs]),
             sd_dt_bias=c(inp["ssd_dt_bias"][0, hs]), sd_a_log=c(inp["ssd_a_log"][0, hs]),
             sd_d_ch=c(_np.repeat(inp["ssd_d"][0, hs], 64)), sd_ng=c(inp["ssd_norm"][0, xs]))
    return p


def lru_bd(w, hh):
    o = _np.zeros((4, 128, 128), _np.float32)
    for bl in range(8):
        t, h = divmod(bl, 2)
        o[t, h * 64:(h + 1) * 64, h * 64:(h + 1) * 64] = w[hh * 8 + bl]
    return o


def odd_params(inp, hh):
    c = lambda a: _np.ascontiguousarray(a, dtype=_np.float32)
    mu = inp["rwkv_mu"][0]
    ch = slice(hh * 512, (hh + 1) * 512)
    p = dict(rw_mu_r=c(mu[0:1024][ch]), rw_mu_k=c(mu[1024:2048][ch]), rw_mu_v=c(mu[2048:3072][ch]), rw_mu_wl=c(mu[3072:3168]), rw_mu_al=c(mu[3168:3264]),
             rw_mu_gl=c(mu[3264:3520]), rw_w0=c(inp["rwkv_w0"][0, ch]), rw_a0=c(inp["rwkv_a0"][0, ch]), rw_k_k=c(inp["rwkv_k_k"][0, ch]),
             rw_k_a=c(inp["rwkv_k_a"][0, ch]), rw_r_k=c(inp["rwkv_r_k"][0].reshape(-1)[ch]), rw_ln_g=c(inp["rwkv_ln_g"][0, ch]), rw_ln_b=c(inp["rwkv_ln_b"][0, ch]),
             rw_w_up=c(inp["rwkv_w_up"][0][:, ch]), rw_a_up=c(inp["rwkv_a_up"][0][:, ch]), rw_g_up=c(inp["rwkv_g_up"][0][:, ch]))
    p.update(lr_conv_w=c(inp["lru_conv_w"][0][:, ch]), lr_conv_b=c(inp["lru_conv_b"][0, ch]), lr_wa_bd=lru_bd(inp["lru_w_a"][0], hh), lr_wx_bd=lru_bd(inp["lru_w_x"][0], hh),
             lr_b_a=c(inp["lru_b_a"][0].reshape(-1)[ch]), lr_b_x=c(inp["lru_b_x"][0].reshape(-1)[ch]), lr_lam=c(inp["lru_lam"][0].reshape(-1)[ch]))
    return p


def dense_w(inp, L):
    c = lambda a: _np.ascontiguousarray(a, dtype=_np.float32)
    return dict(out_t=tile_w(inp["e_out_proj" if L == 0 else "o_out_proj"][0]), w1_t=tile_w(inp["mlp_w1"][L]), w2_t=tile_w(inp["mlp_w2"][L]),
                gate_t=tile_w(inp["pl_gate"][L]), plp_t=tile_w(inp["pl_proj"][L]), nffn=c(inp["norm_ffn"][L]), npl=c(inp["norm_pl"][L]))


class Launch:
    def __init__(self):
        self.nc = bass.Bass("TRN2", target_bir_lowering=False)
        self.st = contextlib.ExitStack()
        self.P = Prog(self.nc, self.st)
        self.S = Shared(self.P)
        self.outs = []

    def inp(self, name, arr):
        return self.P.dram(name, list(arr.shape), kind="ExternalInput")

    def inps(self, d, prefix=""):
        return {k: self.inp(prefix + k, v) for k, v in d.items()}

    def out(self, name, shape):
        v = self.P.dram(name, list(shape), kind="ExternalOutput")
        self.outs.append(v)
        return v

    def run(self, in_maps):
        self.P.wait_all("sp", self.outs)
        self.P.finish()
        self.st.close()
        res = run_bass_kernel_spmd(self.nc, in_maps, core_ids=list(range(len(in_maps))))
        return res.results


def strip(d, prefix):
    return {k[len(prefix):]: v for k, v in d.items() if k.startswith(prefix)}


PAIRS = [[0, 1], [2, 3], [4, 5], [6, 7]]
ALL8 = [list(range(8))]


def pad_cols(W, n):
    o = _np.zeros((W.shape[0], n), _np.float32)
    o[:, :W.shape[1]] = W
    return o


def host_inputs(inp):
    f32c = lambda a: _np.ascontiguousarray(a, dtype=_np.float32)
    x = inp["x"]; p = inp["p"]
    cst = consts_all()
    ein = inp["e_in_proj"][0]; oin = inp["o_in_proj"][0]
    eout = inp["e_out_proj"][0]; oout = inp["o_out_proj"][0]
    shared = {}
    for L in range(2):
        shared["w1_%d" % L] = tile_w(inp["mlp_w1"][L]); shared["w2_%d" % L] = tile_w(inp["mlp_w2"][L])
        shared["gate_%d" % L] = tile_w(inp["pl_gate"][L]); shared["plp_%d" % L] = tile_w(inp["pl_proj"][L])
    per_hh = []
    for hh in range(2):
        d = {}
        cols0 = _np.concatenate([ein[:, hh * 256:(hh + 1) * 256], ein[:, 512 + hh * 768:512 + (hh + 1) * 768], ein[:, 2048 + hh * 768:2048 + (hh + 1) * 768],
                                 ein[:, 3584 + hh * 256:3584 + (hh + 1) * 256], ein[:, 4096 + hh * 256:4096 + (hh + 1) * 256],
                                 pad_cols(ein[:, 4608 + hh * 12:4608 + (hh + 1) * 12], 128)], 1)
        d["win0_t"] = tile_w(cols0)
        ch = lambda o: oin[:, o + hh * 512:o + (hh + 1) * 512]
        cols1 = _np.concatenate([ch(0), ch(1024), ch(2048), pad_cols(oin[:, 3072:3168], 128), pad_cols(oin[:, 3168:3264], 128), oin[:, 3264:3520], ch(3520), ch(4544)], 1)
        d["win1_t"] = tile_w(cols1)
        d["wout0_t"] = tile_w(_np.concatenate([eout[hh * 256:(hh + 1) * 256], eout[512 + hh * 768:512 + (hh + 1) * 768]], 0))
        d["wout1_t"] = tile_w(_np.concatenate([oout[hh * 512:(hh + 1) * 512], oout[1024 + hh * 512:1024 + (hh + 1) * 512]], 0))
        d["gluw_t"] = tile_w(inp["s5_glu_w"][0][hh * 256:(hh + 1) * 256])
        d["glub"] = f32c(inp["s5_glu_b"][0][hh * 256:(hh + 1) * 256])
        d.update(even_params(inp, hh)); d.update(odd_params(inp, hh))
        per_hh.append(d)
    ins = []
    for c in range(8):
        b, hh = divmod(c, 2)
        ts_ = slice(hh * NT_CORE, (hh + 1) * NT_CORE)
        d = dict(xT=f32c(x[b, ts_, :].T), pT0=f32c(p[0, b, ts_, :].T), pT1=f32c(p[1, b, ts_, :].T))
        for k, v in shared.items():
            d[k + "_t"] = v
        xf = x[b].T.reshape(8, 256, 2, NT_CORE).transpose(0, 2, 1, 3)
        d["xfull"] = f32c(xf)
        for L in range(2):
            d["nmix%d" % L] = f32c(inp["norm_mix"][L]); d["nffn%d" % L] = f32c(inp["norm_ffn"][L]); d["npl%d" % L] = f32c(inp["norm_pl"][L])
        d["nfin"] = f32c(inp["norm_final"])
        d.update(per_hh[hh]); d.update({"c_" + k: v for k, v in cst.items()})
        ins.append(d)
    return ins


def build_fused(ex):
    la = Launch()
    P, S = la.P, la.S
    dd = la.inps(ex)
    cs_d = strip(dd, "c_")
    T = SEQ; NT = NT_CORE
    hfull0 = dd["xfull"]
    Wf = [{}, {}]
    for L in range(2):
        for nm in ("w1", "w2", "gate", "plp"):
            Wf[L][nm + "_t"] = dd["%s_%d_t" % (nm, L)]
        Wf[L]["nffn"] = dd["nffn%d" % L]; Wf[L]["npl"] = dd["npl%d" % L]

    def rs_mix(mp, mix):
        for q in range(4):
            P.collective("ReduceScatter", mix[q * 512:(q + 1) * 512, :], mp[q].re("s f t -> (s f) t"), PAIRS, op=ALU.add)
    pin0 = P.dram("pin0", [19 * 128, T])
    for s in range(2):
        with P.scope():
            emit_dense_in(P, S, hfull0[:, s], dd["nmix0"], dd["win0_t"], 19, None, pin0[:, s * NT:(s + 1) * NT], NT)
    yT0 = P.dram("yT0", [1024, T])
    emit_s5(P, S, pin0[0:256], strip(dd, "s5_"), cs_d, yT0[0:256], T, 16)
    emit_ssd(P, S, pin0[256:1024], pin0[1024:1792], pin0[1792:2048], pin0[2048:2304], pin0[2304:2316], strip(dd, "sd_"), cs_d, yT0[256:1024], T, 12, 2)
    zp = P.dram("zp", [2, 256, T]); zr = P.dram("zr", [256, T])
    emit_glu_partial(P, S, yT0[0:256], dd["gluw_t"], zp, T)
    P.collective("ReduceScatter", zr, zp.re("s r t -> (s r) t"), PAIRS, op=ALU.add)
    mp0 = P.dram("mp0", [4, 2, 512, NT]); mix0 = P.dram("mix0", [2048, NT])
    emit_outproj_partial(P, S, yT0, dd["wout0_t"], mp0, T, glu=(zr, dd["glub"]))
    rs_mix(mp0, mix0)
    hb1 = P.dram("hb1", [2048, NT]); h1d0 = P.dram("h1d0", [2048, NT])
    emit_mlp_gate_v2(P, S, 0, dd["xT"], mix0, dd["pT0"], hb1, NT, Wf[0], h1d0)
    hfull1 = P.dram("hfull1", [8, 2, 256, NT])
    for q in range(8):
        P.collective("AllGather", hfull1[q].re("s f t -> (s f) t"), hb1[q * 256:(q + 1) * 256, :], PAIRS)
    pin1 = P.dram("pin1", [24 * 128, T])
    for s in range(2):
        with P.scope():
            emit_dense_in(P, S, hfull1[:, s], dd["nmix1"], dd["win1_t"], 24, None, pin1[:, s * NT:(s + 1) * NT], NT)
    yT1 = P.dram("yT1", [1024, T])
    emit_rwkv(P, S, pin1[0:512], pin1[512:1024], pin1[1024:1536], pin1[1536:1632], pin1[1664:1760], pin1[1792:2048], strip(dd, "rw_"), cs_d, yT1[0:512], T, 4)
    emit_lru(P, S, pin1[2048:2560], pin1[2560:3072], strip(dd, "lr_"), yT1[512:1024], T, 4)
    mp1 = P.dram("mp1", [4, 2, 512, NT]); mix1 = P.dram("mix1", [2048, NT])
    emit_outproj_partial(P, S, yT1, dd["wout1_t"], mp1, T)
    rs_mix(mp1, mix1)
    oT = la.out("outT", [2048, NT])
    h1d1 = P.dram("h1d1", [2048, NT])
    emit_mlp_gate_v2(P, S, 1, hb1, mix1, dd["pT1"], None, NT, Wf[1], h1d1, final=dd["nfin"], outT=oT)
    return la


def kernel(**inp):
    inp = {k: _np.asarray(v) for k, v in inp.items()}
    ins = host_inputs(inp)
    la = build_fused(ins[0])
    res = la.run(ins)
    out = _np.zeros((NB, SEQ, 2048), _np.float32)
    for c in range(8):
        b, hh = divmod(c, 2)
        out[b, hh * NT_CORE:(hh + 1) * NT_CORE, :] = res[c]["outT"].T
    return out
```
